# Optimizing a Trainium2 kernel written in Bass

```python
import jax, jax.numpy as jnp
from jax import lax
import numpy as np

D_MODEL = 1024
BATCH = 16
SEQ = 4096
DEPTH = 1
DEC_BATCH = 32
DEC_SEQ = 64
PAST_LEN = 1024

CHUNK = 64
SUB = 16
N_SUB = CHUNK // SUB
GLA_HEADS = 4
GLA_V = D_MODEL // 2
GLA_K = GLA_V // 2
GLA_DK = GLA_K // GLA_HEADS
GLA_DV = GLA_V // GLA_HEADS
GATE_RANK = 16
GATE_TAU = 16.0
HGRN_HEADS = 4
HGRN_W = D_MODEL - GLA_V
HGRN_D = HGRN_W // HGRN_HEADS
D_MIX = GLA_V + HGRN_W
IN_SPLITS = (GLA_K, GLA_K, GLA_V, GLA_V, GATE_RANK, HGRN_W, HGRN_W, HGRN_W, HGRN_W)
IN_COLS = sum(IN_SPLITS)
EPS = 1e-6

kernel_name = 'hymba_gla_hgrn2_stream_step'


def rms_norm(x, g):
    xf = x.astype(jnp.float32)
    y = xf * lax.rsqrt(jnp.mean(xf * xf, axis=-1, keepdims=True) + EPS)
    return (y * g.astype(jnp.float32)).astype(x.dtype)


def chunked_gla(q, k, v, g, s0):
    B, T, H, K = q.shape
    V = v.shape[-1]
    out_dtype = v.dtype
    pad = (-T) % CHUNK
    n = (T + pad) // CHUNK

    def to_chunks(a):
        a = jnp.pad(a.astype(jnp.float32), ((0, 0), (0, pad), (0, 0), (0, 0)))
        return a.reshape(B, n, CHUNK, H, a.shape[-1]).transpose(1, 0, 3, 2, 4)

    pos = jnp.arange(CHUNK)
    off_mask = pos[None, None, :] < (jnp.arange(N_SUB) * SUB)[:, None, None]
    diag_mask = jnp.tril(jnp.ones((SUB, SUB), dtype=bool))[:, :, None]

    def step(S, inp):
        qc, kc, vc, gc = inp
        b = jnp.cumsum(gc, axis=2)
        o_inter = jnp.einsum('bhck,bhkv->bhcv', qc * jnp.exp(b), S)
        qs = qc.reshape(B, H, N_SUB, SUB, K)
        ks = kc.reshape(B, H, N_SUB, SUB, K)
        vs = vc.reshape(B, H, N_SUB, SUB, V)
        bs = b.reshape(B, H, N_SUB, SUB, K)
        r = bs[:, :, :, 0] - gc.reshape(B, H, N_SUB, SUB, K)[:, :, :, 0]
        q_off = qs * jnp.exp(bs - r[:, :, :, None])
        k_off = kc[:, :, None] * jnp.exp(jnp.minimum(r[:, :, :, None] - b[:, :, None], 0.0))
        a_off = jnp.where(off_mask, jnp.einsum('bhnlk,bhnsk->bhnls', q_off, k_off), 0.0)
        diff = bs[:, :, :, :, None] - bs[:, :, :, None]
        decay = jnp.exp(jnp.where(diag_mask, diff, -jnp.inf))
        a_diag = jnp.einsum('bhntsk,bhnsk->bhnts', qs[:, :, :, :, None] * decay, ks)
        o_intra = (jnp.einsum('bhnls,bhsv->bhnlv', a_off, vc)
                   + jnp.einsum('bhnts,bhnsv->bhntv', a_diag, vs))
        b_last = b[:, :, -1]
        S_new = (S * jnp.exp(b_last)[..., None]
                 + jnp.einsum('bhck,bhcv->bhkv', kc * jnp.exp(b_last[:, :, None] - b), vc))
        return S_new, o_inter + o_intra.reshape(B, H, CHUNK, V)

    S, o = lax.scan(step, s0.astype(jnp.float32),
                    (to_chunks(q), to_chunks(k), to_chunks(v), to_chunks(g)))
    o = o.transpose(1, 0, 3, 2, 4).reshape(B, n * CHUNK, H, V)[:, :T]
    return o.astype(out_dtype), S


def mixer_layer(x, c, s_gla, s_hgrn, w_ada, b_ada, g_pre, w_in, w_alpha, b_alpha,
                g_on_gla, lb, g_on_hgrn, w_out, g_post):
    B, T, _ = x.shape
    mod = jnp.einsum('bd,de->be', c, w_ada) + b_ada
    shift, scale, gate = jnp.split(mod, 3, axis=-1)
    h = rms_norm(x, g_pre) * (1 + scale[:, None]) + shift[:, None]
    p = jnp.einsum('btd,de->bte', h, w_in)
    q_a, k_a, v_a, z_a, a_lr, q_h, f_h, i_h, z_h = jnp.split(
        p, np.cumsum(IN_SPLITS)[:-1].tolist(), axis=-1)

    log_alpha = jax.nn.log_sigmoid(
        (jnp.einsum('btr,rk->btk', a_lr, w_alpha) + b_alpha).astype(jnp.float32)) / GATE_TAU
    o_a, s_a = chunked_gla(
        q_a.reshape(B, T, GLA_HEADS, GLA_DK) * GLA_DK ** -0.5,
        k_a.reshape(B, T, GLA_HEADS, GLA_DK),
        v_a.reshape(B, T, GLA_HEADS, GLA_DV),
        log_alpha.reshape(B, T, GLA_HEADS, GLA_DK), s_gla)

    f = lb + (1.0 - lb) * jax.nn.sigmoid(f_h.astype(jnp.float32))
    o_h, s_h = chunked_gla(
        jax.nn.silu(q_h).reshape(B, T, HGRN_HEADS, HGRN_D),
        (1.0 - f).reshape(B, T, HGRN_HEADS, HGRN_D),
        i_h.reshape(B, T, HGRN_HEADS, HGRN_D),
        jnp.log(f).reshape(B, T, HGRN_HEADS, HGRN_D), s_hgrn)

    o_a = rms_norm(o_a, g_on_gla).reshape(B, T, GLA_V) * jax.nn.silu(z_a)
    o_h = rms_norm(o_h, g_on_hgrn).reshape(B, T, HGRN_W) * jax.nn.silu(z_h)
    o = jnp.einsum('bte,ed->btd', jnp.concatenate([o_a, o_h], axis=-1), w_out)
    y = x + gate[:, None] * rms_norm(o, g_post)
    return y, s_a, s_h


def setup_inputs(seed: int = 0) -> dict:
    key = jax.random.key(seed)
    ks = jax.random.split(key, 20)
    nrm = jax.random.normal
    f32 = jnp.float32
    return {
        'x_prompt': nrm(ks[0], (BATCH, SEQ, D_MODEL), f32),
        'x_sample': nrm(ks[1], (DEC_BATCH, DEC_SEQ, D_MODEL), f32),
        'c_prompt': nrm(ks[2], (BATCH, D_MODEL), f32),
        'c_sample': nrm(ks[3], (DEC_BATCH, D_MODEL), f32),
        'state_gla': 0.3 * nrm(ks[4], (DEPTH, DEC_BATCH, GLA_HEADS, GLA_DK, GLA_DV), f32),
        'state_hgrn': 0.3 * nrm(ks[5], (DEPTH, DEC_BATCH, HGRN_HEADS, HGRN_D, HGRN_D), f32),
        'w_ada': 0.5 * D_MODEL ** -0.5 * nrm(ks[6], (DEPTH, D_MODEL, 3 * D_MODEL), f32),
        'b_ada': 0.02 * nrm(ks[7], (DEPTH, 3 * D_MODEL), f32),
        'g_pre': 1.0 + 0.05 * nrm(ks[8], (DEPTH, D_MODEL), f32),
        'w_in': D_MODEL ** -0.5 * nrm(ks[9], (DEPTH, D_MODEL, IN_COLS), f32),
        'w_alpha': GATE_RANK ** -0.5 * nrm(ks[10], (DEPTH, GATE_RANK, GLA_K), f32),
        'b_alpha': 0.1 * nrm(ks[11], (DEPTH, GLA_K), f32),
        'g_onorm_gla': 1.0 + 0.05 * nrm(ks[12], (DEPTH, GLA_DV), f32),
        'hgrn_lb_logits': 0.3 * nrm(ks[13], (DEPTH + 1, HGRN_W), f32),
        'g_onorm_hgrn': 1.0 + 0.05 * nrm(ks[14], (DEPTH, HGRN_D), f32),
        'w_out': D_MIX ** -0.5 * nrm(ks[15], (DEPTH, D_MIX, D_MODEL), f32),
        'g_post': 1.0 + 0.05 * nrm(ks[16], (DEPTH, D_MODEL), f32),
    }


def reference(x_prompt, x_sample, c_prompt, c_sample, state_gla, state_hgrn, w_ada, b_ada, g_pre,
              w_in, w_alpha, b_alpha, g_onorm_gla, hgrn_lb_logits, g_onorm_hgrn, w_out, g_post):
    lower_bounds = jnp.cumsum(jax.nn.softmax(hgrn_lb_logits.astype(jnp.float32), axis=0), axis=0)
    bp = x_prompt.shape[0]
    zero_gla = jnp.zeros((bp, GLA_HEADS, GLA_DK, GLA_DV), jnp.float32)
    zero_hgrn = jnp.zeros((bp, HGRN_HEADS, HGRN_D, HGRN_D), jnp.float32)
    yp, ys = x_prompt, x_sample
    gla_p, hgrn_p, gla_s, hgrn_s = [], [], [], []
    for l in range(DEPTH):
        wl = (w_ada[l], b_ada[l], g_pre[l], w_in[l], w_alpha[l], b_alpha[l], g_onorm_gla[l],
              lower_bounds[l], g_onorm_hgrn[l], w_out[l], g_post[l])
        yp, sa, sh = mixer_layer(yp, c_prompt, zero_gla, zero_hgrn, *wl)
        gla_p.append(sa)
        hgrn_p.append(sh)
        ys, sa, sh = mixer_layer(ys, c_sample, state_gla[l], state_hgrn[l], *wl)
        gla_s.append(sa)
        hgrn_s.append(sh)
    new_gla_prompt = jnp.stack(gla_p).astype(x_prompt.dtype)
    new_hgrn_prompt = jnp.stack(hgrn_p).astype(x_prompt.dtype)
    new_gla_sample = jnp.stack(gla_s).astype(state_gla.dtype)
    new_hgrn_sample = jnp.stack(hgrn_s).astype(state_hgrn.dtype)
    return (yp, ys, new_gla_prompt, new_hgrn_prompt, new_gla_sample, new_hgrn_sample)
```

```python
from contextlib import ExitStack

import numpy as np
import concourse.bass as bass
import concourse.mybir as mybir
from concourse.bass_utils import run_bass_kernel_spmd

F32 = mybir.dt.float32
BF16 = mybir.dt.bfloat16
AF = mybir.ActivationFunctionType
ALU = mybir.AluOpType
AX = mybir.AxisListType

D = 1024
NCOL = 3600
EPS = 1e-6
N_CORES = 8
SEQ = 4096
ENGS = ("sync", "scalar", "vector", "gpsimd", "tensor")

C_QA, C_KA, C_VA, C_ZA, C_AL, C_QH, C_FH, C_IH, C_ZH = 0, 256, 512, 1024, 1536, 1552, 2064, 2576, 3088


class Prog:
    def __init__(self, nc):
        self.nc = nc
        self.q = {e: [] for e in ENGS}
        self.sems = {}
        self.waited = {e: {} for e in ENGS}
        self.bufs = {}
        self.bank_last = {}
        self._cms = []
        import os as _os
        self.limit = int(_os.environ.get("KLIMIT", "0")) or None
        self.count = 0

    def sem(self, name):
        if name not in self.sems:
            cm = self.nc.semaphore(name)
            h = cm.__enter__()
            self._cms.append(cm)
            self.sems[name] = [h, 0]
        return self.sems[name]

    def close(self):
        for cm in reversed(self._cms):
            cm.__exit__(None, None, None)

    def _deps(self, eng, reads, writes, is_dma, banks):
        deps = []
        for b in banks:
            t = self.bank_last.get(b)
            if t is not None and t[2] != eng:
                deps.append((t, "bank"))
        for b in reads:
            st = self.bufs.get(b)
            if st and st[0] is not None:
                deps.append((st[0], "raw"))
        for b in writes:
            st = self.bufs.get(b)
            if st:
                if st[0] is not None:
                    deps.append((st[0], "waw"))
                for t in st[1]:
                    deps.append((t, "war"))
        waits = []
        for tok, kind in deps:
            sname, val, teng, tdma = tok
            if not tdma and teng == eng and not is_dma:
                if eng == "tensor" or kind in ("war", "waw"):
                    continue
            if self.waited[eng].get(sname, 0) >= val:
                continue
            self.waited[eng][sname] = val
            waits.append((sname, val))
        return waits

    def op(self, eng, fns, reads=(), writes=(), dma_sem=None, banks=()):
        if callable(fns):
            fns = [fns]
        self.count += 1
        if self.limit is not None and self.count > self.limit:
            return None
        is_dma = dma_sem is not None
        waits = self._deps(eng, reads, writes, is_dma, banks)
        if is_dma:
            s = self.sem(dma_sem)
            s[1] += 16
            tok = (dma_sem, s[1], eng, True)
            inc = (dma_sem, 16)
        else:
            sname = "p_" + eng
            s = self.sem(sname)
            s[1] += 1
            tok = (sname, s[1], eng, False)
            inc = (sname, 1)
        self.q[eng].append((waits, fns, inc))
        for b in banks:
            self.bank_last[b] = tok
        for b in writes:
            self.bufs[b] = [tok, []]
        for b in reads:
            if b in writes:
                continue
            self.bufs.setdefault(b, [None, []])[1].append(tok)
        return tok

    def retoken(self, names, tok):
        for b in names:
            self.bufs[b] = [tok, []]

    def wait_token(self, eng, tok):
        sname, val = tok[0], tok[1]
        if self.waited[eng].get(sname, 0) >= val:
            return
        self.waited[eng][sname] = val
        self.q[eng].append(([(sname, val)], [], None))

    def barrier(self):
        snap = [(n, s[1]) for n, s in self.sems.items() if s[1] > 0]
        for e in ENGS:
            w = []
            for n, v in snap:
                if self.waited[e].get(n, 0) < v:
                    self.waited[e][n] = v
                    w.append((n, v))
            if w:
                self.q[e].append((w, [], None))

    def replay(self, block):
        P = self

        def run(engobj, name):
            for waits, fns, inc in P.q[name]:
                for sname, val in waits:
                    engobj.wait_ge(P.sems[sname][0], val)
                ins = None
                for f in fns:
                    ins = f(engobj)
                if inc is not None and ins is not None:
                    ins.then_inc(P.sems[inc[0]][0], inc[1])

        @block.sync
        def _(e):
            run(e, "sync")

        @block.scalar
        def _(e):
            run(e, "scalar")

        @block.vector
        def _(e):
            run(e, "vector")

        @block.gpsimd
        def _(e):
            run(e, "gpsimd")

        @block.tensor
        def _(e):
            run(e, "tensor")


def v3(ap, t=128):
    return ap.rearrange("p (c t) -> p c t", t=t)


def build(tp_tiles):
    TP = tp_tiles * 128
    nc = bass.Bass("TRN2", target_bir_lowering=False)

    def din(name, shape):
        return nc.dram_tensor(name, shape, F32, kind="ExternalInput").ap()

    def dout(name, shape):
        return nc.dram_tensor(name, shape, F32, kind="ExternalOutput").ap()

    xp = din("xp", [2, TP, D])
    xs = din("xs", [4, 64, D])
    cT_d = din("cT", [128, 8, 6])
    stg = din("stg", [4, 2, 128, 128])
    sth = din("sth", [4, 4, 128, 128])
    wada = din("wada", [128, 8, 3072])
    bada = din("bada", [6, 3072])
    gpre_d = din("gpre", [128, 8])
    win = din("win", [128, 8, NCOL])
    walpha_d = din("walpha", [16, 256])
    balpha_d = din("balpha", [128, 2])
    gon_d = din("gon", [128, 2])
    lbl_d = din("lbl", [128, 2, 4])
    wout = din("wout", [128, 8, D])
    gpost_d = din("gpost", [128, D])
    identf_d = din("identf", [128, 128])
    maskp_d = din("maskp", [128, 128])
    masks_d = din("masks", [128, 128])
    smaskp_d = din("smaskp", [128, 768])
    smasks_d = din("smasks", [128, 768])
    sel_d = din("sel", [6, 4, 128])

    yp = dout("yp", [2, TP, D])
    ys = dout("ys", [4, 64, D])
    sgp = dout("sgp", [2, 2, 128, 128])
    shp = dout("shp", [2, 4, 128, 128])
    sgs = dout("sgs", [4, 2, 128, 128])
    shs = dout("shs", [4, 4, 128, 128])

    es = ExitStack()
    with es:
        def sb(name, shape, dt=F32):
            return es.enter_context(nc.sbuf_tensor(name, shape, dt))

        def ps(name, shape, dt=F32):
            return es.enter_context(nc.psum_tensor(name, shape, dt))

        P = Prog(nc)

        w_in_bf = sb("w_in_bf", [128, 8, NCOL], BF16)
        w_out_bf = sb("w_out_bf", [128, 8, D], BF16)
        walpha_bf = sb("walpha_bf", [16, 256], BF16)
        identf = sb("identf_sb", [128, 128])
        identb = sb("identb", [128, 128], BF16)
        maskp = sb("maskp_sb", [128, 128])
        masks = sb("masks_sb", [128, 128])
        smaskp = sb("smaskp_sb", [128, 768])
        smasks = sb("smasks_sb", [128, 768])
        cst = sb("cst", [128, 32])
        nbalpha = cst[:, 0:2]
        lb = cst[:, 2:6]
        ln1mlb = cst[:, 6:10]
        balpha = cst[:, 10:12]
        gon = cst[:, 12:14]
        tmp4 = cst[:, 14:18]
        tmp4b = cst[:, 18:22]
        lbl = sb("lbl_sb", [128, 2, 4])
        gpre = sb("gpre_sb", [128, 8])
        aT = sb("aT", [128, 8, 6])
        sT = sb("sT", [128, 8, 6])
        GG = [sb(f"GG{g}", [128, D]) for g in range(4)]
        S_all = sb("S_all", [128, 6, 6, 128])

        PA = ps("PA", [128, 1024])
        PB = ps("PB", [128, 1024])
        PC = ps("PC", [128, 512])
        PD = ps("PD", [128, 512])
        PT0 = ps("PT0", [128, 1024], BF16)
        PT1 = ps("PT1", [128, 1024], BF16)

        ses = ExitStack()
        with ses:
            def ssb(name, shape, dt=F32):
                return ses.enter_context(nc.sbuf_tensor(name, shape, dt))
            stage = [ssb(f"stage{i}", [128, 4096]) for i in range(2)]
            mod_sb = ssb("mod_sb", [6, 3072])
            bada_sb = ssb("bada_sb", [6, 3072])
            gpost = ssb("gpost_sb", [128, D])
            cT = ssb("cT_sb", [128, 8, 6])
            sel = ssb("sel_sb", [6, 4, 128])
            tmp48 = ssb("tmp48", [128, 8, 6])

            small = [
                (cT[:], cT_d, "cT"), (bada_sb[:], bada, "bada"), (gpre[:], gpre_d, "gpre"),
                (balpha, balpha_d, "balpha"), (gon, gon_d, "gon"), (lbl[:], lbl_d, "lbl"),
                (identf[:], identf_d, "identf"), (maskp[:], maskp_d, "maskp"), (masks[:], masks_d, "masks"),
                (smaskp[:], smaskp_d, "smaskp"), (smasks[:], smasks_d, "smasks"), (sel[:], sel_d, "sel"),
                (gpost[:], gpost_d, "gpost"),
            ]
            names = []
            tok = None
            for o_, i_, nm in small:
                tok = P.op("sync", lambda e, o_=o_, i_=i_: e.dma_start(out=o_, in_=i_), writes=[nm], dma_sem="ld_s")
                names.append(nm)
            walpha_f = ssb("walpha_f", [16, 256])
            stage_wa = walpha_f[:]
            tok = P.op("sync", lambda e: e.dma_start(out=stage_wa, in_=walpha_d), writes=["walpha_f"], dma_sem="ld_s")
            names.append("walpha_f")
            for b in range(4):
                tok = P.op("sync", lambda e, b=b: e.dma_start(out=S_all[:, 2 + b, 0:2, :], in_=stg[b].rearrange("c p v -> p c v")),
                           writes=[f"S{2 + b}"], dma_sem="ld_s")
                tok = P.op("sync", lambda e, b=b: e.dma_start(out=S_all[:, 2 + b, 2:6, :], in_=sth[b].rearrange("c p v -> p c v")),
                           writes=[f"S{2 + b}h"], dma_sem="ld_s")
            P.retoken(names + [f"S{2 + b}" for b in range(4)] + [f"S{2 + b}h" for b in range(4)], tok)

            P.op("vector", lambda e: e.tensor_copy(out=identb[:], in_=identf[:]), reads=["identf"], writes=["identb"])
            P.op("vector", lambda e: e.tensor_copy(out=walpha_bf[:], in_=stage_wa), reads=["walpha_f"], writes=["walpha_bf"])
            P.op("vector", lambda e: e.tensor_scalar(out=nbalpha, in0=balpha, scalar1=-1.0, scalar2=None, op0=ALU.mult),
                 reads=["balpha"], writes=["nbalpha"])
            P.op("vector", lambda e: e.tensor_tensor(out=tmp4, in0=lbl[:, 1, :], in1=lbl[:, 0, :], op=ALU.subtract),
                 reads=["lbl"], writes=["tmp4"])
            P.op("scalar", lambda e: e.activation(out=tmp4, in_=tmp4, func=AF.Exp), reads=["tmp4"], writes=["tmp4"])
            P.op("vector", lambda e: e.tensor_scalar(out=tmp4b, in0=tmp4, scalar1=1.0, scalar2=None, op0=ALU.add),
                 reads=["tmp4"], writes=["tmp4b"])
            P.op("vector", lambda e: e.reciprocal(out=lb, in_=tmp4b), reads=["tmp4b"], writes=["lb"])
            P.op("vector", lambda e: e.tensor_tensor(out=tmp4b, in0=tmp4, in1=lb, op=ALU.mult), reads=["tmp4", "lb"], writes=["tmp4b"])
            P.op("scalar", lambda e: e.activation(out=ln1mlb, in_=tmp4b, func=AF.Ln), reads=["tmp4b"], writes=["ln1mlb"])
            P.op("gpsimd", lambda e: e.memset(S_all[:, 0:2, :, :], 0.0), writes=["S0", "S0h", "S1", "S1h"])

            si = 0
            for n in range(12):
                stg_ = stage[si % 2]
                sname = f"stage{si % 2}"
                st3 = stg_[:, 0:2048].rearrange("p (j n) -> p j n", n=256)
                P.op("sync", lambda e, st3=st3, n=n: e.dma_start(out=st3, in_=wada[:, :, n * 256:(n + 1) * 256]),
                     writes=[sname], dma_sem="ld_" + sname)
                P.op("tensor", [lambda e, j=j, st3=st3: e.matmul(PC[0:6, 0:256], lhsT=cT[:, j, :], rhs=st3[:, j, :],
                                                                  start=(j == 0), stop=(j == 7)) for j in range(8)],
                     reads=[sname, "cT"], writes=["pc"], banks=["c"])
                P.op("vector", lambda e, n=n: e.tensor_tensor(out=mod_sb[0:6, n * 256:(n + 1) * 256], in0=PC[0:6, 0:256],
                                                             in1=bada_sb[0:6, n * 256:(n + 1) * 256], op=ALU.add),
                     reads=["pc", "bada"], writes=["mod"], banks=["c"])
                si += 1
            P.op("tensor", [lambda e, k=k: e.transpose(out=PD[:, k * 6:(k + 1) * 6], in_=mod_sb[0:6, k * 128:(k + 1) * 128],
                                                       identity=identf[0:6, 0:6]) for k in range(16)],
                 reads=["mod", "identf"], writes=["pd"], banks=["d"])
            P.op("vector", lambda e: e.tensor_copy(out=sT[:], in_=PD[:, 0:48].rearrange("p (j b) -> p j b", b=6)),
                 reads=["pd"], writes=["sT"], banks=["d"])
            P.op("vector", lambda e: e.tensor_scalar(out=tmp48[:], in0=PD[:, 48:96].rearrange("p (j b) -> p j b", b=6),
                                                     scalar1=1.0, scalar2=None, op0=ALU.add),
                 reads=["pd"], writes=["tmp48"], banks=["d"])
            P.op("vector", lambda e: e.tensor_tensor(out=aT[:], in0=tmp48[:], in1=gpre[:].unsqueeze(2).broadcast_to([128, 8, 6]),
                                                     op=ALU.mult), reads=["tmp48", "gpre"], writes=["aT"])
            for g in range(4):
                for n in range(2):
                    P.op("tensor", lambda e, g=g, n=n: e.matmul(PC[:, 0:512], lhsT=sel[0:6, g, :],
                                                                  rhs=mod_sb[0:6, 2048 + n * 512:2048 + (n + 1) * 512],
                                                                  start=True, stop=True),
                         reads=["mod", "sel"], writes=["pc"], banks=["c"])
                    P.op("vector", lambda e, g=g, n=n: e.tensor_tensor(out=GG[g][:, n * 512:(n + 1) * 512], in0=PC[:, 0:512],
                                                                      in1=gpost[:, n * 512:(n + 1) * 512], op=ALU.mult),
                         reads=["pc", "gpost"], writes=[f"GG{g}"], banks=["c"])
            for j in range(8):
                for hlf in range(2):
                    stg_ = stage[si % 2]
                    sname = f"stage{si % 2}"
                    c0 = hlf * 1800
                    P.op("sync", lambda e, stg_=stg_, j=j, c0=c0: e.dma_start(out=stg_[:, 0:1800], in_=win[:, j, c0:c0 + 1800]),
                         writes=[sname], dma_sem="ld_" + sname)
                    eng = "vector" if hlf == 0 else "scalar"
                    if eng == "vector":
                        P.op("vector", lambda e, stg_=stg_, j=j, c0=c0: e.tensor_copy(out=w_in_bf[:, j, c0:c0 + 1800], in_=stg_[:, 0:1800]),
                             reads=[sname], writes=[f"win{j}_{hlf}"])
                    else:
                        P.op("scalar", lambda e, stg_=stg_, j=j, c0=c0: e.activation(out=w_in_bf[:, j, c0:c0 + 1800], in_=stg_[:, 0:1800],
                                                                                      func=AF.Copy),
                             reads=[sname], writes=[f"win{j}_{hlf}"])
                    si += 1
            for jj in range(4):
                stg_ = stage[si % 2]
                sname = f"stage{si % 2}"
                st3 = stg_[:, 0:2048].rearrange("p (j n) -> p j n", n=1024)
                P.op("sync", lambda e, st3=st3, jj=jj: e.dma_start(out=st3, in_=wout[:, 2 * jj:2 * jj + 2, :]),
                     writes=[sname], dma_sem="ld_" + sname)
                for jl in range(2):
                    j = 2 * jj + jl
                    gcol = gon[:, 0:1] if j < 4 else gon[:, 1:2]
                    P.op("gpsimd", lambda e, st3=st3, jl=jl, j=j, gcol=gcol: e.tensor_scalar(
                        out=w_out_bf[:, j, :], in0=st3[:, jl, :], scalar1=gcol, scalar2=1.0, op0=ALU.mult, op1=ALU.mult),
                        reads=[sname, "gon"], writes=[f"wout{j}"])
                si += 1
            P.barrier()
        WIN = [f"win{j}_{h}" for j in range(8) for h in range(2)]
        WOUT = [f"wout{j}" for j in range(8)]

        NXS = 3
        x_sb = [sb(f"x_sb{i}", [128, D]) for i in range(NXS)]
        junk = sb("junk", [128, D], BF16)
        stat = sb("stat", [128, 16])
        xn = sb("xn", [128, D], BF16)
        hT = sb("hT", [128, 8, 128], BF16)
        alr = sb("alr", [16, 128], BF16)
        e1 = sb("e1", [128, 256])
        eh = sb("eh", [128, 512])
        L1 = sb("L1", [128, 512])
        L2 = sb("L2", [128, 512])
        gT = sb("gT", [128, 768])
        bT = sb("bT", [128, 768])
        eqb = sb("eqb", [128, 512])
        EQa = sb("EQa", [128, 256])
        EKa = sb("EKa", [128, 256])
        QT = sb("QT", [128, 6, 128], BF16)
        KT = sb("KT", [128, 6, 128], BF16)
        Ktm2 = sb("Ktm2", [128, 2, 6, 128], BF16)
        QTa2 = sb("QTa2", [128, 2, 2, 128], BF16)
        V = sb("V", [128, D], BF16)
        ez = sb("ez", [128, D])
        sm = sb("sm", [128, 2, 6, 6])
        Sp = sb("Sp", [128, 2, 6, 128], BF16)
        Se2 = sb("Se2", [128, 2, 6, 128])
        AT = sb("AT", [128, 8, 128], BF16)
        sq = sb("sq", [128, D])
        so = sb("so", [128, 24])
        ohat = sb("ohat", [128, D], BF16)
        ohT = sb("ohT", [128, 8, 128], BF16)
        y1 = sb("y1", [128, D])

        bT3 = v3(bT[:])
        state = {"xi": 0}
        P.op("gpsimd", lambda e: e.memset(Ktm2[:], 0.0), writes=["Ktma", "Ktmh"])
        P.op("gpsimd", lambda e: e.memset(QTa2[:], 0.0), writes=["QTa"])

        def load_x(i, xsrc):
            slot = i % NXS
            P.op("sync", lambda e: e.dma_start(out=x_sb[slot][:], in_=xsrc), writes=[f"x{slot}"], dma_sem=f"xld{slot}")

        def tile(xsrc, ydst, segs, gg, sample):
            slot = state["xi"] % NXS
            state["xi"] += 1
            xs_ = x_sb[slot]
            xb = f"x{slot}"
            mask = masks if sample else maskp
            smask = smasks if sample else smaskp
            if all(sg["b"] == segs[0]["b"] for sg in segs):
                mods = [(0, 128, segs[0]["b"])]
            else:
                mods = [(sg["lo"], sg["n"], sg["b"]) for sg in segs]
            P.op("scalar", lambda e: e.activation(out=junk[:], in_=xs_[:], func=AF.Square, accum_out=stat[:, 0:1]),
                 reads=[xb], writes=["junk", "st0"])
            P.op("scalar", lambda e: e.activation(out=stat[:, 1:2], in_=stat[:, 0:1], func=AF.Ln, scale=1.0 / D, bias=EPS),
                 reads=["st0"], writes=["st1"])
            P.op("scalar", lambda e: e.activation(out=stat[:, 2:3], in_=stat[:, 1:2], func=AF.Exp, scale=-0.5),
                 reads=["st1"], writes=["st2"])
            P.op("gpsimd", lambda e: e.tensor_scalar(out=xn[:], in0=xs_[:], scalar1=stat[:, 2:3], scalar2=1.0,
                                                       op0=ALU.mult, op1=ALU.mult), reads=[xb, "st2"], writes=["xn"])
            for half, (PT, bank, eng) in enumerate(((PT0, "t0", "scalar"), (PT1, "t1", "vector"))):
                PT3 = v3(PT[:])
                P.op("tensor", [lambda e, j=j, PT3=PT3, half=half: e.transpose(
                    out=PT3[:, j, :], in_=xn[:, (4 * half + j) * 128:(4 * half + j + 1) * 128], identity=identb[:]) for j in range(4)],
                    reads=["xn", "identb"], writes=[bank], banks=[bank])
                for j in range(4):
                    jj = 4 * half + j
                    for (lo, n, b) in mods:
                        if eng == "scalar":
                            P.op("scalar", lambda e, j=j, jj=jj, lo=lo, n=n, b=b, PT3=PT3: e.activation(
                                out=hT[:, jj, lo:lo + n], in_=PT3[:, j, lo:lo + n], func=AF.Identity,
                                scale=aT[:, jj, b:b + 1], bias=sT[:, jj, b:b + 1]),
                                reads=[bank, "aT", "sT"], writes=[f"hT{half}"], banks=[bank])
                        else:
                            P.op("vector", lambda e, j=j, jj=jj, lo=lo, n=n, b=b, PT3=PT3: e.tensor_scalar(
                                out=hT[:, jj, lo:lo + n], in0=PT3[:, j, lo:lo + n], scalar1=aT[:, jj, b:b + 1],
                                scalar2=sT[:, jj, b:b + 1], op0=ALU.mult, op1=ALU.add),
                                reads=[bank, "aT", "sT"], writes=[f"hT{half}"], banks=[bank])
            HT = ["hT0", "hT1"]

            def fm(out_ap, col0, m):
                return [lambda e, j=j: e.matmul(out_ap, lhsT=w_in_bf[:, j, col0:col0 + m], rhs=hT[:, j, :],
                                                start=(j == 0), stop=(j == 7)) for j in range(8)]

            def tm(out_ap, col0):
                return [lambda e, j=j: e.matmul(out_ap, lhsT=hT[:, j, :], rhs=w_in_bf[:, j, col0:col0 + 512],
                                                start=(j == 0), stop=(j == 7)) for j in range(8)]

            P.op("tensor", fm(PA[0:16, 0:128], C_AL, 16), reads=HT + WIN, writes=["p_alr"], banks=["a0"])
            P.op("vector", lambda e: e.tensor_copy(out=alr[:], in_=PA[0:16, 0:128]), reads=["p_alr"], writes=["alr"], banks=["a0"])
            fl = []
            for c in range(4):
                fl += fm(PA[:, 512 + c * 128:512 + (c + 1) * 128], C_FH + c * 128, 128)
            P.op("tensor", fl, reads=HT + WIN, writes=["p_fh"], banks=["a1"])
            fl = []
            for c in range(4):
                fl += fm(PC[:, c * 128:(c + 1) * 128], C_QH + c * 128, 128)
            P.op("tensor", fl, reads=HT + WIN, writes=["p_qh"], banks=["c"])
            P.op("tensor", [lambda e, c=c: e.matmul(PA[:, 128 + c * 128:256 + c * 128], lhsT=walpha_bf[0:16, c * 128:(c + 1) * 128],
                                                    rhs=alr[0:16, :], start=True, stop=True) for c in range(2)],
                 reads=["alr", "walpha_bf"], writes=["p_u"], banks=["a0"])
            P.op("tensor", tm(PB[:, 0:512], C_ZA), reads=HT + WIN, writes=["p_za"], banks=["b0"])
            fl = []
            for c in range(2):
                fl += fm(PD[:, c * 128:(c + 1) * 128], C_QA + c * 128, 128)
            for c in range(2):
                fl += fm(PD[:, (2 + c) * 128:(3 + c) * 128], C_KA + c * 128, 128)
            P.op("tensor", fl, reads=HT + WIN, writes=["p_qk"], banks=["d"])
            P.op("tensor", tm(PB[:, 512:1024], C_VA), reads=HT + WIN, writes=["p_v"], banks=["b1"])
            P.op("vector", lambda e: e.tensor_copy(out=V[:, 0:512], in_=PB[:, 512:1024]), reads=["p_v"], writes=["Va"], banks=["b1"])
            P.op("tensor", tm(PB[:, 512:1024], C_IH), reads=HT + WIN, writes=["p_v"], banks=["b1"])
            P.op("vector", lambda e: e.tensor_copy(out=V[:, 512:1024], in_=PB[:, 512:1024]), reads=["p_v"], writes=["Vh"], banks=["b1"])

            def zgate(ps_ap, ez_ap, pbuf, gbuf, bank):
                P.op("scalar", lambda e: e.activation(out=ez_ap, in_=ps_ap, func=AF.Exp, scale=-1.0),
                     reads=[pbuf], writes=[gbuf], banks=[bank])
                P.op("gpsimd", lambda e: e.tensor_scalar(out=ez_ap, in0=ez_ap, scalar1=1.0, scalar2=1.0, op0=ALU.add, op1=ALU.mult),
                     reads=[gbuf], writes=[gbuf])
                P.op("vector", lambda e: e.reciprocal(out=ez_ap, in_=ez_ap), reads=[gbuf], writes=[gbuf])
                P.op("vector", lambda e: e.tensor_tensor(out=ez_ap, in0=ps_ap, in1=ez_ap, op=ALU.mult),
                     reads=[pbuf, gbuf], writes=[gbuf], banks=[bank])
            zgate(PB[:, 0:512], ez[:, 0:512], "p_za", "gza", "b0")
            P.op("tensor", tm(PB[:, 0:512], C_ZH), reads=HT + WIN, writes=["p_za"], banks=["b0"])
            zgate(PB[:, 0:512], ez[:, 512:1024], "p_za", "gzh", "b0")

            for c in range(2):
                P.op("scalar", lambda e, c=c: e.activation(out=e1[:, c * 128:(c + 1) * 128], in_=PA[:, 128 + c * 128:256 + c * 128],
                                                           func=AF.Exp, scale=-1.0, bias=nbalpha[:, c:c + 1]),
                     reads=["p_u", "nbalpha"], writes=["e1"], banks=["a0"])
            P.op("scalar", lambda e: e.activation(out=e1[:], in_=e1[:], func=AF.Ln, bias=1.0), reads=["e1"], writes=["e1"])
            P.op("gpsimd", lambda e: e.tensor_scalar(out=gT[:, 0:256], in0=e1[:], scalar1=-1.0 / 16.0, scalar2=1.0,
                                                       op0=ALU.mult, op1=ALU.mult), reads=["e1"], writes=["gTa"])
            P.op("scalar", lambda e: e.activation(out=eh[:], in_=PA[:, 512:1024], func=AF.Exp, scale=-1.0),
                 reads=["p_fh"], writes=["eh"], banks=["a1"])
            P.op("scalar", lambda e: e.activation(out=L1[:], in_=eh[:], func=AF.Ln, bias=1.0), reads=["eh"], writes=["L1"])
            for c in range(4):
                P.op("scalar", lambda e, c=c: e.activation(out=L2[:, c * 128:(c + 1) * 128], in_=eh[:, c * 128:(c + 1) * 128],
                                                           func=AF.Ln, bias=1.0, scale=lb[:, c:c + 1]),
                     reads=["eh", "lb"], writes=["L2"])
            P.op("vector", lambda e: e.tensor_tensor(out=gT[:, 256:768], in0=L2[:], in1=L1[:], op=ALU.subtract),
                 reads=["L1", "L2"], writes=["gTh"])
            P.op("vector", lambda e: e.tensor_tensor(out=L1[:], in0=PA[:, 512:1024], in1=L1[:], op=ALU.add),
                 reads=["p_fh", "L1"], writes=["L1"], banks=["a1"])
            P.op("scalar", lambda e: e.activation(out=eqb[:], in_=PC[:, :], func=AF.Exp, scale=-1.0),
                 reads=["p_qh"], writes=["eqb"], banks=["c"])
            P.op("scalar", lambda e: e.activation(out=eqb[:], in_=eqb[:], func=AF.Ln, bias=1.0), reads=["eqb"], writes=["eqb"])
            P.op("vector", lambda e: e.tensor_tensor_scan(out=bT[:], data0=smask[:], data1=gT[:], initial=0.0,
                                                          op0=ALU.mult, op1=ALU.add),
                 reads=["gTa", "gTh", "smaskp", "smasks"], writes=["bT"])
            for si_, sg in enumerate(segs):
                lo, n = sg["lo"], sg["n"]
                r = lo + n // 2 - 1
                l = lo + n - 1
                P.op("vector", lambda e, si_=si_, r=r: e.tensor_scalar(out=sm[:, si_, 0, :], in0=bT3[:, :, r], scalar1=-1.0,
                                                                      scalar2=None, op0=ALU.mult),
                     reads=["bT"], writes=[f"sm{si_}"])
                P.op("vector", lambda e, si_=si_, r=r: e.tensor_tensor(out=sm[:, si_, 1, 2:6], in0=bT3[:, 2:6, r], in1=ln1mlb,
                                                                      op=ALU.add), reads=["bT", "ln1mlb"], writes=[f"sm{si_}"])
                P.op("vector", lambda e, si_=si_, r=r, l=l: e.tensor_tensor(out=sm[:, si_, 2, :], in0=bT3[:, :, l], in1=bT3[:, :, r],
                                                                           op=ALU.subtract), reads=["bT"], writes=[f"sm{si_}"])
                P.op("scalar", lambda e, si_=si_, r=r: e.activation(out=sm[:, si_, 3, :], in_=bT3[:, :, r], func=AF.Exp),
                     reads=["bT"], writes=[f"sm{si_}e"])
                P.op("scalar", lambda e, si_=si_, l=l: e.activation(out=sm[:, si_, 4, :], in_=bT3[:, :, l], func=AF.Exp),
                     reads=["bT"], writes=[f"sm{si_}e"])
                P.op("scalar", lambda e, si_=si_: e.activation(out=sm[:, si_, 5, :], in_=sm[:, si_, 2, :], func=AF.Exp),
                     reads=[f"sm{si_}"], writes=[f"sm{si_}e"])
            for c in range(2):
                for si_, sg in enumerate(segs):
                    lo, n = sg["lo"], sg["n"]
                    r = lo + n // 2 - 1
                    c0 = c * 128 + lo
                    P.op("scalar", lambda e, c=c, si_=si_, c0=c0, n=n: e.activation(
                        out=EQa[:, c0:c0 + n], in_=bT[:, c0:c0 + n], func=AF.Exp, bias=sm[:, si_, 0, c:c + 1]),
                        reads=["bT", f"sm{si_}"], writes=["EQa"])
                    P.op("scalar", lambda e, c=c, r=r, c0=c0, n=n: e.activation(
                        out=EKa[:, c0:c0 + n], in_=bT[:, c0:c0 + n], func=AF.Exp, scale=-1.0, bias=bT3[:, c, r:r + 1]),
                        reads=["bT"], writes=["EKa"])
            for hh in range(2):
                P.op("vector", lambda e, hh=hh: e.scalar_tensor_tensor(
                    out=QTa2[hh * 64:(hh + 1) * 64, hh, :, :], in0=v3(PD[hh * 64:(hh + 1) * 64, 0:256]), scalar=0.125,
                    in1=v3(EQa[hh * 64:(hh + 1) * 64, :]), op0=ALU.mult, op1=ALU.mult),
                    reads=["p_qk", "EQa"], writes=["QTa"], banks=["d"])
            P.op("vector", lambda e: e.tensor_tensor(out=KT[:, 0:2, :], in0=v3(PD[:, 256:512]), in1=v3(EKa[:]), op=ALU.mult),
                 reads=["p_qk", "EKa"], writes=["KTa"], banks=["d"])
            P.op("vector", lambda e: e.tensor_tensor(out=L1[:], in0=L1[:], in1=bT[:, 256:768], op=ALU.add),
                 reads=["L1", "bT"], writes=["L1"])
            P.op("vector", lambda e: e.tensor_tensor(out=eqb[:], in0=bT[:, 256:768], in1=eqb[:], op=ALU.subtract),
                 reads=["eqb", "bT"], writes=["eqb"])
            for c in range(4):
                for si_, sg in enumerate(segs):
                    lo, n = sg["lo"], sg["n"]
                    c0 = c * 128 + lo
                    P.op("scalar", lambda e, c=c, si_=si_, c0=c0, lo=lo, n=n: e.activation(
                        out=KT[:, 2 + c, lo:lo + n], in_=L1[:, c0:c0 + n], func=AF.Exp, scale=-1.0, bias=sm[:, si_, 1, 2 + c:3 + c]),
                        reads=["L1", f"sm{si_}"], writes=["KTh"])
                    P.op("scalar", lambda e, c=c, si_=si_, c0=c0, n=n: e.activation(
                        out=eqb[:, c0:c0 + n], in_=eqb[:, c0:c0 + n], func=AF.Exp, bias=sm[:, si_, 0, 2 + c:3 + c]),
                        reads=["eqb", f"sm{si_}"], writes=["eqb"])
            P.op("vector", lambda e: e.tensor_tensor(out=QT[:, 2:6, :], in0=v3(PC[:, :]), in1=v3(eqb[:]), op=ALU.mult),
                 reads=["p_qh", "eqb"], writes=["QTh"], banks=["c"])
            PT03, PT13 = v3(PT0[:]), v3(PT1[:])
            P.op("tensor", [lambda e, c=c: e.transpose(out=PT03[:, c, :], in_=KT[:, c, :], identity=identb[:]) for c in range(2)],
                 reads=["KTa", "identb"], writes=["t0"], banks=["t0"])
            for si_, sg in enumerate(segs):
                lo, n = sg["lo"], sg["n"]
                P.op("scalar", lambda e, si_=si_, lo=lo, n=n: e.activation(out=Ktm2[lo:lo + n, si_, 0:2, :], in_=PT03[lo:lo + n, 0:2, :],
                                                                          func=AF.Copy),
                     reads=["t0"], writes=["Ktma"], banks=["t0"])
            P.op("tensor", [lambda e, c=c: e.transpose(out=PT13[:, c, :], in_=KT[:, 2 + c, :], identity=identb[:]) for c in range(4)],
                 reads=["KTh", "identb"], writes=["t1"], banks=["t1"])
            for si_, sg in enumerate(segs):
                lo, n = sg["lo"], sg["n"]
                P.op("vector", lambda e, si_=si_, lo=lo, n=n: e.tensor_copy(out=Ktm2[lo:lo + n, si_, 2:6, :], in_=PT13[lo:lo + n, 0:4, :]),
                     reads=["t1"], writes=["Ktmh"], banks=["t1"])
            fl = []
            for h in range(4):
                c, r0 = h // 2, (h % 2) * 64
                fl.append(lambda e, h=h, c=c: e.matmul(PA[:, h * 128:(h + 1) * 128], lhsT=KT[:, c, :],
                                                       rhs=QTa2[:, h % 2, c, :], start=True, stop=True))
            P.op("tensor", fl, reads=["KTa", "QTa"], writes=["p_ata"], banks=["a0"])
            P.op("vector", lambda e: e.tensor_tensor(out=AT[:, 0:4, :], in0=v3(PA[:, 0:512]),
                                                     in1=mask[:].unsqueeze(1).broadcast_to([128, 4, 128]), op=ALU.mult),
                 reads=["p_ata", "maskp", "masks"], writes=["ATa"], banks=["a0"])
            P.op("tensor", [lambda e, h=h: e.matmul(PA[:, (4 + h) * 128:(5 + h) * 128], lhsT=KT[:, 2 + h, :], rhs=QT[:, 2 + h, :],
                                                    start=True, stop=True) for h in range(4)],
                 reads=["KTh", "QTh"], writes=["p_ath"], banks=["a1"])
            P.op("vector", lambda e: e.tensor_tensor(out=AT[:, 4:8, :], in0=v3(PA[:, 512:1024]),
                                                     in1=mask[:].unsqueeze(1).broadcast_to([128, 4, 128]), op=ALU.mult),
                 reads=["p_ath", "maskp", "masks"], writes=["ATh"], banks=["a1"])
            for si_, sg in enumerate(segs):
                lo, n, st = sg["lo"], sg["n"], sg["st"]
                P.op("gpsimd", lambda e, si_=si_, st=st: e.tensor_tensor(
                    out=Sp[:, si_], in0=S_all[:, st], in1=sm[:, si_, 3, :].unsqueeze(2).broadcast_to([128, 6, 128]), op=ALU.mult),
                    reads=[f"S{st}", f"S{st}h", f"sm{si_}e"], writes=[f"Sp{si_}"])
                P.op("gpsimd", lambda e, si_=si_, st=st: e.tensor_tensor(
                    out=Se2[:, si_], in0=S_all[:, st], in1=sm[:, si_, 4, :].unsqueeze(2).broadcast_to([128, 6, 128]), op=ALU.mult),
                    reads=[f"S{st}", f"S{st}h", f"sm{si_}e"], writes=[f"Se2{si_}"])
                fl = []
                for h in range(4):
                    c, r0 = h // 2, (h % 2) * 64
                    fl.append(lambda e, h=h, lo=lo, n=n: e.matmul(
                        PB[lo:lo + n, h * 128:(h + 1) * 128], lhsT=AT[:, h, lo:lo + n], rhs=V[:, h * 128:(h + 1) * 128],
                        start=True, stop=False))
                    fl.append(lambda e, h=h, c=c, lo=lo, n=n, si_=si_: e.matmul(
                        PB[lo:lo + n, h * 128:(h + 1) * 128], lhsT=QTa2[:, h % 2, c, lo:lo + n], rhs=Sp[:, si_, c, :],
                        start=False, stop=True))
                P.op("tensor", fl, reads=["ATa", "Va", "QTa", f"Sp{si_}"], writes=["p_oa"], banks=["b0"])
                fl = []
                for h in range(4):
                    fl.append(lambda e, h=h, lo=lo, n=n: e.matmul(
                        PB[lo:lo + n, (4 + h) * 128:(5 + h) * 128], lhsT=AT[:, 4 + h, lo:lo + n],
                        rhs=V[:, (4 + h) * 128:(5 + h) * 128], start=True, stop=False))
                    fl.append(lambda e, h=h, lo=lo, n=n, si_=si_: e.matmul(
                        PB[lo:lo + n, (4 + h) * 128:(5 + h) * 128], lhsT=QT[:, 2 + h, lo:lo + n], rhs=Sp[:, si_, 2 + h, :],
                        start=False, stop=True))
                P.op("tensor", fl, reads=["ATh", "Vh", "QTh", f"Sp{si_}"], writes=["p_oh"], banks=["b1"])
                fl = []
                for h in range(4):
                    c, r0 = h // 2, (h % 2) * 64
                    fl.append(lambda e, h=h, c=c, r0=r0, si_=si_: e.matmul(
                        PD[r0:r0 + 64, c * 128:(c + 1) * 128], lhsT=Ktm2[:, si_, c, r0:r0 + 64], rhs=V[:, h * 128:(h + 1) * 128],
                        start=True, stop=True))
                P.op("tensor", fl, reads=["Ktma", "Va"], writes=["p_sa"], banks=["d"])
                for c in range(2):
                    P.op("vector", lambda e, c=c, si_=si_, st=st: e.scalar_tensor_tensor(
                        out=S_all[:, st, c, :], in0=PD[:, c * 128:(c + 1) * 128], scalar=sm[:, si_, 5, c:c + 1], in1=Se2[:, si_, c, :],
                        op0=ALU.mult, op1=ALU.add),
                        reads=["p_sa", f"sm{si_}e", f"Se2{si_}"], writes=[f"S{st}"], banks=["d"])
                fl = []
                for h in range(4):
                    fl.append(lambda e, h=h, si_=si_: e.matmul(
                        PC[:, h * 128:(h + 1) * 128], lhsT=Ktm2[:, si_, 2 + h, :], rhs=V[:, (4 + h) * 128:(5 + h) * 128],
                        start=True, stop=True))
                P.op("tensor", fl, reads=["Ktmh", "Vh"], writes=["p_sh"], banks=["c"])
                for h in range(4):
                    P.op("vector", lambda e, h=h, si_=si_, st=st: e.scalar_tensor_tensor(
                        out=S_all[:, st, 2 + h, :], in0=PC[:, h * 128:(h + 1) * 128], scalar=sm[:, si_, 5, 2 + h:3 + h],
                        in1=Se2[:, si_, 2 + h, :], op0=ALU.mult, op1=ALU.add),
                        reads=["p_sh", f"sm{si_}e", f"Se2{si_}"], writes=[f"S{st}h"], banks=["c"])
            P.op("scalar", lambda e: e.activation(out=sq[:, 0:512], in_=PB[:, 0:512], func=AF.Square),
                 reads=["p_oa"], writes=["sqa"], banks=["b0"])
            P.op("scalar", lambda e: e.activation(out=sq[:, 512:1024], in_=PB[:, 512:1024], func=AF.Square),
                 reads=["p_oh"], writes=["sqh"], banks=["b1"])
            P.op("vector", lambda e: e.reduce_sum(out=so[:, 0:8], in_=v3(sq[:]), axis=AX.X), reads=["sqa", "sqh"], writes=["so0"])
            P.op("scalar", lambda e: e.activation(out=so[:, 8:16], in_=so[:, 0:8], func=AF.Ln, scale=1.0 / 128, bias=EPS),
                 reads=["so0"], writes=["so1"])
            P.op("scalar", lambda e: e.activation(out=so[:, 16:24], in_=so[:, 8:16], func=AF.Exp, scale=-0.5),
                 reads=["so1"], writes=["so2"])
            P.op("gpsimd", lambda e: e.tensor_tensor(out=v3(ez[:]), in0=v3(ez[:]),
                                                      in1=so[:, 16:24].unsqueeze(2).broadcast_to([128, 8, 128]), op=ALU.mult),
                 reads=["gza", "gzh", "so2"], writes=["gzr"])
            P.op("vector", lambda e: e.tensor_tensor(out=ohat[:, 0:512], in0=PB[:, 0:512], in1=ez[:, 0:512], op=ALU.mult),
                 reads=["p_oa", "gzr"], writes=["ohata"], banks=["b0"])
            P.op("vector", lambda e: e.tensor_tensor(out=ohat[:, 512:1024], in0=PB[:, 512:1024], in1=ez[:, 512:1024], op=ALU.mult),
                 reads=["p_oh", "gzr"], writes=["ohath"], banks=["b1"])
            P.op("tensor", [lambda e, j=j: e.transpose(out=PT03[:, j, :], in_=ohat[:, j * 128:(j + 1) * 128], identity=identb[:])
                            for j in range(4)], reads=["ohata", "identb"], writes=["t0"], banks=["t0"])
            P.op("scalar", lambda e: e.activation(out=ohT[:, 0:4, :], in_=PT03[:, 0:4, :], func=AF.Copy),
                 reads=["t0"], writes=["ohTa"], banks=["t0"])
            P.op("tensor", [lambda e, j=j: e.transpose(out=PT13[:, j, :], in_=ohat[:, (4 + j) * 128:(5 + j) * 128], identity=identb[:])
                            for j in range(4)], reads=["ohath", "identb"], writes=["t1"], banks=["t1"])
            P.op("vector", lambda e: e.tensor_copy(out=ohT[:, 4:8, :], in_=PT13[:, 0:4, :]), reads=["t1"], writes=["ohTh"], banks=["t1"])
            for n_ in range(2):
                P.op("tensor", [lambda e, j=j, n_=n_: e.matmul(PA[:, n_ * 512:(n_ + 1) * 512], lhsT=ohT[:, j, :],
                                                                 rhs=w_out_bf[:, j, n_ * 512:(n_ + 1) * 512], start=(j == 0), stop=(j == 7))
                                for j in range(8)], reads=["ohTa", "ohTh"] + WOUT, writes=[f"p_wo{n_}"], banks=[f"a{n_}"])
            P.op("scalar", lambda e: e.activation(out=junk[:], in_=PA[:, :], func=AF.Square, accum_out=stat[:, 4:5]),
                 reads=["p_wo0", "p_wo1"], writes=["junk", "st4"], banks=["a0", "a1"])
            P.op("scalar", lambda e: e.activation(out=stat[:, 5:6], in_=stat[:, 4:5], func=AF.Ln, scale=1.0 / D, bias=EPS),
                 reads=["st4"], writes=["st5"])
            P.op("scalar", lambda e: e.activation(out=stat[:, 6:7], in_=stat[:, 5:6], func=AF.Exp, scale=-0.5),
                 reads=["st5"], writes=["st6"])
            P.op("vector", lambda e: e.scalar_tensor_tensor(out=y1[:], in0=PA[:, :], scalar=stat[:, 6:7], in1=GG[gg][:],
                                                            op0=ALU.mult, op1=ALU.mult),
                 reads=["p_wo0", "p_wo1", "st6", f"GG{gg}"], writes=["y1"], banks=["a0", "a1"])
            P.op("gpsimd", lambda e: e.tensor_tensor(out=xs_[:], in0=xs_[:], in1=y1[:], op=ALU.add), reads=[xb, "y1"], writes=[xb])
            P.op("sync", lambda e: e.dma_start(out=ydst, in_=xs_[:]), reads=[xb], dma_sem=f"yst{slot}")

        def store_state(st, gdst, hdst):
            P.op("sync", lambda e: e.dma_start(out=gdst.rearrange("c p v -> p c v"), in_=S_all[:, st, 0:2, :]),
                 reads=[f"S{st}"], dma_sem="sout")
            P.op("sync", lambda e: e.dma_start(out=hdst.rearrange("c p v -> p c v"), in_=S_all[:, st, 2:6, :]),
                 reads=[f"S{st}h"], dma_sem="sout")

        tiles = []
        for k in range(2):
            segs = [dict(lo=0, n=64, b=2 + 2 * k, st=2 + 2 * k), dict(lo=64, n=64, b=3 + 2 * k, st=3 + 2 * k)]
            tiles.append((xs[2 * k:2 * k + 2].rearrange("b t d -> (b t) d"), ys[2 * k:2 * k + 2].rearrange("b t d -> (b t) d"),
                          segs, 2 + k, True, k))
        for t in range(tp_tiles):
            for s in range(2):
                tiles.append((xp[s, t * 128:(t + 1) * 128, :], yp[s, t * 128:(t + 1) * 128, :],
                              [dict(lo=0, n=64, b=s, st=s), dict(lo=64, n=64, b=s, st=s)], s, True, None))
        for i in range(min(NXS - 1, len(tiles))):
            load_x(i, tiles[i][0])
        for i, (xsrc, ydst, segs, gg, sample, k) in enumerate(tiles):
            if i + NXS - 1 < len(tiles):
                load_x(i + NXS - 1, tiles[i + NXS - 1][0])
            tile(xsrc, ydst, segs, gg, sample)
            if k is not None:
                store_state(2 + 2 * k, sgs[2 * k], shs[2 * k])
                store_state(3 + 2 * k, sgs[2 * k + 1], shs[2 * k + 1])
        for s in range(2):
            store_state(s, sgp[s], shp[s])
        for nm, s in list(P.sems.items()):
            if nm.startswith("yst") or nm == "sout":
                P.wait_token("sync", (nm, s[1]))
        with nc.Block() as block:
            P.replay(block)
        P.close()
        import os as _os
        if _os.environ.get("KVERB"):
            print("total ops recorded", P.count)
    return nc


def host_inputs(core, x_prompt, x_sample, c_prompt, c_sample, state_gla, state_hgrn, w_ada, b_ada, g_pre,
                w_in, w_alpha, b_alpha, g_onorm_gla, hgrn_lb_logits, g_onorm_hgrn, w_out, g_post, consts):
    f = np.float32
    c6 = np.concatenate([c_prompt[2 * core:2 * core + 2], c_sample[4 * core:4 * core + 4]], 0)

    def pj(a):
        return np.ascontiguousarray(a.reshape(8, 128, a.shape[1]).transpose(1, 0, 2))

    m = {
        "xp": np.ascontiguousarray(x_prompt[2 * core:2 * core + 2]),
        "xs": np.ascontiguousarray(x_sample[4 * core:4 * core + 4]),
        "cT": np.ascontiguousarray(c6.T.reshape(8, 128, 6).transpose(1, 0, 2)),
        "stg": np.ascontiguousarray(state_gla[0, 4 * core:4 * core + 4].reshape(4, 2, 128, 128)),
        "sth": np.ascontiguousarray(state_hgrn[0, 4 * core:4 * core + 4]),
        "wada": pj(w_ada[0]),
        "bada": np.ascontiguousarray(np.broadcast_to(b_ada[0][None, :], (6, 3072))),
        "gpre": np.ascontiguousarray(g_pre[0].reshape(8, 128).T),
        "win": pj(w_in[0]),
        "walpha": np.ascontiguousarray(w_alpha[0]),
        "balpha": np.ascontiguousarray(b_alpha[0].reshape(2, 128).T),
        "gon": np.ascontiguousarray(np.stack([g_onorm_gla[0], g_onorm_hgrn[0]], 1)),
        "lbl": np.ascontiguousarray(hgrn_lb_logits.reshape(2, 4, 128).transpose(2, 0, 1)),
        "wout": pj(w_out[0]),
        "gpost": np.ascontiguousarray(np.broadcast_to(g_post[0][None, :], (128, 1024))),
    }
    m.update(consts)
    return {k: np.ascontiguousarray(v, dtype=f) for k, v in m.items()}


def make_consts():
    f = np.float32
    maskp = np.triu(np.ones((128, 128), f))
    masks = maskp.copy()
    masks[0:64, 64:128] = 0.0
    smaskp = np.ones((128, 768), f)
    smaskp[:, 0::128] = 0.0
    smasks = smaskp.copy()
    smasks[:, 64::128] = 0.0
    sel = np.zeros((6, 4, 128), f)
    sel[0, 0, :] = 1.0
    sel[1, 1, :] = 1.0
    sel[2, 2, 0:64] = 1.0
    sel[3, 2, 64:128] = 1.0
    sel[4, 3, 0:64] = 1.0
    sel[5, 3, 64:128] = 1.0
    return {"identf": np.eye(128, dtype=f), "maskp": maskp, "masks": masks, "smaskp": smaskp, "smasks": smasks, "sel": sel}


def assemble(results, TP):
    f = np.float32
    yp = np.concatenate([r["yp"] for r in results], 0).astype(f)
    ys = np.concatenate([r["ys"] for r in results], 0).astype(f)
    sgp = np.concatenate([r["sgp"].reshape(2, 4, 64, 128) for r in results], 0)[None].astype(f)
    shp = np.concatenate([r["shp"] for r in results], 0)[None].astype(f)
    sgs = np.concatenate([r["sgs"].reshape(4, 4, 64, 128) for r in results], 0)[None].astype(f)
    shs = np.concatenate([r["shs"] for r in results], 0)[None].astype(f)
    return (yp, ys, sgp, shp, sgs, shs)


def kernel(**inputs):
    inputs = {k: np.asarray(v) for k, v in inputs.items()}
    TP = inputs["x_prompt"].shape[1]
    nc = build(TP // 128)
    consts = make_consts()
    in_maps = [host_inputs(i, consts=consts, **inputs) for i in range(N_CORES)]
    res = run_bass_kernel_spmd(nc, in_maps, core_ids=list(range(N_CORES)))
    return assemble(res.results, TP)
```

```python
from contextlib import ExitStack

import numpy as np
import concourse.bass as bass
import concourse.mybir as mybir
from concourse.bass_utils import run_bass_kernel_spmd

F32 = mybir.dt.float32
BF16 = mybir.dt.bfloat16
AF = mybir.ActivationFunctionType
ALU = mybir.AluOpType
AX = mybir.AxisListType

D = 1024
NCOL = 3600
EPS = 1e-6
N_CORES = 8
SEQ = 4096
ENGS = ("sync", "scalar", "vector", "gpsimd", "tensor")

C_QA, C_KA, C_VA, C_ZA, C_AL, C_QH, C_FH, C_IH, C_ZH = 0, 256, 512, 1024, 1536, 1552, 2064, 2576, 3088


class Prog:
    def __init__(self, nc):
        self.nc = nc
        self.q = {e: [] for e in ENGS}
        self.sems = {}
        self.waited = {e: {} for e in ENGS}
        self.bufs = {}
        self.bank_last = {}
        self._cms = []
        import os as _os
        self.limit = int(_os.environ.get("KLIMIT", "0")) or None
        self.count = 0

    def sem(self, name):
        if name not in self.sems:
            cm = self.nc.semaphore(name)
            h = cm.__enter__()
            self._cms.append(cm)
            self.sems[name] = [h, 0]
        return self.sems[name]

    def close(self):
        for cm in reversed(self._cms):
            cm.__exit__(None, None, None)

    def _deps(self, eng, reads, writes, is_dma, banks):
        deps = []
        for b in banks:
            t = self.bank_last.get(b)
            if t is not None and t[2] != eng:
                deps.append((t, "bank"))
        for b in reads:
            st = self.bufs.get(b)
            if st and st[0] is not None:
                deps.append((st[0], "raw"))
        for b in writes:
            st = self.bufs.get(b)
            if st:
                if st[0] is not None:
                    deps.append((st[0], "waw"))
                for t in st[1]:
                    deps.append((t, "war"))
        waits = []
        for tok, kind in deps:
            sname, val, teng, tdma = tok
            if not tdma and teng == eng and not is_dma:
                if eng == "tensor" or kind in ("war", "waw"):
                    continue
            if self.waited[eng].get(sname, 0) >= val:
                continue
            self.waited[eng][sname] = val
            waits.append((sname, val))
        return waits

    def op(self, eng, fns, reads=(), writes=(), dma_sem=None, banks=()):
        if callable(fns):
            fns = [fns]
        self.count += 1
        if self.limit is not None and self.count > self.limit:
            return None
        is_dma = dma_sem is not None
        waits = self._deps(eng, reads, writes, is_dma, banks)
        if is_dma:
            s = self.sem(dma_sem)
            s[1] += 16
            tok = (dma_sem, s[1], eng, True)
            inc = (dma_sem, 16)
        else:
            sname = "p_" + eng
            s = self.sem(sname)
            s[1] += 1
            tok = (sname, s[1], eng, False)
            inc = (sname, 1)
        self.q[eng].append((waits, fns, inc))
        for b in banks:
            self.bank_last[b] = tok
        for b in writes:
            self.bufs[b] = [tok, []]
        for b in reads:
            if b in writes:
                continue
            self.bufs.setdefault(b, [None, []])[1].append(tok)
        return tok

    def retoken(self, names, tok):
        for b in names:
            self.bufs[b] = [tok, []]

    def wait_token(self, eng, tok):
        sname, val = tok[0], tok[1]
        if self.waited[eng].get(sname, 0) >= val:
            return
        self.waited[eng][sname] = val
        self.q[eng].append(([(sname, val)], [], None))

    def barrier(self):
        snap = [(n, s[1]) for n, s in self.sems.items() if s[1] > 0]
        for e in ENGS:
            w = []
            for n, v in snap:
                if self.waited[e].get(n, 0) < v:
                    self.waited[e][n] = v
                    w.append((n, v))
            if w:
                self.q[e].append((w, [], None))

    def replay(self, block):
        P = self

        def run(engobj, name):
            for waits, fns, inc in P.q[name]:
                for sname, val in waits:
                    engobj.wait_ge(P.sems[sname][0], val)
                ins = None
                for f in fns:
                    ins = f(engobj)
                if inc is not None and ins is not None:
                    ins.then_inc(P.sems[inc[0]][0], inc[1])

        @block.sync
        def _(e):
            run(e, "sync")

        @block.scalar
        def _(e):
            run(e, "scalar")

        @block.vector
        def _(e):
            run(e, "vector")

        @block.gpsimd
        def _(e):
            run(e, "gpsimd")

        @block.tensor
        def _(e):
            run(e, "tensor")


def v3(ap, t=128):
    return ap.rearrange("p (c t) -> p c t", t=t)


def build(tp_tiles):
    TP = tp_tiles * 128
    nc = bass.Bass("TRN2", target_bir_lowering=False)

    def din(name, shape):
        return nc.dram_tensor(name, shape, F32, kind="ExternalInput").ap()

    def dout(name, shape):
        return nc.dram_tensor(name, shape, F32, kind="ExternalOutput").ap()

    xp = din("xp", [2, TP, D])
    xs = din("xs", [4, 64, D])
    cT_d = din("cT", [128, 8, 6])
    stg = din("stg", [4, 2, 128, 128])
    sth = din("sth", [4, 4, 128, 128])
    wada = din("wada", [128, 8, 3072])
    bada = din("bada", [6, 3072])
    gpre_d = din("gpre", [128, 8])
    win = din("win", [128, 8, NCOL])
    walpha_d = din("walpha", [16, 256])
    balpha_d = din("balpha", [128, 2])
    gon_d = din("gon", [128, 2])
    lbl_d = din("lbl", [128, 2, 4])
    wout = din("wout", [128, 8, D])
    gpost_d = din("gpost", [128, D])
    identf_d = din("identf", [128, 128])
    maskp_d = din("maskp", [128, 128])
    masks_d = din("masks", [128, 128])
    smaskp_d = din("smaskp", [128, 768])
    smasks_d = din("smasks", [128, 768])
    sel_d = din("sel", [6, 4, 128])

    yp = dout("yp", [2, TP, D])
    ys = dout("ys", [4, 64, D])
    sgp = dout("sgp", [2, 2, 128, 128])
    shp = dout("shp", [2, 4, 128, 128])
    sgs = dout("sgs", [4, 2, 128, 128])
    shs = dout("shs", [4, 4, 128, 128])

    es = ExitStack()
    with es:
        def sb(name, shape, dt=F32):
            return es.enter_context(nc.sbuf_tensor(name, shape, dt))

        def ps(name, shape, dt=F32):
            return es.enter_context(nc.psum_tensor(name, shape, dt))

        P = Prog(nc)

        w_in_bf = sb("w_in_bf", [128, 8, NCOL], BF16)
        w_out_bf = sb("w_out_bf", [128, 8, D], BF16)
        walpha_bf = sb("walpha_bf", [16, 256], BF16)
        identf = sb("identf_sb", [128, 128])
        identb = sb("identb", [128, 128], BF16)
        maskp = sb("maskp_sb", [128, 128])
        masks = sb("masks_sb", [128, 128])
        smaskp = sb("smaskp_sb", [128, 768])
        smasks = sb("smasks_sb", [128, 768])
        cst = sb("cst", [128, 32])
        nbalpha = cst[:, 0:2]
        lb = cst[:, 2:6]
        ln1mlb = cst[:, 6:10]
        balpha = cst[:, 10:12]
        gon = cst[:, 12:14]
        tmp4 = cst[:, 14:18]
        tmp4b = cst[:, 18:22]
        lbl = sb("lbl_sb", [128, 2, 4])
        gpre = sb("gpre_sb", [128, 8])
        aT = sb("aT", [128, 8, 6])
        sT = sb("sT", [128, 8, 6])
        GG = [sb(f"GG{g}", [128, D]) for g in range(4)]
        S_all = sb("S_all", [128, 6, 6, 128])

        PA = ps("PA", [128, 1024])
        PB = ps("PB", [128, 1024])
        PC = ps("PC", [128, 512])
        PD = ps("PD", [128, 512])
        PT0 = ps("PT0", [128, 512])
        PT1 = ps("PT1", [128, 512])

        ses = ExitStack()
        with ses:
            def ssb(name, shape, dt=F32):
                return ses.enter_context(nc.sbuf_tensor(name, shape, dt))
            stage = [ssb(f"stage{i}", [128, 4096]) for i in range(2)]
            mod_sb = ssb("mod_sb", [6, 3072])
            bada_sb = ssb("bada_sb", [6, 3072])
            gpost = ssb("gpost_sb", [128, D])
            cT = ssb("cT_sb", [128, 8, 6])
            sel = ssb("sel_sb", [6, 4, 128])
            tmp48 = ssb("tmp48", [128, 8, 6])

            small = [
                (cT[:], cT_d, "cT"), (bada_sb[:], bada, "bada"), (gpre[:], gpre_d, "gpre"),
                (balpha, balpha_d, "balpha"), (gon, gon_d, "gon"), (lbl[:], lbl_d, "lbl"),
                (identf[:], identf_d, "identf"), (maskp[:], maskp_d, "maskp"), (masks[:], masks_d, "masks"),
                (smaskp[:], smaskp_d, "smaskp"), (smasks[:], smasks_d, "smasks"), (sel[:], sel_d, "sel"),
                (gpost[:], gpost_d, "gpost"),
            ]
            names = []
            tok = None
            for o_, i_, nm in small:
                tok = P.op("sync", lambda e, o_=o_, i_=i_: e.dma_start(out=o_, in_=i_), writes=[nm], dma_sem="ld_s")
                names.append(nm)
            walpha_f = ssb("walpha_f", [16, 256])
            stage_wa = walpha_f[:]
            tok = P.op("sync", lambda e: e.dma_start(out=stage_wa, in_=walpha_d), writes=["walpha_f"], dma_sem="ld_s")
            names.append("walpha_f")
            for b in range(4):
                tok = P.op("sync", lambda e, b=b: e.dma_start(out=S_all[:, 2 + b, 0:2, :], in_=stg[b].rearrange("c p v -> p c v")),
                           writes=[f"S{2 + b}"], dma_sem="ld_s")
                tok = P.op("sync", lambda e, b=b: e.dma_start(out=S_all[:, 2 + b, 2:6, :], in_=sth[b].rearrange("c p v -> p c v")),
                           writes=[f"S{2 + b}h"], dma_sem="ld_s")
            P.retoken(names + [f"S{2 + b}" for b in range(4)] + [f"S{2 + b}h" for b in range(4)], tok)

            P.op("vector", lambda e: e.tensor_copy(out=identb[:], in_=identf[:]), reads=["identf"], writes=["identb"])
            P.op("vector", lambda e: e.tensor_copy(out=walpha_bf[:], in_=stage_wa), reads=["walpha_f"], writes=["walpha_bf"])
            P.op("vector", lambda e: e.tensor_scalar(out=nbalpha, in0=balpha, scalar1=-1.0, scalar2=None, op0=ALU.mult),
                 reads=["balpha"], writes=["nbalpha"])
            P.op("vector", lambda e: e.tensor_tensor(out=tmp4, in0=lbl[:, 1, :], in1=lbl[:, 0, :], op=ALU.subtract),
                 reads=["lbl"], writes=["tmp4"])
            P.op("scalar", lambda e: e.activation(out=tmp4, in_=tmp4, func=AF.Exp), reads=["tmp4"], writes=["tmp4"])
            P.op("vector", lambda e: e.tensor_scalar(out=tmp4b, in0=tmp4, scalar1=1.0, scalar2=None, op0=ALU.add),
                 reads=["tmp4"], writes=["tmp4b"])
            P.op("vector", lambda e: e.reciprocal(out=lb, in_=tmp4b), reads=["tmp4b"], writes=["lb"])
            P.op("vector", lambda e: e.tensor_tensor(out=tmp4b, in0=tmp4, in1=lb, op=ALU.mult), reads=["tmp4", "lb"], writes=["tmp4b"])
            P.op("scalar", lambda e: e.activation(out=ln1mlb, in_=tmp4b, func=AF.Ln), reads=["tmp4b"], writes=["ln1mlb"])
            P.op("gpsimd", lambda e: e.memset(S_all[:, 0:2, :, :], 0.0), writes=["S0", "S0h", "S1", "S1h"])

            si = 0
            for n in range(12):
                stg_ = stage[si % 2]
                sname = f"stage{si % 2}"
                st3 = stg_[:, 0:2048].rearrange("p (j n) -> p j n", n=256)
                P.op("sync", lambda e, st3=st3, n=n: e.dma_start(out=st3, in_=wada[:, :, n * 256:(n + 1) * 256]),
                     writes=[sname], dma_sem="ld_" + sname)
                P.op("tensor", [lambda e, j=j, st3=st3: e.matmul(PC[0:6, 0:256], lhsT=cT[:, j, :], rhs=st3[:, j, :],
                                                                  start=(j == 0), stop=(j == 7)) for j in range(8)],
                     reads=[sname, "cT"], writes=["pc"], banks=["c"])
                P.op("vector", lambda e, n=n: e.tensor_tensor(out=mod_sb[0:6, n * 256:(n + 1) * 256], in0=PC[0:6, 0:256],
                                                             in1=bada_sb[0:6, n * 256:(n + 1) * 256], op=ALU.add),
                     reads=["pc", "bada"], writes=["mod"], banks=["c"])
                si += 1
            P.op("tensor", [lambda e, k=k: e.transpose(out=PD[:, k * 6:(k + 1) * 6], in_=mod_sb[0:6, k * 128:(k + 1) * 128],
                                                       identity=identf[0:6, 0:6]) for k in range(16)],
                 reads=["mod", "identf"], writes=["pd"], banks=["d"])
            P.op("vector", lambda e: e.tensor_copy(out=sT[:], in_=PD[:, 0:48].rearrange("p (j b) -> p j b", b=6)),
                 reads=["pd"], writes=["sT"], banks=["d"])
            P.op("vector", lambda e: e.tensor_scalar(out=tmp48[:], in0=PD[:, 48:96].rearrange("p (j b) -> p j b", b=6),
                                                     scalar1=1.0, scalar2=None, op0=ALU.add),
                 reads=["pd"], writes=["tmp48"], banks=["d"])
            P.op("vector", lambda e: e.tensor_tensor(out=aT[:], in0=tmp48[:], in1=gpre[:].unsqueeze(2).broadcast_to([128, 8, 6]),
                                                     op=ALU.mult), reads=["tmp48", "gpre"], writes=["aT"])
            for g in range(4):
                for n in range(2):
                    P.op("tensor", lambda e, g=g, n=n: e.matmul(PC[:, 0:512], lhsT=sel[0:6, g, :],
                                                                  rhs=mod_sb[0:6, 2048 + n * 512:2048 + (n + 1) * 512],
                                                                  start=True, stop=True),
                         reads=["mod", "sel"], writes=["pc"], banks=["c"])
                    P.op("vector", lambda e, g=g, n=n: e.tensor_tensor(out=GG[g][:, n * 512:(n + 1) * 512], in0=PC[:, 0:512],
                                                                      in1=gpost[:, n * 512:(n + 1) * 512], op=ALU.mult),
                         reads=["pc", "gpost"], writes=[f"GG{g}"], banks=["c"])
            for j in range(8):
                for hlf in range(2):
                    stg_ = stage[si % 2]
                    sname = f"stage{si % 2}"
                    c0 = hlf * 1800
                    P.op("sync", lambda e, stg_=stg_, j=j, c0=c0: e.dma_start(out=stg_[:, 0:1800], in_=win[:, j, c0:c0 + 1800]),
                         writes=[sname], dma_sem="ld_" + sname)
                    eng = "vector" if hlf == 0 else "scalar"
                    if eng == "vector":
                        P.op("vector", lambda e, stg_=stg_, j=j, c0=c0: e.tensor_copy(out=w_in_bf[:, j, c0:c0 + 1800], in_=stg_[:, 0:1800]),
                             reads=[sname], writes=[f"win{j}_{hlf}"])
                    else:
                        P.op("scalar", lambda e, stg_=stg_, j=j, c0=c0: e.activation(out=w_in_bf[:, j, c0:c0 + 1800], in_=stg_[:, 0:1800],
                                                                                      func=AF.Copy),
                             reads=[sname], writes=[f"win{j}_{hlf}"])
                    si += 1
            for jj in range(4):
                stg_ = stage[si % 2]
                sname = f"stage{si % 2}"
                st3 = stg_[:, 0:2048].rearrange("p (j n) -> p j n", n=1024)
                P.op("sync", lambda e, st3=st3, jj=jj: e.dma_start(out=st3, in_=wout[:, 2 * jj:2 * jj + 2, :]),
                     writes=[sname], dma_sem="ld_" + sname)
                for jl in range(2):
                    j = 2 * jj + jl
                    gcol = gon[:, 0:1] if j < 4 else gon[:, 1:2]
                    P.op("gpsimd", lambda e, st3=st3, jl=jl, j=j, gcol=gcol: e.tensor_scalar(
                        out=w_out_bf[:, j, :], in0=st3[:, jl, :], scalar1=gcol, scalar2=1.0, op0=ALU.mult, op1=ALU.mult),
                        reads=[sname, "gon"], writes=[f"wout{j}"])
                si += 1
            P.barrier()
        WIN = [f"win{j}_{h}" for j in range(8) for h in range(2)]
        WOUT = [f"wout{j}" for j in range(8)]

        NXS = 4
        x_sb = [sb(f"x_sb{i}", [128, D]) for i in range(NXS)]
        statA = sb("statA", [128, 8])
        xn = sb("xn", [128, D], BF16)
        hT = sb("hT", [128, 8, 128], BF16)
        alr = sb("alr", [16, 128], BF16)
        e1 = sb("e1", [128, 256])
        eh = sb("eh", [128, 512])
        L1 = sb("L1", [128, 512])
        L2 = sb("L2", [128, 512])
        gT = sb("gT", [128, 768])
        bT = sb("bT", [128, 768])
        eqb = sb("eqb", [128, 512])
        EQa = sb("EQa", [128, 256])
        EKa = sb("EKa", [128, 256])
        QT_ = [sb(f"QT{p}", [128, 4, 128], BF16) for p in range(2)]
        QTa2_ = [sb(f"QTa2{p}", [128, 2, 2, 128], BF16) for p in range(2)]
        KT_ = [sb(f"KT{p}", [128, 6, 128], BF16) for p in range(2)]
        V_ = [sb(f"V{p}", [128, D], BF16) for p in range(2)]
        ez_ = [sb(f"ez{p}", [128, D]) for p in range(2)]
        sm_ = [sb(f"sm{p}", [128, 2, 6, 6]) for p in range(2)]
        Ktm2 = sb("Ktm2", [128, 2, 6, 128], BF16)
        Sp = sb("Sp", [128, 2, 6, 128], BF16)
        AT = sb("AT", [128, 8, 128], BF16)
        sq = sb("sq", [128, D])
        so = sb("so", [128, 24])
        statB = sb("statB", [128, 8])
        ohat = sb("ohat", [128, D], BF16)
        ohT = sb("ohT", [128, 8, 128], BF16)

        bT3 = v3(bT[:])
        PT0b = PT0[:].bitcast(BF16)
        PT1b = PT1[:].bitcast(BF16)
        PCb = PC[:].bitcast(BF16)
        PDb = PD[:].bitcast(BF16)
        P.op("gpsimd", lambda e: e.memset(Ktm2[:], 0.0), writes=["Ktma", "Ktmh"])
        for p in range(2):
            P.op("gpsimd", lambda e, p=p: e.memset(QTa2_[p][:], 0.0), writes=[f"QTa{p}"])

        def load_x(i, xsrc):
            slot = i % NXS
            P.op("sync", lambda e: e.dma_start(out=x_sb[slot][:], in_=xsrc), writes=[f"x{slot}"], dma_sem=f"xld{slot}")

        def stage12(ctx):
            i, segs, par = ctx["i"], ctx["segs"], ctx["i"] % 2
            slot = i % NXS
            xs_, xb = x_sb[slot], f"x{slot}"
            QT, QTa2, KT, V, ez, sm = QT_[par], QTa2_[par], KT_[par], V_[par], ez_[par], sm_[par]
            nQTh, nQTa, nKTa, nKTh, nVa, nVh, ngza, ngzh = (f"{n}{par}" for n in ("QTh", "QTa", "KTa", "KTh", "Va", "Vh", "gza", "gzh"))
            smn = [f"sm{par}_{s}" for s in range(2)]
            sme = [f"sm{par}_{s}e" for s in range(2)]
            if all(sg["b"] == segs[0]["b"] for sg in segs):
                mods = [(0, 128, segs[0]["b"])]
            else:
                mods = [(sg["lo"], sg["n"], sg["b"]) for sg in segs]
            P.op("scalar", lambda e: e.activation(out=hT[:].rearrange("p j t -> p (j t)"), in_=xs_[:], func=AF.Square, accum_out=statA[:, 0:1]),
                 reads=[xb], writes=["hT0", "hT1", "sa0"])
            yield
            P.op("scalar", lambda e: e.activation(out=statA[:, 1:2], in_=statA[:, 0:1], func=AF.Ln, scale=1.0 / D, bias=EPS),
                 reads=["sa0"], writes=["sa1"])
            yield
            P.op("scalar", lambda e: e.activation(out=statA[:, 2:3], in_=statA[:, 1:2], func=AF.Exp, scale=-0.5),
                 reads=["sa1"], writes=["sa2"])
            yield
            P.op("gpsimd", lambda e: e.tensor_scalar(out=xn[:], in0=xs_[:], scalar1=statA[:, 2:3], scalar2=1.0,
                                                       op0=ALU.mult, op1=ALU.mult), reads=[xb, "sa2"], writes=["xn"])
            yield
            for half, (PTb, bank, eng) in enumerate(((PCb, "c", "scalar"), (PDb, "d", "vector"))):
                PT3 = v3(PTb)
                P.op("tensor", [lambda e, j=j, PT3=PT3, half=half: e.transpose(
                    out=PT3[:, j, :], in_=xn[:, (4 * half + j) * 128:(4 * half + j + 1) * 128], identity=identb[:]) for j in range(4)],
                    reads=["xn", "identb"], writes=[bank], banks=[bank])
                yield
                for j in range(4):
                    jj = 4 * half + j
                    for (lo, n, b) in mods:
                        if eng == "scalar":
                            P.op("scalar", lambda e, j=j, jj=jj, lo=lo, n=n, b=b, PT3=PT3: e.activation(
                                out=hT[:, jj, lo:lo + n], in_=PT3[:, j, lo:lo + n], func=AF.Identity,
                                scale=aT[:, jj, b:b + 1], bias=sT[:, jj, b:b + 1]),
                                reads=[bank, "aT", "sT"], writes=[f"hT{half}"], banks=[bank])
                        else:
                            P.op("vector", lambda e, j=j, jj=jj, lo=lo, n=n, b=b, PT3=PT3: e.tensor_scalar(
                                out=hT[:, jj, lo:lo + n], in0=PT3[:, j, lo:lo + n], scalar1=aT[:, jj, b:b + 1],
                                scalar2=sT[:, jj, b:b + 1], op0=ALU.mult, op1=ALU.add),
                                reads=[bank, "aT", "sT"], writes=[f"hT{half}"], banks=[bank])
                        yield
            HT = ["hT0", "hT1"]

            def fm(out_ap, col0, m):
                return [lambda e, j=j: e.matmul(out_ap, lhsT=w_in_bf[:, j, col0:col0 + m], rhs=hT[:, j, :],
                                                start=(j == 0), stop=(j == 7)) for j in range(8)]

            def tm(out_ap, col0):
                return [lambda e, j=j: e.matmul(out_ap, lhsT=hT[:, j, :], rhs=w_in_bf[:, j, col0:col0 + 512],
                                                start=(j == 0), stop=(j == 7)) for j in range(8)]

            P.op("tensor", fm(PA[0:16, 0:128], C_AL, 16), reads=HT + WIN, writes=["a0"], banks=["a0"])
            yield
            P.op("vector", lambda e: e.tensor_copy(out=alr[:], in_=PA[0:16, 0:128]), reads=["a0"], writes=["alr"], banks=["a0"])
            yield
            fl = []
            for c in range(4):
                fl += fm(PA[:, 512 + c * 128:512 + (c + 1) * 128], C_FH + c * 128, 128)
            P.op("tensor", fl, reads=HT + WIN, writes=["a1"], banks=["a1"])
            yield
            P.op("tensor", [lambda e, c=c: e.matmul(PA[:, 128 + c * 128:256 + c * 128], lhsT=walpha_bf[0:16, c * 128:(c + 1) * 128],
                                                    rhs=alr[0:16, :], start=True, stop=True) for c in range(2)],
                 reads=["alr", "walpha_bf"], writes=["a0"], banks=["a0"])
            yield
            P.op("scalar", lambda e: e.activation(out=eh[:], in_=PA[:, 512:1024], func=AF.Exp, scale=-1.0),
                 reads=["a1"], writes=["eh"], banks=["a1"])
            yield
            fl = []
            for c in range(4):
                fl += fm(PC[:, c * 128:(c + 1) * 128], C_QH + c * 128, 128)
            P.op("tensor", fl, reads=HT + WIN, writes=["c"], banks=["c"])
            yield
            for c in range(2):
                P.op("scalar", lambda e, c=c: e.activation(out=e1[:, c * 128:(c + 1) * 128], in_=PA[:, 128 + c * 128:256 + c * 128],
                                                           func=AF.Exp, scale=-1.0, bias=nbalpha[:, c:c + 1]),
                     reads=["a0", "nbalpha"], writes=["e1"], banks=["a0"])
                yield
            P.op("scalar", lambda e: e.activation(out=L1[:], in_=eh[:], func=AF.Ln, bias=1.0), reads=["eh"], writes=["L1"])
            yield
            fl = []
            for c in range(2):
                fl += fm(PD[:, c * 128:(c + 1) * 128], C_QA + c * 128, 128)
            for c in range(2):
                fl += fm(PD[:, (2 + c) * 128:(3 + c) * 128], C_KA + c * 128, 128)
            P.op("tensor", fl, reads=HT + WIN, writes=["d"], banks=["d"])
            yield
            P.op("scalar", lambda e: e.activation(out=e1[:], in_=e1[:], func=AF.Ln, bias=1.0), reads=["e1"], writes=["e1"])
            yield
            P.op("gpsimd", lambda e: e.tensor_scalar(out=gT[:, 0:256], in0=e1[:], scalar1=-1.0 / 16.0, scalar2=1.0,
                                                       op0=ALU.mult, op1=ALU.mult), reads=["e1"], writes=["gTa"])
            yield
            for c in range(4):
                P.op("scalar", lambda e, c=c: e.activation(out=L2[:, c * 128:(c + 1) * 128], in_=eh[:, c * 128:(c + 1) * 128],
                                                           func=AF.Ln, bias=1.0, scale=lb[:, c:c + 1]),
                     reads=["eh", "lb"], writes=["L2"])
                yield
            P.op("vector", lambda e: e.tensor_tensor(out=gT[:, 256:768], in0=L2[:], in1=L1[:], op=ALU.subtract),
                 reads=["L1", "L2"], writes=["gTh"])
            yield
            P.op("vector", lambda e: e.tensor_tensor(out=L1[:], in0=PA[:, 512:1024], in1=L1[:], op=ALU.add),
                 reads=["a1", "L1"], writes=["L1"], banks=["a1"])
            yield
            P.op("vector", lambda e: e.tensor_tensor_scan(out=bT[:], data0=smasks[:], data1=gT[:], initial=0.0,
                                                          op0=ALU.mult, op1=ALU.add),
                 reads=["gTa", "gTh", "smasks"], writes=["bT"])
            yield
            P.op("tensor", tm(PA[:, 0:512], C_ZA), reads=HT + WIN, writes=["a0"], banks=["a0"])
            yield
            P.op("tensor", tm(PA[:, 512:1024], C_VA), reads=HT + WIN, writes=["a1"], banks=["a1"])
            yield
            P.op("vector", lambda e: e.tensor_copy(out=V[:, 0:512], in_=PA[:, 512:1024]), reads=["a1"], writes=[nVa], banks=["a1"])
            yield
            P.op("tensor", tm(PA[:, 512:1024], C_IH), reads=HT + WIN, writes=["a1"], banks=["a1"])
            yield
            P.op("scalar", lambda e: e.activation(out=eqb[:], in_=PC[:, :], func=AF.Exp, scale=-1.0),
                 reads=["c"], writes=["eqb"], banks=["c"])
            yield
            P.op("scalar", lambda e: e.activation(out=eqb[:], in_=eqb[:], func=AF.Ln, bias=1.0), reads=["eqb"], writes=["eqb"])
            yield

            def zgate(ps_ap, ez_ap, gbuf, bank):
                P.op("scalar", lambda e: e.activation(out=ez_ap, in_=ps_ap, func=AF.Exp, scale=-1.0),
                     reads=[bank], writes=[gbuf], banks=[bank])
                yield
                P.op("scalar", lambda e: e.activation(out=ez_ap, in_=ez_ap, func=AF.Ln, bias=1.0), reads=[gbuf], writes=[gbuf])
                yield
                P.op("scalar", lambda e: e.activation(out=ez_ap, in_=ez_ap, func=AF.Exp, scale=-1.0), reads=[gbuf], writes=[gbuf])
                yield
                P.op("vector", lambda e: e.tensor_tensor(out=ez_ap, in0=ps_ap, in1=ez_ap, op=ALU.mult),
                     reads=[bank, gbuf], writes=[gbuf], banks=[bank])
                yield
            yield from zgate(PA[:, 0:512], ez[:, 0:512], ngza, "a0")
            P.op("vector", lambda e: e.tensor_copy(out=V[:, 512:1024], in_=PA[:, 512:1024]), reads=["a1"], writes=[nVh], banks=["a1"])
            yield
            P.op("tensor", tm(PA[:, 0:512], C_ZH), reads=HT + WIN, writes=["a0"], banks=["a0"])
            yield
            for si_, sg in enumerate(segs):
                lo, n = sg["lo"], sg["n"]
                r = lo + n // 2 - 1
                l = lo + n - 1
                P.op("vector", lambda e, si_=si_, r=r: e.tensor_scalar(out=sm[:, si_, 0, :], in0=bT3[:, :, r], scalar1=-1.0,
                                                                      scalar2=None, op0=ALU.mult),
                     reads=["bT"], writes=[smn[si_]])
                yield
                P.op("vector", lambda e, si_=si_, r=r: e.tensor_tensor(out=sm[:, si_, 1, 2:6], in0=bT3[:, 2:6, r], in1=ln1mlb,
                                                                      op=ALU.add), reads=["bT", "ln1mlb"], writes=[smn[si_]])
                yield
                P.op("vector", lambda e, si_=si_, r=r, l=l: e.tensor_tensor(out=sm[:, si_, 2, :], in0=bT3[:, :, l], in1=bT3[:, :, r],
                                                                           op=ALU.subtract), reads=["bT"], writes=[smn[si_]])
                yield
                P.op("scalar", lambda e, si_=si_, r=r: e.activation(out=sm[:, si_, 3, :], in_=bT3[:, :, r], func=AF.Exp),
                     reads=["bT"], writes=[sme[si_]])
                yield
                P.op("scalar", lambda e, si_=si_, l=l: e.activation(out=sm[:, si_, 4, :], in_=bT3[:, :, l], func=AF.Exp),
                     reads=["bT"], writes=[sme[si_]])
                yield
                P.op("scalar", lambda e, si_=si_: e.activation(out=sm[:, si_, 5, :], in_=sm[:, si_, 2, :], func=AF.Exp),
                     reads=[smn[si_]], writes=[sme[si_]])
                yield
            for c in range(2):
                for si_, sg in enumerate(segs):
                    lo, n = sg["lo"], sg["n"]
                    r = lo + n // 2 - 1
                    c0 = c * 128 + lo
                    P.op("scalar", lambda e, c=c, si_=si_, c0=c0, n=n: e.activation(
                        out=EQa[:, c0:c0 + n], in_=bT[:, c0:c0 + n], func=AF.Exp, bias=sm[:, si_, 0, c:c + 1]),
                        reads=["bT", smn[si_]], writes=["EQa"])
                    yield
                    P.op("scalar", lambda e, c=c, r=r, c0=c0, n=n: e.activation(
                        out=EKa[:, c0:c0 + n], in_=bT[:, c0:c0 + n], func=AF.Exp, scale=-1.0, bias=bT3[:, c, r:r + 1]),
                        reads=["bT"], writes=["EKa"])
                    yield
            for hh in range(2):
                P.op("vector", lambda e, hh=hh: e.scalar_tensor_tensor(
                    out=QTa2[hh * 64:(hh + 1) * 64, hh, :, :], in0=v3(PD[hh * 64:(hh + 1) * 64, 0:256]), scalar=0.125,
                    in1=v3(EQa[hh * 64:(hh + 1) * 64, :]), op0=ALU.mult, op1=ALU.mult),
                    reads=["d", "EQa"], writes=[nQTa], banks=["d"])
                yield
            P.op("vector", lambda e: e.tensor_tensor(out=KT[:, 0:2, :], in0=v3(PD[:, 256:512]), in1=v3(EKa[:]), op=ALU.mult),
                 reads=["d", "EKa"], writes=[nKTa], banks=["d"])
            yield
            P.op("vector", lambda e: e.tensor_tensor(out=L1[:], in0=L1[:], in1=bT[:, 256:768], op=ALU.add),
                 reads=["L1", "bT"], writes=["L1"])
            yield
            P.op("vector", lambda e: e.tensor_tensor(out=eqb[:], in0=bT[:, 256:768], in1=eqb[:], op=ALU.subtract),
                 reads=["eqb", "bT"], writes=["eqb"])
            yield
            for c in range(4):
                for si_, sg in enumerate(segs):
                    lo, n = sg["lo"], sg["n"]
                    c0 = c * 128 + lo
                    P.op("scalar", lambda e, c=c, si_=si_, c0=c0, lo=lo, n=n: e.activation(
                        out=KT[:, 2 + c, lo:lo + n], in_=L1[:, c0:c0 + n], func=AF.Exp, scale=-1.0, bias=sm[:, si_, 1, 2 + c:3 + c]),
                        reads=["L1", smn[si_]], writes=[nKTh])
                    yield
                    P.op("scalar", lambda e, c=c, si_=si_, c0=c0, n=n: e.activation(
                        out=eqb[:, c0:c0 + n], in_=eqb[:, c0:c0 + n], func=AF.Exp, bias=sm[:, si_, 0, 2 + c:3 + c]),
                        reads=["eqb", smn[si_]], writes=["eqb"])
                    yield
            P.op("vector", lambda e: e.tensor_tensor(out=QT[:, :, :], in0=v3(PC[:, :]), in1=v3(eqb[:]), op=ALU.mult),
                 reads=["c", "eqb"], writes=[nQTh], banks=["c"])
            yield
            yield from zgate(PA[:, 0:512], ez[:, 512:1024], ngzh, "a0")

        def stage34(ctx):
            i, segs, par, gg = ctx["i"], ctx["segs"], ctx["i"] % 2, ctx["gg"]
            slot = i % NXS
            xs_, xb = x_sb[slot], f"x{slot}"
            QT, QTa2, KT, V, ez, sm = QT_[par], QTa2_[par], KT_[par], V_[par], ez_[par], sm_[par]
            nQTh, nQTa, nKTa, nKTh, nVa, nVh, ngza, ngzh = (f"{n}{par}" for n in ("QTh", "QTa", "KTa", "KTh", "Va", "Vh", "gza", "gzh"))
            sme = [f"sm{par}_{s}e" for s in range(2)]
            PT03, PT13 = v3(PT0b), v3(PT1b)
            P.op("tensor", [lambda e, c=c: e.transpose(out=PT03[:, c, :], in_=KT[:, c, :], identity=identb[:]) for c in range(6)],
                 reads=[nKTa, nKTh, "identb"], writes=["t0"], banks=["t0"])
            yield
            for si_, sg in enumerate(segs):
                lo, n = sg["lo"], sg["n"]
                P.op("scalar", lambda e, si_=si_, lo=lo, n=n: e.activation(out=Ktm2[lo:lo + n, si_, 0:6, :], in_=PT03[lo:lo + n, 0:6, :],
                                                                          func=AF.Copy),
                     reads=["t0"], writes=["Ktm"], banks=["t0"])
                yield
            P.op("tensor", [lambda e, h=h: e.matmul(PB[:, h * 128:(h + 1) * 128], lhsT=KT[:, h // 2, :], rhs=QTa2[:, h % 2, h // 2, :],
                                                    start=True, stop=True) for h in range(4)],
                 reads=[nKTa, nQTa], writes=["b0"], banks=["b0"])
            yield
            P.op("vector", lambda e: e.tensor_tensor(out=AT[:, 0:4, :], in0=v3(PB[:, 0:512]),
                                                     in1=masks[:].unsqueeze(1).broadcast_to([128, 4, 128]), op=ALU.mult),
                 reads=["b0", "masks"], writes=["ATa"], banks=["b0"])
            yield
            P.op("tensor", [lambda e, h=h: e.matmul(PB[:, (4 + h) * 128:(5 + h) * 128], lhsT=KT[:, 2 + h, :], rhs=QT[:, h, :],
                                                    start=True, stop=True) for h in range(4)],
                 reads=[nKTh, nQTh], writes=["b1"], banks=["b1"])
            yield
            P.op("vector", lambda e: e.tensor_tensor(out=AT[:, 4:8, :], in0=v3(PB[:, 512:1024]),
                                                     in1=masks[:].unsqueeze(1).broadcast_to([128, 4, 128]), op=ALU.mult),
                 reads=["b1", "masks"], writes=["ATh"], banks=["b1"])
            yield
            for si_, sg in enumerate(segs):
                lo, n, st = sg["lo"], sg["n"], sg["st"]
                P.op("gpsimd", lambda e, si_=si_, st=st: e.tensor_tensor(
                    out=Sp[:, si_], in0=S_all[:, st], in1=sm[:, si_, 3, :].unsqueeze(2).broadcast_to([128, 6, 128]), op=ALU.mult),
                    reads=[f"S{st}", f"S{st}h", sme[si_]], writes=[f"Sp{si_}"])
                yield
                P.op("gpsimd", lambda e, si_=si_, st=st: e.tensor_tensor(
                    out=S_all[:, st], in0=S_all[:, st], in1=sm[:, si_, 4, :].unsqueeze(2).broadcast_to([128, 6, 128]), op=ALU.mult),
                    reads=[sme[si_]], writes=[f"S{st}", f"S{st}h"])
                yield
                fl = []
                for h in range(4):
                    c = h // 2
                    fl.append(lambda e, h=h, lo=lo, n=n: e.matmul(
                        PB[lo:lo + n, h * 128:(h + 1) * 128], lhsT=AT[:, h, lo:lo + n], rhs=V[:, h * 128:(h + 1) * 128],
                        start=True, stop=False))
                    fl.append(lambda e, h=h, c=c, lo=lo, n=n, si_=si_: e.matmul(
                        PB[lo:lo + n, h * 128:(h + 1) * 128], lhsT=QTa2[:, h % 2, c, lo:lo + n], rhs=Sp[:, si_, c, :],
                        start=False, stop=True))
                P.op("tensor", fl, reads=["ATa", nVa, nQTa, f"Sp{si_}"], writes=["b0"], banks=["b0"])
                yield
                fl = []
                for h in range(4):
                    fl.append(lambda e, h=h, lo=lo, n=n: e.matmul(
                        PB[lo:lo + n, (4 + h) * 128:(5 + h) * 128], lhsT=AT[:, 4 + h, lo:lo + n],
                        rhs=V[:, (4 + h) * 128:(5 + h) * 128], start=True, stop=False))
                    fl.append(lambda e, h=h, lo=lo, n=n, si_=si_: e.matmul(
                        PB[lo:lo + n, (4 + h) * 128:(5 + h) * 128], lhsT=QT[:, h, lo:lo + n], rhs=Sp[:, si_, 2 + h, :],
                        start=False, stop=True))
                P.op("tensor", fl, reads=["ATh", nVh, nQTh, f"Sp{si_}"], writes=["b1"], banks=["b1"])
                yield
                fl = []
                for h in range(4):
                    c, r0 = h // 2, (h % 2) * 64
                    fl.append(lambda e, h=h, c=c, r0=r0, si_=si_: e.matmul(
                        PT0[r0:r0 + 64, c * 128:(c + 1) * 128], lhsT=Ktm2[:, si_, c, r0:r0 + 64], rhs=V[:, h * 128:(h + 1) * 128],
                        start=True, stop=True))
                P.op("tensor", fl, reads=["Ktm", nVa], writes=["t0"], banks=["t0"])
                yield
                fl = []
                for h in range(4):
                    fl.append(lambda e, h=h, si_=si_: e.matmul(
                        PT1[:, h * 128:(h + 1) * 128], lhsT=Ktm2[:, si_, 2 + h, :], rhs=V[:, (4 + h) * 128:(5 + h) * 128],
                        start=True, stop=True))
                P.op("tensor", fl, reads=["Ktm", nVh], writes=["t1"], banks=["t1"])
                yield
                for c in range(2):
                    P.op("vector", lambda e, c=c, si_=si_, st=st: e.scalar_tensor_tensor(
                        out=S_all[:, st, c, :], in0=PT0[:, c * 128:(c + 1) * 128], scalar=sm[:, si_, 5, c:c + 1], in1=S_all[:, st, c, :],
                        op0=ALU.mult, op1=ALU.add),
                        reads=["t0", sme[si_]], writes=[f"S{st}"], banks=["t0"])
                    yield
                for h in range(4):
                    P.op("vector", lambda e, h=h, si_=si_, st=st: e.scalar_tensor_tensor(
                        out=S_all[:, st, 2 + h, :], in0=PT1[:, h * 128:(h + 1) * 128], scalar=sm[:, si_, 5, 2 + h:3 + h],
                        in1=S_all[:, st, 2 + h, :], op0=ALU.mult, op1=ALU.add),
                        reads=["t1", sme[si_]], writes=[f"S{st}h"], banks=["t1"])
                    yield
            P.op("scalar", lambda e: e.activation(out=sq[:, 0:512], in_=PB[:, 0:512], func=AF.Square),
                 reads=["b0"], writes=["sqa"], banks=["b0"])
            yield
            P.op("scalar", lambda e: e.activation(out=sq[:, 512:1024], in_=PB[:, 512:1024], func=AF.Square),
                 reads=["b1"], writes=["sqh"], banks=["b1"])
            yield
            P.op("vector", lambda e: e.reduce_sum(out=so[:, 0:8], in_=v3(sq[:]), axis=AX.X), reads=["sqa", "sqh"], writes=["so0"])
            yield
            P.op("scalar", lambda e: e.activation(out=so[:, 8:16], in_=so[:, 0:8], func=AF.Ln, scale=1.0 / 128, bias=EPS),
                 reads=["so0"], writes=["so1"])
            yield
            P.op("scalar", lambda e: e.activation(out=so[:, 16:24], in_=so[:, 8:16], func=AF.Exp, scale=-0.5),
                 reads=["so1"], writes=["so2"])
            yield
            P.op("gpsimd", lambda e: e.tensor_tensor(out=v3(ez[:]), in0=v3(ez[:]),
                                                      in1=so[:, 16:24].unsqueeze(2).broadcast_to([128, 8, 128]), op=ALU.mult),
                 reads=["so2"], writes=[ngza, ngzh])
            yield
            P.op("vector", lambda e: e.tensor_tensor(out=ohat[:, 0:512], in0=PB[:, 0:512], in1=ez[:, 0:512], op=ALU.mult),
                 reads=["b0", ngza], writes=["ohata"], banks=["b0"])
            yield
            P.op("vector", lambda e: e.tensor_tensor(out=ohat[:, 512:1024], in0=PB[:, 512:1024], in1=ez[:, 512:1024], op=ALU.mult),
                 reads=["b1", ngzh], writes=["ohath"], banks=["b1"])
            yield
            P.op("tensor", [lambda e, j=j: e.transpose(out=PT03[:, j, :], in_=ohat[:, j * 128:(j + 1) * 128], identity=identb[:])
                            for j in range(4)], reads=["ohata", "identb"], writes=["t0"], banks=["t0"])
            yield
            P.op("scalar", lambda e: e.activation(out=ohT[:, 0:4, :], in_=PT03[:, 0:4, :], func=AF.Copy),
                 reads=["t0"], writes=["ohTa"], banks=["t0"])
            yield
            P.op("tensor", [lambda e, j=j: e.transpose(out=PT13[:, j, :], in_=ohat[:, (4 + j) * 128:(5 + j) * 128], identity=identb[:])
                            for j in range(4)], reads=["ohath", "identb"], writes=["t1"], banks=["t1"])
            yield
            P.op("vector", lambda e: e.tensor_copy(out=ohT[:, 4:8, :], in_=PT13[:, 0:4, :]), reads=["t1"], writes=["ohTh"], banks=["t1"])
            yield
            for n_ in range(2):
                P.op("tensor", [lambda e, j=j, n_=n_: e.matmul(PB[:, n_ * 512:(n_ + 1) * 512], lhsT=ohT[:, j, :],
                                                                 rhs=w_out_bf[:, j, n_ * 512:(n_ + 1) * 512], start=(j == 0), stop=(j == 7))
                                for j in range(8)], reads=["ohTa", "ohTh"] + WOUT, writes=[f"b{n_}"], banks=[f"b{n_}"])
                yield
            P.op("scalar", lambda e: e.activation(out=ohat[:], in_=PB[:, :], func=AF.Square, accum_out=statB[:, 0:1]),
                 reads=["b0", "b1"], writes=["ohata", "ohath", "sb0"], banks=["b0", "b1"])
            yield
            P.op("scalar", lambda e: e.activation(out=statB[:, 1:2], in_=statB[:, 0:1], func=AF.Ln, scale=1.0 / D, bias=EPS),
                 reads=["sb0"], writes=["sb1"])
            yield
            P.op("scalar", lambda e: e.activation(out=statB[:, 2:3], in_=statB[:, 1:2], func=AF.Exp, scale=-0.5),
                 reads=["sb1"], writes=["sb2"])
            yield
            P.op("vector", lambda e: e.scalar_tensor_tensor(out=sq[:], in0=PB[:, :], scalar=statB[:, 2:3], in1=GG[gg][:],
                                                            op0=ALU.mult, op1=ALU.mult),
                 reads=["b0", "b1", "sb2", f"GG{gg}"], writes=["sqa", "sqh"], banks=["b0", "b1"])
            yield
            P.op("gpsimd", lambda e: e.tensor_tensor(out=xs_[:], in0=xs_[:], in1=sq[:], op=ALU.add), reads=[xb, "sqa", "sqh"], writes=[xb])
            yield
            P.op("sync", lambda e: e.dma_start(out=ctx["ydst"], in_=xs_[:]), reads=[xb], dma_sem=f"yst{slot}")
            yield
            k = ctx["k"]
            if k is not None:
                store_state(2 + 2 * k, sgs[2 * k], shs[2 * k])
                store_state(3 + 2 * k, sgs[2 * k + 1], shs[2 * k + 1])

        def store_state(st, gdst, hdst):
            P.op("sync", lambda e: e.dma_start(out=gdst.rearrange("c p v -> p c v"), in_=S_all[:, st, 0:2, :]),
                 reads=[f"S{st}"], dma_sem="sout")
            P.op("sync", lambda e: e.dma_start(out=hdst.rearrange("c p v -> p c v"), in_=S_all[:, st, 2:6, :]),
                 reads=[f"S{st}h"], dma_sem="sout")

        tiles = []
        for k in range(2):
            segs = [dict(lo=0, n=64, b=2 + 2 * k, st=2 + 2 * k), dict(lo=64, n=64, b=3 + 2 * k, st=3 + 2 * k)]
            tiles.append(dict(xsrc=xs[2 * k:2 * k + 2].rearrange("b t d -> (b t) d"), ydst=ys[2 * k:2 * k + 2].rearrange("b t d -> (b t) d"),
                              segs=segs, gg=2 + k, k=k))
        for t in range(tp_tiles):
            for s in range(2):
                tiles.append(dict(xsrc=xp[s, t * 128:(t + 1) * 128, :], ydst=yp[s, t * 128:(t + 1) * 128, :],
                                  segs=[dict(lo=0, n=64, b=s, st=s), dict(lo=64, n=64, b=s, st=s)], gg=s, k=None))
        for i, t in enumerate(tiles):
            t["i"] = i
        NT = len(tiles)
        PRE = 2
        for i in range(min(PRE, NT)):
            load_x(i, tiles[i]["xsrc"])
        for r in range(NT + 1):
            if r + PRE < NT:
                load_x(r + PRE, tiles[r + PRE]["xsrc"])
            gens = []
            if r >= 1:
                gens.append(stage34(tiles[r - 1]))
            if r < NT:
                gens.append(stage12(tiles[r]))
            while gens:
                for g in list(gens):
                    try:
                        next(g)
                    except StopIteration:
                        gens.remove(g)
        for s in range(2):
            store_state(s, sgp[s], shp[s])
        for nm, s in list(P.sems.items()):
            if nm.startswith("yst") or nm == "sout":
                P.wait_token("sync", (nm, s[1]))
        with nc.Block() as block:
            P.replay(block)
        P.close()
        import os as _os
        if _os.environ.get("KVERB"):
            print("total ops recorded", P.count)
    return nc


def host_inputs(core, x_prompt, x_sample, c_prompt, c_sample, state_gla, state_hgrn, w_ada, b_ada, g_pre,
                w_in, w_alpha, b_alpha, g_onorm_gla, hgrn_lb_logits, g_onorm_hgrn, w_out, g_post, consts):
    f = np.float32
    c6 = np.concatenate([c_prompt[2 * core:2 * core + 2], c_sample[4 * core:4 * core + 4]], 0)

    def pj(a):
        return np.ascontiguousarray(a.reshape(8, 128, a.shape[1]).transpose(1, 0, 2))

    m = {
        "xp": np.ascontiguousarray(x_prompt[2 * core:2 * core + 2]),
        "xs": np.ascontiguousarray(x_sample[4 * core:4 * core + 4]),
        "cT": np.ascontiguousarray(c6.T.reshape(8, 128, 6).transpose(1, 0, 2)),
        "stg": np.ascontiguousarray(state_gla[0, 4 * core:4 * core + 4].reshape(4, 2, 128, 128)),
        "sth": np.ascontiguousarray(state_hgrn[0, 4 * core:4 * core + 4]),
        "wada": pj(w_ada[0]),
        "bada": np.ascontiguousarray(np.broadcast_to(b_ada[0][None, :], (6, 3072))),
        "gpre": np.ascontiguousarray(g_pre[0].reshape(8, 128).T),
        "win": pj(w_in[0]),
        "walpha": np.ascontiguousarray(w_alpha[0]),
        "balpha": np.ascontiguousarray(b_alpha[0].reshape(2, 128).T),
        "gon": np.ascontiguousarray(np.stack([g_onorm_gla[0], g_onorm_hgrn[0]], 1)),
        "lbl": np.ascontiguousarray(hgrn_lb_logits.reshape(2, 4, 128).transpose(2, 0, 1)),
        "wout": pj(w_out[0]),
        "gpost": np.ascontiguousarray(np.broadcast_to(g_post[0][None, :], (128, 1024))),
    }
    m.update(consts)
    return {k: np.ascontiguousarray(v, dtype=f) for k, v in m.items()}


def make_consts():
    f = np.float32
    maskp = np.triu(np.ones((128, 128), f))
    masks = maskp.copy()
    masks[0:64, 64:128] = 0.0
    smaskp = np.ones((128, 768), f)
    smaskp[:, 0::128] = 0.0
    smasks = smaskp.copy()
    smasks[:, 64::128] = 0.0
    sel = np.zeros((6, 4, 128), f)
    sel[0, 0, :] = 1.0
    sel[1, 1, :] = 1.0
    sel[2, 2, 0:64] = 1.0
    sel[3, 2, 64:128] = 1.0
    sel[4, 3, 0:64] = 1.0
    sel[5, 3, 64:128] = 1.0
    return {"identf": np.eye(128, dtype=f), "maskp": maskp, "masks": masks, "smaskp": smaskp, "smasks": smasks, "sel": sel}


def assemble(results, TP):
    f = np.float32
    yp = np.concatenate([r["yp"] for r in results], 0).astype(f)
    ys = np.concatenate([r["ys"] for r in results], 0).astype(f)
    sgp = np.concatenate([r["sgp"].reshape(2, 4, 64, 128) for r in results], 0)[None].astype(f)
    shp = np.concatenate([r["shp"] for r in results], 0)[None].astype(f)
    sgs = np.concatenate([r["sgs"].reshape(4, 4, 64, 128) for r in results], 0)[None].astype(f)
    shs = np.concatenate([r["shs"] for r in results], 0)[None].astype(f)
    return (yp, ys, sgp, shp, sgs, shs)


def kernel(**inputs):
    inputs = {k: np.asarray(v) for k, v in inputs.items()}
    TP = inputs["x_prompt"].shape[1]
    nc = build(TP // 128)
    consts = make_consts()
    in_maps = [host_inputs(i, consts=consts, **inputs) for i in range(N_CORES)]
    res = run_bass_kernel_spmd(nc, in_maps, core_ids=list(range(N_CORES)))
    return assemble(res.results, TP)
```

```python
from contextlib import ExitStack

import numpy as np
import concourse.bass as bass
import concourse.mybir as mybir
from concourse.bass_utils import run_bass_kernel_spmd

F32 = mybir.dt.float32
BF16 = mybir.dt.bfloat16
AF = mybir.ActivationFunctionType
ALU = mybir.AluOpType
AX = mybir.AxisListType

D = 1024
NCOL = 3600
EPS = 1e-6
N_CORES = 8
SEQ = 4096
ENGS = ("sync", "scalar", "vector", "gpsimd", "tensor")

C_QA, C_KA, C_VA, C_ZA, C_AL, C_QH, C_FH, C_IH, C_ZH = 0, 256, 512, 1024, 1536, 1552, 2064, 2576, 3088


class _Probe:
    def __init__(self):
        self.calls = []

    def __getattr__(self, name):
        def f(*a, **k):
            out = k.get("out", a[0] if a else None)
            self.calls.append((name, out, k))
            return None
        return f


def _est_ns(eng, fns):
    pr = _Probe()
    for f in fns:
        try:
            f(pr)
        except Exception:
            pass
    tot = 0.0
    for name, out, k in pr.calls:
        try:
            shp = out.shape
            n = 1
            for d in shp[1:]:
                n *= int(d)
        except Exception:
            n = 128
        if eng == "tensor":
            tot += 64.0 if name == "transpose" else 15.0 + 0.43 * n
        elif eng == "scalar":
            tot += 200.0 + 0.75 * n + (100.0 if k.get("accum_out") is not None else 0.0)
        elif eng == "vector":
            tot += (60.0 + 2.1 * n) if name == "tensor_tensor_scan" else 150.0 + 1.04 * n
        elif eng == "gpsimd":
            tot += 100.0 + 1.8 * n
        else:
            tot += 2000.0
    return max(tot, 50.0)


class Prog:
    def __init__(self, nc):
        self.nc = nc
        self.q = {e: [] for e in ENGS}
        self.sems = {}
        self.waited = {e: {} for e in ENGS}
        self.bufs = {}
        self.bank_last = {}
        self._cms = []
        self.eng_free = {e: 0.0 for e in ENGS}
        self.tok_time = {}
        import os as _os
        self.limit = int(_os.environ.get("KLIMIT", "0")) or None
        self.count = 0

    def sem(self, name):
        if name not in self.sems:
            cm = self.nc.semaphore(name)
            h = cm.__enter__()
            self._cms.append(cm)
            self.sems[name] = [h, 0]
        return self.sems[name]

    def close(self):
        for cm in reversed(self._cms):
            cm.__exit__(None, None, None)

    def _deps(self, eng, reads, writes, is_dma, banks):
        deps = []
        for b in banks:
            t = self.bank_last.get(b)
            if t is not None and t[2] != eng:
                deps.append((t, "bank"))
        for b in reads:
            st = self.bufs.get(b)
            if st and st[0] is not None:
                deps.append((st[0], "raw"))
        for b in writes:
            st = self.bufs.get(b)
            if st:
                if st[0] is not None:
                    deps.append((st[0], "waw"))
                for t in st[1]:
                    deps.append((t, "war"))
        waits = []
        for tok, kind in deps:
            sname, val, teng, tdma = tok
            if not tdma and teng == eng and not is_dma:
                if eng == "tensor" or kind in ("war", "waw"):
                    continue
            if self.waited[eng].get(sname, 0) >= val:
                continue
            self.waited[eng][sname] = val
            waits.append((sname, val))
        return waits

    def _dep_tokens(self, eng, reads, writes, banks):
        toks = []
        for b in banks:
            t = self.bank_last.get(b)
            if t is not None:
                toks.append(t)
        for b in reads:
            st = self.bufs.get(b)
            if st and st[0] is not None:
                toks.append(st[0])
        for b in writes:
            st = self.bufs.get(b)
            if st:
                if st[0] is not None:
                    toks.append(st[0])
                toks.extend(st[1])
        return toks

    def _ready(self, eng, toks):
        t = self.eng_free[eng]
        for tok in toks:
            tt = self.tok_time.get((tok[0], tok[1]), 0.0) + (0.0 if tok[2] == eng else 150.0)
            if tt > t:
                t = tt
        return t

    def est_start(self, desc):
        eng, fns, reads, writes, banks, dma_sem = desc
        return self._ready(eng, self._dep_tokens(eng, reads, writes, banks))

    def op(self, eng, fns, reads=(), writes=(), dma_sem=None, banks=()):
        if callable(fns):
            fns = [fns]
        _t0 = self._ready(eng, self._dep_tokens(eng, reads, writes, banks))
        _dur = _est_ns(eng, fns)
        self.count += 1
        if self.limit is not None and self.count > self.limit:
            return None
        is_dma = dma_sem is not None
        waits = self._deps(eng, reads, writes, is_dma, banks)
        if is_dma:
            s = self.sem(dma_sem)
            s[1] += 16
            tok = (dma_sem, s[1], eng, True)
            inc = (dma_sem, 16)
        else:
            sname = "p_" + eng
            s = self.sem(sname)
            s[1] += 1
            tok = (sname, s[1], eng, False)
            inc = (sname, 1)
        self.q[eng].append((waits, fns, inc))
        if is_dma:
            self.eng_free[eng] = _t0 + 60.0
        else:
            self.eng_free[eng] = _t0 + _dur
        self.tok_time[(tok[0], tok[1])] = _t0 + _dur
        for b in banks:
            self.bank_last[b] = tok
        for b in writes:
            self.bufs[b] = [tok, []]
        for b in reads:
            if b in writes:
                continue
            self.bufs.setdefault(b, [None, []])[1].append(tok)
        return tok

    def retoken(self, names, tok):
        for b in names:
            self.bufs[b] = [tok, []]

    def wait_token(self, eng, tok):
        sname, val = tok[0], tok[1]
        if self.waited[eng].get(sname, 0) >= val:
            return
        self.waited[eng][sname] = val
        self.q[eng].append(([(sname, val)], [], None))

    def barrier(self):
        snap = [(n, s[1]) for n, s in self.sems.items() if s[1] > 0]
        for e in ENGS:
            w = []
            for n, v in snap:
                if self.waited[e].get(n, 0) < v:
                    self.waited[e][n] = v
                    w.append((n, v))
            if w:
                self.q[e].append((w, [], None))

    def replay(self, block):
        P = self

        def run(engobj, name):
            for waits, fns, inc in P.q[name]:
                for sname, val in waits:
                    engobj.wait_ge(P.sems[sname][0], val)
                ins = None
                for f in fns:
                    ins = f(engobj)
                if inc is not None and ins is not None:
                    ins.then_inc(P.sems[inc[0]][0], inc[1])

        @block.sync
        def _(e):
            run(e, "sync")

        @block.scalar
        def _(e):
            run(e, "scalar")

        @block.vector
        def _(e):
            run(e, "vector")

        @block.gpsimd
        def _(e):
            run(e, "gpsimd")

        @block.tensor
        def _(e):
            run(e, "tensor")


def v3(ap, t=128):
    return ap.rearrange("p (c t) -> p c t", t=t)


def build(tp_tiles):
    TP = tp_tiles * 128
    nc = bass.Bass("TRN2", target_bir_lowering=False)

    def din(name, shape):
        return nc.dram_tensor(name, shape, F32, kind="ExternalInput").ap()

    def dout(name, shape):
        return nc.dram_tensor(name, shape, F32, kind="ExternalOutput").ap()

    xp = din("xp", [2, TP, D])
    xs = din("xs", [4, 64, D])
    cT_d = din("cT", [128, 8, 6])
    stg = din("stg", [4, 2, 128, 128])
    sth = din("sth", [4, 4, 128, 128])
    wada = din("wada", [128, 8, 3072])
    bada = din("bada", [6, 3072])
    gpre_d = din("gpre", [128, 8])
    win = din("win", [128, 8, NCOL])
    walpha_d = din("walpha", [16, 256])
    balpha_d = din("balpha", [128, 2])
    gon_d = din("gon", [128, 2])
    lbl_d = din("lbl", [128, 2, 4])
    wout = din("wout", [128, 8, D])
    gpost_d = din("gpost", [128, D])
    identf_d = din("identf", [128, 128])
    maskp_d = din("maskp", [128, 128])
    masks_d = din("masks", [128, 128])
    smaskp_d = din("smaskp", [128, 768])
    smasks_d = din("smasks", [128, 768])
    sel_d = din("sel", [6, 4, 128])

    yp = dout("yp", [2, TP, D])
    ys = dout("ys", [4, 64, D])
    sgp = dout("sgp", [2, 2, 128, 128])
    shp = dout("shp", [2, 4, 128, 128])
    sgs = dout("sgs", [4, 2, 128, 128])
    shs = dout("shs", [4, 4, 128, 128])

    es = ExitStack()
    with es:
        def sb(name, shape, dt=F32):
            return es.enter_context(nc.sbuf_tensor(name, shape, dt))

        def ps(name, shape, dt=F32):
            return es.enter_context(nc.psum_tensor(name, shape, dt))

        P = Prog(nc)

        w_in_bf = sb("w_in_bf", [128, 8, NCOL], BF16)
        w_out_bf = sb("w_out_bf", [128, 8, D], BF16)
        walpha_bf = sb("walpha_bf", [16, 256], BF16)
        identf = sb("identf_sb", [128, 128])
        identb = sb("identb", [128, 128], BF16)
        maskp = sb("maskp_sb", [128, 128])
        masks = sb("masks_sb", [128, 128])
        smaskp = sb("smaskp_sb", [128, 768])
        smasks = sb("smasks_sb", [128, 768])
        cst = sb("cst", [128, 32])
        nbalpha = cst[:, 0:2]
        lb = cst[:, 2:6]
        ln1mlb = cst[:, 6:10]
        balpha = cst[:, 10:12]
        gon = cst[:, 12:14]
        tmp4 = cst[:, 14:18]
        tmp4b = cst[:, 18:22]
        lbl = sb("lbl_sb", [128, 2, 4])
        gpre = sb("gpre_sb", [128, 8])
        aT = sb("aT", [128, 8, 6])
        sT = sb("sT", [128, 8, 6])
        GG = [sb(f"GG{g}", [128, D]) for g in range(4)]
        S_all = sb("S_all", [128, 6, 6, 128])

        PA = ps("PA", [128, 1024])
        PB = ps("PB", [128, 1024])
        PC = ps("PC", [128, 512])
        PD = ps("PD", [128, 512])
        PT0 = ps("PT0", [128, 512])
        PT1 = ps("PT1", [128, 512])

        ses = ExitStack()
        with ses:
            def ssb(name, shape, dt=F32):
                return ses.enter_context(nc.sbuf_tensor(name, shape, dt))
            stage = [ssb(f"stage{i}", [128, 4096]) for i in range(2)]
            mod_sb = ssb("mod_sb", [6, 3072])
            bada_sb = ssb("bada_sb", [6, 3072])
            gpost = ssb("gpost_sb", [128, D])
            cT = ssb("cT_sb", [128, 8, 6])
            sel = ssb("sel_sb", [6, 4, 128])
            tmp48 = ssb("tmp48", [128, 8, 6])

            small = [
                (cT[:], cT_d, "cT"), (bada_sb[:], bada, "bada"), (gpre[:], gpre_d, "gpre"),
                (balpha, balpha_d, "balpha"), (gon, gon_d, "gon"), (lbl[:], lbl_d, "lbl"),
                (identf[:], identf_d, "identf"), (maskp[:], maskp_d, "maskp"), (masks[:], masks_d, "masks"),
                (smaskp[:], smaskp_d, "smaskp"), (smasks[:], smasks_d, "smasks"), (sel[:], sel_d, "sel"),
                (gpost[:], gpost_d, "gpost"),
            ]
            names = []
            tok = None
            for o_, i_, nm in small:
                tok = P.op("sync", lambda e, o_=o_, i_=i_: e.dma_start(out=o_, in_=i_), writes=[nm], dma_sem="ld_s")
                names.append(nm)
            walpha_f = ssb("walpha_f", [16, 256])
            stage_wa = walpha_f[:]
            tok = P.op("sync", lambda e: e.dma_start(out=stage_wa, in_=walpha_d), writes=["walpha_f"], dma_sem="ld_s")
            names.append("walpha_f")
            for b in range(4):
                tok = P.op("sync", lambda e, b=b: e.dma_start(out=S_all[:, 2 + b, 0:2, :], in_=stg[b].rearrange("c p v -> p c v")),
                           writes=[f"S{2 + b}"], dma_sem="ld_s")
                tok = P.op("sync", lambda e, b=b: e.dma_start(out=S_all[:, 2 + b, 2:6, :], in_=sth[b].rearrange("c p v -> p c v")),
                           writes=[f"S{2 + b}h"], dma_sem="ld_s")
            P.retoken(names + [f"S{2 + b}" for b in range(4)] + [f"S{2 + b}h" for b in range(4)], tok)

            P.op("vector", lambda e: e.tensor_copy(out=identb[:], in_=identf[:]), reads=["identf"], writes=["identb"])
            P.op("vector", lambda e: e.tensor_copy(out=walpha_bf[:], in_=stage_wa), reads=["walpha_f"], writes=["walpha_bf"])
            P.op("vector", lambda e: e.tensor_scalar(out=nbalpha, in0=balpha, scalar1=-1.0, scalar2=None, op0=ALU.mult),
                 reads=["balpha"], writes=["nbalpha"])
            P.op("vector", lambda e: e.tensor_tensor(out=tmp4, in0=lbl[:, 1, :], in1=lbl[:, 0, :], op=ALU.subtract),
                 reads=["lbl"], writes=["tmp4"])
            P.op("scalar", lambda e: e.activation(out=tmp4, in_=tmp4, func=AF.Exp), reads=["tmp4"], writes=["tmp4"])
            P.op("vector", lambda e: e.tensor_scalar(out=tmp4b, in0=tmp4, scalar1=1.0, scalar2=None, op0=ALU.add),
                 reads=["tmp4"], writes=["tmp4b"])
            P.op("vector", lambda e: e.reciprocal(out=lb, in_=tmp4b), reads=["tmp4b"], writes=["lb"])
            P.op("vector", lambda e: e.tensor_tensor(out=tmp4b, in0=tmp4, in1=lb, op=ALU.mult), reads=["tmp4", "lb"], writes=["tmp4b"])
            P.op("scalar", lambda e: e.activation(out=ln1mlb, in_=tmp4b, func=AF.Ln), reads=["tmp4b"], writes=["ln1mlb"])
            P.op("gpsimd", lambda e: e.memset(S_all[:, 0:2, :, :], 0.0), writes=["S0", "S0h", "S1", "S1h"])

            si = 0
            for n in range(12):
                stg_ = stage[si % 2]
                sname = f"stage{si % 2}"
                st3 = stg_[:, 0:2048].rearrange("p (j n) -> p j n", n=256)
                P.op("sync", lambda e, st3=st3, n=n: e.dma_start(out=st3, in_=wada[:, :, n * 256:(n + 1) * 256]),
                     writes=[sname], dma_sem="ld_" + sname)
                P.op("tensor", [lambda e, j=j, st3=st3: e.matmul(PC[0:6, 0:256], lhsT=cT[:, j, :], rhs=st3[:, j, :],
                                                                  start=(j == 0), stop=(j == 7)) for j in range(8)],
                     reads=[sname, "cT"], writes=["pc"], banks=["c"])
                P.op("vector", lambda e, n=n: e.tensor_tensor(out=mod_sb[0:6, n * 256:(n + 1) * 256], in0=PC[0:6, 0:256],
                                                             in1=bada_sb[0:6, n * 256:(n + 1) * 256], op=ALU.add),
                     reads=["pc", "bada"], writes=["mod"], banks=["c"])
                si += 1
            P.op("tensor", [lambda e, k=k: e.transpose(out=PD[:, k * 6:(k + 1) * 6], in_=mod_sb[0:6, k * 128:(k + 1) * 128],
                                                       identity=identf[0:6, 0:6]) for k in range(16)],
                 reads=["mod", "identf"], writes=["pd"], banks=["d"])
            P.op("vector", lambda e: e.tensor_copy(out=sT[:], in_=PD[:, 0:48].rearrange("p (j b) -> p j b", b=6)),
                 reads=["pd"], writes=["sT"], banks=["d"])
            P.op("vector", lambda e: e.tensor_scalar(out=tmp48[:], in0=PD[:, 48:96].rearrange("p (j b) -> p j b", b=6),
                                                     scalar1=1.0, scalar2=None, op0=ALU.add),
                 reads=["pd"], writes=["tmp48"], banks=["d"])
            P.op("vector", lambda e: e.tensor_tensor(out=aT[:], in0=tmp48[:], in1=gpre[:].unsqueeze(2).broadcast_to([128, 8, 6]),
                                                     op=ALU.mult), reads=["tmp48", "gpre"], writes=["aT"])
            for g in range(4):
                for n in range(2):
                    P.op("tensor", lambda e, g=g, n=n: e.matmul(PC[:, 0:512], lhsT=sel[0:6, g, :],
                                                                  rhs=mod_sb[0:6, 2048 + n * 512:2048 + (n + 1) * 512],
                                                                  start=True, stop=True),
                         reads=["mod", "sel"], writes=["pc"], banks=["c"])
                    P.op("vector", lambda e, g=g, n=n: e.tensor_tensor(out=GG[g][:, n * 512:(n + 1) * 512], in0=PC[:, 0:512],
                                                                      in1=gpost[:, n * 512:(n + 1) * 512], op=ALU.mult),
                         reads=["pc", "gpost"], writes=[f"GG{g}"], banks=["c"])
            for j in range(8):
                for hlf in range(2):
                    stg_ = stage[si % 2]
                    sname = f"stage{si % 2}"
                    c0 = hlf * 1800
                    P.op("sync", lambda e, stg_=stg_, j=j, c0=c0: e.dma_start(out=stg_[:, 0:1800], in_=win[:, j, c0:c0 + 1800]),
                         writes=[sname], dma_sem="ld_" + sname)
                    eng = "vector" if hlf == 0 else "scalar"
                    if eng == "vector":
                        P.op("vector", lambda e, stg_=stg_, j=j, c0=c0: e.tensor_copy(out=w_in_bf[:, j, c0:c0 + 1800], in_=stg_[:, 0:1800]),
                             reads=[sname], writes=[f"win{j}_{hlf}"])
                    else:
                        P.op("scalar", lambda e, stg_=stg_, j=j, c0=c0: e.activation(out=w_in_bf[:, j, c0:c0 + 1800], in_=stg_[:, 0:1800],
                                                                                      func=AF.Copy),
                             reads=[sname], writes=[f"win{j}_{hlf}"])
                    si += 1
            for jj in range(4):
                stg_ = stage[si % 2]
                sname = f"stage{si % 2}"
                st3 = stg_[:, 0:2048].rearrange("p (j n) -> p j n", n=1024)
                P.op("sync", lambda e, st3=st3, jj=jj: e.dma_start(out=st3, in_=wout[:, 2 * jj:2 * jj + 2, :]),
                     writes=[sname], dma_sem="ld_" + sname)
                for jl in range(2):
                    j = 2 * jj + jl
                    gcol = gon[:, 0:1] if j < 4 else gon[:, 1:2]
                    P.op("gpsimd", lambda e, st3=st3, jl=jl, j=j, gcol=gcol: e.tensor_scalar(
                        out=w_out_bf[:, j, :], in0=st3[:, jl, :], scalar1=gcol, scalar2=1.0, op0=ALU.mult, op1=ALU.mult),
                        reads=[sname, "gon"], writes=[f"wout{j}"])
                si += 1
            P.barrier()
        WIN = [f"win{j}_{h}" for j in range(8) for h in range(2)]
        WOUT = [f"wout{j}" for j in range(8)]

        NXS = 4
        x_sb = [sb(f"x_sb{i}", [128, D]) for i in range(NXS)]
        statA = sb("statA", [128, 8])
        xn = sb("xn", [128, D], BF16)
        hT = sb("hT", [128, 8, 128], BF16)
        alr = sb("alr", [16, 128], BF16)
        e1 = sb("e1", [128, 256])
        eh = sb("eh", [128, 512])
        L1 = sb("L1", [128, 512])
        L2 = sb("L2", [128, 512])
        gT = sb("gT", [128, 768])
        bT = sb("bT", [128, 768])
        bTc = sb("bTc", [128, 768])
        eqb = sb("eqb", [128, 512])
        EQa = sb("EQa", [128, 256])
        EKa = sb("EKa", [128, 256])
        QT_ = [sb(f"QT{p}", [128, 4, 128], BF16) for p in range(2)]
        QTa2_ = [sb(f"QTa2{p}", [128, 2, 2, 128], BF16) for p in range(2)]
        KT_ = [sb(f"KT{p}", [128, 6, 128], BF16) for p in range(2)]
        V_ = [sb(f"V{p}", [128, D], BF16) for p in range(2)]
        ez_ = [sb(f"ez{p}", [128, D]) for p in range(2)]
        sm_ = [sb(f"sm{p}", [128, 3, 6, 2]) for p in range(2)]
        Ktm2 = sb("Ktm2", [128, 2, 6, 128], BF16)
        Sp = sb("Sp", [128, 2, 6, 128], BF16)
        AT = sb("AT", [128, 8, 128], BF16)
        sq = sb("sq", [128, D])
        so = sb("so", [128, 24])
        statB = sb("statB", [128, 8])
        ohat = sb("ohat", [128, D], BF16)
        ohT = sb("ohT", [128, 8, 128], BF16)

        bT3 = v3(bT[:])
        PT0b = PT0[:].bitcast(BF16)
        PT1b = PT1[:].bitcast(BF16)
        PCb = PC[:].bitcast(BF16)
        PDb = PD[:].bitcast(BF16)
        P.op("gpsimd", lambda e: e.memset(Ktm2[:], 0.0), writes=["Ktma", "Ktmh"])
        for p in range(2):
            P.op("gpsimd", lambda e, p=p: e.memset(QTa2_[p][:], 0.0), writes=[f"QTa{p}"])

        def O(eng, fns, reads=(), writes=(), banks=(), dma_sem=None):
            return (eng, fns, reads, writes, banks, dma_sem)

        def load_x(i, xsrc):
            slot = i % NXS
            P.op("sync", lambda e: e.dma_start(out=x_sb[slot][:], in_=xsrc), writes=[f"x{slot}"], dma_sem=f"xld{slot}")

        def stage12(ctx):
            i, segs, par = ctx["i"], ctx["segs"], ctx["i"] % 2
            slot = i % NXS
            xs_, xb = x_sb[slot], f"x{slot}"
            QT, QTa2, KT, V, ez, sm = QT_[par], QTa2_[par], KT_[par], V_[par], ez_[par], sm_[par]
            nQTh, nQTa, nKTa, nKTh, nVa, nVh, ngza, ngzh = (f"{n}{par}" for n in ("QTh", "QTa", "KTa", "KTh", "Va", "Vh", "gza", "gzh"))
            smE = f"smE{par}"
            if all(sg["b"] == segs[0]["b"] for sg in segs):
                mods = [(0, 128, segs[0]["b"])]
            else:
                mods = [(sg["lo"], sg["n"], sg["b"]) for sg in segs]
            yield O("scalar", lambda e: e.activation(out=hT[:].rearrange("p j t -> p (j t)"), in_=xs_[:], func=AF.Square, accum_out=statA[:, 0:1]),
                 reads=[xb], writes=["hT0", "hT1", "sa0"])
            yield O("scalar", lambda e: e.activation(out=statA[:, 1:2], in_=statA[:, 0:1], func=AF.Ln, scale=1.0 / D, bias=EPS),
                 reads=["sa0"], writes=["sa1"])
            yield O("scalar", lambda e: e.activation(out=statA[:, 2:3], in_=statA[:, 1:2], func=AF.Exp, scale=-0.5),
                 reads=["sa1"], writes=["sa2"])
            yield O("gpsimd", lambda e: e.tensor_scalar(out=xn[:], in0=xs_[:], scalar1=statA[:, 2:3], scalar2=1.0,
                                                       op0=ALU.mult, op1=ALU.mult), reads=[xb, "sa2"], writes=["xn"])
            for half, (PTb, bank, eng) in enumerate(((PCb, "c", "scalar"), (PDb, "d", "vector"))):
                PT3 = v3(PTb)
                yield O("tensor", [lambda e, j=j, PT3=PT3, half=half: e.transpose(
                    out=PT3[:, j, :], in_=xn[:, (4 * half + j) * 128:(4 * half + j + 1) * 128], identity=identb[:]) for j in range(4)],
                    reads=["xn", "identb"], writes=[bank], banks=[bank])
                for j in range(4):
                    jj = 4 * half + j
                    for (lo, n, b) in mods:
                        if eng == "scalar":
                            yield O("scalar", lambda e, j=j, jj=jj, lo=lo, n=n, b=b, PT3=PT3: e.activation(
                                out=hT[:, jj, lo:lo + n], in_=PT3[:, j, lo:lo + n], func=AF.Identity,
                                scale=aT[:, jj, b:b + 1], bias=sT[:, jj, b:b + 1]),
                                reads=[bank, "aT", "sT"], writes=[f"hT{half}"], banks=[bank])
                        else:
                            yield O("vector", lambda e, j=j, jj=jj, lo=lo, n=n, b=b, PT3=PT3: e.tensor_scalar(
                                out=hT[:, jj, lo:lo + n], in0=PT3[:, j, lo:lo + n], scalar1=aT[:, jj, b:b + 1],
                                scalar2=sT[:, jj, b:b + 1], op0=ALU.mult, op1=ALU.add),
                                reads=[bank, "aT", "sT"], writes=[f"hT{half}"], banks=[bank])
            HT = ["hT0", "hT1"]

            def fm(out_ap, col0, m):
                return [lambda e, j=j: e.matmul(out_ap, lhsT=w_in_bf[:, j, col0:col0 + m], rhs=hT[:, j, :],
                                                start=(j == 0), stop=(j == 7)) for j in range(8)]

            def tm(out_ap, col0):
                return [lambda e, j=j: e.matmul(out_ap, lhsT=hT[:, j, :], rhs=w_in_bf[:, j, col0:col0 + 512],
                                                start=(j == 0), stop=(j == 7)) for j in range(8)]

            yield O("tensor", fm(PA[0:16, 0:128], C_AL, 16), reads=HT + WIN, writes=["a0"], banks=["a0"])
            yield O("vector", lambda e: e.tensor_copy(out=alr[:], in_=PA[0:16, 0:128]), reads=["a0"], writes=["alr"], banks=["a0"])
            for c in range(4):
                yield O("tensor", fm(PA[:, 512 + c * 128:512 + (c + 1) * 128], C_FH + c * 128, 128), reads=HT + WIN, writes=["a1"], banks=["a1"])
            yield O("tensor", [lambda e, c=c: e.matmul(PA[:, 128 + c * 128:256 + c * 128], lhsT=walpha_bf[0:16, c * 128:(c + 1) * 128],
                                                    rhs=alr[0:16, :], start=True, stop=True) for c in range(2)],
                 reads=["alr", "walpha_bf"], writes=["a0"], banks=["a0"])
            yield O("scalar", lambda e: e.activation(out=eh[:], in_=PA[:, 512:1024], func=AF.Exp, scale=-1.0),
                 reads=["a1"], writes=["eh"], banks=["a1"])
            for c in range(4):
                yield O("tensor", fm(PC[:, c * 128:(c + 1) * 128], C_QH + c * 128, 128), reads=HT + WIN, writes=["c"], banks=["c"])
            for c in range(2):
                yield O("scalar", lambda e, c=c: e.activation(out=e1[:, c * 128:(c + 1) * 128], in_=PA[:, 128 + c * 128:256 + c * 128],
                                                           func=AF.Exp, scale=-1.0, bias=nbalpha[:, c:c + 1]),
                     reads=["a0", "nbalpha"], writes=["e1"], banks=["a0"])
            yield O("scalar", lambda e: e.activation(out=L1[:], in_=eh[:], func=AF.Ln, bias=1.0), reads=["eh"], writes=["L1"])
            for c in range(2):
                yield O("tensor", fm(PD[:, c * 128:(c + 1) * 128], C_QA + c * 128, 128), reads=HT + WIN, writes=["d"], banks=["d"])
            for c in range(2):
                yield O("tensor", fm(PD[:, (2 + c) * 128:(3 + c) * 128], C_KA + c * 128, 128), reads=HT + WIN, writes=["d"], banks=["d"])
            yield O("scalar", lambda e: e.activation(out=e1[:], in_=e1[:], func=AF.Ln, bias=1.0), reads=["e1"], writes=["e1"])
            yield O("gpsimd", lambda e: e.tensor_scalar(out=gT[:, 0:256], in0=e1[:], scalar1=-1.0 / 16.0, scalar2=1.0,
                                                       op0=ALU.mult, op1=ALU.mult), reads=["e1"], writes=["gTa"])
            for c in range(4):
                yield O("scalar", lambda e, c=c: e.activation(out=L2[:, c * 128:(c + 1) * 128], in_=eh[:, c * 128:(c + 1) * 128],
                                                           func=AF.Ln, bias=1.0, scale=lb[:, c:c + 1]),
                     reads=["eh", "lb"], writes=["L2"])
            yield O("vector", lambda e: e.tensor_tensor(out=gT[:, 256:768], in0=L2[:], in1=L1[:], op=ALU.subtract),
                 reads=["L1", "L2"], writes=["gTh"])
            yield O("vector", lambda e: e.tensor_tensor(out=L1[:], in0=PA[:, 512:1024], in1=L1[:], op=ALU.add),
                 reads=["a1", "L1"], writes=["L1"], banks=["a1"])
            yield O("vector", lambda e: e.tensor_tensor_scan(out=bT[:], data0=smasks[:], data1=gT[:], initial=0.0,
                                                          op0=ALU.mult, op1=ALU.add),
                 reads=["gTa", "gTh", "smasks"], writes=["bT"])
            yield O("tensor", tm(PA[:, 0:512], C_ZA), reads=HT + WIN, writes=["a0"], banks=["a0"])
            yield O("tensor", tm(PA[:, 512:1024], C_VA), reads=HT + WIN, writes=["a1"], banks=["a1"])
            bT4 = bT[:].rearrange("p (c s t) -> p c s t", s=2, t=64)
            bTc4 = bTc[:].rearrange("p (c s t) -> p c s t", s=2, t=64)
            yield O("vector", lambda e: e.tensor_tensor(out=bTc4, in0=bT4, in1=bT4[:, :, :, 31:32].broadcast_to([128, 6, 2, 64]),
                                                        op=ALU.subtract), reads=["bT"], writes=["bTc"])
            yield O("scalar", lambda e: e.activation(out=sm[:, 0, :, :], in_=bT4[:, :, :, 31], func=AF.Exp), reads=["bT"], writes=[smE])
            yield O("scalar", lambda e: e.activation(out=sm[:, 1, :, :], in_=bT4[:, :, :, 63], func=AF.Exp), reads=["bT"], writes=[smE])
            yield O("scalar", lambda e: e.activation(out=sm[:, 2, :, :], in_=bTc4[:, :, :, 63], func=AF.Exp), reads=["bTc"], writes=[smE])
            yield O("scalar", lambda e: e.activation(out=eqb[:], in_=PC[:, :], func=AF.Exp, scale=-1.0),
                 reads=["c"], writes=["eqb"], banks=["c"])
            yield O("scalar", lambda e: e.activation(out=eqb[:], in_=eqb[:], func=AF.Ln, bias=1.0), reads=["eqb"], writes=["eqb"])
            yield O("scalar", lambda e: e.activation(out=EQa[:], in_=bTc[:, 0:256], func=AF.Exp), reads=["bTc"], writes=["EQa"])
            yield O("scalar", lambda e: e.activation(out=EKa[:], in_=bTc[:, 0:256], func=AF.Exp, scale=-1.0), reads=["bTc"], writes=["EKa"])
            yield O("vector", lambda e: e.tensor_copy(out=V[:, 0:512], in_=PA[:, 512:1024]), reads=["a1"], writes=[nVa], banks=["a1"])
            yield O("tensor", tm(PA[:, 512:1024], C_IH), reads=HT + WIN, writes=["a1"], banks=["a1"])
            for hh in range(2):
                yield O("vector", lambda e, hh=hh: e.scalar_tensor_tensor(
                    out=QTa2[hh * 64:(hh + 1) * 64, hh, :, :], in0=v3(PD[hh * 64:(hh + 1) * 64, 0:256]), scalar=0.125,
                    in1=v3(EQa[hh * 64:(hh + 1) * 64, :]), op0=ALU.mult, op1=ALU.mult),
                    reads=["d", "EQa"], writes=[nQTa], banks=["d"])
            yield O("vector", lambda e: e.tensor_tensor(out=KT[:, 0:2, :], in0=v3(PD[:, 256:512]), in1=v3(EKa[:]), op=ALU.mult),
                 reads=["d", "EKa"], writes=[nKTa], banks=["d"])
            yield O("vector", lambda e: e.tensor_tensor(out=L1[:], in0=L1[:], in1=bTc[:, 256:768], op=ALU.add),
                 reads=["L1", "bTc"], writes=["L1"])
            for c in range(4):
                yield O("scalar", lambda e, c=c: e.activation(
                    out=KT[:, 2 + c, :], in_=L1[:, c * 128:(c + 1) * 128], func=AF.Exp, scale=-1.0, bias=ln1mlb[:, c:c + 1]),
                    reads=["L1", "ln1mlb"], writes=[nKTh])
            yield O("vector", lambda e: e.tensor_tensor(out=eqb[:], in0=bTc[:, 256:768], in1=eqb[:], op=ALU.subtract),
                 reads=["eqb", "bTc"], writes=["eqb"])
            yield O("scalar", lambda e: e.activation(out=eqb[:], in_=eqb[:], func=AF.Exp), reads=["eqb"], writes=["eqb"])
            yield O("vector", lambda e: e.tensor_tensor(out=QT[:, :, :], in0=v3(PC[:, :]), in1=v3(eqb[:]), op=ALU.mult),
                 reads=["c", "eqb"], writes=[nQTh], banks=["c"])

            def zgate(ps_ap, ez_ap, gbuf, bank):
                yield O("scalar", lambda e: e.activation(out=ez_ap, in_=ps_ap, func=AF.Exp, scale=-1.0),
                     reads=[bank], writes=[gbuf], banks=[bank])
                yield O("scalar", lambda e: e.activation(out=ez_ap, in_=ez_ap, func=AF.Ln, bias=1.0), reads=[gbuf], writes=[gbuf])
                yield O("scalar", lambda e: e.activation(out=ez_ap, in_=ez_ap, func=AF.Exp, scale=-1.0), reads=[gbuf], writes=[gbuf])
                yield O("vector", lambda e: e.tensor_tensor(out=ez_ap, in0=ps_ap, in1=ez_ap, op=ALU.mult),
                     reads=[bank, gbuf], writes=[gbuf], banks=[bank])
            yield from zgate(PA[:, 0:512], ez[:, 0:512], ngza, "a0")
            yield O("vector", lambda e: e.tensor_copy(out=V[:, 512:1024], in_=PA[:, 512:1024]), reads=["a1"], writes=[nVh], banks=["a1"])
            yield O("tensor", tm(PA[:, 0:512], C_ZH), reads=HT + WIN, writes=["a0"], banks=["a0"])
            yield from zgate(PA[:, 0:512], ez[:, 512:1024], ngzh, "a0")

        def stage34(ctx):
            i, segs, par, gg = ctx["i"], ctx["segs"], ctx["i"] % 2, ctx["gg"]
            slot = i % NXS
            xs_, xb = x_sb[slot], f"x{slot}"
            QT, QTa2, KT, V, ez, sm = QT_[par], QTa2_[par], KT_[par], V_[par], ez_[par], sm_[par]
            nQTh, nQTa, nKTa, nKTh, nVa, nVh, ngza, ngzh = (f"{n}{par}" for n in ("QTh", "QTa", "KTa", "KTh", "Va", "Vh", "gza", "gzh"))
            smE = f"smE{par}"
            PT03, PT13 = v3(PT0b), v3(PT1b)
            yield O("tensor", [lambda e, c=c: e.transpose(out=PT03[:, c, :], in_=KT[:, c, :], identity=identb[:]) for c in range(6)],
                 reads=[nKTa, nKTh, "identb"], writes=["t0"], banks=["t0"])
            for si_, sg in enumerate(segs):
                lo, n = sg["lo"], sg["n"]
                yield O("scalar", lambda e, si_=si_, lo=lo, n=n: e.activation(out=Ktm2[lo:lo + n, si_, 0:6, :], in_=PT03[lo:lo + n, 0:6, :],
                                                                          func=AF.Copy),
                     reads=["t0"], writes=["Ktm"], banks=["t0"])
            yield O("tensor", [lambda e, h=h: e.matmul(PB[:, h * 128:(h + 1) * 128], lhsT=KT[:, h // 2, :], rhs=QTa2[:, h % 2, h // 2, :],
                                                    start=True, stop=True) for h in range(4)],
                 reads=[nKTa, nQTa], writes=["b0"], banks=["b0"])
            yield O("vector", lambda e: e.tensor_tensor(out=AT[:, 0:4, :], in0=v3(PB[:, 0:512]),
                                                     in1=masks[:].unsqueeze(1).broadcast_to([128, 4, 128]), op=ALU.mult),
                 reads=["b0", "masks"], writes=["ATa"], banks=["b0"])
            yield O("tensor", [lambda e, h=h: e.matmul(PB[:, (4 + h) * 128:(5 + h) * 128], lhsT=KT[:, 2 + h, :], rhs=QT[:, h, :],
                                                    start=True, stop=True) for h in range(4)],
                 reads=[nKTh, nQTh], writes=["b1"], banks=["b1"])
            yield O("vector", lambda e: e.tensor_tensor(out=AT[:, 4:8, :], in0=v3(PB[:, 512:1024]),
                                                     in1=masks[:].unsqueeze(1).broadcast_to([128, 4, 128]), op=ALU.mult),
                 reads=["b1", "masks"], writes=["ATh"], banks=["b1"])
            for si_, sg in enumerate(segs):
                lo, n, st = sg["lo"], sg["n"], sg["st"]
                yield O("gpsimd", lambda e, si_=si_, st=st: e.tensor_tensor(
                    out=Sp[:, si_], in0=S_all[:, st], in1=sm[:, 0, :, si_].unsqueeze(2).broadcast_to([128, 6, 128]), op=ALU.mult),
                    reads=[f"S{st}", f"S{st}h", smE], writes=[f"Sp{si_}"])
                yield O("gpsimd", lambda e, si_=si_, st=st: e.tensor_tensor(
                    out=S_all[:, st], in0=S_all[:, st], in1=sm[:, 1, :, si_].unsqueeze(2).broadcast_to([128, 6, 128]), op=ALU.mult),
                    reads=[smE], writes=[f"S{st}", f"S{st}h"])
                fl = []
                for h in range(4):
                    c = h // 2
                    fl.append(lambda e, h=h, lo=lo, n=n: e.matmul(
                        PB[lo:lo + n, h * 128:(h + 1) * 128], lhsT=AT[:, h, lo:lo + n], rhs=V[:, h * 128:(h + 1) * 128],
                        start=True, stop=False))
                    fl.append(lambda e, h=h, c=c, lo=lo, n=n, si_=si_: e.matmul(
                        PB[lo:lo + n, h * 128:(h + 1) * 128], lhsT=QTa2[:, h % 2, c, lo:lo + n], rhs=Sp[:, si_, c, :],
                        start=False, stop=True))
                yield O("tensor", fl, reads=["ATa", nVa, nQTa, f"Sp{si_}"], writes=["b0"], banks=["b0"])
                fl = []
                for h in range(4):
                    fl.append(lambda e, h=h, lo=lo, n=n: e.matmul(
                        PB[lo:lo + n, (4 + h) * 128:(5 + h) * 128], lhsT=AT[:, 4 + h, lo:lo + n],
                        rhs=V[:, (4 + h) * 128:(5 + h) * 128], start=True, stop=False))
                    fl.append(lambda e, h=h, lo=lo, n=n, si_=si_: e.matmul(
                        PB[lo:lo + n, (4 + h) * 128:(5 + h) * 128], lhsT=QT[:, h, lo:lo + n], rhs=Sp[:, si_, 2 + h, :],
                        start=False, stop=True))
                yield O("tensor", fl, reads=["ATh", nVh, nQTh, f"Sp{si_}"], writes=["b1"], banks=["b1"])
                fl = []
                for h in range(4):
                    c, r0 = h // 2, (h % 2) * 64
                    fl.append(lambda e, h=h, c=c, r0=r0, si_=si_: e.matmul(
                        PT0[r0:r0 + 64, c * 128:(c + 1) * 128], lhsT=Ktm2[:, si_, c, r0:r0 + 64], rhs=V[:, h * 128:(h + 1) * 128],
                        start=True, stop=True))
                yield O("tensor", fl, reads=["Ktm", nVa], writes=["t0"], banks=["t0"])
                fl = []
                for h in range(4):
                    fl.append(lambda e, h=h, si_=si_: e.matmul(
                        PT1[:, h * 128:(h + 1) * 128], lhsT=Ktm2[:, si_, 2 + h, :], rhs=V[:, (4 + h) * 128:(5 + h) * 128],
                        start=True, stop=True))
                yield O("tensor", fl, reads=["Ktm", nVh], writes=["t1"], banks=["t1"])
                for c in range(2):
                    yield O("vector", lambda e, c=c, si_=si_, st=st: e.scalar_tensor_tensor(
                        out=S_all[:, st, c, :], in0=PT0[:, c * 128:(c + 1) * 128], scalar=sm[:, 2, c, si_:si_ + 1], in1=S_all[:, st, c, :],
                        op0=ALU.mult, op1=ALU.add),
                        reads=["t0", smE], writes=[f"S{st}"], banks=["t0"])
                for h in range(4):
                    yield O("vector", lambda e, h=h, si_=si_, st=st: e.scalar_tensor_tensor(
                        out=S_all[:, st, 2 + h, :], in0=PT1[:, h * 128:(h + 1) * 128], scalar=sm[:, 2, 2 + h, si_:si_ + 1],
                        in1=S_all[:, st, 2 + h, :], op0=ALU.mult, op1=ALU.add),
                        reads=["t1", smE], writes=[f"S{st}h"], banks=["t1"])
            yield O("scalar", lambda e: e.activation(out=sq[:, 0:512], in_=PB[:, 0:512], func=AF.Square),
                 reads=["b0"], writes=["sqa"], banks=["b0"])
            yield O("scalar", lambda e: e.activation(out=sq[:, 512:1024], in_=PB[:, 512:1024], func=AF.Square),
                 reads=["b1"], writes=["sqh"], banks=["b1"])
            yield O("vector", lambda e: e.reduce_sum(out=so[:, 0:8], in_=v3(sq[:]), axis=AX.X), reads=["sqa", "sqh"], writes=["so0"])
            yield O("scalar", lambda e: e.activation(out=so[:, 8:16], in_=so[:, 0:8], func=AF.Ln, scale=1.0 / 128, bias=EPS),
                 reads=["so0"], writes=["so1"])
            yield O("scalar", lambda e: e.activation(out=so[:, 16:24], in_=so[:, 8:16], func=AF.Exp, scale=-0.5),
                 reads=["so1"], writes=["so2"])
            yield O("gpsimd", lambda e: e.tensor_tensor(out=v3(ez[:]), in0=v3(ez[:]),
                                                      in1=so[:, 16:24].unsqueeze(2).broadcast_to([128, 8, 128]), op=ALU.mult),
                 reads=["so2"], writes=[ngza, ngzh])
            yield O("vector", lambda e: e.tensor_tensor(out=ohat[:, 0:512], in0=PB[:, 0:512], in1=ez[:, 0:512], op=ALU.mult),
                 reads=["b0", ngza], writes=["ohata"], banks=["b0"])
            yield O("vector", lambda e: e.tensor_tensor(out=ohat[:, 512:1024], in0=PB[:, 512:1024], in1=ez[:, 512:1024], op=ALU.mult),
                 reads=["b1", ngzh], writes=["ohath"], banks=["b1"])
            yield O("tensor", [lambda e, j=j: e.transpose(out=PT03[:, j, :], in_=ohat[:, j * 128:(j + 1) * 128], identity=identb[:])
                            for j in range(4)], reads=["ohata", "identb"], writes=["t0"], banks=["t0"])
            yield O("scalar", lambda e: e.activation(out=ohT[:, 0:4, :], in_=PT03[:, 0:4, :], func=AF.Copy),
                 reads=["t0"], writes=["ohTa"], banks=["t0"])
            yield O("tensor", [lambda e, j=j: e.transpose(out=PT13[:, j, :], in_=ohat[:, (4 + j) * 128:(5 + j) * 128], identity=identb[:])
                            for j in range(4)], reads=["ohath", "identb"], writes=["t1"], banks=["t1"])
            yield O("vector", lambda e: e.tensor_copy(out=ohT[:, 4:8, :], in_=PT13[:, 0:4, :]), reads=["t1"], writes=["ohTh"], banks=["t1"])
            for n_ in range(2):
                yield O("tensor", [lambda e, j=j, n_=n_: e.matmul(PB[:, n_ * 512:(n_ + 1) * 512], lhsT=ohT[:, j, :],
                                                                 rhs=w_out_bf[:, j, n_ * 512:(n_ + 1) * 512], start=(j == 0), stop=(j == 7))
                                for j in range(8)], reads=["ohTa", "ohTh"] + WOUT, writes=[f"b{n_}"], banks=[f"b{n_}"])
            yield O("scalar", lambda e: e.activation(out=ohat[:], in_=PB[:, :], func=AF.Square, accum_out=statB[:, 0:1]),
                 reads=["b0", "b1"], writes=["ohata", "ohath", "sb0"], banks=["b0", "b1"])
            yield O("scalar", lambda e: e.activation(out=statB[:, 1:2], in_=statB[:, 0:1], func=AF.Ln, scale=1.0 / D, bias=EPS),
                 reads=["sb0"], writes=["sb1"])
            yield O("scalar", lambda e: e.activation(out=statB[:, 2:3], in_=statB[:, 1:2], func=AF.Exp, scale=-0.5),
                 reads=["sb1"], writes=["sb2"])
            yield O("vector", lambda e: e.scalar_tensor_tensor(out=sq[:], in0=PB[:, :], scalar=statB[:, 2:3], in1=GG[gg][:],
                                                            op0=ALU.mult, op1=ALU.mult),
                 reads=["b0", "b1", "sb2", f"GG{gg}"], writes=["sqa", "sqh"], banks=["b0", "b1"])
            yield O("gpsimd", lambda e: e.tensor_tensor(out=xs_[:], in0=xs_[:], in1=sq[:], op=ALU.add), reads=[xb, "sqa", "sqh"], writes=[xb])
            yield O("sync", lambda e: e.dma_start(out=ctx["ydst"], in_=xs_[:]), reads=[xb], dma_sem=f"yst{slot}")
            k = ctx["k"]
            if k is not None:
                store_state(2 + 2 * k, sgs[2 * k], shs[2 * k])
                store_state(3 + 2 * k, sgs[2 * k + 1], shs[2 * k + 1])

        def store_state(st, gdst, hdst):
            P.op("sync", lambda e: e.dma_start(out=gdst.rearrange("c p v -> p c v"), in_=S_all[:, st, 0:2, :]),
                 reads=[f"S{st}"], dma_sem="sout")
            P.op("sync", lambda e: e.dma_start(out=hdst.rearrange("c p v -> p c v"), in_=S_all[:, st, 2:6, :]),
                 reads=[f"S{st}h"], dma_sem="sout")

        tiles = []
        for k in range(2):
            segs = [dict(lo=0, n=64, b=2 + 2 * k, st=2 + 2 * k), dict(lo=64, n=64, b=3 + 2 * k, st=3 + 2 * k)]
            tiles.append(dict(xsrc=xs[2 * k:2 * k + 2].rearrange("b t d -> (b t) d"), ydst=ys[2 * k:2 * k + 2].rearrange("b t d -> (b t) d"),
                              segs=segs, gg=2 + k, k=k))
        for t in range(tp_tiles):
            for s in range(2):
                tiles.append(dict(xsrc=xp[s, t * 128:(t + 1) * 128, :], ydst=yp[s, t * 128:(t + 1) * 128, :],
                                  segs=[dict(lo=0, n=64, b=s, st=s), dict(lo=64, n=64, b=s, st=s)], gg=s, k=None))
        for i, t in enumerate(tiles):
            t["i"] = i
        NT = len(tiles)
        PRE = 2
        import os as _os2
        _os_kverb = bool(_os2.environ.get("KVERB2"))
        for i in range(min(PRE, NT)):
            load_x(i, tiles[i]["xsrc"])
        for r in range(NT + 1):
            if r + PRE < NT:
                load_x(r + PRE, tiles[r + PRE]["xsrc"])
            if _os_kverb:
                print("round", r, "model t_us", {k: round(v / 1e3, 1) for k, v in P.eng_free.items()})
            gens = []
            if r >= 1:
                gens.append(stage34(tiles[r - 1]))
            if r < NT:
                gens.append(stage12(tiles[r]))
            heads = []
            for g in gens:
                try:
                    heads.append([g, next(g)])
                except StopIteration:
                    pass
            while heads:
                best, bt = None, None
                for hd in heads:
                    t = P.est_start(hd[1])
                    if bt is None or t < bt:
                        best, bt = hd, t
                eng, fns, reads, writes, banks, dma_sem = best[1]
                if _os_kverb and r == 6:
                    _ts = P.est_start(best[1])
                    print("OP", "S34" if best[0] is gens[0] else "S12", eng, "start %.2f dur %.2f engfree %.2f" % (_ts / 1e3, _est_ns(eng, fns if not callable(fns) else [fns]) / 1e3, P.eng_free[eng] / 1e3), list(writes)[:3], list(banks))
                P.op(eng, fns, reads=reads, writes=writes, banks=banks, dma_sem=dma_sem)
                try:
                    best[1] = next(best[0])
                except StopIteration:
                    heads.remove(best)
        for s in range(2):
            store_state(s, sgp[s], shp[s])
        for nm, s in list(P.sems.items()):
            if nm.startswith("yst") or nm == "sout":
                P.wait_token("sync", (nm, s[1]))
        with nc.Block() as block:
            P.replay(block)
        P.close()
        import os as _os
        if _os.environ.get("KVERB"):
            print("total ops recorded", P.count, "model makespan us", max(P.eng_free.values()) / 1e3)
    return nc


def host_inputs(core, x_prompt, x_sample, c_prompt, c_sample, state_gla, state_hgrn, w_ada, b_ada, g_pre,
                w_in, w_alpha, b_alpha, g_onorm_gla, hgrn_lb_logits, g_onorm_hgrn, w_out, g_post, consts):
    f = np.float32
    c6 = np.concatenate([c_prompt[2 * core:2 * core + 2], c_sample[4 * core:4 * core + 4]], 0)

    def pj(a):
        return np.ascontiguousarray(a.reshape(8, 128, a.shape[1]).transpose(1, 0, 2))

    m = {
        "xp": np.ascontiguousarray(x_prompt[2 * core:2 * core + 2]),
        "xs": np.ascontiguousarray(x_sample[4 * core:4 * core + 4]),
        "cT": np.ascontiguousarray(c6.T.reshape(8, 128, 6).transpose(1, 0, 2)),
        "stg": np.ascontiguousarray(state_gla[0, 4 * core:4 * core + 4].reshape(4, 2, 128, 128)),
        "sth": np.ascontiguousarray(state_hgrn[0, 4 * core:4 * core + 4]),
        "wada": pj(w_ada[0]),
        "bada": np.ascontiguousarray(np.broadcast_to(b_ada[0][None, :], (6, 3072))),
        "gpre": np.ascontiguousarray(g_pre[0].reshape(8, 128).T),
        "win": pj(w_in[0]),
        "walpha": np.ascontiguousarray(w_alpha[0]),
        "balpha": np.ascontiguousarray(b_alpha[0].reshape(2, 128).T),
        "gon": np.ascontiguousarray(np.stack([g_onorm_gla[0], g_onorm_hgrn[0]], 1)),
        "lbl": np.ascontiguousarray(hgrn_lb_logits.reshape(2, 4, 128).transpose(2, 0, 1)),
        "wout": pj(w_out[0]),
        "gpost": np.ascontiguousarray(np.broadcast_to(g_post[0][None, :], (128, 1024))),
    }
    m.update(consts)
    return {k: np.ascontiguousarray(v, dtype=f) for k, v in m.items()}


def make_consts():
    f = np.float32
    maskp = np.triu(np.ones((128, 128), f))
    masks = maskp.copy()
    masks[0:64, 64:128] = 0.0
    smaskp = np.ones((128, 768), f)
    smaskp[:, 0::128] = 0.0
    smasks = smaskp.copy()
    smasks[:, 64::128] = 0.0
    sel = np.zeros((6, 4, 128), f)
    sel[0, 0, :] = 1.0
    sel[1, 1, :] = 1.0
    sel[2, 2, 0:64] = 1.0
    sel[3, 2, 64:128] = 1.0
    sel[4, 3, 0:64] = 1.0
    sel[5, 3, 64:128] = 1.0
    return {"identf": np.eye(128, dtype=f), "maskp": maskp, "masks": masks, "smaskp": smaskp, "smasks": smasks, "sel": sel}


def assemble(results, TP):
    f = np.float32
    yp = np.concatenate([r["yp"] for r in results], 0).astype(f)
    ys = np.concatenate([r["ys"] for r in results], 0).astype(f)
    sgp = np.concatenate([r["sgp"].reshape(2, 4, 64, 128) for r in results], 0)[None].astype(f)
    shp = np.concatenate([r["shp"] for r in results], 0)[None].astype(f)
    sgs = np.concatenate([r["sgs"].reshape(4, 4, 64, 128) for r in results], 0)[None].astype(f)
    shs = np.concatenate([r["shs"] for r in results], 0)[None].astype(f)
    return (yp, ys, sgp, shp, sgs, shs)


def kernel(**inputs):
    inputs = {k: np.asarray(v) for k, v in inputs.items()}
    TP = inputs["x_prompt"].shape[1]
    nc = build(TP // 128)
    consts = make_consts()
    in_maps = [host_inputs(i, consts=consts, **inputs) for i in range(N_CORES)]
    res = run_bass_kernel_spmd(nc, in_maps, core_ids=list(range(N_CORES)))
    return assemble(res.results, TP)
```

```python
from contextlib import ExitStack

import numpy as np
import concourse.bass as bass
import concourse.mybir as mybir
from concourse.bass_utils import run_bass_kernel_spmd

F32 = mybir.dt.float32
BF16 = mybir.dt.bfloat16
AF = mybir.ActivationFunctionType
ALU = mybir.AluOpType
AX = mybir.AxisListType

D = 1024
NCOL = 3600
EPS = 1e-6
N_CORES = 8
SEQ = 4096
ENGS = ("sync", "scalar", "vector", "gpsimd", "tensor")

C_QA, C_KA, C_VA, C_ZA, C_AL, C_QH, C_FH, C_IH, C_ZH = 0, 256, 512, 1024, 1536, 1552, 2064, 2576, 3088


class _Probe:
    def __init__(self):
        self.calls = []

    def __getattr__(self, name):
        def f(*a, **k):
            out = k.get("out", a[0] if a else None)
            self.calls.append((name, out, k))
            return None
        return f


def _ap_n(ap):
    try:
        n = 1
        for d in ap.shape[1:]:
            n *= int(d)
        return n
    except Exception:
        return 0


def _est_ns(eng, fns):
    pr = _Probe()
    for f in fns:
        try:
            f(pr)
        except Exception:
            pass
    tot = 0.0
    for name, out, k in pr.calls:
        n = max([_ap_n(out)] + [_ap_n(k.get(kk)) for kk in ("in_", "in0", "data0", "rhs")] + [1])
        if eng == "tensor":
            tot += 95.0 if name == "transpose" else 15.0 + 0.43 * n
        elif eng == "scalar":
            tot += (480.0 if n <= 16 else 200.0 + 0.75 * n) + (100.0 if k.get("accum_out") is not None else 0.0)
        elif eng == "vector":
            tot += (60.0 + 2.1 * n) if name == "tensor_tensor_scan" else 150.0 + 1.04 * n
        elif eng == "gpsimd":
            tot += (100.0 + 1.0 * n) if name == "tensor_scalar" else 100.0 + 2.0 * n
        else:
            tot += 2000.0
    return max(tot, 50.0)


class Prog:
    def __init__(self, nc):
        self.nc = nc
        self.q = {e: [] for e in ENGS}
        self.sems = {}
        self.waited = {e: {} for e in ENGS}
        self.bufs = {}
        self.bank_last = {}
        self._cms = []
        self.eng_free = {e: 0.0 for e in ENGS}
        self.tok_time = {}
        import os as _os
        self.limit = int(_os.environ.get("KLIMIT", "0")) or None
        self.count = 0

    def sem(self, name):
        if name not in self.sems:
            cm = self.nc.semaphore(name)
            h = cm.__enter__()
            self._cms.append(cm)
            self.sems[name] = [h, 0]
        return self.sems[name]

    def close(self):
        for cm in reversed(self._cms):
            cm.__exit__(None, None, None)

    def _deps(self, eng, reads, writes, is_dma, banks):
        deps = []
        for b in banks:
            t = self.bank_last.get(b)
            if t is not None and t[2] != eng:
                deps.append((t, "bank"))
        for b in reads:
            st = self.bufs.get(b)
            if st and st[0] is not None:
                deps.append((st[0], "raw"))
        for b in writes:
            st = self.bufs.get(b)
            if st:
                if st[0] is not None:
                    deps.append((st[0], "waw"))
                for t in st[1]:
                    deps.append((t, "war"))
        waits = []
        for tok, kind in deps:
            sname, val, teng, tdma = tok
            if not tdma and teng == eng and not is_dma:
                if eng == "tensor" or kind in ("war", "waw"):
                    continue
            if self.waited[eng].get(sname, 0) >= val:
                continue
            self.waited[eng][sname] = val
            waits.append((sname, val))
        return waits

    def _dep_tokens(self, eng, reads, writes, banks):
        toks = []
        for b in banks:
            t = self.bank_last.get(b)
            if t is not None:
                toks.append(t)
        for b in reads:
            st = self.bufs.get(b)
            if st and st[0] is not None:
                toks.append(st[0])
        for b in writes:
            st = self.bufs.get(b)
            if st:
                if st[0] is not None:
                    toks.append(st[0])
                toks.extend(st[1])
        return toks

    def _ready(self, eng, toks):
        t = self.eng_free[eng]
        for tok in toks:
            tt = self.tok_time.get((tok[0], tok[1]), 0.0) + (0.0 if tok[2] == eng else 150.0)
            if tt > t:
                t = tt
        return t

    def est_start(self, desc):
        eng, fns, reads, writes, banks, dma_sem = desc
        return self._ready(eng, self._dep_tokens(eng, reads, writes, banks))

    def op(self, eng, fns, reads=(), writes=(), dma_sem=None, banks=()):
        if callable(fns):
            fns = [fns]
        _t0 = self._ready(eng, self._dep_tokens(eng, reads, writes, banks))
        _dur = _est_ns(eng, fns)
        self.count += 1
        if self.limit is not None and self.count > self.limit:
            return None
        is_dma = dma_sem is not None
        waits = self._deps(eng, reads, writes, is_dma, banks)
        if is_dma:
            s = self.sem(dma_sem)
            s[1] += 16
            tok = (dma_sem, s[1], eng, True)
            inc = (dma_sem, 16)
        else:
            sname = "p_" + eng
            s = self.sem(sname)
            s[1] += 1
            tok = (sname, s[1], eng, False)
            inc = (sname, 1)
        self.q[eng].append((waits, fns, inc))
        if is_dma:
            self.eng_free[eng] = _t0 + 60.0
        else:
            self.eng_free[eng] = _t0 + _dur
        self.tok_time[(tok[0], tok[1])] = _t0 + _dur
        for b in banks:
            self.bank_last[b] = tok
        for b in writes:
            self.bufs[b] = [tok, []]
        for b in reads:
            if b in writes:
                continue
            self.bufs.setdefault(b, [None, []])[1].append(tok)
        return tok

    def retoken(self, names, tok):
        for b in names:
            self.bufs[b] = [tok, []]

    def wait_token(self, eng, tok):
        sname, val = tok[0], tok[1]
        if self.waited[eng].get(sname, 0) >= val:
            return
        self.waited[eng][sname] = val
        self.q[eng].append(([(sname, val)], [], None))

    def barrier(self):
        snap = [(n, s[1]) for n, s in self.sems.items() if s[1] > 0]
        for e in ENGS:
            w = []
            for n, v in snap:
                if self.waited[e].get(n, 0) < v:
                    self.waited[e][n] = v
                    w.append((n, v))
            if w:
                self.q[e].append((w, [], None))

    def replay(self, block):
        P = self

        def run(engobj, name):
            for waits, fns, inc in P.q[name]:
                for sname, val in waits:
                    engobj.wait_ge(P.sems[sname][0], val)
                ins = None
                for f in fns:
                    ins = f(engobj)
                if inc is not None and ins is not None:
                    ins.then_inc(P.sems[inc[0]][0], inc[1])

        @block.sync
        def _(e):
            run(e, "sync")

        @block.scalar
        def _(e):
            run(e, "scalar")

        @block.vector
        def _(e):
            run(e, "vector")

        @block.gpsimd
        def _(e):
            run(e, "gpsimd")

        @block.tensor
        def _(e):
            run(e, "tensor")


def v3(ap, t=128):
    return ap.rearrange("p (c t) -> p c t", t=t)


def build(tp_tiles):
    TP = tp_tiles * 128
    nc = bass.Bass("TRN2", target_bir_lowering=False)

    def din(name, shape):
        return nc.dram_tensor(name, shape, F32, kind="ExternalInput").ap()

    def dout(name, shape):
        return nc.dram_tensor(name, shape, F32, kind="ExternalOutput").ap()

    xp = din("xp", [2, TP, D])
    xs = din("xs", [4, 64, D])
    cT_d = din("cT", [128, 8, 6])
    stg = din("stg", [4, 2, 128, 128])
    sth = din("sth", [4, 4, 128, 128])
    wada = din("wada", [128, 8, 3072])
    bada = din("bada", [6, 3072])
    gpre_d = din("gpre", [128, 8])
    win = din("win", [128, 8, NCOL])
    walpha_d = din("walpha", [16, 256])
    balpha_d = din("balpha", [128, 2])
    gon_d = din("gon", [128, 2])
    lbl_d = din("lbl", [128, 2, 4])
    wout = din("wout", [128, 8, D])
    gpost_d = din("gpost", [128, D])
    identf_d = din("identf", [128, 128])
    maskp_d = din("maskp", [128, 128])
    masks_d = din("masks", [128, 128])
    smaskp_d = din("smaskp", [128, 768])
    smasks_d = din("smasks", [128, 768])
    sel_d = din("sel", [6, 4, 128])

    yp = dout("yp", [2, TP, D])
    ys = dout("ys", [4, 64, D])
    sgp = dout("sgp", [2, 2, 128, 128])
    shp = dout("shp", [2, 4, 128, 128])
    sgs = dout("sgs", [4, 2, 128, 128])
    shs = dout("shs", [4, 4, 128, 128])

    es = ExitStack()
    with es:
        def sb(name, shape, dt=F32):
            return es.enter_context(nc.sbuf_tensor(name, shape, dt))

        def ps(name, shape, dt=F32):
            return es.enter_context(nc.psum_tensor(name, shape, dt))

        P = Prog(nc)

        w_in_bf = sb("w_in_bf", [128, 8, NCOL], BF16)
        w_out_bf = sb("w_out_bf", [128, 8, D], BF16)
        walpha_bf = sb("walpha_bf", [16, 256], BF16)
        identf = sb("identf_sb", [128, 128])
        identb = sb("identb", [128, 128], BF16)
        maskp = sb("maskp_sb", [128, 128])
        masks = sb("masks_sb", [128, 128])
        smaskp = sb("smaskp_sb", [128, 768])
        smasks = sb("smasks_sb", [128, 768])
        cst = sb("cst", [128, 32])
        nbalpha = cst[:, 0:2]
        lb = cst[:, 2:6]
        ln1mlb = cst[:, 6:10]
        balpha = cst[:, 10:12]
        gon = cst[:, 12:14]
        tmp4 = cst[:, 14:18]
        tmp4b = cst[:, 18:22]
        lbl = sb("lbl_sb", [128, 2, 4])
        gpre = sb("gpre_sb", [128, 8])
        aT = sb("aT", [128, 8, 6])
        sT = sb("sT", [128, 8, 6])
        GG = [sb(f"GG{g}", [128, D]) for g in range(4)]
        S_all = sb("S_all", [128, 6, 6, 128])

        PA = ps("PA", [128, 1024])
        PB = ps("PB", [128, 1024])
        PC = ps("PC", [128, 512])
        PD = ps("PD", [128, 512])
        PT0 = ps("PT0", [128, 512])
        PT1 = ps("PT1", [128, 512])

        ses = ExitStack()
        with ses:
            def ssb(name, shape, dt=F32):
                return ses.enter_context(nc.sbuf_tensor(name, shape, dt))
            stage = [ssb(f"stage{i}", [128, 4096]) for i in range(2)]
            mod_sb = ssb("mod_sb", [6, 3072])
            bada_sb = ssb("bada_sb", [6, 3072])
            gpost = ssb("gpost_sb", [128, D])
            cT = ssb("cT_sb", [128, 8, 6])
            sel = ssb("sel_sb", [6, 4, 128])
            tmp48 = ssb("tmp48", [128, 8, 6])

            small = [
                (cT[:], cT_d, "cT"), (bada_sb[:], bada, "bada"), (gpre[:], gpre_d, "gpre"),
                (balpha, balpha_d, "balpha"), (gon, gon_d, "gon"), (lbl[:], lbl_d, "lbl"),
                (identf[:], identf_d, "identf"), (maskp[:], maskp_d, "maskp"), (masks[:], masks_d, "masks"),
                (smaskp[:], smaskp_d, "smaskp"), (smasks[:], smasks_d, "smasks"), (sel[:], sel_d, "sel"),
                (gpost[:], gpost_d, "gpost"),
            ]
            names = []
            tok = None
            for o_, i_, nm in small:
                tok = P.op("sync", lambda e, o_=o_, i_=i_: e.dma_start(out=o_, in_=i_), writes=[nm], dma_sem="ld_s")
                names.append(nm)
            walpha_f = ssb("walpha_f", [16, 256])
            stage_wa = walpha_f[:]
            tok = P.op("sync", lambda e: e.dma_start(out=stage_wa, in_=walpha_d), writes=["walpha_f"], dma_sem="ld_s")
            names.append("walpha_f")
            for b in range(4):
                tok = P.op("sync", lambda e, b=b: e.dma_start(out=S_all[:, 2 + b, 0:2, :], in_=stg[b].rearrange("c p v -> p c v")),
                           writes=[f"S{2 + b}"], dma_sem="ld_s")
                tok = P.op("sync", lambda e, b=b: e.dma_start(out=S_all[:, 2 + b, 2:6, :], in_=sth[b].rearrange("c p v -> p c v")),
                           writes=[f"S{2 + b}h"], dma_sem="ld_s")
            P.retoken(names + [f"S{2 + b}" for b in range(4)] + [f"S{2 + b}h" for b in range(4)], tok)

            P.op("vector", lambda e: e.tensor_copy(out=identb[:], in_=identf[:]), reads=["identf"], writes=["identb"])
            P.op("vector", lambda e: e.tensor_copy(out=walpha_bf[:], in_=stage_wa), reads=["walpha_f"], writes=["walpha_bf"])
            P.op("vector", lambda e: e.tensor_scalar(out=nbalpha, in0=balpha, scalar1=-1.0, scalar2=None, op0=ALU.mult),
                 reads=["balpha"], writes=["nbalpha"])
            P.op("vector", lambda e: e.tensor_tensor(out=tmp4, in0=lbl[:, 1, :], in1=lbl[:, 0, :], op=ALU.subtract),
                 reads=["lbl"], writes=["tmp4"])
            P.op("scalar", lambda e: e.activation(out=tmp4, in_=tmp4, func=AF.Exp), reads=["tmp4"], writes=["tmp4"])
            P.op("vector", lambda e: e.tensor_scalar(out=tmp4b, in0=tmp4, scalar1=1.0, scalar2=None, op0=ALU.add),
                 reads=["tmp4"], writes=["tmp4b"])
            P.op("vector", lambda e: e.reciprocal(out=lb, in_=tmp4b), reads=["tmp4b"], writes=["lb"])
            P.op("vector", lambda e: e.tensor_tensor(out=tmp4b, in0=tmp4, in1=lb, op=ALU.mult), reads=["tmp4", "lb"], writes=["tmp4b"])
            P.op("scalar", lambda e: e.activation(out=ln1mlb, in_=tmp4b, func=AF.Ln), reads=["tmp4b"], writes=["ln1mlb"])
            P.op("gpsimd", lambda e: e.memset(S_all[:, 0:2, :, :], 0.0), writes=["S0", "S0h", "S1", "S1h"])

            si = 0
            for n in range(12):
                stg_ = stage[si % 2]
                sname = f"stage{si % 2}"
                st3 = stg_[:, 0:2048].rearrange("p (j n) -> p j n", n=256)
                P.op("sync", lambda e, st3=st3, n=n: e.dma_start(out=st3, in_=wada[:, :, n * 256:(n + 1) * 256]),
                     writes=[sname], dma_sem="ld_" + sname)
                P.op("tensor", [lambda e, j=j, st3=st3: e.matmul(PC[0:6, 0:256], lhsT=cT[:, j, :], rhs=st3[:, j, :],
                                                                  start=(j == 0), stop=(j == 7)) for j in range(8)],
                     reads=[sname, "cT"], writes=["pc"], banks=["c"])
                P.op("vector", lambda e, n=n: e.tensor_tensor(out=mod_sb[0:6, n * 256:(n + 1) * 256], in0=PC[0:6, 0:256],
                                                             in1=bada_sb[0:6, n * 256:(n + 1) * 256], op=ALU.add),
                     reads=["pc", "bada"], writes=["mod"], banks=["c"])
                si += 1
            P.op("tensor", [lambda e, k=k: e.transpose(out=PD[:, k * 6:(k + 1) * 6], in_=mod_sb[0:6, k * 128:(k + 1) * 128],
                                                       identity=identf[0:6, 0:6]) for k in range(16)],
                 reads=["mod", "identf"], writes=["pd"], banks=["d"])
            P.op("vector", lambda e: e.tensor_copy(out=sT[:], in_=PD[:, 0:48].rearrange("p (j b) -> p j b", b=6)),
                 reads=["pd"], writes=["sT"], banks=["d"])
            P.op("vector", lambda e: e.tensor_scalar(out=tmp48[:], in0=PD[:, 48:96].rearrange("p (j b) -> p j b", b=6),
                                                     scalar1=1.0, scalar2=None, op0=ALU.add),
                 reads=["pd"], writes=["tmp48"], banks=["d"])
            P.op("vector", lambda e: e.tensor_tensor(out=aT[:], in0=tmp48[:], in1=gpre[:].unsqueeze(2).broadcast_to([128, 8, 6]),
                                                     op=ALU.mult), reads=["tmp48", "gpre"], writes=["aT"])
            for g in range(4):
                for n in range(2):
                    P.op("tensor", lambda e, g=g, n=n: e.matmul(PC[:, 0:512], lhsT=sel[0:6, g, :],
                                                                  rhs=mod_sb[0:6, 2048 + n * 512:2048 + (n + 1) * 512],
                                                                  start=True, stop=True),
                         reads=["mod", "sel"], writes=["pc"], banks=["c"])
                    P.op("vector", lambda e, g=g, n=n: e.tensor_tensor(out=GG[g][:, n * 512:(n + 1) * 512], in0=PC[:, 0:512],
                                                                      in1=gpost[:, n * 512:(n + 1) * 512], op=ALU.mult),
                         reads=["pc", "gpost"], writes=[f"GG{g}"], banks=["c"])
            for j in range(8):
                for hlf in range(2):
                    stg_ = stage[si % 2]
                    sname = f"stage{si % 2}"
                    c0 = hlf * 1800
                    P.op("sync", lambda e, stg_=stg_, j=j, c0=c0: e.dma_start(out=stg_[:, 0:1800], in_=win[:, j, c0:c0 + 1800]),
                         writes=[sname], dma_sem="ld_" + sname)
                    eng = "vector" if hlf == 0 else "scalar"
                    if eng == "vector":
                        P.op("vector", lambda e, stg_=stg_, j=j, c0=c0: e.tensor_copy(out=w_in_bf[:, j, c0:c0 + 1800], in_=stg_[:, 0:1800]),
                             reads=[sname], writes=[f"win{j}_{hlf}"])
                    else:
                        P.op("scalar", lambda e, stg_=stg_, j=j, c0=c0: e.activation(out=w_in_bf[:, j, c0:c0 + 1800], in_=stg_[:, 0:1800],
                                                                                      func=AF.Copy),
                             reads=[sname], writes=[f"win{j}_{hlf}"])
                    si += 1
            for jj in range(4):
                stg_ = stage[si % 2]
                sname = f"stage{si % 2}"
                st3 = stg_[:, 0:2048].rearrange("p (j n) -> p j n", n=1024)
                P.op("sync", lambda e, st3=st3, jj=jj: e.dma_start(out=st3, in_=wout[:, 2 * jj:2 * jj + 2, :]),
                     writes=[sname], dma_sem="ld_" + sname)
                for jl in range(2):
                    j = 2 * jj + jl
                    gcol = gon[:, 0:1] if j < 4 else gon[:, 1:2]
                    P.op("gpsimd", lambda e, st3=st3, jl=jl, j=j, gcol=gcol: e.tensor_scalar(
                        out=w_out_bf[:, j, :], in0=st3[:, jl, :], scalar1=gcol, scalar2=1.0, op0=ALU.mult, op1=ALU.mult),
                        reads=[sname, "gon"], writes=[f"wout{j}"])
                si += 1
            P.barrier()
        WIN = [f"win{j}_{h}" for j in range(8) for h in range(2)]
        WOUT = [f"wout{j}" for j in range(8)]

        NXS = 4
        x_sb = [sb(f"x_sb{i}", [128, D]) for i in range(NXS)]
        statA = sb("statA", [128, 8])
        xn = sb("xn", [128, D], BF16)
        hT = sb("hT", [128, 8, 128], BF16)
        alr = sb("alr", [16, 128], BF16)
        e1 = sb("e1", [128, 256])
        eh = sb("eh", [128, 512])
        L1 = sb("L1", [128, 512])
        L2 = sb("L2", [128, 512])
        gT = sb("gT", [128, 768])
        bT = sb("bT", [128, 768])
        bTc = sb("bTc", [128, 768])
        eqb = sb("eqb", [128, 512])
        EQa = sb("EQa", [128, 256])
        EKa = sb("EKa", [128, 256])
        QT_ = [sb(f"QT{p}", [128, 4, 128], BF16) for p in range(2)]
        QTa2_ = [sb(f"QTa2{p}", [128, 2, 2, 128], BF16) for p in range(2)]
        KT_ = [sb(f"KT{p}", [128, 6, 128], BF16) for p in range(2)]
        V_ = [sb(f"V{p}", [128, D], BF16) for p in range(2)]
        ez_ = [sb(f"ez{p}", [128, D]) for p in range(2)]
        sm_ = [sb(f"sm{p}", [128, 3, 6, 2]) for p in range(2)]
        Ktm2 = sb("Ktm2", [128, 2, 6, 128], BF16)
        Sp = sb("Sp", [128, 2, 6, 128], BF16)
        AT = sb("AT", [128, 8, 128], BF16)
        sq = sb("sq", [128, D])
        so = sb("so", [128, 24])
        statB = sb("statB", [128, 8])
        ohat = sb("ohat", [128, D], BF16)
        ohT = sb("ohT", [128, 8, 128], BF16)
        ztmp = sb("ztmp", [128, D])

        bT3 = v3(bT[:])
        PT0b = PT0[:].bitcast(BF16)
        PT1b = PT1[:].bitcast(BF16)
        PCb = PC[:].bitcast(BF16)
        PDb = PD[:].bitcast(BF16)
        P.op("gpsimd", lambda e: e.memset(Ktm2[:], 0.0), writes=["Ktma", "Ktmh"])
        for p in range(2):
            P.op("gpsimd", lambda e, p=p: e.memset(QTa2_[p][:], 0.0), writes=[f"QTa{p}"])

        def O(eng, fns, reads=(), writes=(), banks=(), dma_sem=None):
            return (eng, fns, reads, writes, banks, dma_sem)

        def load_x(i, xsrc):
            slot = i % NXS
            P.op("sync", lambda e: e.dma_start(out=x_sb[slot][:], in_=xsrc), writes=[f"x{slot}"], dma_sem=f"xld{slot}")

        def stage12(ctx):
            i, segs, par = ctx["i"], ctx["segs"], ctx["i"] % 2
            slot = i % NXS
            xs_, xb = x_sb[slot], f"x{slot}"
            QT, QTa2, KT, V, ez, sm = QT_[par], QTa2_[par], KT_[par], V_[par], ez_[par], sm_[par]
            nQTh, nQTa, nKTa, nKTh, nVa, nVh, ngza, ngzh = (f"{n}{par}" for n in ("QTh", "QTa", "KTa", "KTh", "Va", "Vh", "gza", "gzh"))
            smE = f"smE{par}"
            if all(sg["b"] == segs[0]["b"] for sg in segs):
                mods = [(0, 128, segs[0]["b"])]
            else:
                mods = [(sg["lo"], sg["n"], sg["b"]) for sg in segs]
            yield O("scalar", lambda e: e.activation(out=hT[:].rearrange("p j t -> p (j t)"), in_=xs_[:], func=AF.Square, accum_out=statA[:, 0:1]),
                 reads=[xb], writes=["hT0", "hT1", "sa0"])
            yield O("scalar", lambda e: e.activation(out=statA[:, 1:2], in_=statA[:, 0:1], func=AF.Ln, scale=1.0 / D, bias=EPS),
                 reads=["sa0"], writes=["sa1"])
            yield O("scalar", lambda e: e.activation(out=statA[:, 2:3], in_=statA[:, 1:2], func=AF.Exp, scale=-0.5),
                 reads=["sa1"], writes=["sa2"])
            yield O("gpsimd", lambda e: e.tensor_scalar(out=xn[:], in0=xs_[:], scalar1=statA[:, 2:3], scalar2=1.0,
                                                       op0=ALU.mult, op1=ALU.mult), reads=[xb, "sa2"], writes=["xn"])
            for half, (PTb, bank, eng) in enumerate(((PCb, "c", "scalar"), (PDb, "d", "vector"))):
                PT3 = v3(PTb)
                yield O("tensor", [lambda e, j=j, PT3=PT3, half=half: e.transpose(
                    out=PT3[:, j, :], in_=xn[:, (4 * half + j) * 128:(4 * half + j + 1) * 128], identity=identb[:]) for j in range(4)],
                    reads=["xn", "identb"], writes=[bank], banks=[bank])
                for j in range(4):
                    jj = 4 * half + j
                    for (lo, n, b) in mods:
                        if eng == "scalar":
                            yield O("scalar", lambda e, j=j, jj=jj, lo=lo, n=n, b=b, PT3=PT3: e.activation(
                                out=hT[:, jj, lo:lo + n], in_=PT3[:, j, lo:lo + n], func=AF.Identity,
                                scale=aT[:, jj, b:b + 1], bias=sT[:, jj, b:b + 1]),
                                reads=[bank, "aT", "sT"], writes=[f"hT{half}"], banks=[bank])
                        else:
                            yield O("vector", lambda e, j=j, jj=jj, lo=lo, n=n, b=b, PT3=PT3: e.tensor_scalar(
                                out=hT[:, jj, lo:lo + n], in0=PT3[:, j, lo:lo + n], scalar1=aT[:, jj, b:b + 1],
                                scalar2=sT[:, jj, b:b + 1], op0=ALU.mult, op1=ALU.add),
                                reads=[bank, "aT", "sT"], writes=[f"hT{half}"], banks=[bank])
            HT = ["hT0", "hT1"]

            def fm(out_ap, col0, m):
                return [lambda e, j=j: e.matmul(out_ap, lhsT=w_in_bf[:, j, col0:col0 + m], rhs=hT[:, j, :],
                                                start=(j == 0), stop=(j == 7)) for j in range(8)]

            def tm(out_ap, col0):
                return [lambda e, j=j: e.matmul(out_ap, lhsT=hT[:, j, :], rhs=w_in_bf[:, j, col0:col0 + 512],
                                                start=(j == 0), stop=(j == 7)) for j in range(8)]

            yield O("tensor", fm(PA[0:16, 0:128], C_AL, 16), reads=HT + WIN, writes=["a0"], banks=["a0"])
            yield O("vector", lambda e: e.tensor_copy(out=alr[:], in_=PA[0:16, 0:128]), reads=["a0"], writes=["alr"], banks=["a0"])
            for c in range(4):
                yield O("tensor", fm(PA[:, 512 + c * 128:512 + (c + 1) * 128], C_FH + c * 128, 128), reads=HT + WIN, writes=["a1"], banks=["a1"])
            yield O("tensor", [lambda e, c=c: e.matmul(PA[:, 128 + c * 128:256 + c * 128], lhsT=walpha_bf[0:16, c * 128:(c + 1) * 128],
                                                    rhs=alr[0:16, :], start=True, stop=True) for c in range(2)],
                 reads=["alr", "walpha_bf"], writes=["a0"], banks=["a0"])
            yield O("scalar", lambda e: e.activation(out=eh[:], in_=PA[:, 512:1024], func=AF.Exp, scale=-1.0),
                 reads=["a1"], writes=["eh"], banks=["a1"])
            for c in range(4):
                yield O("tensor", fm(PC[:, c * 128:(c + 1) * 128], C_QH + c * 128, 128), reads=HT + WIN, writes=["c"], banks=["c"])
            for c in range(2):
                yield O("scalar", lambda e, c=c: e.activation(out=e1[:, c * 128:(c + 1) * 128], in_=PA[:, 128 + c * 128:256 + c * 128],
                                                           func=AF.Exp, scale=-1.0, bias=nbalpha[:, c:c + 1]),
                     reads=["a0", "nbalpha"], writes=["e1"], banks=["a0"])
            yield O("scalar", lambda e: e.activation(out=L1[:], in_=eh[:], func=AF.Ln, bias=1.0), reads=["eh"], writes=["L1"])
            for c in range(2):
                yield O("tensor", fm(PD[:, c * 128:(c + 1) * 128], C_QA + c * 128, 128), reads=HT + WIN, writes=["d"], banks=["d"])
            for c in range(2):
                yield O("tensor", fm(PD[:, (2 + c) * 128:(3 + c) * 128], C_KA + c * 128, 128), reads=HT + WIN, writes=["d"], banks=["d"])
            yield O("scalar", lambda e: e.activation(out=e1[:], in_=e1[:], func=AF.Ln, bias=1.0), reads=["e1"], writes=["e1"])
            yield O("gpsimd", lambda e: e.tensor_scalar(out=gT[:, 0:256], in0=e1[:], scalar1=-1.0 / 16.0, scalar2=1.0,
                                                       op0=ALU.mult, op1=ALU.mult), reads=["e1"], writes=["gTa"])
            for c in range(4):
                yield O("scalar", lambda e, c=c: e.activation(out=L2[:, c * 128:(c + 1) * 128], in_=eh[:, c * 128:(c + 1) * 128],
                                                           func=AF.Ln, bias=1.0, scale=lb[:, c:c + 1]),
                     reads=["eh", "lb"], writes=["L2"])
            yield O("vector", lambda e: e.tensor_tensor(out=gT[:, 256:768], in0=L2[:], in1=L1[:], op=ALU.subtract),
                 reads=["L1", "L2"], writes=["gTh"])
            yield O("vector", lambda e: e.tensor_tensor(out=L1[:], in0=PA[:, 512:1024], in1=L1[:], op=ALU.add),
                 reads=["a1", "L1"], writes=["L1"], banks=["a1"])
            yield O("vector", lambda e: e.tensor_tensor_scan(out=bT[:], data0=smasks[:], data1=gT[:], initial=0.0,
                                                          op0=ALU.mult, op1=ALU.add),
                 reads=["gTa", "gTh", "smasks"], writes=["bT"])
            yield O("tensor", tm(PA[:, 0:512], C_ZA), reads=HT + WIN, writes=["a0"], banks=["a0"])
            yield O("tensor", tm(PA[:, 512:1024], C_VA), reads=HT + WIN, writes=["a1"], banks=["a1"])
            yield O("scalar", lambda e: e.activation(out=ez[:, 0:512], in_=PA[:, 0:512], func=AF.Copy), reads=["a0"], writes=[ngza], banks=["a0"])
            yield O("vector", lambda e: e.tensor_copy(out=V[:, 0:512], in_=PA[:, 512:1024]), reads=["a1"], writes=[nVa], banks=["a1"])
            yield O("tensor", tm(PA[:, 0:512], C_ZH), reads=HT + WIN, writes=["a0"], banks=["a0"])
            yield O("tensor", tm(PA[:, 512:1024], C_IH), reads=HT + WIN, writes=["a1"], banks=["a1"])
            bT4 = bT[:].rearrange("p (c s t) -> p c s t", s=2, t=64)
            bTc4 = bTc[:].rearrange("p (c s t) -> p c s t", s=2, t=64)
            yield O("vector", lambda e: e.tensor_tensor(out=bTc4, in0=bT4, in1=bT4[:, :, :, 31:32].broadcast_to([128, 6, 2, 64]),
                                                        op=ALU.subtract), reads=["bT"], writes=["bTc"])
            yield O("scalar", lambda e: e.activation(out=sm[:, 0, :, :], in_=bT4[:, :, :, 31], func=AF.Exp), reads=["bT"], writes=[smE])
            yield O("scalar", lambda e: e.activation(out=sm[:, 1, :, :], in_=bT4[:, :, :, 63], func=AF.Exp), reads=["bT"], writes=[smE])
            yield O("scalar", lambda e: e.activation(out=sm[:, 2, :, :], in_=bTc4[:, :, :, 63], func=AF.Exp), reads=["bTc"], writes=[smE])
            yield O("scalar", lambda e: e.activation(out=eqb[:], in_=PC[:, :], func=AF.Exp, scale=-1.0),
                 reads=["c"], writes=["eqb"], banks=["c"])
            yield O("scalar", lambda e: e.activation(out=eqb[:], in_=eqb[:], func=AF.Ln, bias=1.0), reads=["eqb"], writes=["eqb"])
            yield O("scalar", lambda e: e.activation(out=EQa[:], in_=bTc[:, 0:256], func=AF.Exp), reads=["bTc"], writes=["EQa"])
            yield O("scalar", lambda e: e.activation(out=EKa[:], in_=bTc[:, 0:256], func=AF.Exp, scale=-1.0), reads=["bTc"], writes=["EKa"])
            yield O("scalar", lambda e: e.activation(out=ez[:, 512:1024], in_=PA[:, 0:512], func=AF.Copy), reads=["a0"], writes=[ngzh], banks=["a0"])
            yield O("vector", lambda e: e.tensor_copy(out=V[:, 512:1024], in_=PA[:, 512:1024]), reads=["a1"], writes=[nVh], banks=["a1"])
            for hh in range(2):
                yield O("vector", lambda e, hh=hh: e.scalar_tensor_tensor(
                    out=QTa2[hh * 64:(hh + 1) * 64, hh, :, :], in0=v3(PD[hh * 64:(hh + 1) * 64, 0:256]), scalar=0.125,
                    in1=v3(EQa[hh * 64:(hh + 1) * 64, :]), op0=ALU.mult, op1=ALU.mult),
                    reads=["d", "EQa"], writes=[nQTa], banks=["d"])
            yield O("vector", lambda e: e.tensor_tensor(out=KT[:, 0:2, :], in0=v3(PD[:, 256:512]), in1=v3(EKa[:]), op=ALU.mult),
                 reads=["d", "EKa"], writes=[nKTa], banks=["d"])
            yield O("vector", lambda e: e.tensor_tensor(out=L1[:], in0=L1[:], in1=bTc[:, 256:768], op=ALU.add),
                 reads=["L1", "bTc"], writes=["L1"])
            for c in range(4):
                yield O("scalar", lambda e, c=c: e.activation(
                    out=KT[:, 2 + c, :], in_=L1[:, c * 128:(c + 1) * 128], func=AF.Exp, scale=-1.0, bias=ln1mlb[:, c:c + 1]),
                    reads=["L1", "ln1mlb"], writes=[nKTh])
            yield O("vector", lambda e: e.tensor_tensor(out=eqb[:], in0=bTc[:, 256:768], in1=eqb[:], op=ALU.subtract),
                 reads=["eqb", "bTc"], writes=["eqb"])
            yield O("scalar", lambda e: e.activation(out=eqb[:], in_=eqb[:], func=AF.Exp), reads=["eqb"], writes=["eqb"])
            yield O("vector", lambda e: e.tensor_tensor(out=QT[:, :, :], in0=v3(PC[:, :]), in1=v3(eqb[:]), op=ALU.mult),
                 reads=["c", "eqb"], writes=[nQTh], banks=["c"])


        def stage34(ctx):
            i, segs, par, gg = ctx["i"], ctx["segs"], ctx["i"] % 2, ctx["gg"]
            slot = i % NXS
            xs_, xb = x_sb[slot], f"x{slot}"
            QT, QTa2, KT, V, ez, sm = QT_[par], QTa2_[par], KT_[par], V_[par], ez_[par], sm_[par]
            nQTh, nQTa, nKTa, nKTh, nVa, nVh, ngza, ngzh = (f"{n}{par}" for n in ("QTh", "QTa", "KTa", "KTh", "Va", "Vh", "gza", "gzh"))
            smE = f"smE{par}"
            PT03, PT13 = v3(PT0b), v3(PT1b)
            yield O("tensor", [lambda e, c=c: e.transpose(out=PT03[:, c, :], in_=KT[:, c, :], identity=identb[:]) for c in range(6)],
                 reads=[nKTa, nKTh, "identb"], writes=["t0"], banks=["t0"])
            for si_, sg in enumerate(segs):
                lo, n = sg["lo"], sg["n"]
                yield O("vector", lambda e, si_=si_, lo=lo, n=n: e.tensor_copy(out=Ktm2[lo:lo + n, si_, 0:6, :], in_=PT03[lo:lo + n, 0:6, :]),
                     reads=["t0"], writes=["Ktm"], banks=["t0"])
            yield O("tensor", [lambda e, h=h: e.matmul(PB[:, h * 128:(h + 1) * 128], lhsT=KT[:, h // 2, :], rhs=QTa2[:, h % 2, h // 2, :],
                                                    start=True, stop=True) for h in range(4)],
                 reads=[nKTa, nQTa], writes=["b0"], banks=["b0"])
            yield O("vector", lambda e: e.tensor_tensor(out=AT[:, 0:4, :], in0=v3(PB[:, 0:512]),
                                                     in1=masks[:].unsqueeze(1).broadcast_to([128, 4, 128]), op=ALU.mult),
                 reads=["b0", "masks"], writes=["ATa"], banks=["b0"])
            yield O("tensor", [lambda e, h=h: e.matmul(PB[:, (4 + h) * 128:(5 + h) * 128], lhsT=KT[:, 2 + h, :], rhs=QT[:, h, :],
                                                    start=True, stop=True) for h in range(4)],
                 reads=[nKTh, nQTh], writes=["b1"], banks=["b1"])
            yield O("vector", lambda e: e.tensor_tensor(out=AT[:, 4:8, :], in0=v3(PB[:, 512:1024]),
                                                     in1=masks[:].unsqueeze(1).broadcast_to([128, 4, 128]), op=ALU.mult),
                 reads=["b1", "masks"], writes=["ATh"], banks=["b1"])
            for si_, sg in enumerate(segs):
                lo, n, st = sg["lo"], sg["n"], sg["st"]
                yield O("vector", lambda e, si_=si_, st=st: e.tensor_tensor(
                    out=Sp[:, si_], in0=S_all[:, st], in1=sm[:, 0, :, si_].unsqueeze(2).broadcast_to([128, 6, 128]), op=ALU.mult),
                    reads=[f"S{st}", f"S{st}h", smE], writes=[f"Sp{si_}"])
                yield O("gpsimd", lambda e, si_=si_, st=st: e.tensor_tensor(
                    out=S_all[:, st], in0=S_all[:, st], in1=sm[:, 1, :, si_].unsqueeze(2).broadcast_to([128, 6, 128]), op=ALU.mult),
                    reads=[smE], writes=[f"S{st}", f"S{st}h"])
                fl = []
                for h in range(4):
                    c = h // 2
                    fl.append(lambda e, h=h, lo=lo, n=n: e.matmul(
                        PB[lo:lo + n, h * 128:(h + 1) * 128], lhsT=AT[:, h, lo:lo + n], rhs=V[:, h * 128:(h + 1) * 128],
                        start=True, stop=False))
                    fl.append(lambda e, h=h, c=c, lo=lo, n=n, si_=si_: e.matmul(
                        PB[lo:lo + n, h * 128:(h + 1) * 128], lhsT=QTa2[:, h % 2, c, lo:lo + n], rhs=Sp[:, si_, c, :],
                        start=False, stop=True))
                yield O("tensor", fl, reads=["ATa", nVa, nQTa, f"Sp{si_}"], writes=["b0"], banks=["b0"])
                fl = []
                for h in range(4):
                    fl.append(lambda e, h=h, lo=lo, n=n: e.matmul(
                        PB[lo:lo + n, (4 + h) * 128:(5 + h) * 128], lhsT=AT[:, 4 + h, lo:lo + n],
                        rhs=V[:, (4 + h) * 128:(5 + h) * 128], start=True, stop=False))
                    fl.append(lambda e, h=h, lo=lo, n=n, si_=si_: e.matmul(
                        PB[lo:lo + n, (4 + h) * 128:(5 + h) * 128], lhsT=QT[:, h, lo:lo + n], rhs=Sp[:, si_, 2 + h, :],
                        start=False, stop=True))
                yield O("tensor", fl, reads=["ATh", nVh, nQTh, f"Sp{si_}"], writes=["b1"], banks=["b1"])
                fl = []
                for h in range(4):
                    c, r0 = h // 2, (h % 2) * 64
                    fl.append(lambda e, h=h, c=c, r0=r0, si_=si_: e.matmul(
                        PT0[r0:r0 + 64, c * 128:(c + 1) * 128], lhsT=Ktm2[:, si_, c, r0:r0 + 64], rhs=V[:, h * 128:(h + 1) * 128],
                        start=True, stop=True))
                yield O("tensor", fl, reads=["Ktm", nVa], writes=["t0"], banks=["t0"])
                fl = []
                for h in range(4):
                    fl.append(lambda e, h=h, si_=si_: e.matmul(
                        PT1[:, h * 128:(h + 1) * 128], lhsT=Ktm2[:, si_, 2 + h, :], rhs=V[:, (4 + h) * 128:(5 + h) * 128],
                        start=True, stop=True))
                yield O("tensor", fl, reads=["Ktm", nVh], writes=["t1"], banks=["t1"])
                for c in range(2):
                    yield O("vector", lambda e, c=c, si_=si_, st=st: e.scalar_tensor_tensor(
                        out=S_all[:, st, c, :], in0=PT0[:, c * 128:(c + 1) * 128], scalar=sm[:, 2, c, si_:si_ + 1], in1=S_all[:, st, c, :],
                        op0=ALU.mult, op1=ALU.add),
                        reads=["t0", smE], writes=[f"S{st}"], banks=["t0"])
                for h in range(4):
                    yield O("vector", lambda e, h=h, si_=si_, st=st: e.scalar_tensor_tensor(
                        out=S_all[:, st, 2 + h, :], in0=PT1[:, h * 128:(h + 1) * 128], scalar=sm[:, 2, 2 + h, si_:si_ + 1],
                        in1=S_all[:, st, 2 + h, :], op0=ALU.mult, op1=ALU.add),
                        reads=["t1", smE], writes=[f"S{st}h"], banks=["t1"])
            yield O("scalar", lambda e: e.activation(out=ztmp[:], in_=ez[:], func=AF.Exp, scale=-1.0), reads=[ngza, ngzh], writes=["ztmp"])
            yield O("scalar", lambda e: e.activation(out=ztmp[:], in_=ztmp[:], func=AF.Ln, bias=1.0), reads=["ztmp"], writes=["ztmp"])
            yield O("scalar", lambda e: e.activation(out=ztmp[:], in_=ztmp[:], func=AF.Exp, scale=-1.0), reads=["ztmp"], writes=["ztmp"])
            yield O("vector", lambda e: e.tensor_tensor(out=ez[:], in0=ez[:], in1=ztmp[:], op=ALU.mult), reads=["ztmp"], writes=[ngza, ngzh])
            yield O("scalar", lambda e: e.activation(out=sq[:, 0:512], in_=PB[:, 0:512], func=AF.Square),
                 reads=["b0"], writes=["sqa"], banks=["b0"])
            yield O("scalar", lambda e: e.activation(out=sq[:, 512:1024], in_=PB[:, 512:1024], func=AF.Square),
                 reads=["b1"], writes=["sqh"], banks=["b1"])
            yield O("vector", lambda e: e.reduce_sum(out=so[:, 0:8], in_=v3(sq[:]), axis=AX.X), reads=["sqa", "sqh"], writes=["so0"])
            yield O("scalar", lambda e: e.activation(out=so[:, 8:16], in_=so[:, 0:8], func=AF.Ln, scale=1.0 / 128, bias=EPS),
                 reads=["so0"], writes=["so1"])
            yield O("scalar", lambda e: e.activation(out=so[:, 16:24], in_=so[:, 8:16], func=AF.Exp, scale=-0.5),
                 reads=["so1"], writes=["so2"])
            yield O("gpsimd", lambda e: e.tensor_tensor(out=v3(ez[:]), in0=v3(ez[:]),
                                                      in1=so[:, 16:24].unsqueeze(2).broadcast_to([128, 8, 128]), op=ALU.mult),
                 reads=["so2"], writes=[ngza, ngzh])
            yield O("vector", lambda e: e.tensor_tensor(out=ohat[:, 0:512], in0=PB[:, 0:512], in1=ez[:, 0:512], op=ALU.mult),
                 reads=["b0", ngza], writes=["ohata"], banks=["b0"])
            yield O("vector", lambda e: e.tensor_tensor(out=ohat[:, 512:1024], in0=PB[:, 512:1024], in1=ez[:, 512:1024], op=ALU.mult),
                 reads=["b1", ngzh], writes=["ohath"], banks=["b1"])
            yield O("tensor", [lambda e, j=j: e.transpose(out=PT03[:, j, :], in_=ohat[:, j * 128:(j + 1) * 128], identity=identb[:])
                            for j in range(4)], reads=["ohata", "identb"], writes=["t0"], banks=["t0"])
            yield O("scalar", lambda e: e.activation(out=ohT[:, 0:4, :], in_=PT03[:, 0:4, :], func=AF.Copy),
                 reads=["t0"], writes=["ohTa"], banks=["t0"])
            yield O("tensor", [lambda e, j=j: e.transpose(out=PT13[:, j, :], in_=ohat[:, (4 + j) * 128:(5 + j) * 128], identity=identb[:])
                            for j in range(4)], reads=["ohath", "identb"], writes=["t1"], banks=["t1"])
            yield O("vector", lambda e: e.tensor_copy(out=ohT[:, 4:8, :], in_=PT13[:, 0:4, :]), reads=["t1"], writes=["ohTh"], banks=["t1"])
            for n_ in range(2):
                yield O("tensor", [lambda e, j=j, n_=n_: e.matmul(PB[:, n_ * 512:(n_ + 1) * 512], lhsT=ohT[:, j, :],
                                                                 rhs=w_out_bf[:, j, n_ * 512:(n_ + 1) * 512], start=(j == 0), stop=(j == 7))
                                for j in range(8)], reads=["ohTa", "ohTh"] + WOUT, writes=[f"b{n_}"], banks=[f"b{n_}"])
            yield O("scalar", lambda e: e.activation(out=ohat[:], in_=PB[:, :], func=AF.Square, accum_out=statB[:, 0:1]),
                 reads=["b0", "b1"], writes=["ohata", "ohath", "sb0"], banks=["b0", "b1"])
            yield O("scalar", lambda e: e.activation(out=statB[:, 1:2], in_=statB[:, 0:1], func=AF.Ln, scale=1.0 / D, bias=EPS),
                 reads=["sb0"], writes=["sb1"])
            yield O("scalar", lambda e: e.activation(out=statB[:, 2:3], in_=statB[:, 1:2], func=AF.Exp, scale=-0.5),
                 reads=["sb1"], writes=["sb2"])
            yield O("vector", lambda e: e.scalar_tensor_tensor(out=sq[:], in0=PB[:, :], scalar=statB[:, 2:3], in1=GG[gg][:],
                                                            op0=ALU.mult, op1=ALU.mult),
                 reads=["b0", "b1", "sb2", f"GG{gg}"], writes=["sqa", "sqh"], banks=["b0", "b1"])
            yield O("gpsimd", lambda e: e.tensor_tensor(out=xs_[:], in0=xs_[:], in1=sq[:], op=ALU.add), reads=[xb, "sqa", "sqh"], writes=[xb])
            yield O("sync", lambda e: e.dma_start(out=ctx["ydst"], in_=xs_[:]), reads=[xb], dma_sem=f"yst{slot}")
            k = ctx["k"]
            if k is not None:
                store_state(2 + 2 * k, sgs[2 * k], shs[2 * k])
                store_state(3 + 2 * k, sgs[2 * k + 1], shs[2 * k + 1])

        def store_state(st, gdst, hdst):
            P.op("sync", lambda e: e.dma_start(out=gdst.rearrange("c p v -> p c v"), in_=S_all[:, st, 0:2, :]),
                 reads=[f"S{st}"], dma_sem="sout")
            P.op("sync", lambda e: e.dma_start(out=hdst.rearrange("c p v -> p c v"), in_=S_all[:, st, 2:6, :]),
                 reads=[f"S{st}h"], dma_sem="sout")

        tiles = []
        for k in range(2):
            segs = [dict(lo=0, n=64, b=2 + 2 * k, st=2 + 2 * k), dict(lo=64, n=64, b=3 + 2 * k, st=3 + 2 * k)]
            tiles.append(dict(xsrc=xs[2 * k:2 * k + 2].rearrange("b t d -> (b t) d"), ydst=ys[2 * k:2 * k + 2].rearrange("b t d -> (b t) d"),
                              segs=segs, gg=2 + k, k=k))
        for t in range(tp_tiles):
            for s in range(2):
                tiles.append(dict(xsrc=xp[s, t * 128:(t + 1) * 128, :], ydst=yp[s, t * 128:(t + 1) * 128, :],
                                  segs=[dict(lo=0, n=64, b=s, st=s), dict(lo=64, n=64, b=s, st=s)], gg=s, k=None))
        for i, t in enumerate(tiles):
            t["i"] = i
        NT = len(tiles)
        PRE = 2
        import os as _os2
        _os_kverb = bool(_os2.environ.get("KVERB2"))
        for i in range(min(PRE, NT)):
            load_x(i, tiles[i]["xsrc"])
        for r in range(NT + 1):
            if r + PRE < NT:
                load_x(r + PRE, tiles[r + PRE]["xsrc"])
            if _os_kverb:
                print("round", r, "model t_us", {k: round(v / 1e3, 1) for k, v in P.eng_free.items()})
            gens = []
            if r >= 1:
                gens.append(stage34(tiles[r - 1]))
            if r < NT:
                gens.append(stage12(tiles[r]))
            heads = []
            for g in gens:
                try:
                    heads.append([g, next(g)])
                except StopIteration:
                    pass
            while heads:
                best, bt = None, None
                for hd in heads:
                    t = P.est_start(hd[1])
                    if bt is None or t < bt:
                        best, bt = hd, t
                eng, fns, reads, writes, banks, dma_sem = best[1]
                if _os_kverb and r == 6:
                    _ts = P.est_start(best[1])
                    print("OP", "S34" if best[0] is gens[0] else "S12", eng, "start %.2f dur %.2f engfree %.2f" % (_ts / 1e3, _est_ns(eng, fns if not callable(fns) else [fns]) / 1e3, P.eng_free[eng] / 1e3), list(writes)[:3], list(banks))
                P.op(eng, fns, reads=reads, writes=writes, banks=banks, dma_sem=dma_sem)
                try:
                    best[1] = next(best[0])
                except StopIteration:
                    heads.remove(best)
        for s in range(2):
            store_state(s, sgp[s], shp[s])
        for nm, s in list(P.sems.items()):
            if nm.startswith("yst") or nm == "sout":
                P.wait_token("sync", (nm, s[1]))
        with nc.Block() as block:
            P.replay(block)
        P.close()
        import os as _os
        if _os.environ.get("KVERB"):
            print("total ops recorded", P.count, "model makespan us", max(P.eng_free.values()) / 1e3)
    return nc


def host_inputs(core, x_prompt, x_sample, c_prompt, c_sample, state_gla, state_hgrn, w_ada, b_ada, g_pre,
                w_in, w_alpha, b_alpha, g_onorm_gla, hgrn_lb_logits, g_onorm_hgrn, w_out, g_post, consts):
    f = np.float32
    c6 = np.concatenate([c_prompt[2 * core:2 * core + 2], c_sample[4 * core:4 * core + 4]], 0)

    def pj(a):
        return np.ascontiguousarray(a.reshape(8, 128, a.shape[1]).transpose(1, 0, 2))

    m = {
        "xp": np.ascontiguousarray(x_prompt[2 * core:2 * core + 2]),
        "xs": np.ascontiguousarray(x_sample[4 * core:4 * core + 4]),
        "cT": np.ascontiguousarray(c6.T.reshape(8, 128, 6).transpose(1, 0, 2)),
        "stg": np.ascontiguousarray(state_gla[0, 4 * core:4 * core + 4].reshape(4, 2, 128, 128)),
        "sth": np.ascontiguousarray(state_hgrn[0, 4 * core:4 * core + 4]),
        "wada": pj(w_ada[0]),
        "bada": np.ascontiguousarray(np.broadcast_to(b_ada[0][None, :], (6, 3072))),
        "gpre": np.ascontiguousarray(g_pre[0].reshape(8, 128).T),
        "win": pj(w_in[0]),
        "walpha": np.ascontiguousarray(w_alpha[0]),
        "balpha": np.ascontiguousarray(b_alpha[0].reshape(2, 128).T),
        "gon": np.ascontiguousarray(np.stack([g_onorm_gla[0], g_onorm_hgrn[0]], 1)),
        "lbl": np.ascontiguousarray(hgrn_lb_logits.reshape(2, 4, 128).transpose(2, 0, 1)),
        "wout": pj(w_out[0]),
        "gpost": np.ascontiguousarray(np.broadcast_to(g_post[0][None, :], (128, 1024))),
    }
    m.update(consts)
    return {k: np.ascontiguousarray(v, dtype=f) for k, v in m.items()}


def make_consts():
    f = np.float32
    maskp = np.triu(np.ones((128, 128), f))
    masks = maskp.copy()
    masks[0:64, 64:128] = 0.0
    smaskp = np.ones((128, 768), f)
    smaskp[:, 0::128] = 0.0
    smasks = smaskp.copy()
    smasks[:, 64::128] = 0.0
    sel = np.zeros((6, 4, 128), f)
    sel[0, 0, :] = 1.0
    sel[1, 1, :] = 1.0
    sel[2, 2, 0:64] = 1.0
    sel[3, 2, 64:128] = 1.0
    sel[4, 3, 0:64] = 1.0
    sel[5, 3, 64:128] = 1.0
    return {"identf": np.eye(128, dtype=f), "maskp": maskp, "masks": masks, "smaskp": smaskp, "smasks": smasks, "sel": sel}


def assemble(results, TP):
    f = np.float32
    yp = np.concatenate([r["yp"] for r in results], 0).astype(f)
    ys = np.concatenate([r["ys"] for r in results], 0).astype(f)
    sgp = np.concatenate([r["sgp"].reshape(2, 4, 64, 128) for r in results], 0)[None].astype(f)
    shp = np.concatenate([r["shp"] for r in results], 0)[None].astype(f)
    sgs = np.concatenate([r["sgs"].reshape(4, 4, 64, 128) for r in results], 0)[None].astype(f)
    shs = np.concatenate([r["shs"] for r in results], 0)[None].astype(f)
    return (yp, ys, sgp, shp, sgs, shs)


def kernel(**inputs):
    inputs = {k: np.asarray(v) for k, v in inputs.items()}
    TP = inputs["x_prompt"].shape[1]
    nc = build(TP // 128)
    consts = make_consts()
    in_maps = [host_inputs(i, consts=consts, **inputs) for i in range(N_CORES)]
    res = run_bass_kernel_spmd(nc, in_maps, core_ids=list(range(N_CORES)))
    return assemble(res.results, TP)
```

```python
from contextlib import ExitStack

import numpy as np
import concourse.bass as bass
import concourse.mybir as mybir
from concourse.bass_utils import run_bass_kernel_spmd

F32 = mybir.dt.float32
BF16 = mybir.dt.bfloat16
AF = mybir.ActivationFunctionType
ALU = mybir.AluOpType
AX = mybir.AxisListType

D = 1024
NCOL = 3600
EPS = 1e-6
N_CORES = 8
SEQ = 4096
ENGS = ("sync", "scalar", "vector", "gpsimd", "tensor")

C_QA, C_KA, C_VA, C_ZA, C_AL, C_QH, C_FH, C_IH, C_ZH = 0, 256, 512, 1024, 1536, 1552, 2064, 2576, 3088


class _Probe:
    def __init__(self):
        self.calls = []

    def __getattr__(self, name):
        def f(*a, **k):
            out = k.get("out", a[0] if a else None)
            self.calls.append((name, out, k))
            return None
        return f


def _ap_n(ap):
    try:
        n = 1
        for d in ap.shape[1:]:
            n *= int(d)
        return n
    except Exception:
        return 0


def _est_ns(eng, fns):
    pr = _Probe()
    for f in fns:
        try:
            f(pr)
        except Exception:
            pass
    tot = 0.0
    for name, out, k in pr.calls:
        n = max([_ap_n(out)] + [_ap_n(k.get(kk)) for kk in ("in_", "in0", "data0", "rhs")] + [1])
        if eng == "tensor":
            tot += 95.0 if name == "transpose" else 15.0 + 0.43 * n
        elif eng == "scalar":
            tot += (480.0 if n <= 16 else 200.0 + 0.75 * n) + (100.0 if k.get("accum_out") is not None else 0.0)
        elif eng == "vector":
            tot += (60.0 + 2.1 * n) if name == "tensor_tensor_scan" else 150.0 + 1.04 * n
        elif eng == "gpsimd":
            tot += (100.0 + 1.0 * n) if name == "tensor_scalar" else 100.0 + 2.0 * n
        else:
            tot += 2000.0
    return max(tot, 50.0)


class Prog:
    def __init__(self, nc):
        self.nc = nc
        self.q = {e: [] for e in ENGS}
        self.sems = {}
        self.waited = {e: {} for e in ENGS}
        self.bufs = {}
        self.bank_last = {}
        self._cms = []
        self.eng_free = {e: 0.0 for e in ENGS}
        self.tok_time = {}
        import os as _os
        self.limit = int(_os.environ.get("KLIMIT", "0")) or None
        self.count = 0

    def sem(self, name):
        if name not in self.sems:
            cm = self.nc.semaphore(name)
            h = cm.__enter__()
            self._cms.append(cm)
            self.sems[name] = [h, 0]
        return self.sems[name]

    def close(self):
        for cm in reversed(self._cms):
            cm.__exit__(None, None, None)

    def _deps(self, eng, reads, writes, is_dma, banks):
        deps = []
        for b in banks:
            t = self.bank_last.get(b)
            if t is not None and t[2] != eng:
                deps.append((t, "bank"))
        for b in reads:
            st = self.bufs.get(b)
            if st and st[0] is not None:
                deps.append((st[0], "raw"))
        for b in writes:
            st = self.bufs.get(b)
            if st:
                if st[0] is not None:
                    deps.append((st[0], "waw"))
                for t in st[1]:
                    deps.append((t, "war"))
        waits = []
        for tok, kind in deps:
            sname, val, teng, tdma = tok
            if not tdma and teng == eng and not is_dma:
                if eng == "tensor" or kind in ("war", "waw"):
                    continue
            if self.waited[eng].get(sname, 0) >= val:
                continue
            self.waited[eng][sname] = val
            waits.append((sname, val))
        return waits

    def _dep_tokens(self, eng, reads, writes, banks):
        toks = []
        for b in banks:
            t = self.bank_last.get(b)
            if t is not None:
                toks.append(t)
        for b in reads:
            st = self.bufs.get(b)
            if st and st[0] is not None:
                toks.append(st[0])
        for b in writes:
            st = self.bufs.get(b)
            if st:
                if st[0] is not None:
                    toks.append(st[0])
                toks.extend(st[1])
        return toks

    def _ready(self, eng, toks):
        t = self.eng_free[eng]
        for tok in toks:
            tt = self.tok_time.get((tok[0], tok[1]), 0.0) + (0.0 if tok[2] == eng else 150.0)
            if tt > t:
                t = tt
        return t

    def est_start(self, desc):
        eng, fns, reads, writes, banks, dma_sem = desc
        return self._ready(eng, self._dep_tokens(eng, reads, writes, banks))

    def op(self, eng, fns, reads=(), writes=(), dma_sem=None, banks=()):
        if callable(fns):
            fns = [fns]
        _t0 = self._ready(eng, self._dep_tokens(eng, reads, writes, banks))
        _dur = _est_ns(eng, fns)
        self.count += 1
        if self.limit is not None and self.count > self.limit:
            return None
        is_dma = dma_sem is not None
        waits = self._deps(eng, reads, writes, is_dma, banks)
        if is_dma:
            s = self.sem(dma_sem)
            s[1] += 16
            tok = (dma_sem, s[1], eng, True)
            inc = (dma_sem, 16)
        else:
            sname = "p_" + eng
            s = self.sem(sname)
            s[1] += 1
            tok = (sname, s[1], eng, False)
            inc = (sname, 1)
        self.q[eng].append((waits, fns, inc))
        if is_dma:
            self.eng_free[eng] = _t0 + 60.0
        else:
            self.eng_free[eng] = _t0 + _dur
        self.tok_time[(tok[0], tok[1])] = _t0 + _dur
        for b in banks:
            self.bank_last[b] = tok
        for b in writes:
            self.bufs[b] = [tok, []]
        for b in reads:
            if b in writes:
                continue
            self.bufs.setdefault(b, [None, []])[1].append(tok)
        return tok

    def retoken(self, names, tok):
        for b in names:
            self.bufs[b] = [tok, []]

    def wait_token(self, eng, tok):
        sname, val = tok[0], tok[1]
        if self.waited[eng].get(sname, 0) >= val:
            return
        self.waited[eng][sname] = val
        self.q[eng].append(([(sname, val)], [], None))

    def barrier(self):
        snap = [(n, s[1]) for n, s in self.sems.items() if s[1] > 0]
        for e in ENGS:
            w = []
            for n, v in snap:
                if self.waited[e].get(n, 0) < v:
                    self.waited[e][n] = v
                    w.append((n, v))
            if w:
                self.q[e].append((w, [], None))

    def replay(self, block):
        P = self

        def run(engobj, name):
            for waits, fns, inc in P.q[name]:
                for sname, val in waits:
                    engobj.wait_ge(P.sems[sname][0], val)
                ins = None
                for f in fns:
                    ins = f(engobj)
                if inc is not None and ins is not None:
                    ins.then_inc(P.sems[inc[0]][0], inc[1])

        @block.sync
        def _(e):
            run(e, "sync")

        @block.scalar
        def _(e):
            run(e, "scalar")

        @block.vector
        def _(e):
            run(e, "vector")

        @block.gpsimd
        def _(e):
            run(e, "gpsimd")

        @block.tensor
        def _(e):
            run(e, "tensor")


def v3(ap, t=128):
    return ap.rearrange("p (c t) -> p c t", t=t)


def build(tp_tiles):
    TP = tp_tiles * 128
    nc = bass.Bass("TRN2", target_bir_lowering=False)

    def din(name, shape):
        return nc.dram_tensor(name, shape, F32, kind="ExternalInput").ap()

    def dout(name, shape):
        return nc.dram_tensor(name, shape, F32, kind="ExternalOutput").ap()

    xp = din("xp", [2, TP, D])
    xs = din("xs", [4, 64, D])
    cT_d = din("cT", [128, 8, 6])
    stg = din("stg", [4, 2, 128, 128])
    sth = din("sth", [4, 4, 128, 128])
    wada = din("wada", [128, 8, 3072])
    bada = din("bada", [6, 3072])
    gpre_d = din("gpre", [128, 8])
    win = din("win", [128, 8, NCOL])
    walpha_d = din("walpha", [16, 256])
    balpha_d = din("balpha", [128, 2])
    gon_d = din("gon", [128, 2])
    lbl_d = din("lbl", [128, 2, 4])
    wout = din("wout", [128, 8, D])
    gpost_d = din("gpost", [128, D])
    identf_d = din("identf", [128, 128])
    maskp_d = din("maskp", [128, 128])
    masks_d = din("masks", [128, 128])
    smaskp_d = din("smaskp", [128, 768])
    smasks_d = din("smasks", [128, 768])
    sel_d = din("sel", [6, 4, 128])

    yp = dout("yp", [2, TP, D])
    ys = dout("ys", [4, 64, D])
    sgp = dout("sgp", [2, 2, 128, 128])
    shp = dout("shp", [2, 4, 128, 128])
    sgs = dout("sgs", [4, 2, 128, 128])
    shs = dout("shs", [4, 4, 128, 128])

    es = ExitStack()
    with es:
        def sb(name, shape, dt=F32):
            return es.enter_context(nc.sbuf_tensor(name, shape, dt))

        def ps(name, shape, dt=F32):
            return es.enter_context(nc.psum_tensor(name, shape, dt))

        P = Prog(nc)

        w_in_bf = sb("w_in_bf", [128, 8, NCOL], BF16)
        w_out_bf = sb("w_out_bf", [128, 8, D], BF16)
        walpha_bf = sb("walpha_bf", [128, 256], BF16)
        identf = sb("identf_sb", [128, 128])
        identb = sb("identb", [128, 128], BF16)
        maskp = sb("maskp_sb", [128, 128])
        masks = sb("masks_sb", [128, 128])
        smaskp = sb("smaskp_sb", [128, 768])
        smasks = sb("smasks_sb", [128, 768])
        cst = sb("cst", [128, 32])
        nbalpha = cst[:, 0:2]
        lb = cst[:, 2:6]
        ln1mlb = cst[:, 6:10]
        balpha = cst[:, 10:12]
        gon = cst[:, 12:14]
        tmp4 = cst[:, 14:18]
        tmp4b = cst[:, 18:22]
        lbl = sb("lbl_sb", [128, 2, 4])
        gpre = sb("gpre_sb", [128, 8])
        aT = sb("aT", [128, 8, 6])
        sT = sb("sT", [128, 8, 6])
        GG = [sb(f"GG{g}", [128, D]) for g in range(4)]
        S_all = sb("S_all", [128, 6, 6, 128])

        PA = ps("PA", [128, 1024])
        PB = ps("PB", [128, 1024])
        PC = ps("PC", [128, 512])
        PD = ps("PD", [128, 512])
        PT0 = ps("PT0", [128, 512])
        PT1 = ps("PT1", [128, 512])

        ses = ExitStack()
        with ses:
            def ssb(name, shape, dt=F32):
                return ses.enter_context(nc.sbuf_tensor(name, shape, dt))
            NSTG = 4
            stage = [ssb(f"stage{i}", [128, 2048]) for i in range(NSTG)]
            mod_sb = ssb("mod_sb", [6, 3072])
            bada_sb = ssb("bada_sb", [6, 3072])
            gpost = ssb("gpost_sb", [128, D])
            cT = ssb("cT_sb", [128, 8, 6])
            sel = ssb("sel_sb", [6, 4, 128])
            tmp48 = ssb("tmp48", [128, 8, 6])

            small = [
                (cT[:], cT_d, "cT"), (bada_sb[:], bada, "bada"), (gpre[:], gpre_d, "gpre"),
                (balpha, balpha_d, "balpha"), (gon, gon_d, "gon"), (lbl[:], lbl_d, "lbl"),
                (identf[:], identf_d, "identf"), (maskp[:], maskp_d, "maskp"), (masks[:], masks_d, "masks"),
                (smaskp[:], smaskp_d, "smaskp"), (smasks[:], smasks_d, "smasks"), (sel[:], sel_d, "sel"),
                (gpost[:], gpost_d, "gpost"),
            ]
            names = []
            tok = None
            for o_, i_, nm in small:
                tok = P.op("sync", lambda e, o_=o_, i_=i_: e.dma_start(out=o_, in_=i_), writes=[nm], dma_sem="ld_s")
                names.append(nm)
            walpha_f = ssb("walpha_f", [16, 256])
            stage_wa = walpha_f[:]
            tok = P.op("sync", lambda e: e.dma_start(out=stage_wa, in_=walpha_d), writes=["walpha_f"], dma_sem="ld_s")
            names.append("walpha_f")
            for b in range(4):
                tok = P.op("sync", lambda e, b=b: e.dma_start(out=S_all[:, 2 + b, 0:2, :], in_=stg[b].rearrange("c p v -> p c v")),
                           writes=[f"S{2 + b}"], dma_sem="ld_s")
                tok = P.op("sync", lambda e, b=b: e.dma_start(out=S_all[:, 2 + b, 2:6, :], in_=sth[b].rearrange("c p v -> p c v")),
                           writes=[f"S{2 + b}h"], dma_sem="ld_s")
            P.retoken(names + [f"S{2 + b}" for b in range(4)] + [f"S{2 + b}h" for b in range(4)], tok)

            P.op("vector", lambda e: e.tensor_copy(out=identb[:], in_=identf[:]), reads=["identf"], writes=["identb"])
            P.op("gpsimd", lambda e: e.memset(walpha_bf[:], 0.0), writes=["walpha_bf"])
            P.op("vector", lambda e: e.tensor_copy(out=walpha_bf[0:16, :], in_=stage_wa), reads=["walpha_f", "walpha_bf"], writes=["walpha_bf"])
            P.op("vector", lambda e: e.tensor_scalar(out=nbalpha, in0=balpha, scalar1=-1.0, scalar2=None, op0=ALU.mult),
                 reads=["balpha"], writes=["nbalpha"])
            P.op("vector", lambda e: e.tensor_tensor(out=tmp4, in0=lbl[:, 1, :], in1=lbl[:, 0, :], op=ALU.subtract),
                 reads=["lbl"], writes=["tmp4"])
            P.op("scalar", lambda e: e.activation(out=tmp4, in_=tmp4, func=AF.Exp), reads=["tmp4"], writes=["tmp4"])
            P.op("vector", lambda e: e.tensor_scalar(out=tmp4b, in0=tmp4, scalar1=1.0, scalar2=None, op0=ALU.add),
                 reads=["tmp4"], writes=["tmp4b"])
            P.op("vector", lambda e: e.reciprocal(out=lb, in_=tmp4b), reads=["tmp4b"], writes=["lb"])
            P.op("vector", lambda e: e.tensor_tensor(out=tmp4b, in0=tmp4, in1=lb, op=ALU.mult), reads=["tmp4", "lb"], writes=["tmp4b"])
            P.op("scalar", lambda e: e.activation(out=ln1mlb, in_=tmp4b, func=AF.Ln), reads=["tmp4b"], writes=["ln1mlb"])
            P.op("gpsimd", lambda e: e.memset(S_all[:, 0:2, :, :], 0.0), writes=["S0", "S0h", "S1", "S1h"])

            si = 0
            for n in range(12):
                stg_ = stage[si % NSTG]
                sname = f"stage{si % NSTG}"
                st3 = stg_[:, 0:2048].rearrange("p (j n) -> p j n", n=256)
                P.op("sync", lambda e, st3=st3, n=n: e.dma_start(out=st3, in_=wada[:, :, n * 256:(n + 1) * 256]),
                     writes=[sname], dma_sem="ld_" + sname)
                P.op("tensor", [lambda e, j=j, st3=st3: e.matmul(PC[0:6, 0:256], lhsT=cT[:, j, :], rhs=st3[:, j, :],
                                                                  start=(j == 0), stop=(j == 7)) for j in range(8)],
                     reads=[sname, "cT"], writes=["pc"], banks=["c"])
                P.op("vector", lambda e, n=n: e.tensor_tensor(out=mod_sb[0:6, n * 256:(n + 1) * 256], in0=PC[0:6, 0:256],
                                                             in1=bada_sb[0:6, n * 256:(n + 1) * 256], op=ALU.add),
                     reads=["pc", "bada"], writes=["mod"], banks=["c"])
                si += 1
            P.op("tensor", [lambda e, k=k: e.transpose(out=PD[:, k * 6:(k + 1) * 6], in_=mod_sb[0:6, k * 128:(k + 1) * 128],
                                                       identity=identf[0:6, 0:6]) for k in range(16)],
                 reads=["mod", "identf"], writes=["pd"], banks=["d"])
            P.op("vector", lambda e: e.tensor_copy(out=sT[:], in_=PD[:, 0:48].rearrange("p (j b) -> p j b", b=6)),
                 reads=["pd"], writes=["sT"], banks=["d"])
            P.op("vector", lambda e: e.tensor_scalar(out=tmp48[:], in0=PD[:, 48:96].rearrange("p (j b) -> p j b", b=6),
                                                     scalar1=1.0, scalar2=None, op0=ALU.add),
                 reads=["pd"], writes=["tmp48"], banks=["d"])
            P.op("vector", lambda e: e.tensor_tensor(out=aT[:], in0=tmp48[:], in1=gpre[:].unsqueeze(2).broadcast_to([128, 8, 6]),
                                                     op=ALU.mult), reads=["tmp48", "gpre"], writes=["aT"])
            for g in range(4):
                for n in range(2):
                    P.op("tensor", lambda e, g=g, n=n: e.matmul(PC[:, 0:512], lhsT=sel[0:6, g, :],
                                                                  rhs=mod_sb[0:6, 2048 + n * 512:2048 + (n + 1) * 512],
                                                                  start=True, stop=True),
                         reads=["mod", "sel"], writes=["pc"], banks=["c"])
                    P.op("vector", lambda e, g=g, n=n: e.tensor_tensor(out=GG[g][:, n * 512:(n + 1) * 512], in0=PC[:, 0:512],
                                                                      in1=gpost[:, n * 512:(n + 1) * 512], op=ALU.mult),
                         reads=["pc", "gpost"], writes=[f"GG{g}"], banks=["c"])
            for j in range(8):
                for hlf in range(2):
                    stg_ = stage[si % NSTG]
                    sname = f"stage{si % NSTG}"
                    c0 = hlf * 1800
                    P.op("sync", lambda e, stg_=stg_, j=j, c0=c0: e.dma_start(out=stg_[:, 0:1800], in_=win[:, j, c0:c0 + 1800]),
                         writes=[sname], dma_sem="ld_" + sname)
                    eng = "vector" if hlf == 0 else "scalar"
                    if eng == "vector":
                        P.op("vector", lambda e, stg_=stg_, j=j, c0=c0: e.tensor_copy(out=w_in_bf[:, j, c0:c0 + 1800], in_=stg_[:, 0:1800]),
                             reads=[sname], writes=[f"win{j}_{hlf}"])
                    else:
                        P.op("scalar", lambda e, stg_=stg_, j=j, c0=c0: e.activation(out=w_in_bf[:, j, c0:c0 + 1800], in_=stg_[:, 0:1800],
                                                                                      func=AF.Copy),
                             reads=[sname], writes=[f"win{j}_{hlf}"])
                    si += 1
            for jj in range(4):
                stg_ = stage[si % NSTG]
                sname = f"stage{si % NSTG}"
                st3 = stg_[:, 0:2048].rearrange("p (j n) -> p j n", n=1024)
                P.op("sync", lambda e, st3=st3, jj=jj: e.dma_start(out=st3, in_=wout[:, 2 * jj:2 * jj + 2, :]),
                     writes=[sname], dma_sem="ld_" + sname)
                for jl in range(2):
                    j = 2 * jj + jl
                    gcol = gon[:, 0:1] if j < 4 else gon[:, 1:2]
                    P.op("gpsimd", lambda e, st3=st3, jl=jl, j=j, gcol=gcol: e.tensor_scalar(
                        out=w_out_bf[:, j, :], in0=st3[:, jl, :], scalar1=gcol, scalar2=1.0, op0=ALU.mult, op1=ALU.mult),
                        reads=[sname, "gon"], writes=[f"wout{j}"])
                si += 1
            P.barrier()
        WIN = [f"win{j}_{h}" for j in range(8) for h in range(2)]
        WOUT = [f"wout{j}" for j in range(8)]

        NXS = 4
        x_sb = [sb(f"x_sb{i}", [128, D]) for i in range(NXS)]
        statA = sb("statA", [128, 8])
        xn = sb("xn", [128, D], BF16)
        hT = sb("hT", [128, 8, 128], BF16)
        alr = sb("alr", [128, 128], BF16)
        e1 = sb("e1", [128, 256])
        eh = sb("eh", [128, 512])
        L1 = sb("L1", [128, 512])
        L2 = sb("L2", [128, 512])
        gT = sb("gT", [128, 768])
        bT = sb("bT", [128, 768])
        bTc = sb("bTc", [128, 768])
        eqb = sb("eqb", [128, 512])
        EQa = sb("EQa", [128, 256])
        EKa = sb("EKa", [128, 256])
        QT_ = [sb(f"QT{p}", [128, 4, 128], BF16) for p in range(2)]
        QTa2_ = [sb(f"QTa2{p}", [128, 2, 2, 128], BF16) for p in range(2)]
        KT_ = [sb(f"KT{p}", [128, 6, 128], BF16) for p in range(2)]
        V_ = [sb(f"V{p}", [128, D], BF16) for p in range(2)]
        ez_ = [sb(f"ez{p}", [128, D]) for p in range(2)]
        sm_ = [sb(f"sm{p}", [128, 3, 6, 2]) for p in range(2)]
        Ktm2 = sb("Ktm2", [128, 2, 6, 128], BF16)
        Sp = sb("Sp", [128, 2, 6, 128], BF16)
        AT = sb("AT", [128, 8, 128], BF16)
        sq = sb("sq", [128, D])
        so = sb("so", [128, 24])
        statB = sb("statB", [128, 8])
        ohat = sb("ohat", [128, D], BF16)
        ohT = sb("ohT", [128, 8, 128], BF16)
        ztmp = sb("ztmp", [128, D])

        bT3 = v3(bT[:])
        PT0b = PT0[:].bitcast(BF16)
        PT1b = PT1[:].bitcast(BF16)
        PCb = PC[:].bitcast(BF16)
        PDb = PD[:].bitcast(BF16)
        P.op("gpsimd", lambda e: e.memset(Ktm2[:], 0.0), writes=["Ktma", "Ktmh"])
        for p in range(2):
            P.op("gpsimd", lambda e, p=p: e.memset(QTa2_[p][:], 0.0), writes=[f"QTa{p}"])

        def O(eng, fns, reads=(), writes=(), banks=(), dma_sem=None):
            return (eng, fns, reads, writes, banks, dma_sem)

        def load_x(i, xsrc):
            slot = i % NXS
            P.op("sync", lambda e: e.dma_start(out=x_sb[slot][:], in_=xsrc), writes=[f"x{slot}"], dma_sem=f"xld{slot}")

        def stage12(ctx):
            i, segs, par = ctx["i"], ctx["segs"], ctx["i"] % 2
            slot = i % NXS
            xs_, xb = x_sb[slot], f"x{slot}"
            QT, QTa2, KT, V, ez, sm = QT_[par], QTa2_[par], KT_[par], V_[par], ez_[par], sm_[par]
            nQTh, nQTa, nKTa, nKTh, nVa, nVh, ngza, ngzh = (f"{n}{par}" for n in ("QTh", "QTa", "KTa", "KTh", "Va", "Vh", "gza", "gzh"))
            smE = f"smE{par}"
            if all(sg["b"] == segs[0]["b"] for sg in segs):
                mods = [(0, 128, segs[0]["b"])]
            else:
                mods = [(sg["lo"], sg["n"], sg["b"]) for sg in segs]
            yield O("scalar", lambda e: e.activation(out=hT[:].rearrange("p j t -> p (j t)"), in_=xs_[:], func=AF.Square, accum_out=statA[:, 0:1]),
                 reads=[xb], writes=["hT0", "hT1", "sa0"])
            yield O("scalar", lambda e: e.activation(out=statA[:, 1:2], in_=statA[:, 0:1], func=AF.Ln, scale=1.0 / D, bias=EPS),
                 reads=["sa0"], writes=["sa1"])
            yield O("scalar", lambda e: e.activation(out=statA[:, 2:3], in_=statA[:, 1:2], func=AF.Exp, scale=-0.5),
                 reads=["sa1"], writes=["sa2"])
            yield O("gpsimd", lambda e: e.tensor_scalar(out=xn[:], in0=xs_[:], scalar1=statA[:, 2:3], scalar2=1.0,
                                                       op0=ALU.mult, op1=ALU.mult), reads=[xb, "sa2"], writes=["xn"])
            for half, (PTb, bank, eng) in enumerate(((PCb, "c", "scalar"), (PDb, "d", "vector"))):
                PT3 = v3(PTb)
                yield O("tensor", [lambda e, j=j, PT3=PT3, half=half: e.transpose(
                    out=PT3[:, j, :], in_=xn[:, (4 * half + j) * 128:(4 * half + j + 1) * 128], identity=identb[:]) for j in range(4)],
                    reads=["xn", "identb"], writes=[bank], banks=[bank])
                for j in range(4):
                    jj = 4 * half + j
                    for (lo, n, b) in mods:
                        if eng == "scalar":
                            yield O("scalar", lambda e, j=j, jj=jj, lo=lo, n=n, b=b, PT3=PT3: e.activation(
                                out=hT[:, jj, lo:lo + n], in_=PT3[:, j, lo:lo + n], func=AF.Identity,
                                scale=aT[:, jj, b:b + 1], bias=sT[:, jj, b:b + 1]),
                                reads=[bank, "aT", "sT"], writes=[f"hT{half}"], banks=[bank])
                        else:
                            yield O("vector", lambda e, j=j, jj=jj, lo=lo, n=n, b=b, PT3=PT3: e.tensor_scalar(
                                out=hT[:, jj, lo:lo + n], in0=PT3[:, j, lo:lo + n], scalar1=aT[:, jj, b:b + 1],
                                scalar2=sT[:, jj, b:b + 1], op0=ALU.mult, op1=ALU.add),
                                reads=[bank, "aT", "sT"], writes=[f"hT{half}"], banks=[bank])
            HT = ["hT0", "hT1"]

            def fm(out_ap, col0, m):
                return [lambda e, j=j: e.matmul(out_ap, lhsT=w_in_bf[:, j, col0:col0 + m], rhs=hT[:, j, :],
                                                start=(j == 0), stop=(j == 7)) for j in range(8)]

            def tm(out_ap, col0):
                return [lambda e, j=j: e.matmul(out_ap, lhsT=hT[:, j, :], rhs=w_in_bf[:, j, col0:col0 + 512],
                                                start=(j == 0), stop=(j == 7)) for j in range(8)]

            yield O("tensor", fm(PA[:, 0:128], C_AL, 128), reads=HT + WIN, writes=["a0"], banks=["a0"])
            yield O("vector", lambda e: e.tensor_copy(out=alr[:], in_=PA[:, 0:128]), reads=["a0"], writes=["alr"], banks=["a0"])
            for c in range(4):
                yield O("tensor", fm(PA[:, 512 + c * 128:512 + (c + 1) * 128], C_FH + c * 128, 128), reads=HT + WIN, writes=["a1"], banks=["a1"])
            yield O("tensor", [lambda e, c=c: e.matmul(PA[:, 128 + c * 128:256 + c * 128], lhsT=walpha_bf[:, c * 128:(c + 1) * 128],
                                                    rhs=alr[:, :], start=True, stop=True) for c in range(2)],
                 reads=["alr", "walpha_bf"], writes=["a0"], banks=["a0"])
            yield O("scalar", lambda e: e.activation(out=eh[:], in_=PA[:, 512:1024], func=AF.Exp, scale=-1.0),
                 reads=["a1"], writes=["eh"], banks=["a1"])
            for c in range(4):
                yield O("tensor", fm(PC[:, c * 128:(c + 1) * 128], C_QH + c * 128, 128), reads=HT + WIN, writes=["c"], banks=["c"])
            for c in range(2):
                yield O("scalar", lambda e, c=c: e.activation(out=e1[:, c * 128:(c + 1) * 128], in_=PA[:, 128 + c * 128:256 + c * 128],
                                                           func=AF.Exp, scale=-1.0, bias=nbalpha[:, c:c + 1]),
                     reads=["a0", "nbalpha"], writes=["e1"], banks=["a0"])
            yield O("scalar", lambda e: e.activation(out=L1[:], in_=eh[:], func=AF.Ln, bias=1.0), reads=["eh"], writes=["L1"])
            for c in range(2):
                yield O("tensor", fm(PD[:, c * 128:(c + 1) * 128], C_QA + c * 128, 128), reads=HT + WIN, writes=["d"], banks=["d"])
            for c in range(2):
                yield O("tensor", fm(PD[:, (2 + c) * 128:(3 + c) * 128], C_KA + c * 128, 128), reads=HT + WIN, writes=["d"], banks=["d"])
            yield O("scalar", lambda e: e.activation(out=e1[:], in_=e1[:], func=AF.Ln, bias=1.0), reads=["e1"], writes=["e1"])
            yield O("gpsimd", lambda e: e.tensor_scalar(out=gT[:, 0:256], in0=e1[:], scalar1=-1.0 / 16.0, scalar2=1.0,
                                                       op0=ALU.mult, op1=ALU.mult), reads=["e1"], writes=["gTa"])
            for c in range(4):
                yield O("scalar", lambda e, c=c: e.activation(out=L2[:, c * 128:(c + 1) * 128], in_=eh[:, c * 128:(c + 1) * 128],
                                                           func=AF.Ln, bias=1.0, scale=lb[:, c:c + 1]),
                     reads=["eh", "lb"], writes=["L2"])
            yield O("vector", lambda e: e.tensor_tensor(out=gT[:, 256:768], in0=L2[:], in1=L1[:], op=ALU.subtract),
                 reads=["L1", "L2"], writes=["gTh"])
            yield O("vector", lambda e: e.tensor_tensor(out=L1[:], in0=PA[:, 512:1024], in1=L1[:], op=ALU.add),
                 reads=["a1", "L1"], writes=["L1"], banks=["a1"])
            yield O("vector", lambda e: e.tensor_tensor_scan(out=bT[:], data0=smasks[:], data1=gT[:], initial=0.0,
                                                          op0=ALU.mult, op1=ALU.add),
                 reads=["gTa", "gTh", "smasks"], writes=["bT"])
            yield O("tensor", tm(PA[:, 0:512], C_ZA), reads=HT + WIN, writes=["a0"], banks=["a0"])
            yield O("tensor", tm(PA[:, 512:1024], C_VA), reads=HT + WIN, writes=["a1"], banks=["a1"])
            yield O("scalar", lambda e: e.activation(out=ez[:, 0:512], in_=PA[:, 0:512], func=AF.Copy), reads=["a0"], writes=[ngza], banks=["a0"])
            yield O("vector", lambda e: e.tensor_copy(out=V[:, 0:512], in_=PA[:, 512:1024]), reads=["a1"], writes=[nVa], banks=["a1"])
            yield O("tensor", tm(PA[:, 0:512], C_ZH), reads=HT + WIN, writes=["a0"], banks=["a0"])
            yield O("tensor", tm(PA[:, 512:1024], C_IH), reads=HT + WIN, writes=["a1"], banks=["a1"])
            bT4 = bT[:].rearrange("p (c s t) -> p c s t", s=2, t=64)
            bTc4 = bTc[:].rearrange("p (c s t) -> p c s t", s=2, t=64)
            yield O("vector", lambda e: e.tensor_tensor(out=bTc4, in0=bT4, in1=bT4[:, :, :, 31:32].broadcast_to([128, 6, 2, 64]),
                                                        op=ALU.subtract), reads=["bT"], writes=["bTc"])
            yield O("scalar", lambda e: e.activation(out=sm[:, 0, :, :], in_=bT4[:, :, :, 31], func=AF.Exp), reads=["bT"], writes=[smE])
            yield O("scalar", lambda e: e.activation(out=sm[:, 1, :, :], in_=bT4[:, :, :, 63], func=AF.Exp), reads=["bT"], writes=[smE])
            yield O("scalar", lambda e: e.activation(out=sm[:, 2, :, :], in_=bTc4[:, :, :, 63], func=AF.Exp), reads=["bTc"], writes=[smE])
            yield O("scalar", lambda e: e.activation(out=eqb[:], in_=PC[:, :], func=AF.Exp, scale=-1.0),
                 reads=["c"], writes=["eqb"], banks=["c"])
            yield O("scalar", lambda e: e.activation(out=eqb[:], in_=eqb[:], func=AF.Ln, bias=1.0), reads=["eqb"], writes=["eqb"])
            yield O("scalar", lambda e: e.activation(out=EQa[:], in_=bTc[:, 0:256], func=AF.Exp), reads=["bTc"], writes=["EQa"])
            yield O("scalar", lambda e: e.activation(out=EKa[:], in_=bTc[:, 0:256], func=AF.Exp, scale=-1.0), reads=["bTc"], writes=["EKa"])
            yield O("scalar", lambda e: e.activation(out=ez[:, 512:1024], in_=PA[:, 0:512], func=AF.Copy), reads=["a0"], writes=[ngzh], banks=["a0"])
            yield O("vector", lambda e: e.tensor_copy(out=V[:, 512:1024], in_=PA[:, 512:1024]), reads=["a1"], writes=[nVh], banks=["a1"])
            for hh in range(2):
                yield O("vector", lambda e, hh=hh: e.scalar_tensor_tensor(
                    out=QTa2[hh * 64:(hh + 1) * 64, hh, :, :], in0=v3(PD[hh * 64:(hh + 1) * 64, 0:256]), scalar=0.125,
                    in1=v3(EQa[hh * 64:(hh + 1) * 64, :]), op0=ALU.mult, op1=ALU.mult),
                    reads=["d", "EQa"], writes=[nQTa], banks=["d"])
            yield O("vector", lambda e: e.tensor_tensor(out=KT[:, 0:2, :], in0=v3(PD[:, 256:512]), in1=v3(EKa[:]), op=ALU.mult),
                 reads=["d", "EKa"], writes=[nKTa], banks=["d"])
            yield O("vector", lambda e: e.tensor_tensor(out=L1[:], in0=L1[:], in1=bTc[:, 256:768], op=ALU.add),
                 reads=["L1", "bTc"], writes=["L1"])
            for c in range(4):
                yield O("scalar", lambda e, c=c: e.activation(
                    out=KT[:, 2 + c, :], in_=L1[:, c * 128:(c + 1) * 128], func=AF.Exp, scale=-1.0, bias=ln1mlb[:, c:c + 1]),
                    reads=["L1", "ln1mlb"], writes=[nKTh])
            yield O("vector", lambda e: e.tensor_tensor(out=eqb[:], in0=bTc[:, 256:768], in1=eqb[:], op=ALU.subtract),
                 reads=["eqb", "bTc"], writes=["eqb"])
            yield O("scalar", lambda e: e.activation(out=eqb[:], in_=eqb[:], func=AF.Exp), reads=["eqb"], writes=["eqb"])
            yield O("vector", lambda e: e.tensor_tensor(out=QT[:, :, :], in0=v3(PC[:, :]), in1=v3(eqb[:]), op=ALU.mult),
                 reads=["c", "eqb"], writes=[nQTh], banks=["c"])


        def stage34(ctx):
            i, segs, par, gg = ctx["i"], ctx["segs"], ctx["i"] % 2, ctx["gg"]
            slot = i % NXS
            xs_, xb = x_sb[slot], f"x{slot}"
            QT, QTa2, KT, V, ez, sm = QT_[par], QTa2_[par], KT_[par], V_[par], ez_[par], sm_[par]
            nQTh, nQTa, nKTa, nKTh, nVa, nVh, ngza, ngzh = (f"{n}{par}" for n in ("QTh", "QTa", "KTa", "KTh", "Va", "Vh", "gza", "gzh"))
            smE = f"smE{par}"
            PT03, PT13 = v3(PT0b), v3(PT1b)
            yield O("tensor", [lambda e, c=c: e.transpose(out=PT13[:, c, :], in_=KT[:, c, :], identity=identb[:]) for c in range(6)],
                 reads=[nKTa, nKTh, "identb"], writes=["t1"], banks=["t1"])
            for si_, sg in enumerate(segs):
                lo, n = sg["lo"], sg["n"]
                yield O("vector", lambda e, si_=si_, lo=lo, n=n: e.tensor_copy(out=Ktm2[lo:lo + n, si_, 0:6, :], in_=PT13[lo:lo + n, 0:6, :]),
                     reads=["t1"], writes=["Ktm"], banks=["t1"])
            yield O("tensor", [lambda e, h=h: e.matmul(PT0[:, h * 128:(h + 1) * 128], lhsT=KT[:, h // 2, :], rhs=QTa2[:, h % 2, h // 2, :],
                                                    start=True, stop=True) for h in range(4)],
                 reads=[nKTa, nQTa], writes=["t0"], banks=["t0"])
            yield O("vector", lambda e: e.tensor_tensor(out=AT[:, 0:4, :], in0=v3(PT0[:, 0:512]),
                                                     in1=masks[:].unsqueeze(1).broadcast_to([128, 4, 128]), op=ALU.mult),
                 reads=["t0", "masks"], writes=["ATa"], banks=["t0"])
            yield O("tensor", [lambda e, h=h: e.matmul(PT1[:, h * 128:(h + 1) * 128], lhsT=KT[:, 2 + h, :], rhs=QT[:, h, :],
                                                    start=True, stop=True) for h in range(4)],
                 reads=[nKTh, nQTh], writes=["t1"], banks=["t1"])
            yield O("vector", lambda e: e.tensor_tensor(out=AT[:, 4:8, :], in0=v3(PT1[:, 0:512]),
                                                     in1=masks[:].unsqueeze(1).broadcast_to([128, 4, 128]), op=ALU.mult),
                 reads=["t1", "masks"], writes=["ATh"], banks=["t1"])
            for si_, sg in enumerate(segs):
                lo, n, st = sg["lo"], sg["n"], sg["st"]
                yield O("vector", lambda e, si_=si_, st=st: e.tensor_tensor(
                    out=Sp[:, si_], in0=S_all[:, st], in1=sm[:, 0, :, si_].unsqueeze(2).broadcast_to([128, 6, 128]), op=ALU.mult),
                    reads=[f"S{st}", f"S{st}h", smE], writes=[f"Sp{si_}"])
                yield O("gpsimd", lambda e, si_=si_, st=st: e.tensor_tensor(
                    out=S_all[:, st], in0=S_all[:, st], in1=sm[:, 1, :, si_].unsqueeze(2).broadcast_to([128, 6, 128]), op=ALU.mult),
                    reads=[smE], writes=[f"S{st}", f"S{st}h"])
                fl = []
                for h in range(4):
                    c = h // 2
                    fl.append(lambda e, h=h, lo=lo, n=n: e.matmul(
                        PB[lo:lo + n, h * 128:(h + 1) * 128], lhsT=AT[:, h, lo:lo + n], rhs=V[:, h * 128:(h + 1) * 128],
                        start=True, stop=False))
                    fl.append(lambda e, h=h, c=c, lo=lo, n=n, si_=si_: e.matmul(
                        PB[lo:lo + n, h * 128:(h + 1) * 128], lhsT=QTa2[:, h % 2, c, lo:lo + n], rhs=Sp[:, si_, c, :],
                        start=False, stop=True))
                yield O("tensor", fl, reads=["ATa", nVa, nQTa, f"Sp{si_}"], writes=["b0"], banks=["b0"])
                fl = []
                for h in range(4):
                    fl.append(lambda e, h=h, lo=lo, n=n: e.matmul(
                        PB[lo:lo + n, (4 + h) * 128:(5 + h) * 128], lhsT=AT[:, 4 + h, lo:lo + n],
                        rhs=V[:, (4 + h) * 128:(5 + h) * 128], start=True, stop=False))
                    fl.append(lambda e, h=h, lo=lo, n=n, si_=si_: e.matmul(
                        PB[lo:lo + n, (4 + h) * 128:(5 + h) * 128], lhsT=QT[:, h, lo:lo + n], rhs=Sp[:, si_, 2 + h, :],
                        start=False, stop=True))
                yield O("tensor", fl, reads=["ATh", nVh, nQTh, f"Sp{si_}"], writes=["b1"], banks=["b1"])
                fl = []
                for h in range(4):
                    c, r0 = h // 2, (h % 2) * 64
                    fl.append(lambda e, h=h, c=c, r0=r0, si_=si_: e.matmul(
                        PT0[r0:r0 + 64, c * 128:(c + 1) * 128], lhsT=Ktm2[:, si_, c, r0:r0 + 64], rhs=V[:, h * 128:(h + 1) * 128],
                        start=True, stop=True))
                yield O("tensor", fl, reads=["Ktm", nVa], writes=["t0"], banks=["t0"])
                fl = []
                for h in range(4):
                    fl.append(lambda e, h=h, si_=si_: e.matmul(
                        PT1[:, h * 128:(h + 1) * 128], lhsT=Ktm2[:, si_, 2 + h, :], rhs=V[:, (4 + h) * 128:(5 + h) * 128],
                        start=True, stop=True))
                yield O("tensor", fl, reads=["Ktm", nVh], writes=["t1"], banks=["t1"])
                for c in range(2):
                    yield O("vector", lambda e, c=c, si_=si_, st=st: e.scalar_tensor_tensor(
                        out=S_all[:, st, c, :], in0=PT0[:, c * 128:(c + 1) * 128], scalar=sm[:, 2, c, si_:si_ + 1], in1=S_all[:, st, c, :],
                        op0=ALU.mult, op1=ALU.add),
                        reads=["t0", smE], writes=[f"S{st}"], banks=["t0"])
                for h in range(4):
                    yield O("vector", lambda e, h=h, si_=si_, st=st: e.scalar_tensor_tensor(
                        out=S_all[:, st, 2 + h, :], in0=PT1[:, h * 128:(h + 1) * 128], scalar=sm[:, 2, 2 + h, si_:si_ + 1],
                        in1=S_all[:, st, 2 + h, :], op0=ALU.mult, op1=ALU.add),
                        reads=["t1", smE], writes=[f"S{st}h"], banks=["t1"])
            yield O("scalar", lambda e: e.activation(out=ztmp[:], in_=ez[:], func=AF.Exp, scale=-1.0), reads=[ngza, ngzh], writes=["ztmp"])
            yield O("scalar", lambda e: e.activation(out=ztmp[:], in_=ztmp[:], func=AF.Ln, bias=1.0), reads=["ztmp"], writes=["ztmp"])
            yield O("scalar", lambda e: e.activation(out=ztmp[:], in_=ztmp[:], func=AF.Exp, scale=-1.0), reads=["ztmp"], writes=["ztmp"])
            yield O("vector", lambda e: e.tensor_tensor(out=ez[:], in0=ez[:], in1=ztmp[:], op=ALU.mult), reads=["ztmp"], writes=[ngza, ngzh])
            yield O("scalar", lambda e: e.activation(out=sq[:, 0:512], in_=PB[:, 0:512], func=AF.Square),
                 reads=["b0"], writes=["sqa"], banks=["b0"])
            yield O("scalar", lambda e: e.activation(out=sq[:, 512:1024], in_=PB[:, 512:1024], func=AF.Square),
                 reads=["b1"], writes=["sqh"], banks=["b1"])
            yield O("vector", lambda e: e.reduce_sum(out=so[:, 0:8], in_=v3(sq[:]), axis=AX.X), reads=["sqa", "sqh"], writes=["so0"])
            yield O("scalar", lambda e: e.activation(out=so[:, 8:16], in_=so[:, 0:8], func=AF.Ln, scale=1.0 / 128, bias=EPS),
                 reads=["so0"], writes=["so1"])
            yield O("scalar", lambda e: e.activation(out=so[:, 16:24], in_=so[:, 8:16], func=AF.Exp, scale=-0.5),
                 reads=["so1"], writes=["so2"])
            yield O("gpsimd", lambda e: e.tensor_tensor(out=v3(ez[:]), in0=v3(ez[:]),
                                                      in1=so[:, 16:24].unsqueeze(2).broadcast_to([128, 8, 128]), op=ALU.mult),
                 reads=["so2"], writes=[ngza, ngzh])
            yield O("vector", lambda e: e.tensor_tensor(out=ohat[:, 0:512], in0=PB[:, 0:512], in1=ez[:, 0:512], op=ALU.mult),
                 reads=["b0", ngza], writes=["ohata"], banks=["b0"])
            yield O("vector", lambda e: e.tensor_tensor(out=ohat[:, 512:1024], in0=PB[:, 512:1024], in1=ez[:, 512:1024], op=ALU.mult),
                 reads=["b1", ngzh], writes=["ohath"], banks=["b1"])
            yield O("tensor", [lambda e, j=j: e.transpose(out=PT03[:, j, :], in_=ohat[:, j * 128:(j + 1) * 128], identity=identb[:])
                            for j in range(4)], reads=["ohata", "identb"], writes=["t0"], banks=["t0"])
            yield O("scalar", lambda e: e.activation(out=ohT[:, 0:4, :], in_=PT03[:, 0:4, :], func=AF.Copy),
                 reads=["t0"], writes=["ohTa"], banks=["t0"])
            yield O("tensor", [lambda e, j=j: e.transpose(out=PT13[:, j, :], in_=ohat[:, (4 + j) * 128:(5 + j) * 128], identity=identb[:])
                            for j in range(4)], reads=["ohath", "identb"], writes=["t1"], banks=["t1"])
            yield O("vector", lambda e: e.tensor_copy(out=ohT[:, 4:8, :], in_=PT13[:, 0:4, :]), reads=["t1"], writes=["ohTh"], banks=["t1"])
            for n_ in range(2):
                yield O("tensor", [lambda e, j=j, n_=n_: e.matmul(PB[:, n_ * 512:(n_ + 1) * 512], lhsT=ohT[:, j, :],
                                                                 rhs=w_out_bf[:, j, n_ * 512:(n_ + 1) * 512], start=(j == 0), stop=(j == 7))
                                for j in range(8)], reads=["ohTa", "ohTh"] + WOUT, writes=[f"b{n_}"], banks=[f"b{n_}"])
            yield O("scalar", lambda e: e.activation(out=ohat[:], in_=PB[:, :], func=AF.Square, accum_out=statB[:, 0:1]),
                 reads=["b0", "b1"], writes=["ohata", "ohath", "sb0"], banks=["b0", "b1"])
            yield O("scalar", lambda e: e.activation(out=statB[:, 1:2], in_=statB[:, 0:1], func=AF.Ln, scale=1.0 / D, bias=EPS),
                 reads=["sb0"], writes=["sb1"])
            yield O("scalar", lambda e: e.activation(out=statB[:, 2:3], in_=statB[:, 1:2], func=AF.Exp, scale=-0.5),
                 reads=["sb1"], writes=["sb2"])
            yield O("vector", lambda e: e.scalar_tensor_tensor(out=sq[:], in0=PB[:, :], scalar=statB[:, 2:3], in1=GG[gg][:],
                                                            op0=ALU.mult, op1=ALU.mult),
                 reads=["b0", "b1", "sb2", f"GG{gg}"], writes=["sqa", "sqh"], banks=["b0", "b1"])
            yield O("gpsimd", lambda e: e.tensor_tensor(out=xs_[:], in0=xs_[:], in1=sq[:], op=ALU.add), reads=[xb, "sqa", "sqh"], writes=[xb])
            yield O("sync", lambda e: e.dma_start(out=ctx["ydst"], in_=xs_[:]), reads=[xb], dma_sem=f"yst{slot}")
            k = ctx["k"]
            if k is not None:
                for st, gdst, hdst in ((2 + 2 * k, sgs[2 * k], shs[2 * k]), (3 + 2 * k, sgs[2 * k + 1], shs[2 * k + 1])):
                    yield O("sync", lambda e, st=st, gdst=gdst: e.dma_start(out=gdst.rearrange("c p v -> p c v"), in_=S_all[:, st, 0:2, :]),
                            reads=[f"S{st}"], dma_sem="sout")
                    yield O("sync", lambda e, st=st, hdst=hdst: e.dma_start(out=hdst.rearrange("c p v -> p c v"), in_=S_all[:, st, 2:6, :]),
                            reads=[f"S{st}h"], dma_sem="sout")

        def store_state(st, gdst, hdst):
            P.op("sync", lambda e: e.dma_start(out=gdst.rearrange("c p v -> p c v"), in_=S_all[:, st, 0:2, :]),
                 reads=[f"S{st}"], dma_sem="sout")
            P.op("sync", lambda e: e.dma_start(out=hdst.rearrange("c p v -> p c v"), in_=S_all[:, st, 2:6, :]),
                 reads=[f"S{st}h"], dma_sem="sout")

        tiles = []
        for k in range(2):
            segs = [dict(lo=0, n=64, b=2 + 2 * k, st=2 + 2 * k), dict(lo=64, n=64, b=3 + 2 * k, st=3 + 2 * k)]
            tiles.append(dict(xsrc=xs[2 * k:2 * k + 2].rearrange("b t d -> (b t) d"), ydst=ys[2 * k:2 * k + 2].rearrange("b t d -> (b t) d"),
                              segs=segs, gg=2 + k, k=k))
        for t in range(tp_tiles):
            for s in range(2):
                tiles.append(dict(xsrc=xp[s, t * 128:(t + 1) * 128, :], ydst=yp[s, t * 128:(t + 1) * 128, :],
                                  segs=[dict(lo=0, n=64, b=s, st=s), dict(lo=64, n=64, b=s, st=s)], gg=s, k=None))
        for i, t in enumerate(tiles):
            t["i"] = i
        NT = len(tiles)
        PRE = 2
        import os as _os2
        _os_kverb = bool(_os2.environ.get("KVERB2"))
        ALPHA = float(_os2.environ.get("KALPHA", "0.0"))
        for i in range(min(PRE, NT)):
            load_x(i, tiles[i]["xsrc"])
        for r in range(NT + 1):
            if r + PRE < NT:
                load_x(r + PRE, tiles[r + PRE]["xsrc"])
            if _os_kverb:
                print("round", r, "model t_us", {k: round(v / 1e3, 1) for k, v in P.eng_free.items()})
            gens = []
            if r >= 1:
                gens.append(stage34(tiles[r - 1]))
            if r < NT:
                gens.append(stage12(tiles[r]))
            streams = []
            for g in gens:
                ops = list(g)
                tails = [0.0] * (len(ops) + 1)
                for k in range(len(ops) - 1, -1, -1):
                    f = ops[k][1]
                    tails[k] = tails[k + 1] + _est_ns(ops[k][0], [f] if callable(f) else f)
                streams.append([ops, tails, 0])
            while streams:
                best, bk = None, None
                for st_ in streams:
                    ops, tails, k = st_
                    key = P.est_start(ops[k]) - ALPHA * tails[k]
                    if bk is None or key < bk:
                        best, bk = st_, key
                ops, tails, k = best
                eng, fns, reads, writes, banks, dma_sem = ops[k]
                P.op(eng, fns, reads=reads, writes=writes, banks=banks, dma_sem=dma_sem)
                best[2] = k + 1
                if best[2] >= len(ops):
                    streams.remove(best)
        for s in range(2):
            store_state(s, sgp[s], shp[s])
        for nm, s in list(P.sems.items()):
            if nm.startswith("yst") or nm == "sout":
                P.wait_token("sync", (nm, s[1]))
        with nc.Block() as block:
            P.replay(block)
        P.close()
        import os as _os
        if _os.environ.get("KVERB"):
            print("total ops recorded", P.count, "model makespan us", max(P.eng_free.values()) / 1e3)
    return nc


def host_inputs(core, x_prompt, x_sample, c_prompt, c_sample, state_gla, state_hgrn, w_ada, b_ada, g_pre,
                w_in, w_alpha, b_alpha, g_onorm_gla, hgrn_lb_logits, g_onorm_hgrn, w_out, g_post, consts):
    f = np.float32
    c6 = np.concatenate([c_prompt[2 * core:2 * core + 2], c_sample[4 * core:4 * core + 4]], 0)

    def pj(a):
        return np.ascontiguousarray(a.reshape(8, 128, a.shape[1]).transpose(1, 0, 2))

    m = {
        "xp": np.ascontiguousarray(x_prompt[2 * core:2 * core + 2]),
        "xs": np.ascontiguousarray(x_sample[4 * core:4 * core + 4]),
        "cT": np.ascontiguousarray(c6.T.reshape(8, 128, 6).transpose(1, 0, 2)),
        "stg": np.ascontiguousarray(state_gla[0, 4 * core:4 * core + 4].reshape(4, 2, 128, 128)),
        "sth": np.ascontiguousarray(state_hgrn[0, 4 * core:4 * core + 4]),
        "wada": pj(w_ada[0]),
        "bada": np.ascontiguousarray(np.broadcast_to(b_ada[0][None, :], (6, 3072))),
        "gpre": np.ascontiguousarray(g_pre[0].reshape(8, 128).T),
        "win": pj(w_in[0]),
        "walpha": np.ascontiguousarray(w_alpha[0]),
        "balpha": np.ascontiguousarray(b_alpha[0].reshape(2, 128).T),
        "gon": np.ascontiguousarray(np.stack([g_onorm_gla[0], g_onorm_hgrn[0]], 1)),
        "lbl": np.ascontiguousarray(hgrn_lb_logits.reshape(2, 4, 128).transpose(2, 0, 1)),
        "wout": pj(w_out[0]),
        "gpost": np.ascontiguousarray(np.broadcast_to(g_post[0][None, :], (128, 1024))),
    }
    m.update(consts)
    return {k: np.ascontiguousarray(v, dtype=f) for k, v in m.items()}


def make_consts():
    f = np.float32
    maskp = np.triu(np.ones((128, 128), f))
    masks = maskp.copy()
    masks[0:64, 64:128] = 0.0
    smaskp = np.ones((128, 768), f)
    smaskp[:, 0::128] = 0.0
    smasks = smaskp.copy()
    smasks[:, 64::128] = 0.0
    sel = np.zeros((6, 4, 128), f)
    sel[0, 0, :] = 1.0
    sel[1, 1, :] = 1.0
    sel[2, 2, 0:64] = 1.0
    sel[3, 2, 64:128] = 1.0
    sel[4, 3, 0:64] = 1.0
    sel[5, 3, 64:128] = 1.0
    return {"identf": np.eye(128, dtype=f), "maskp": maskp, "masks": masks, "smaskp": smaskp, "smasks": smasks, "sel": sel}


def assemble(results, TP):
    f = np.float32
    yp = np.concatenate([r["yp"] for r in results], 0).astype(f)
    ys = np.concatenate([r["ys"] for r in results], 0).astype(f)
    sgp = np.concatenate([r["sgp"].reshape(2, 4, 64, 128) for r in results], 0)[None].astype(f)
    shp = np.concatenate([r["shp"] for r in results], 0)[None].astype(f)
    sgs = np.concatenate([r["sgs"].reshape(4, 4, 64, 128) for r in results], 0)[None].astype(f)
    shs = np.concatenate([r["shs"] for r in results], 0)[None].astype(f)
    return (yp, ys, sgp, shp, sgs, shs)


def kernel(**inputs):
    inputs = {k: np.asarray(v) for k, v in inputs.items()}
    TP = inputs["x_prompt"].shape[1]
    nc = build(TP // 128)
    consts = make_consts()
    in_maps = [host_inputs(i, consts=consts, **inputs) for i in range(N_CORES)]
    res = run_bass_kernel_spmd(nc, in_maps, core_ids=list(range(N_CORES)))
    return assemble(res.results, TP)
```

```python
from contextlib import ExitStack

import numpy as np
import concourse.bass as bass
import concourse.mybir as mybir
from concourse.bass_utils import run_bass_kernel_spmd

F32 = mybir.dt.float32
BF16 = mybir.dt.bfloat16
AF = mybir.ActivationFunctionType
ALU = mybir.AluOpType
AX = mybir.AxisListType

D = 1024
NCOL = 3600
EPS = 1e-6
N_CORES = 8
SEQ = 4096
ENGS = ("sync", "scalar", "vector", "gpsimd", "tensor")

C_QA, C_KA, C_VA, C_ZA, C_AL, C_QH, C_FH, C_IH, C_ZH = 0, 256, 512, 1024, 1536, 1552, 2064, 2576, 3088


class _Probe:
    def __init__(self):
        self.calls = []

    def __getattr__(self, name):
        def f(*a, **k):
            out = k.get("out", a[0] if a else None)
            self.calls.append((name, out, k))
            return None
        return f


def _ap_n(ap):
    try:
        n = 1
        for d in ap.shape[1:]:
            n *= int(d)
        return n
    except Exception:
        return 0


def _est_ns(eng, fns):
    pr = _Probe()
    for f in fns:
        try:
            f(pr)
        except Exception:
            pass
    tot = 0.0
    for name, out, k in pr.calls:
        n = max([_ap_n(out)] + [_ap_n(k.get(kk)) for kk in ("in_", "in0", "data0", "rhs")] + [1])
        if eng == "tensor":
            tot += 95.0 if name == "transpose" else 15.0 + 0.43 * n
        elif eng == "scalar":
            tot += (480.0 if n <= 16 else 200.0 + 0.75 * n) + (100.0 if k.get("accum_out") is not None else 0.0)
        elif eng == "vector":
            tot += (60.0 + 2.1 * n) if name == "tensor_tensor_scan" else 150.0 + 1.04 * n
        elif eng == "gpsimd":
            tot += (100.0 + 1.0 * n) if name == "tensor_scalar" else 100.0 + 2.0 * n
        else:
            tot += 2000.0
    return max(tot, 50.0)


class Prog:
    def __init__(self, nc):
        self.nc = nc
        self.q = {e: [] for e in ENGS}
        self.sems = {}
        self.waited = {e: {} for e in ENGS}
        self.bufs = {}
        self.bank_last = {}
        self._cms = []
        self.eng_free = {e: 0.0 for e in ENGS}
        self.tok_time = {}
        import os as _os
        self.limit = int(_os.environ.get("KLIMIT", "0")) or None
        self.count = 0

    def sem(self, name):
        if name not in self.sems:
            cm = self.nc.semaphore(name)
            h = cm.__enter__()
            self._cms.append(cm)
            self.sems[name] = [h, 0]
        return self.sems[name]

    def close(self):
        for cm in reversed(self._cms):
            cm.__exit__(None, None, None)

    def _deps(self, eng, reads, writes, is_dma, banks):
        deps = []
        for b in banks:
            t = self.bank_last.get(b)
            if t is not None and t[2] != eng:
                deps.append((t, "bank"))
        for b in reads:
            st = self.bufs.get(b)
            if st and st[0] is not None:
                deps.append((st[0], "raw"))
        for b in writes:
            st = self.bufs.get(b)
            if st:
                if st[0] is not None:
                    deps.append((st[0], "waw"))
                for t in st[1]:
                    deps.append((t, "war"))
        waits = []
        for tok, kind in deps:
            sname, val, teng, tdma = tok
            if not tdma and teng == eng and not is_dma:
                if eng == "tensor" or kind in ("war", "waw"):
                    continue
            if self.waited[eng].get(sname, 0) >= val:
                continue
            self.waited[eng][sname] = val
            waits.append((sname, val))
        return waits

    def _dep_tokens(self, eng, reads, writes, banks):
        toks = []
        for b in banks:
            t = self.bank_last.get(b)
            if t is not None:
                toks.append(t)
        for b in reads:
            st = self.bufs.get(b)
            if st and st[0] is not None:
                toks.append(st[0])
        for b in writes:
            st = self.bufs.get(b)
            if st:
                if st[0] is not None:
                    toks.append(st[0])
                toks.extend(st[1])
        return toks

    def _ready(self, eng, toks):
        t = self.eng_free[eng]
        for tok in toks:
            tt = self.tok_time.get((tok[0], tok[1]), 0.0) + (0.0 if tok[2] == eng else 150.0)
            if tt > t:
                t = tt
        return t

    def est_start(self, desc):
        eng, fns, reads, writes, banks, dma_sem = desc
        return self._ready(eng, self._dep_tokens(eng, reads, writes, banks))

    def op(self, eng, fns, reads=(), writes=(), dma_sem=None, banks=()):
        if callable(fns):
            fns = [fns]
        _t0 = self._ready(eng, self._dep_tokens(eng, reads, writes, banks))
        _dur = _est_ns(eng, fns)
        self.count += 1
        if self.limit is not None and self.count > self.limit:
            return None
        is_dma = dma_sem is not None
        waits = self._deps(eng, reads, writes, is_dma, banks)
        if is_dma:
            s = self.sem(dma_sem)
            s[1] += 16
            tok = (dma_sem, s[1], eng, True)
            inc = (dma_sem, 16)
        else:
            sname = "p_" + eng
            s = self.sem(sname)
            s[1] += 1
            tok = (sname, s[1], eng, False)
            inc = (sname, 1)
        self.q[eng].append((waits, fns, inc))
        if is_dma:
            self.eng_free[eng] = _t0 + 60.0
        else:
            self.eng_free[eng] = _t0 + _dur
        self.tok_time[(tok[0], tok[1])] = _t0 + _dur
        for b in banks:
            self.bank_last[b] = tok
        for b in writes:
            self.bufs[b] = [tok, []]
        for b in reads:
            if b in writes:
                continue
            self.bufs.setdefault(b, [None, []])[1].append(tok)
        return tok

    def retoken(self, names, tok):
        for b in names:
            self.bufs[b] = [tok, []]

    def wait_token(self, eng, tok):
        sname, val = tok[0], tok[1]
        if self.waited[eng].get(sname, 0) >= val:
            return
        self.waited[eng][sname] = val
        self.q[eng].append(([(sname, val)], [], None))

    def barrier(self):
        snap = [(n, s[1]) for n, s in self.sems.items() if s[1] > 0]
        for e in ENGS:
            w = []
            for n, v in snap:
                if self.waited[e].get(n, 0) < v:
                    self.waited[e][n] = v
                    w.append((n, v))
            if w:
                self.q[e].append((w, [], None))

    def replay(self, block):
        P = self

        def run(engobj, name):
            for waits, fns, inc in P.q[name]:
                for sname, val in waits:
                    engobj.wait_ge(P.sems[sname][0], val)
                ins = None
                for f in fns:
                    ins = f(engobj)
                if inc is not None and ins is not None:
                    ins.then_inc(P.sems[inc[0]][0], inc[1])

        @block.sync
        def _(e):
            run(e, "sync")

        @block.scalar
        def _(e):
            run(e, "scalar")

        @block.vector
        def _(e):
            run(e, "vector")

        @block.gpsimd
        def _(e):
            run(e, "gpsimd")

        @block.tensor
        def _(e):
            run(e, "tensor")


def v3(ap, t=128):
    return ap.rearrange("p (c t) -> p c t", t=t)


def build(tp_tiles):
    TP = tp_tiles * 128
    nc = bass.Bass("TRN2", target_bir_lowering=False)

    def din(name, shape):
        return nc.dram_tensor(name, shape, F32, kind="ExternalInput").ap()

    def dout(name, shape):
        return nc.dram_tensor(name, shape, F32, kind="ExternalOutput").ap()

    xp = din("xp", [2, TP, D])
    xs = din("xs", [4, 64, D])
    cT_d = din("cT", [128, 8, 6])
    stg = din("stg", [4, 2, 128, 128])
    sth = din("sth", [4, 4, 128, 128])
    wada = din("wada", [128, 8, 3072])
    bada = din("bada", [6, 3072])
    gpre_d = din("gpre", [128, 8])
    win = din("win", [128, 8, NCOL])
    walpha_d = din("walpha", [16, 256])
    balpha_d = din("balpha", [128, 2])
    gon_d = din("gon", [128, 2])
    lbl_d = din("lbl", [128, 2, 4])
    wout = din("wout", [128, 8, D])
    gpost_d = din("gpost", [128, D])
    identf_d = din("identf", [128, 128])
    maskp_d = din("maskp", [128, 128])
    masks_d = din("masks", [128, 128])
    smaskp_d = din("smaskp", [128, 768])
    smasks_d = din("smasks", [128, 768])
    sel_d = din("sel", [6, 4, 128])

    yp = dout("yp", [2, TP, D])
    ys = dout("ys", [4, 64, D])
    sgp = dout("sgp", [2, 2, 128, 128])
    shp = dout("shp", [2, 4, 128, 128])
    sgs = dout("sgs", [4, 2, 128, 128])
    shs = dout("shs", [4, 4, 128, 128])

    es = ExitStack()
    with es:
        def sb(name, shape, dt=F32):
            return es.enter_context(nc.sbuf_tensor(name, shape, dt))

        def ps(name, shape, dt=F32):
            return es.enter_context(nc.psum_tensor(name, shape, dt))

        P = Prog(nc)

        w_in_bf = sb("w_in_bf", [128, 8, NCOL], BF16)
        w_out_bf = sb("w_out_bf", [128, 8, D], BF16)
        walpha_bf = sb("walpha_bf", [128, 256], BF16)
        identf = sb("identf_sb", [128, 128])
        identb = sb("identb", [128, 128], BF16)
        maskp = sb("maskp_sb", [128, 128])
        masks = sb("masks_sb", [128, 128])
        smaskp = sb("smaskp_sb", [128, 768])
        smasks = sb("smasks_sb", [128, 768])
        cst = sb("cst", [128, 32])
        nbalpha = cst[:, 0:2]
        lb = cst[:, 2:6]
        ln1mlb = cst[:, 6:10]
        balpha = cst[:, 10:12]
        gon = cst[:, 12:14]
        tmp4 = cst[:, 14:18]
        tmp4b = cst[:, 18:22]
        lbl = sb("lbl_sb", [128, 2, 4])
        gpre = sb("gpre_sb", [128, 8])
        aT = sb("aT", [128, 8, 6])
        sT = sb("sT", [128, 8, 6])
        GG = [sb(f"GG{g}", [128, D]) for g in range(4)]
        S_all = sb("S_all", [128, 6, 6, 128])

        PA = ps("PA", [128, 1024])
        PB = ps("PB", [128, 1024])
        PC = ps("PC", [128, 512])
        PD = ps("PD", [128, 512])
        PT0 = ps("PT0", [128, 512])
        PT1 = ps("PT1", [128, 512])

        ses = ExitStack()
        with ses:
            def ssb(name, shape, dt=F32):
                return ses.enter_context(nc.sbuf_tensor(name, shape, dt))
            NSTG = 4
            stage = [ssb(f"stage{i}", [128, 2048]) for i in range(NSTG)]
            mod_sb = ssb("mod_sb", [6, 3072])
            bada_sb = ssb("bada_sb", [6, 3072])
            gpost = ssb("gpost_sb", [128, D])
            cT = ssb("cT_sb", [128, 8, 6])
            sel = ssb("sel_sb", [6, 4, 128])
            tmp48 = ssb("tmp48", [128, 8, 6])

            small = [
                (cT[:], cT_d, "cT"), (bada_sb[:], bada, "bada"), (gpre[:], gpre_d, "gpre"),
                (balpha, balpha_d, "balpha"), (gon, gon_d, "gon"), (lbl[:], lbl_d, "lbl"),
                (identf[:], identf_d, "identf"), (maskp[:], maskp_d, "maskp"), (masks[:], masks_d, "masks"),
                (smaskp[:], smaskp_d, "smaskp"), (smasks[:], smasks_d, "smasks"), (sel[:], sel_d, "sel"),
                (gpost[:], gpost_d, "gpost"),
            ]
            names = []
            tok = None
            for o_, i_, nm in small:
                tok = P.op("sync", lambda e, o_=o_, i_=i_: e.dma_start(out=o_, in_=i_), writes=[nm], dma_sem="ld_s")
                names.append(nm)
            walpha_f = ssb("walpha_f", [16, 256])
            stage_wa = walpha_f[:]
            tok = P.op("sync", lambda e: e.dma_start(out=stage_wa, in_=walpha_d), writes=["walpha_f"], dma_sem="ld_s")
            names.append("walpha_f")
            for b in range(4):
                tok = P.op("sync", lambda e, b=b: e.dma_start(out=S_all[:, 2 + b, 0:2, :], in_=stg[b].rearrange("c p v -> p c v")),
                           writes=[f"S{2 + b}"], dma_sem="ld_s")
                tok = P.op("sync", lambda e, b=b: e.dma_start(out=S_all[:, 2 + b, 2:6, :], in_=sth[b].rearrange("c p v -> p c v")),
                           writes=[f"S{2 + b}h"], dma_sem="ld_s")
            P.retoken(names + [f"S{2 + b}" for b in range(4)] + [f"S{2 + b}h" for b in range(4)], tok)

            P.op("vector", lambda e: e.tensor_copy(out=identb[:], in_=identf[:]), reads=["identf"], writes=["identb"])
            P.op("gpsimd", lambda e: e.memset(walpha_bf[:], 0.0), writes=["walpha_bf"])
            P.op("vector", lambda e: e.tensor_copy(out=walpha_bf[0:16, :], in_=stage_wa), reads=["walpha_f", "walpha_bf"], writes=["walpha_bf"])
            P.op("vector", lambda e: e.tensor_scalar(out=nbalpha, in0=balpha, scalar1=-1.0, scalar2=None, op0=ALU.mult),
                 reads=["balpha"], writes=["nbalpha"])
            P.op("vector", lambda e: e.tensor_tensor(out=tmp4, in0=lbl[:, 1, :], in1=lbl[:, 0, :], op=ALU.subtract),
                 reads=["lbl"], writes=["tmp4"])
            P.op("scalar", lambda e: e.activation(out=tmp4, in_=tmp4, func=AF.Exp), reads=["tmp4"], writes=["tmp4"])
            P.op("vector", lambda e: e.tensor_scalar(out=tmp4b, in0=tmp4, scalar1=1.0, scalar2=None, op0=ALU.add),
                 reads=["tmp4"], writes=["tmp4b"])
            P.op("vector", lambda e: e.reciprocal(out=lb, in_=tmp4b), reads=["tmp4b"], writes=["lb"])
            P.op("vector", lambda e: e.tensor_tensor(out=tmp4b, in0=tmp4, in1=lb, op=ALU.mult), reads=["tmp4", "lb"], writes=["tmp4b"])
            P.op("scalar", lambda e: e.activation(out=ln1mlb, in_=tmp4b, func=AF.Ln), reads=["tmp4b"], writes=["ln1mlb"])
            P.op("gpsimd", lambda e: e.memset(S_all[:, 0:2, :, :], 0.0), writes=["S0", "S0h", "S1", "S1h"])

            si = 0
            for n in range(12):
                stg_ = stage[si % NSTG]
                sname = f"stage{si % NSTG}"
                st3 = stg_[:, 0:2048].rearrange("p (j n) -> p j n", n=256)
                P.op("sync", lambda e, st3=st3, n=n: e.dma_start(out=st3, in_=wada[:, :, n * 256:(n + 1) * 256]),
                     writes=[sname], dma_sem="ld_" + sname)
                P.op("tensor", [lambda e, j=j, st3=st3: e.matmul(PC[0:6, 0:256], lhsT=cT[:, j, :], rhs=st3[:, j, :],
                                                                  start=(j == 0), stop=(j == 7)) for j in range(8)],
                     reads=[sname, "cT"], writes=["pc"], banks=["c"])
                P.op("vector", lambda e, n=n: e.tensor_tensor(out=mod_sb[0:6, n * 256:(n + 1) * 256], in0=PC[0:6, 0:256],
                                                             in1=bada_sb[0:6, n * 256:(n + 1) * 256], op=ALU.add),
                     reads=["pc", "bada"], writes=["mod"], banks=["c"])
                si += 1
            P.op("tensor", [lambda e, k=k: e.transpose(out=PD[:, k * 6:(k + 1) * 6], in_=mod_sb[0:6, k * 128:(k + 1) * 128],
                                                       identity=identf[0:6, 0:6]) for k in range(16)],
                 reads=["mod", "identf"], writes=["pd"], banks=["d"])
            P.op("vector", lambda e: e.tensor_copy(out=sT[:], in_=PD[:, 0:48].rearrange("p (j b) -> p j b", b=6)),
                 reads=["pd"], writes=["sT"], banks=["d"])
            P.op("vector", lambda e: e.tensor_scalar(out=tmp48[:], in0=PD[:, 48:96].rearrange("p (j b) -> p j b", b=6),
                                                     scalar1=1.0, scalar2=None, op0=ALU.add),
                 reads=["pd"], writes=["tmp48"], banks=["d"])
            P.op("vector", lambda e: e.tensor_tensor(out=aT[:], in0=tmp48[:], in1=gpre[:].unsqueeze(2).broadcast_to([128, 8, 6]),
                                                     op=ALU.mult), reads=["tmp48", "gpre"], writes=["aT"])
            for g in range(4):
                for n in range(2):
                    P.op("tensor", lambda e, g=g, n=n: e.matmul(PC[:, 0:512], lhsT=sel[0:6, g, :],
                                                                  rhs=mod_sb[0:6, 2048 + n * 512:2048 + (n + 1) * 512],
                                                                  start=True, stop=True),
                         reads=["mod", "sel"], writes=["pc"], banks=["c"])
                    P.op("vector", lambda e, g=g, n=n: e.tensor_tensor(out=GG[g][:, n * 512:(n + 1) * 512], in0=PC[:, 0:512],
                                                                      in1=gpost[:, n * 512:(n + 1) * 512], op=ALU.mult),
                         reads=["pc", "gpost"], writes=[f"GG{g}"], banks=["c"])
            for j in range(8):
                for hlf in range(2):
                    stg_ = stage[si % NSTG]
                    sname = f"stage{si % NSTG}"
                    c0 = hlf * 1800
                    P.op("sync", lambda e, stg_=stg_, j=j, c0=c0: e.dma_start(out=stg_[:, 0:1800], in_=win[:, j, c0:c0 + 1800]),
                         writes=[sname], dma_sem="ld_" + sname)
                    eng = "vector" if hlf == 0 else "scalar"
                    if eng == "vector":
                        P.op("vector", lambda e, stg_=stg_, j=j, c0=c0: e.tensor_copy(out=w_in_bf[:, j, c0:c0 + 1800], in_=stg_[:, 0:1800]),
                             reads=[sname], writes=[f"win{j}_{hlf}"])
                    else:
                        P.op("scalar", lambda e, stg_=stg_, j=j, c0=c0: e.activation(out=w_in_bf[:, j, c0:c0 + 1800], in_=stg_[:, 0:1800],
                                                                                      func=AF.Copy),
                             reads=[sname], writes=[f"win{j}_{hlf}"])
                    si += 1
            for jj in range(4):
                stg_ = stage[si % NSTG]
                sname = f"stage{si % NSTG}"
                st3 = stg_[:, 0:2048].rearrange("p (j n) -> p j n", n=1024)
                P.op("sync", lambda e, st3=st3, jj=jj: e.dma_start(out=st3, in_=wout[:, 2 * jj:2 * jj + 2, :]),
                     writes=[sname], dma_sem="ld_" + sname)
                for jl in range(2):
                    j = 2 * jj + jl
                    gcol = gon[:, 0:1] if j < 4 else gon[:, 1:2]
                    P.op("gpsimd", lambda e, st3=st3, jl=jl, j=j, gcol=gcol: e.tensor_scalar(
                        out=w_out_bf[:, j, :], in0=st3[:, jl, :], scalar1=gcol, scalar2=1.0, op0=ALU.mult, op1=ALU.mult),
                        reads=[sname, "gon"], writes=[f"wout{j}"])
                si += 1
            P.barrier()
        WIN = [f"win{j}_{h}" for j in range(8) for h in range(2)]
        WOUT = [f"wout{j}" for j in range(8)]

        NXS = 4
        x_sb = [sb(f"x_sb{i}", [128, D]) for i in range(NXS)]
        statA = sb("statA", [128, 8])
        xn = sb("xn", [128, D], BF16)
        hT_ = [sb(f"hT{p}", [128, 8, 128], BF16) for p in range(2)]
        alr = sb("alr", [128, 128], BF16)
        e1 = sb("e1", [128, 256])
        eh = sb("eh", [128, 512])
        L1 = sb("L1", [128, 512])
        L2 = sb("L2", [128, 512])
        gT = sb("gT", [128, 768])
        bT = sb("bT", [128, 768])
        bTc = sb("bTc", [128, 768])
        eqb = sb("eqb", [128, 512])
        qh_sb = sb("qh_sb", [128, 512])
        qk_sb = sb("qk_sb", [128, 512])
        EQa = sb("EQa", [128, 256])
        EKa = sb("EKa", [128, 256])
        QT_ = [sb(f"QT{p}", [128, 4, 128], BF16) for p in range(2)]
        QTa2_ = [sb(f"QTa2{p}", [128, 2, 2, 128], BF16) for p in range(2)]
        KT_ = [sb(f"KT{p}", [128, 6, 128], BF16) for p in range(2)]
        V_ = [sb(f"V{p}", [128, D], BF16) for p in range(2)]
        ez_ = [sb(f"ez{p}", [128, D]) for p in range(2)]
        sm_ = [sb(f"sm{p}", [128, 3, 6, 2]) for p in range(2)]
        Ktm2 = sb("Ktm2", [128, 2, 6, 128], BF16)
        Sp = sb("Sp", [128, 2, 6, 128], BF16)
        AT = sb("AT", [128, 8, 128], BF16)
        sq = sb("sq", [128, D])
        so = sb("so", [128, 24])
        statB = sb("statB", [128, 8])
        ohat = sb("ohat", [128, D], BF16)
        ohT = sb("ohT", [128, 8, 128], BF16)
        ztmp = sb("ztmp", [128, D])

        bT3 = v3(bT[:])
        PT0b = PT0[:].bitcast(BF16)
        PT1b = PT1[:].bitcast(BF16)
        PCb = PC[:].bitcast(BF16)
        PDb = PD[:].bitcast(BF16)
        P.op("gpsimd", lambda e: e.memset(Ktm2[:], 0.0), writes=["Ktma", "Ktmh"])
        for p in range(2):
            P.op("gpsimd", lambda e, p=p: e.memset(QTa2_[p][:], 0.0), writes=[f"QTa{p}"])

        def O(eng, fns, reads=(), writes=(), banks=(), dma_sem=None):
            return (eng, fns, reads, writes, banks, dma_sem)

        def load_x(i, xsrc):
            slot = i % NXS
            P.op("sync", lambda e: e.dma_start(out=x_sb[slot][:], in_=xsrc), writes=[f"x{slot}"], dma_sem=f"xld{slot}")

        def stage1(ctx):
            i, segs, par = ctx["i"], ctx["segs"], ctx["i"] % 2
            slot = i % NXS
            xs_, xb = x_sb[slot], f"x{slot}"
            hT = hT_[par]
            if all(sg["b"] == segs[0]["b"] for sg in segs):
                mods = [(0, 128, segs[0]["b"])]
            else:
                mods = [(sg["lo"], sg["n"], sg["b"]) for sg in segs]
            yield O("scalar", lambda e: e.activation(out=xn[:], in_=xs_[:], func=AF.Square, accum_out=statA[:, 0:1]),
                 reads=[xb], writes=["xn", "sa0"])
            yield O("scalar", lambda e: e.activation(out=statA[:, 1:2], in_=statA[:, 0:1], func=AF.Ln, scale=1.0 / D, bias=EPS),
                 reads=["sa0"], writes=["sa1"])
            yield O("scalar", lambda e: e.activation(out=statA[:, 2:3], in_=statA[:, 1:2], func=AF.Exp, scale=-0.5),
                 reads=["sa1"], writes=["sa2"])
            yield O("gpsimd", lambda e: e.tensor_scalar(out=xn[:], in0=xs_[:], scalar1=statA[:, 2:3], scalar2=1.0,
                                                       op0=ALU.mult, op1=ALU.mult), reads=[xb, "sa2"], writes=["xn"])
            for half, (PTb, bank, eng) in enumerate(((PCb, "c", "scalar"), (PDb, "d", "vector"))):
                PT3 = v3(PTb)
                yield O("tensor", [lambda e, j=j, PT3=PT3, half=half: e.transpose(
                    out=PT3[:, j, :], in_=xn[:, (4 * half + j) * 128:(4 * half + j + 1) * 128], identity=identb[:]) for j in range(4)],
                    reads=["xn", "identb"], writes=[bank], banks=[bank])
                for j in range(4):
                    jj = 4 * half + j
                    for (lo, n, b) in mods:
                        if eng == "scalar":
                            yield O("scalar", lambda e, j=j, jj=jj, lo=lo, n=n, b=b, PT3=PT3: e.activation(
                                out=hT[:, jj, lo:lo + n], in_=PT3[:, j, lo:lo + n], func=AF.Identity,
                                scale=aT[:, jj, b:b + 1], bias=sT[:, jj, b:b + 1]),
                                reads=[bank, "aT", "sT"], writes=[f"hT{half}_{par}"], banks=[bank])
                        else:
                            yield O("vector", lambda e, j=j, jj=jj, lo=lo, n=n, b=b, PT3=PT3: e.tensor_scalar(
                                out=hT[:, jj, lo:lo + n], in0=PT3[:, j, lo:lo + n], scalar1=aT[:, jj, b:b + 1],
                                scalar2=sT[:, jj, b:b + 1], op0=ALU.mult, op1=ALU.add),
                                reads=[bank, "aT", "sT"], writes=[f"hT{half}_{par}"], banks=[bank])

        def stage2(ctx):
            i, segs, par = ctx["i"], ctx["segs"], ctx["i"] % 2
            hT = hT_[par]
            QT, QTa2, KT, V, ez, sm = QT_[par], QTa2_[par], KT_[par], V_[par], ez_[par], sm_[par]
            nQTh, nQTa, nKTa, nKTh, nVa, nVh, ngza, ngzh = (f"{n}{par}" for n in ("QTh", "QTa", "KTa", "KTh", "Va", "Vh", "gza", "gzh"))
            smE = f"smE{par}"
            HT = [f"hT0_{par}", f"hT1_{par}"]

            def fm(out_ap, col0, m):
                return [lambda e, j=j: e.matmul(out_ap, lhsT=w_in_bf[:, j, col0:col0 + m], rhs=hT[:, j, :],
                                                start=(j == 0), stop=(j == 7)) for j in range(8)]

            def tm(out_ap, col0):
                return [lambda e, j=j: e.matmul(out_ap, lhsT=hT[:, j, :], rhs=w_in_bf[:, j, col0:col0 + 512],
                                                start=(j == 0), stop=(j == 7)) for j in range(8)]

            yield O("tensor", fm(PA[:, 0:128], C_AL, 128), reads=HT + WIN, writes=["a0"], banks=["a0"])
            yield O("vector", lambda e: e.tensor_copy(out=alr[:], in_=PA[:, 0:128]), reads=["a0"], writes=["alr"], banks=["a0"])
            for c in range(4):
                yield O("tensor", fm(PA[:, 512 + c * 128:512 + (c + 1) * 128], C_FH + c * 128, 128), reads=HT + WIN, writes=["a1"], banks=["a1"])
            yield O("tensor", [lambda e, c=c: e.matmul(PA[:, 128 + c * 128:256 + c * 128], lhsT=walpha_bf[:, c * 128:(c + 1) * 128],
                                                    rhs=alr[:, :], start=True, stop=True) for c in range(2)],
                 reads=["alr", "walpha_bf"], writes=["a0"], banks=["a0"])
            yield O("scalar", lambda e: e.activation(out=eh[:], in_=PA[:, 512:1024], func=AF.Exp, scale=-1.0),
                 reads=["a1"], writes=["eh"], banks=["a1"])
            for c in range(4):
                yield O("tensor", fm(PC[:, c * 128:(c + 1) * 128], C_QH + c * 128, 128), reads=HT + WIN, writes=["c"], banks=["c"])
            for c in range(2):
                yield O("scalar", lambda e, c=c: e.activation(out=e1[:, c * 128:(c + 1) * 128], in_=PA[:, 128 + c * 128:256 + c * 128],
                                                           func=AF.Exp, scale=-1.0, bias=nbalpha[:, c:c + 1]),
                     reads=["a0", "nbalpha"], writes=["e1"], banks=["a0"])
            yield O("scalar", lambda e: e.activation(out=L1[:], in_=eh[:], func=AF.Ln, bias=1.0), reads=["eh"], writes=["L1"])
            for c in range(2):
                yield O("tensor", fm(PD[:, c * 128:(c + 1) * 128], C_QA + c * 128, 128), reads=HT + WIN, writes=["d"], banks=["d"])
            for c in range(2):
                yield O("tensor", fm(PD[:, (2 + c) * 128:(3 + c) * 128], C_KA + c * 128, 128), reads=HT + WIN, writes=["d"], banks=["d"])
            yield O("scalar", lambda e: e.activation(out=e1[:], in_=e1[:], func=AF.Ln, bias=1.0), reads=["e1"], writes=["e1"])
            yield O("gpsimd", lambda e: e.tensor_scalar(out=gT[:, 0:256], in0=e1[:], scalar1=-1.0 / 16.0, scalar2=1.0,
                                                       op0=ALU.mult, op1=ALU.mult), reads=["e1"], writes=["gTa"])
            for c in range(4):
                yield O("scalar", lambda e, c=c: e.activation(out=L2[:, c * 128:(c + 1) * 128], in_=eh[:, c * 128:(c + 1) * 128],
                                                           func=AF.Ln, bias=1.0, scale=lb[:, c:c + 1]),
                     reads=["eh", "lb"], writes=["L2"])
            yield O("scalar", lambda e: e.activation(out=qh_sb[:], in_=PC[:, :], func=AF.Copy), reads=["c"], writes=["qh_sb"], banks=["c"])
            yield O("scalar", lambda e: e.activation(out=qk_sb[:], in_=PD[:, :], func=AF.Copy), reads=["d"], writes=["qk_sb"], banks=["d"])
            yield ("MARK",)
            yield O("vector", lambda e: e.tensor_tensor(out=gT[:, 256:768], in0=L2[:], in1=L1[:], op=ALU.subtract),
                 reads=["L1", "L2"], writes=["gTh"])
            yield O("vector", lambda e: e.tensor_tensor(out=L1[:], in0=PA[:, 512:1024], in1=L1[:], op=ALU.add),
                 reads=["a1", "L1"], writes=["L1"], banks=["a1"])
            yield O("vector", lambda e: e.tensor_tensor_scan(out=bT[:], data0=smasks[:], data1=gT[:], initial=0.0,
                                                          op0=ALU.mult, op1=ALU.add),
                 reads=["gTa", "gTh", "smasks"], writes=["bT"])
            yield O("tensor", tm(PA[:, 0:512], C_ZA), reads=HT + WIN, writes=["a0"], banks=["a0"])
            yield O("tensor", tm(PA[:, 512:1024], C_VA), reads=HT + WIN, writes=["a1"], banks=["a1"])
            yield O("scalar", lambda e: e.activation(out=ez[:, 0:512], in_=PA[:, 0:512], func=AF.Copy), reads=["a0"], writes=[ngza], banks=["a0"])
            yield O("vector", lambda e: e.tensor_copy(out=V[:, 0:512], in_=PA[:, 512:1024]), reads=["a1"], writes=[nVa], banks=["a1"])
            yield O("tensor", tm(PA[:, 0:512], C_ZH), reads=HT + WIN, writes=["a0"], banks=["a0"])
            yield O("tensor", tm(PA[:, 512:1024], C_IH), reads=HT + WIN, writes=["a1"], banks=["a1"])
            bT4 = bT[:].rearrange("p (c s t) -> p c s t", s=2, t=64)
            bTc4 = bTc[:].rearrange("p (c s t) -> p c s t", s=2, t=64)
            yield O("vector", lambda e: e.tensor_tensor(out=bTc4, in0=bT4, in1=bT4[:, :, :, 31:32].broadcast_to([128, 6, 2, 64]),
                                                        op=ALU.subtract), reads=["bT"], writes=["bTc"])
            yield O("scalar", lambda e: e.activation(out=sm[:, 0, :, :], in_=bT4[:, :, :, 31], func=AF.Exp), reads=["bT"], writes=[smE])
            yield O("scalar", lambda e: e.activation(out=sm[:, 1, :, :], in_=bT4[:, :, :, 63], func=AF.Exp), reads=["bT"], writes=[smE])
            yield O("scalar", lambda e: e.activation(out=sm[:, 2, :, :], in_=bTc4[:, :, :, 63], func=AF.Exp), reads=["bTc"], writes=[smE])
            yield O("scalar", lambda e: e.activation(out=eqb[:], in_=qh_sb[:], func=AF.Exp, scale=-1.0),
                 reads=["qh_sb"], writes=["eqb"])
            yield O("scalar", lambda e: e.activation(out=eqb[:], in_=eqb[:], func=AF.Ln, bias=1.0), reads=["eqb"], writes=["eqb"])
            yield O("scalar", lambda e: e.activation(out=EQa[:], in_=bTc[:, 0:256], func=AF.Exp), reads=["bTc"], writes=["EQa"])
            yield O("scalar", lambda e: e.activation(out=EKa[:], in_=bTc[:, 0:256], func=AF.Exp, scale=-1.0), reads=["bTc"], writes=["EKa"])
            yield O("scalar", lambda e: e.activation(out=ez[:, 512:1024], in_=PA[:, 0:512], func=AF.Copy), reads=["a0"], writes=[ngzh], banks=["a0"])
            yield O("vector", lambda e: e.tensor_copy(out=V[:, 512:1024], in_=PA[:, 512:1024]), reads=["a1"], writes=[nVh], banks=["a1"])
            for hh in range(2):
                yield O("vector", lambda e, hh=hh: e.scalar_tensor_tensor(
                    out=QTa2[hh * 64:(hh + 1) * 64, hh, :, :], in0=v3(qk_sb[hh * 64:(hh + 1) * 64, 0:256]), scalar=0.125,
                    in1=v3(EQa[hh * 64:(hh + 1) * 64, :]), op0=ALU.mult, op1=ALU.mult),
                    reads=["qk_sb", "EQa"], writes=[nQTa])
            yield O("vector", lambda e: e.tensor_tensor(out=KT[:, 0:2, :], in0=v3(qk_sb[:, 256:512]), in1=v3(EKa[:]), op=ALU.mult),
                 reads=["qk_sb", "EKa"], writes=[nKTa])
            yield O("vector", lambda e: e.tensor_tensor(out=L1[:], in0=L1[:], in1=bTc[:, 256:768], op=ALU.add),
                 reads=["L1", "bTc"], writes=["L1"])
            for c in range(4):
                yield O("scalar", lambda e, c=c: e.activation(
                    out=KT[:, 2 + c, :], in_=L1[:, c * 128:(c + 1) * 128], func=AF.Exp, scale=-1.0, bias=ln1mlb[:, c:c + 1]),
                    reads=["L1", "ln1mlb"], writes=[nKTh])
            yield O("vector", lambda e: e.tensor_tensor(out=eqb[:], in0=bTc[:, 256:768], in1=eqb[:], op=ALU.subtract),
                 reads=["eqb", "bTc"], writes=["eqb"])
            yield O("scalar", lambda e: e.activation(out=eqb[:], in_=eqb[:], func=AF.Exp), reads=["eqb"], writes=["eqb"])
            yield O("vector", lambda e: e.tensor_tensor(out=QT[:, :, :], in0=v3(qh_sb[:]), in1=v3(eqb[:]), op=ALU.mult),
                 reads=["qh_sb", "eqb"], writes=[nQTh])


        def stage34(ctx):
            i, segs, par, gg = ctx["i"], ctx["segs"], ctx["i"] % 2, ctx["gg"]
            slot = i % NXS
            xs_, xb = x_sb[slot], f"x{slot}"
            QT, QTa2, KT, V, ez, sm = QT_[par], QTa2_[par], KT_[par], V_[par], ez_[par], sm_[par]
            nQTh, nQTa, nKTa, nKTh, nVa, nVh, ngza, ngzh = (f"{n}{par}" for n in ("QTh", "QTa", "KTa", "KTh", "Va", "Vh", "gza", "gzh"))
            smE = f"smE{par}"
            PT03, PT13 = v3(PT0b), v3(PT1b)
            yield O("tensor", [lambda e, c=c: e.transpose(out=PT13[:, c, :], in_=KT[:, c, :], identity=identb[:]) for c in range(6)],
                 reads=[nKTa, nKTh, "identb"], writes=["t1"], banks=["t1"])
            for si_, sg in enumerate(segs):
                lo, n = sg["lo"], sg["n"]
                yield O("vector", lambda e, si_=si_, lo=lo, n=n: e.tensor_copy(out=Ktm2[lo:lo + n, si_, 0:6, :], in_=PT13[lo:lo + n, 0:6, :]),
                     reads=["t1"], writes=["Ktm"], banks=["t1"])
            yield O("tensor", [lambda e, h=h: e.matmul(PT0[:, h * 128:(h + 1) * 128], lhsT=KT[:, h // 2, :], rhs=QTa2[:, h % 2, h // 2, :],
                                                    start=True, stop=True) for h in range(4)],
                 reads=[nKTa, nQTa], writes=["t0"], banks=["t0"])
            yield O("vector", lambda e: e.tensor_tensor(out=AT[:, 0:4, :], in0=v3(PT0[:, 0:512]),
                                                     in1=masks[:].unsqueeze(1).broadcast_to([128, 4, 128]), op=ALU.mult),
                 reads=["t0", "masks"], writes=["ATa"], banks=["t0"])
            yield O("tensor", [lambda e, h=h: e.matmul(PT1[:, h * 128:(h + 1) * 128], lhsT=KT[:, 2 + h, :], rhs=QT[:, h, :],
                                                    start=True, stop=True) for h in range(4)],
                 reads=[nKTh, nQTh], writes=["t1"], banks=["t1"])
            yield O("vector", lambda e: e.tensor_tensor(out=AT[:, 4:8, :], in0=v3(PT1[:, 0:512]),
                                                     in1=masks[:].unsqueeze(1).broadcast_to([128, 4, 128]), op=ALU.mult),
                 reads=["t1", "masks"], writes=["ATh"], banks=["t1"])
            for si_, sg in enumerate(segs):
                lo, n, st = sg["lo"], sg["n"], sg["st"]
                yield O("vector", lambda e, si_=si_, st=st: e.tensor_tensor(
                    out=Sp[:, si_], in0=S_all[:, st], in1=sm[:, 0, :, si_].unsqueeze(2).broadcast_to([128, 6, 128]), op=ALU.mult),
                    reads=[f"S{st}", f"S{st}h", smE], writes=[f"Sp{si_}"])
                yield O("gpsimd", lambda e, si_=si_, st=st: e.tensor_tensor(
                    out=S_all[:, st], in0=S_all[:, st], in1=sm[:, 1, :, si_].unsqueeze(2).broadcast_to([128, 6, 128]), op=ALU.mult),
                    reads=[smE], writes=[f"S{st}", f"S{st}h"])
                fl = []
                for h in range(4):
                    c = h // 2
                    fl.append(lambda e, h=h, lo=lo, n=n: e.matmul(
                        PB[lo:lo + n, h * 128:(h + 1) * 128], lhsT=AT[:, h, lo:lo + n], rhs=V[:, h * 128:(h + 1) * 128],
                        start=True, stop=False))
                    fl.append(lambda e, h=h, c=c, lo=lo, n=n, si_=si_: e.matmul(
                        PB[lo:lo + n, h * 128:(h + 1) * 128], lhsT=QTa2[:, h % 2, c, lo:lo + n], rhs=Sp[:, si_, c, :],
                        start=False, stop=True))
                yield O("tensor", fl, reads=["ATa", nVa, nQTa, f"Sp{si_}"], writes=["b0"], banks=["b0"])
                fl = []
                for h in range(4):
                    fl.append(lambda e, h=h, lo=lo, n=n: e.matmul(
                        PB[lo:lo + n, (4 + h) * 128:(5 + h) * 128], lhsT=AT[:, 4 + h, lo:lo + n],
                        rhs=V[:, (4 + h) * 128:(5 + h) * 128], start=True, stop=False))
                    fl.append(lambda e, h=h, lo=lo, n=n, si_=si_: e.matmul(
                        PB[lo:lo + n, (4 + h) * 128:(5 + h) * 128], lhsT=QT[:, h, lo:lo + n], rhs=Sp[:, si_, 2 + h, :],
                        start=False, stop=True))
                yield O("tensor", fl, reads=["ATh", nVh, nQTh, f"Sp{si_}"], writes=["b1"], banks=["b1"])
                fl = []
                for h in range(4):
                    c, r0 = h // 2, (h % 2) * 64
                    fl.append(lambda e, h=h, c=c, r0=r0, si_=si_: e.matmul(
                        PT0[r0:r0 + 64, c * 128:(c + 1) * 128], lhsT=Ktm2[:, si_, c, r0:r0 + 64], rhs=V[:, h * 128:(h + 1) * 128],
                        start=True, stop=True))
                yield O("tensor", fl, reads=["Ktm", nVa], writes=["t0"], banks=["t0"])
                fl = []
                for h in range(4):
                    fl.append(lambda e, h=h, si_=si_: e.matmul(
                        PT1[:, h * 128:(h + 1) * 128], lhsT=Ktm2[:, si_, 2 + h, :], rhs=V[:, (4 + h) * 128:(5 + h) * 128],
                        start=True, stop=True))
                yield O("tensor", fl, reads=["Ktm", nVh], writes=["t1"], banks=["t1"])
                for c in range(2):
                    yield O("vector", lambda e, c=c, si_=si_, st=st: e.scalar_tensor_tensor(
                        out=S_all[:, st, c, :], in0=PT0[:, c * 128:(c + 1) * 128], scalar=sm[:, 2, c, si_:si_ + 1], in1=S_all[:, st, c, :],
                        op0=ALU.mult, op1=ALU.add),
                        reads=["t0", smE], writes=[f"S{st}"], banks=["t0"])
                for h in range(4):
                    yield O("vector", lambda e, h=h, si_=si_, st=st: e.scalar_tensor_tensor(
                        out=S_all[:, st, 2 + h, :], in0=PT1[:, h * 128:(h + 1) * 128], scalar=sm[:, 2, 2 + h, si_:si_ + 1],
                        in1=S_all[:, st, 2 + h, :], op0=ALU.mult, op1=ALU.add),
                        reads=["t1", smE], writes=[f"S{st}h"], banks=["t1"])
            yield O("scalar", lambda e: e.activation(out=ztmp[:], in_=ez[:], func=AF.Exp, scale=-1.0), reads=[ngza, ngzh], writes=["ztmp"])
            yield O("scalar", lambda e: e.activation(out=ztmp[:], in_=ztmp[:], func=AF.Ln, bias=1.0), reads=["ztmp"], writes=["ztmp"])
            yield O("scalar", lambda e: e.activation(out=ztmp[:], in_=ztmp[:], func=AF.Exp, scale=-1.0), reads=["ztmp"], writes=["ztmp"])
            yield O("vector", lambda e: e.tensor_tensor(out=ez[:], in0=ez[:], in1=ztmp[:], op=ALU.mult), reads=["ztmp"], writes=[ngza, ngzh])
            yield O("scalar", lambda e: e.activation(out=sq[:, 0:512], in_=PB[:, 0:512], func=AF.Square),
                 reads=["b0"], writes=["sqa"], banks=["b0"])
            yield O("scalar", lambda e: e.activation(out=sq[:, 512:1024], in_=PB[:, 512:1024], func=AF.Square),
                 reads=["b1"], writes=["sqh"], banks=["b1"])
            yield O("vector", lambda e: e.reduce_sum(out=so[:, 0:8], in_=v3(sq[:]), axis=AX.X), reads=["sqa", "sqh"], writes=["so0"])
            yield O("scalar", lambda e: e.activation(out=so[:, 8:16], in_=so[:, 0:8], func=AF.Ln, scale=1.0 / 128, bias=EPS),
                 reads=["so0"], writes=["so1"])
            yield O("scalar", lambda e: e.activation(out=so[:, 16:24], in_=so[:, 8:16], func=AF.Exp, scale=-0.5),
                 reads=["so1"], writes=["so2"])
            yield O("gpsimd", lambda e: e.tensor_tensor(out=v3(ez[:]), in0=v3(ez[:]),
                                                      in1=so[:, 16:24].unsqueeze(2).broadcast_to([128, 8, 128]), op=ALU.mult),
                 reads=["so2"], writes=[ngza, ngzh])
            yield O("vector", lambda e: e.tensor_tensor(out=ohat[:, 0:512], in0=PB[:, 0:512], in1=ez[:, 0:512], op=ALU.mult),
                 reads=["b0", ngza], writes=["ohata"], banks=["b0"])
            yield O("vector", lambda e: e.tensor_tensor(out=ohat[:, 512:1024], in0=PB[:, 512:1024], in1=ez[:, 512:1024], op=ALU.mult),
                 reads=["b1", ngzh], writes=["ohath"], banks=["b1"])
            yield O("tensor", [lambda e, j=j: e.transpose(out=PT03[:, j, :], in_=ohat[:, j * 128:(j + 1) * 128], identity=identb[:])
                            for j in range(4)], reads=["ohata", "identb"], writes=["t0"], banks=["t0"])
            yield O("scalar", lambda e: e.activation(out=ohT[:, 0:4, :], in_=PT03[:, 0:4, :], func=AF.Copy),
                 reads=["t0"], writes=["ohTa"], banks=["t0"])
            yield O("tensor", [lambda e, j=j: e.transpose(out=PT13[:, j, :], in_=ohat[:, (4 + j) * 128:(5 + j) * 128], identity=identb[:])
                            for j in range(4)], reads=["ohath", "identb"], writes=["t1"], banks=["t1"])
            yield O("vector", lambda e: e.tensor_copy(out=ohT[:, 4:8, :], in_=PT13[:, 0:4, :]), reads=["t1"], writes=["ohTh"], banks=["t1"])
            for n_ in range(2):
                yield O("tensor", [lambda e, j=j, n_=n_: e.matmul(PB[:, n_ * 512:(n_ + 1) * 512], lhsT=ohT[:, j, :],
                                                                 rhs=w_out_bf[:, j, n_ * 512:(n_ + 1) * 512], start=(j == 0), stop=(j == 7))
                                for j in range(8)], reads=["ohTa", "ohTh"] + WOUT, writes=[f"b{n_}"], banks=[f"b{n_}"])
            yield O("scalar", lambda e: e.activation(out=ohat[:], in_=PB[:, :], func=AF.Square, accum_out=statB[:, 0:1]),
                 reads=["b0", "b1"], writes=["ohata", "ohath", "sb0"], banks=["b0", "b1"])
            yield O("scalar", lambda e: e.activation(out=statB[:, 1:2], in_=statB[:, 0:1], func=AF.Ln, scale=1.0 / D, bias=EPS),
                 reads=["sb0"], writes=["sb1"])
            yield O("scalar", lambda e: e.activation(out=statB[:, 2:3], in_=statB[:, 1:2], func=AF.Exp, scale=-0.5),
                 reads=["sb1"], writes=["sb2"])
            yield O("vector", lambda e: e.scalar_tensor_tensor(out=sq[:], in0=PB[:, :], scalar=statB[:, 2:3], in1=GG[gg][:],
                                                            op0=ALU.mult, op1=ALU.mult),
                 reads=["b0", "b1", "sb2", f"GG{gg}"], writes=["sqa", "sqh"], banks=["b0", "b1"])
            yield O("gpsimd", lambda e: e.tensor_tensor(out=xs_[:], in0=xs_[:], in1=sq[:], op=ALU.add), reads=[xb, "sqa", "sqh"], writes=[xb])
            yield O("sync", lambda e: e.dma_start(out=ctx["ydst"], in_=xs_[:]), reads=[xb], dma_sem=f"yst{slot}")
            k = ctx["k"]
            if k is not None:
                for st, gdst, hdst in ((2 + 2 * k, sgs[2 * k], shs[2 * k]), (3 + 2 * k, sgs[2 * k + 1], shs[2 * k + 1])):
                    yield O("sync", lambda e, st=st, gdst=gdst: e.dma_start(out=gdst.rearrange("c p v -> p c v"), in_=S_all[:, st, 0:2, :]),
                            reads=[f"S{st}"], dma_sem="sout")
                    yield O("sync", lambda e, st=st, hdst=hdst: e.dma_start(out=hdst.rearrange("c p v -> p c v"), in_=S_all[:, st, 2:6, :]),
                            reads=[f"S{st}h"], dma_sem="sout")

        def store_state(st, gdst, hdst):
            P.op("sync", lambda e: e.dma_start(out=gdst.rearrange("c p v -> p c v"), in_=S_all[:, st, 0:2, :]),
                 reads=[f"S{st}"], dma_sem="sout")
            P.op("sync", lambda e: e.dma_start(out=hdst.rearrange("c p v -> p c v"), in_=S_all[:, st, 2:6, :]),
                 reads=[f"S{st}h"], dma_sem="sout")

        tiles = []
        for k in range(2):
            segs = [dict(lo=0, n=64, b=2 + 2 * k, st=2 + 2 * k), dict(lo=64, n=64, b=3 + 2 * k, st=3 + 2 * k)]
            tiles.append(dict(xsrc=xs[2 * k:2 * k + 2].rearrange("b t d -> (b t) d"), ydst=ys[2 * k:2 * k + 2].rearrange("b t d -> (b t) d"),
                              segs=segs, gg=2 + k, k=k))
        for t in range(tp_tiles):
            for s in range(2):
                tiles.append(dict(xsrc=xp[s, t * 128:(t + 1) * 128, :], ydst=yp[s, t * 128:(t + 1) * 128, :],
                                  segs=[dict(lo=0, n=64, b=s, st=s), dict(lo=64, n=64, b=s, st=s)], gg=s, k=None))
        for i, t in enumerate(tiles):
            t["i"] = i
        NT = len(tiles)
        PRE = 2
        import os as _os2
        _os_kverb = bool(_os2.environ.get("KVERB2"))
        ALPHA = float(_os2.environ.get("KALPHA", "0.0"))
        for i in range(min(PRE, NT)):
            load_x(i, tiles[i]["xsrc"])
        for r in range(NT + 1):
            if r + PRE < NT:
                load_x(r + PRE, tiles[r + PRE]["xsrc"])
            if _os_kverb:
                print("round", r, "model t_us", {k: round(v / 1e3, 1) for k, v in P.eng_free.items()})
            if r == 0:
                for d_ in stage1(tiles[0]):
                    P.op(d_[0], d_[1], reads=d_[2], writes=d_[3], banks=d_[4], dma_sem=d_[5])
            streams = []
            if r >= 1:
                streams.append([list(stage34(tiles[r - 1])), 0])
            if r < NT:
                streams.append([list(stage2(tiles[r])), 0])
            pending = [list(stage1(tiles[r + 1])), 0] if r + 1 < NT else None
            if pending is not None and r >= NT:
                streams.append(pending)
                pending = None
            while streams:
                best, bk = None, None
                for st_ in list(streams):
                    ops, k = st_
                    while k < len(ops) and ops[k][0] == "MARK":
                        k += 1
                        st_[1] = k
                        if pending is not None:
                            streams.append(pending)
                            pending = None
                    if k >= len(ops):
                        streams.remove(st_)
                        continue
                for st_ in streams:
                    ops, k = st_
                    key = P.est_start(ops[k])
                    if bk is None or key < bk:
                        best, bk = st_, key
                if best is None:
                    break
                ops, k = best
                eng, fns, reads, writes, banks, dma_sem = ops[k]
                P.op(eng, fns, reads=reads, writes=writes, banks=banks, dma_sem=dma_sem)
                best[1] = k + 1
                if best[1] >= len(ops):
                    streams.remove(best)
            if pending is not None:
                for d_ in pending[0]:
                    P.op(d_[0], d_[1], reads=d_[2], writes=d_[3], banks=d_[4], dma_sem=d_[5])
        for s in range(2):
            store_state(s, sgp[s], shp[s])
        for nm, s in list(P.sems.items()):
            if nm.startswith("yst") or nm == "sout":
                P.wait_token("sync", (nm, s[1]))
        with nc.Block() as block:
            P.replay(block)
        P.close()
        import os as _os
        if _os.environ.get("KVERB"):
            print("total ops recorded", P.count, "model makespan us", max(P.eng_free.values()) / 1e3)
    return nc


def host_inputs(core, x_prompt, x_sample, c_prompt, c_sample, state_gla, state_hgrn, w_ada, b_ada, g_pre,
                w_in, w_alpha, b_alpha, g_onorm_gla, hgrn_lb_logits, g_onorm_hgrn, w_out, g_post, consts):
    f = np.float32
    c6 = np.concatenate([c_prompt[2 * core:2 * core + 2], c_sample[4 * core:4 * core + 4]], 0)

    def pj(a):
        return np.ascontiguousarray(a.reshape(8, 128, a.shape[1]).transpose(1, 0, 2))

    m = {
        "xp": np.ascontiguousarray(x_prompt[2 * core:2 * core + 2]),
        "xs": np.ascontiguousarray(x_sample[4 * core:4 * core + 4]),
        "cT": np.ascontiguousarray(c6.T.reshape(8, 128, 6).transpose(1, 0, 2)),
        "stg": np.ascontiguousarray(state_gla[0, 4 * core:4 * core + 4].reshape(4, 2, 128, 128)),
        "sth": np.ascontiguousarray(state_hgrn[0, 4 * core:4 * core + 4]),
        "wada": pj(w_ada[0]),
        "bada": np.ascontiguousarray(np.broadcast_to(b_ada[0][None, :], (6, 3072))),
        "gpre": np.ascontiguousarray(g_pre[0].reshape(8, 128).T),
        "win": pj(w_in[0]),
        "walpha": np.ascontiguousarray(w_alpha[0]),
        "balpha": np.ascontiguousarray(b_alpha[0].reshape(2, 128).T),
        "gon": np.ascontiguousarray(np.stack([g_onorm_gla[0], g_onorm_hgrn[0]], 1)),
        "lbl": np.ascontiguousarray(hgrn_lb_logits.reshape(2, 4, 128).transpose(2, 0, 1)),
        "wout": pj(w_out[0]),
        "gpost": np.ascontiguousarray(np.broadcast_to(g_post[0][None, :], (128, 1024))),
    }
    m.update(consts)
    return {k: np.ascontiguousarray(v, dtype=f) for k, v in m.items()}


def make_consts():
    f = np.float32
    maskp = np.triu(np.ones((128, 128), f))
    masks = maskp.copy()
    masks[0:64, 64:128] = 0.0
    smaskp = np.ones((128, 768), f)
    smaskp[:, 0::128] = 0.0
    smasks = smaskp.copy()
    smasks[:, 64::128] = 0.0
    sel = np.zeros((6, 4, 128), f)
    sel[0, 0, :] = 1.0
    sel[1, 1, :] = 1.0
    sel[2, 2, 0:64] = 1.0
    sel[3, 2, 64:128] = 1.0
    sel[4, 3, 0:64] = 1.0
    sel[5, 3, 64:128] = 1.0
    return {"identf": np.eye(128, dtype=f), "maskp": maskp, "masks": masks, "smaskp": smaskp, "smasks": smasks, "sel": sel}


def assemble(results, TP):
    f = np.float32
    yp = np.concatenate([r["yp"] for r in results], 0).astype(f)
    ys = np.concatenate([r["ys"] for r in results], 0).astype(f)
    sgp = np.concatenate([r["sgp"].reshape(2, 4, 64, 128) for r in results], 0)[None].astype(f)
    shp = np.concatenate([r["shp"] for r in results], 0)[None].astype(f)
    sgs = np.concatenate([r["sgs"].reshape(4, 4, 64, 128) for r in results], 0)[None].astype(f)
    shs = np.concatenate([r["shs"] for r in results], 0)[None].astype(f)
    return (yp, ys, sgp, shp, sgs, shs)


def kernel(**inputs):
    inputs = {k: np.asarray(v) for k, v in inputs.items()}
    TP = inputs["x_prompt"].shape[1]
    nc = build(TP // 128)
    consts = make_consts()
    in_maps = [host_inputs(i, consts=consts, **inputs) for i in range(N_CORES)]
    res = run_bass_kernel_spmd(nc, in_maps, core_ids=list(range(N_CORES)))
    return assemble(res.results, TP)
```

```python
from contextlib import ExitStack

import numpy as np
import concourse.bass as bass
import concourse.mybir as mybir
from concourse.bass_utils import run_bass_kernel_spmd

F32 = mybir.dt.float32
BF16 = mybir.dt.bfloat16
AF = mybir.ActivationFunctionType
ALU = mybir.AluOpType
AX = mybir.AxisListType

D = 1024
NCOL = 3600
EPS = 1e-6
N_CORES = 8
SEQ = 4096
ENGS = ("sync", "scalar", "vector", "gpsimd", "tensor")

C_QA, C_KA, C_VA, C_ZA, C_AL, C_QH, C_FH, C_IH, C_ZH = 0, 256, 512, 1024, 1536, 1552, 2064, 2576, 3088


class _Probe:
    def __init__(self):
        self.calls = []

    def __getattr__(self, name):
        def f(*a, **k):
            out = k.get("out", a[0] if a else None)
            self.calls.append((name, out, k))
            return None
        return f


def _ap_n(ap):
    try:
        n = 1
        for d in ap.shape[1:]:
            n *= int(d)
        return n
    except Exception:
        return 0


def _est_ns(eng, fns):
    pr = _Probe()
    for f in fns:
        try:
            f(pr)
        except Exception:
            pass
    tot = 0.0
    for name, out, k in pr.calls:
        n = max([_ap_n(out)] + [_ap_n(k.get(kk)) for kk in ("in_", "in0", "data0", "rhs")] + [1])
        if eng == "tensor":
            tot += 95.0 if name == "transpose" else 15.0 + 0.43 * n
        elif eng == "scalar":
            tot += (480.0 if n <= 16 else 200.0 + 0.75 * n) + (100.0 if k.get("accum_out") is not None else 0.0)
        elif eng == "vector":
            tot += (60.0 + 2.1 * n) if name == "tensor_tensor_scan" else 150.0 + 1.04 * n
        elif eng == "gpsimd":
            tot += (100.0 + 1.0 * n) if name == "tensor_scalar" else 100.0 + 2.0 * n
        else:
            tot += 2000.0
    return max(tot, 50.0)


class Prog:
    def __init__(self, nc):
        self.nc = nc
        self.q = {e: [] for e in ENGS}
        self.sems = {}
        self.waited = {e: {} for e in ENGS}
        self.bufs = {}
        self.bank_last = {}
        self._cms = []
        self.eng_free = {e: 0.0 for e in ENGS}
        self.tok_time = {}
        import os as _os
        self.limit = int(_os.environ.get("KLIMIT", "0")) or None
        self.count = 0

    def sem(self, name):
        if name not in self.sems:
            cm = self.nc.semaphore(name)
            h = cm.__enter__()
            self._cms.append(cm)
            self.sems[name] = [h, 0]
        return self.sems[name]

    def close(self):
        for cm in reversed(self._cms):
            cm.__exit__(None, None, None)

    def _deps(self, eng, reads, writes, is_dma, banks):
        deps = []
        for b in banks:
            t = self.bank_last.get(b)
            if t is not None and t[2] != eng:
                deps.append((t, "bank"))
        for b in reads:
            st = self.bufs.get(b)
            if st and st[0] is not None:
                deps.append((st[0], "raw"))
        for b in writes:
            st = self.bufs.get(b)
            if st:
                if st[0] is not None:
                    deps.append((st[0], "waw"))
                for t in st[1]:
                    deps.append((t, "war"))
        waits = []
        for tok, kind in deps:
            sname, val, teng, tdma = tok
            if not tdma and teng == eng and not is_dma:
                if eng == "tensor" or kind in ("war", "waw"):
                    continue
            if self.waited[eng].get(sname, 0) >= val:
                continue
            self.waited[eng][sname] = val
            waits.append((sname, val))
        return waits

    def _dep_tokens(self, eng, reads, writes, banks):
        toks = []
        for b in banks:
            t = self.bank_last.get(b)
            if t is not None:
                toks.append(t)
        for b in reads:
            st = self.bufs.get(b)
            if st and st[0] is not None:
                toks.append(st[0])
        for b in writes:
            st = self.bufs.get(b)
            if st:
                if st[0] is not None:
                    toks.append(st[0])
                toks.extend(st[1])
        return toks

    def _ready(self, eng, toks):
        t = self.eng_free[eng]
        for tok in toks:
            tt = self.tok_time.get((tok[0], tok[1]), 0.0) + (0.0 if tok[2] == eng else 150.0)
            if tt > t:
                t = tt
        return t

    def est_start(self, desc):
        eng, fns, reads, writes, banks, dma_sem = desc
        return self._ready(eng, self._dep_tokens(eng, reads, writes, banks))

    def op(self, eng, fns, reads=(), writes=(), dma_sem=None, banks=()):
        if callable(fns):
            fns = [fns]
        _t0 = self._ready(eng, self._dep_tokens(eng, reads, writes, banks))
        _dur = _est_ns(eng, fns)
        self.count += 1
        if self.limit is not None and self.count > self.limit:
            return None
        is_dma = dma_sem is not None
        waits = self._deps(eng, reads, writes, is_dma, banks)
        if is_dma:
            s = self.sem(dma_sem)
            s[1] += 16
            tok = (dma_sem, s[1], eng, True)
            inc = (dma_sem, 16)
        else:
            sname = "p_" + eng
            s = self.sem(sname)
            s[1] += 1
            tok = (sname, s[1], eng, False)
            inc = (sname, 1)
        self.q[eng].append((waits, fns, inc))
        if is_dma:
            self.eng_free[eng] = _t0 + 60.0
        else:
            self.eng_free[eng] = _t0 + _dur
        self.tok_time[(tok[0], tok[1])] = _t0 + _dur
        for b in banks:
            self.bank_last[b] = tok
        for b in writes:
            self.bufs[b] = [tok, []]
        for b in reads:
            if b in writes:
                continue
            self.bufs.setdefault(b, [None, []])[1].append(tok)
        return tok

    def retoken(self, names, tok):
        for b in names:
            self.bufs[b] = [tok, []]

    def wait_token(self, eng, tok):
        sname, val = tok[0], tok[1]
        if self.waited[eng].get(sname, 0) >= val:
            return
        self.waited[eng][sname] = val
        self.q[eng].append(([(sname, val)], [], None))

    def barrier(self):
        snap = [(n, s[1]) for n, s in self.sems.items() if s[1] > 0]
        for e in ENGS:
            w = []
            for n, v in snap:
                if self.waited[e].get(n, 0) < v:
                    self.waited[e][n] = v
                    w.append((n, v))
            if w:
                self.q[e].append((w, [], None))

    def replay(self, block):
        P = self

        def run(engobj, name):
            for waits, fns, inc in P.q[name]:
                for sname, val in waits:
                    engobj.wait_ge(P.sems[sname][0], val)
                ins = None
                for f in fns:
                    ins = f(engobj)
                if inc is not None and ins is not None:
                    ins.then_inc(P.sems[inc[0]][0], inc[1])

        @block.sync
        def _(e):
            run(e, "sync")

        @block.scalar
        def _(e):
            run(e, "scalar")

        @block.vector
        def _(e):
            run(e, "vector")

        @block.gpsimd
        def _(e):
            run(e, "gpsimd")

        @block.tensor
        def _(e):
            run(e, "tensor")


def v3(ap, t=128):
    return ap.rearrange("p (c t) -> p c t", t=t)


def build(tp_tiles):
    TP = tp_tiles * 128
    nc = bass.Bass("TRN2", target_bir_lowering=False)

    def din(name, shape):
        return nc.dram_tensor(name, shape, F32, kind="ExternalInput").ap()

    def dout(name, shape):
        return nc.dram_tensor(name, shape, F32, kind="ExternalOutput").ap()

    xp = din("xp", [2, TP, D])
    xs = din("xs", [4, 64, D])
    cT_d = din("cT", [128, 8, 6])
    stg = din("stg", [4, 2, 128, 128])
    sth = din("sth", [4, 4, 128, 128])
    wada = din("wada", [128, 8, 3072])
    bada = din("bada", [6, 3072])
    gpre_d = din("gpre", [128, 8])
    win = din("win", [128, 8, NCOL])
    walpha_d = din("walpha", [16, 256])
    balpha_d = din("balpha", [128, 2])
    gon_d = din("gon", [128, 2])
    lbl_d = din("lbl", [128, 2, 4])
    wout = din("wout", [128, 8, D])
    gpost_d = din("gpost", [128, D])
    identf_d = din("identf", [128, 128])
    maskp_d = din("maskp", [128, 128])
    masks_d = din("masks", [128, 128])
    smaskp_d = din("smaskp", [128, 768])
    smasks_d = din("smasks", [128, 768])
    sel_d = din("sel", [6, 4, 128])

    yp = dout("yp", [2, TP, D])
    ys = dout("ys", [4, 64, D])
    sgp = dout("sgp", [2, 2, 128, 128])
    shp = dout("shp", [2, 4, 128, 128])
    sgs = dout("sgs", [4, 2, 128, 128])
    shs = dout("shs", [4, 4, 128, 128])

    es = ExitStack()
    with es:
        def sb(name, shape, dt=F32):
            return es.enter_context(nc.sbuf_tensor(name, shape, dt))

        def ps(name, shape, dt=F32):
            return es.enter_context(nc.psum_tensor(name, shape, dt))

        P = Prog(nc)

        w_in_bf = sb("w_in_bf", [128, 8, NCOL], BF16)
        w_out_bf = sb("w_out_bf", [128, 8, D], BF16)
        walpha_bf = sb("walpha_bf", [128, 256], BF16)
        identf = sb("identf_sb", [128, 128])
        identb = sb("identb", [128, 128], BF16)
        maskp = sb("maskp_sb", [128, 128])
        masks = sb("masks_sb", [128, 128])
        smaskp = sb("smaskp_sb", [128, 768])
        smasks = sb("smasks_sb", [128, 768])
        cst = sb("cst", [128, 32])
        nbalpha = cst[:, 0:2]
        lb = cst[:, 2:6]
        ln1mlb = cst[:, 6:10]
        balpha = cst[:, 10:12]
        gon = cst[:, 12:14]
        tmp4 = cst[:, 14:18]
        tmp4b = cst[:, 18:22]
        lbl = sb("lbl_sb", [128, 2, 4])
        gpre = sb("gpre_sb", [128, 8])
        aT = sb("aT", [128, 8, 6])
        sT = sb("sT", [128, 8, 6])
        GG = [sb(f"GG{g}", [128, D]) for g in range(4)]
        S_all = sb("S_all", [128, 6, 6, 128])

        PA = ps("PA", [128, 1024])
        PB = ps("PB", [128, 1024])
        PC = ps("PC", [128, 512])
        PD = ps("PD", [128, 512])
        PT0 = ps("PT0", [128, 512])
        PT1 = ps("PT1", [128, 512])

        ses = ExitStack()
        with ses:
            def ssb(name, shape, dt=F32):
                return ses.enter_context(nc.sbuf_tensor(name, shape, dt))
            NSTG = 4
            stage = [ssb(f"stage{i}", [128, 2048]) for i in range(NSTG)]
            mod_sb = ssb("mod_sb", [6, 3072])
            bada_sb = ssb("bada_sb", [6, 3072])
            gpost = ssb("gpost_sb", [128, D])
            cT = ssb("cT_sb", [128, 8, 6])
            sel = ssb("sel_sb", [6, 4, 128])
            tmp48 = ssb("tmp48", [128, 8, 6])

            small = [
                (cT[:], cT_d, "cT"), (bada_sb[:], bada, "bada"), (gpre[:], gpre_d, "gpre"),
                (balpha, balpha_d, "balpha"), (gon, gon_d, "gon"), (lbl[:], lbl_d, "lbl"),
                (identf[:], identf_d, "identf"), (maskp[:], maskp_d, "maskp"), (masks[:], masks_d, "masks"),
                (smaskp[:], smaskp_d, "smaskp"), (smasks[:], smasks_d, "smasks"), (sel[:], sel_d, "sel"),
                (gpost[:], gpost_d, "gpost"),
            ]
            names = []
            tok = None
            for o_, i_, nm in small:
                tok = P.op("sync", lambda e, o_=o_, i_=i_: e.dma_start(out=o_, in_=i_), writes=[nm], dma_sem="ld_s")
                names.append(nm)
            walpha_f = ssb("walpha_f", [16, 256])
            stage_wa = walpha_f[:]
            tok = P.op("sync", lambda e: e.dma_start(out=stage_wa, in_=walpha_d), writes=["walpha_f"], dma_sem="ld_s")
            names.append("walpha_f")
            for b in range(4):
                tok = P.op("sync", lambda e, b=b: e.dma_start(out=S_all[:, 2 + b, 0:2, :], in_=stg[b].rearrange("c p v -> p c v")),
                           writes=[f"S{2 + b}"], dma_sem="ld_s")
                tok = P.op("sync", lambda e, b=b: e.dma_start(out=S_all[:, 2 + b, 2:6, :], in_=sth[b].rearrange("c p v -> p c v")),
                           writes=[f"S{2 + b}h"], dma_sem="ld_s")
            P.retoken(names + [f"S{2 + b}" for b in range(4)] + [f"S{2 + b}h" for b in range(4)], tok)

            P.op("vector", lambda e: e.tensor_copy(out=identb[:], in_=identf[:]), reads=["identf"], writes=["identb"])
            P.op("gpsimd", lambda e: e.memset(walpha_bf[:], 0.0), writes=["walpha_bf"])
            P.op("vector", lambda e: e.tensor_copy(out=walpha_bf[0:16, :], in_=stage_wa), reads=["walpha_f", "walpha_bf"], writes=["walpha_bf"])
            P.op("vector", lambda e: e.tensor_scalar(out=nbalpha, in0=balpha, scalar1=-1.0, scalar2=None, op0=ALU.mult),
                 reads=["balpha"], writes=["nbalpha"])
            P.op("vector", lambda e: e.tensor_tensor(out=tmp4, in0=lbl[:, 1, :], in1=lbl[:, 0, :], op=ALU.subtract),
                 reads=["lbl"], writes=["tmp4"])
            P.op("scalar", lambda e: e.activation(out=tmp4, in_=tmp4, func=AF.Exp), reads=["tmp4"], writes=["tmp4"])
            P.op("vector", lambda e: e.tensor_scalar(out=tmp4b, in0=tmp4, scalar1=1.0, scalar2=None, op0=ALU.add),
                 reads=["tmp4"], writes=["tmp4b"])
            P.op("vector", lambda e: e.reciprocal(out=lb, in_=tmp4b), reads=["tmp4b"], writes=["lb"])
            P.op("vector", lambda e: e.tensor_tensor(out=tmp4b, in0=tmp4, in1=lb, op=ALU.mult), reads=["tmp4", "lb"], writes=["tmp4b"])
            P.op("scalar", lambda e: e.activation(out=ln1mlb, in_=tmp4b, func=AF.Ln), reads=["tmp4b"], writes=["ln1mlb"])
            P.op("gpsimd", lambda e: e.memset(S_all[:, 0:2, :, :], 0.0), writes=["S0", "S0h", "S1", "S1h"])

            si = 0
            for n in range(12):
                stg_ = stage[si % NSTG]
                sname = f"stage{si % NSTG}"
                st3 = stg_[:, 0:2048].rearrange("p (j n) -> p j n", n=256)
                P.op("sync", lambda e, st3=st3, n=n: e.dma_start(out=st3, in_=wada[:, :, n * 256:(n + 1) * 256]),
                     writes=[sname], dma_sem="ld_" + sname)
                P.op("tensor", [lambda e, j=j, st3=st3: e.matmul(PC[0:6, 0:256], lhsT=cT[:, j, :], rhs=st3[:, j, :],
                                                                  start=(j == 0), stop=(j == 7)) for j in range(8)],
                     reads=[sname, "cT"], writes=["pc"], banks=["c"])
                P.op("vector", lambda e, n=n: e.tensor_tensor(out=mod_sb[0:6, n * 256:(n + 1) * 256], in0=PC[0:6, 0:256],
                                                             in1=bada_sb[0:6, n * 256:(n + 1) * 256], op=ALU.add),
                     reads=["pc", "bada"], writes=["mod"], banks=["c"])
                si += 1
            P.op("tensor", [lambda e, k=k: e.transpose(out=PD[:, k * 6:(k + 1) * 6], in_=mod_sb[0:6, k * 128:(k + 1) * 128],
                                                       identity=identf[0:6, 0:6]) for k in range(16)],
                 reads=["mod", "identf"], writes=["pd"], banks=["d"])
            P.op("vector", lambda e: e.tensor_copy(out=sT[:], in_=PD[:, 0:48].rearrange("p (j b) -> p j b", b=6)),
                 reads=["pd"], writes=["sT"], banks=["d"])
            P.op("vector", lambda e: e.tensor_scalar(out=tmp48[:], in0=PD[:, 48:96].rearrange("p (j b) -> p j b", b=6),
                                                     scalar1=1.0, scalar2=None, op0=ALU.add),
                 reads=["pd"], writes=["tmp48"], banks=["d"])
            P.op("vector", lambda e: e.tensor_tensor(out=aT[:], in0=tmp48[:], in1=gpre[:].unsqueeze(2).broadcast_to([128, 8, 6]),
                                                     op=ALU.mult), reads=["tmp48", "gpre"], writes=["aT"])
            for g in range(4):
                for n in range(2):
                    P.op("tensor", lambda e, g=g, n=n: e.matmul(PC[:, 0:512], lhsT=sel[0:6, g, :],
                                                                  rhs=mod_sb[0:6, 2048 + n * 512:2048 + (n + 1) * 512],
                                                                  start=True, stop=True),
                         reads=["mod", "sel"], writes=["pc"], banks=["c"])
                    P.op("vector", lambda e, g=g, n=n: e.tensor_tensor(out=GG[g][:, n * 512:(n + 1) * 512], in0=PC[:, 0:512],
                                                                      in1=gpost[:, n * 512:(n + 1) * 512], op=ALU.mult),
                         reads=["pc", "gpost"], writes=[f"GG{g}"], banks=["c"])
            for j in range(8):
                for hlf in range(2):
                    stg_ = stage[si % NSTG]
                    sname = f"stage{si % NSTG}"
                    c0 = hlf * 1800
                    P.op("sync", lambda e, stg_=stg_, j=j, c0=c0: e.dma_start(out=stg_[:, 0:1800], in_=win[:, j, c0:c0 + 1800]),
                         writes=[sname], dma_sem="ld_" + sname)
                    eng = "vector" if hlf == 0 else "scalar"
                    if eng == "vector":
                        P.op("vector", lambda e, stg_=stg_, j=j, c0=c0: e.tensor_copy(out=w_in_bf[:, j, c0:c0 + 1800], in_=stg_[:, 0:1800]),
                             reads=[sname], writes=[f"win{j}_{hlf}"])
                    else:
                        P.op("scalar", lambda e, stg_=stg_, j=j, c0=c0: e.activation(out=w_in_bf[:, j, c0:c0 + 1800], in_=stg_[:, 0:1800],
                                                                                      func=AF.Copy),
                             reads=[sname], writes=[f"win{j}_{hlf}"])
                    si += 1
            for jj in range(4):
                stg_ = stage[si % NSTG]
                sname = f"stage{si % NSTG}"
                st3 = stg_[:, 0:2048].rearrange("p (j n) -> p j n", n=1024)
                P.op("sync", lambda e, st3=st3, jj=jj: e.dma_start(out=st3, in_=wout[:, 2 * jj:2 * jj + 2, :]),
                     writes=[sname], dma_sem="ld_" + sname)
                for jl in range(2):
                    j = 2 * jj + jl
                    gcol = gon[:, 0:1] if j < 4 else gon[:, 1:2]
                    P.op("gpsimd", lambda e, st3=st3, jl=jl, j=j, gcol=gcol: e.tensor_scalar(
                        out=w_out_bf[:, j, :], in0=st3[:, jl, :], scalar1=gcol, scalar2=1.0, op0=ALU.mult, op1=ALU.mult),
                        reads=[sname, "gon"], writes=[f"wout{j}"])
                si += 1
            P.barrier()
        WIN = [f"win{j}_{h}" for j in range(8) for h in range(2)]
        WOUT = [f"wout{j}" for j in range(8)]

        NXS = 4
        x_sb = [sb(f"x_sb{i}", [128, D]) for i in range(NXS)]
        statA = sb("statA", [128, 8])
        xn = sb("xn", [128, D], BF16)
        hT_ = [sb(f"hT{p}", [128, 8, 128], BF16) for p in range(2)]
        alr = sb("alr", [128, 128], BF16)
        e1 = sb("e1", [128, 256])
        eh = sb("eh", [128, 512])
        L1 = sb("L1", [128, 512])
        L2 = sb("L2", [128, 512])
        gT = sb("gT", [128, 768])
        bT = sb("bT", [128, 768])
        bTc = sb("bTc", [128, 768])
        eqb = sb("eqb", [128, 512])
        qh_sb = sb("qh_sb", [128, 512])
        qk_sb = sb("qk_sb", [128, 512])
        EQa = sb("EQa", [128, 256])
        EKa = sb("EKa", [128, 256])
        QT_ = [sb(f"QT{p}", [128, 4, 128], BF16) for p in range(2)]
        QTa2_ = [sb(f"QTa2{p}", [128, 2, 2, 128], BF16) for p in range(2)]
        KT_ = [sb(f"KT{p}", [128, 6, 128], BF16) for p in range(2)]
        V_ = [sb(f"V{p}", [128, D], BF16) for p in range(2)]
        ez_ = [sb(f"ez{p}", [128, D]) for p in range(2)]
        sm_ = [sb(f"sm{p}", [128, 3, 6, 2]) for p in range(2)]
        Ktm2 = sb("Ktm2", [128, 2, 6, 128], BF16)
        Sp = sb("Sp", [128, 2, 6, 128], BF16)
        AT = sb("AT", [128, 8, 128], BF16)
        sq = sb("sq", [128, D])
        so = sb("so", [128, 24])
        statB = sb("statB", [128, 8])
        ohat = sb("ohat", [128, D], BF16)
        ohT = sb("ohT", [128, 8, 128], BF16)
        ztmp = sb("ztmp", [128, D])

        bT3 = v3(bT[:])
        PT0b = PT0[:].bitcast(BF16)
        PT1b = PT1[:].bitcast(BF16)
        PCb = PC[:].bitcast(BF16)
        PDb = PD[:].bitcast(BF16)
        P.op("gpsimd", lambda e: e.memset(Ktm2[:], 0.0), writes=["Ktma", "Ktmh"])
        for p in range(2):
            P.op("gpsimd", lambda e, p=p: e.memset(QTa2_[p][:], 0.0), writes=[f"QTa{p}"])

        def O(eng, fns, reads=(), writes=(), banks=(), dma_sem=None):
            return (eng, fns, reads, writes, banks, dma_sem)

        def load_x(i, xsrc):
            slot = i % NXS
            P.op("sync", lambda e: e.dma_start(out=x_sb[slot][:], in_=xsrc), writes=[f"x{slot}"], dma_sem=f"xld{slot}")

        def stage1(ctx):
            i, segs, par = ctx["i"], ctx["segs"], ctx["i"] % 2
            slot = i % NXS
            xs_, xb = x_sb[slot], f"x{slot}"
            hT = hT_[par]
            if all(sg["b"] == segs[0]["b"] for sg in segs):
                mods = [(0, 128, segs[0]["b"])]
            else:
                mods = [(sg["lo"], sg["n"], sg["b"]) for sg in segs]
            yield O("scalar", lambda e: e.activation(out=xn[:], in_=xs_[:], func=AF.Square, accum_out=statA[:, 0:1]),
                 reads=[xb], writes=["xn", "sa0"])
            yield O("scalar", lambda e: e.activation(out=statA[:, 1:2], in_=statA[:, 0:1], func=AF.Ln, scale=1.0 / D, bias=EPS),
                 reads=["sa0"], writes=["sa1"])
            yield O("scalar", lambda e: e.activation(out=statA[:, 2:3], in_=statA[:, 1:2], func=AF.Exp, scale=-0.5),
                 reads=["sa1"], writes=["sa2"])
            yield O("scalar", lambda e: e.activation(out=xn[:], in_=xs_[:], func=AF.Copy, scale=statA[:, 2:3]),
                 reads=[xb, "sa2"], writes=["xn"])
            for half, (PTb, bank, eng) in enumerate(((PCb, "c", "scalar"), (PDb, "d", "vector"))):
                PT3 = v3(PTb)
                yield O("tensor", [lambda e, j=j, PT3=PT3, half=half: e.transpose(
                    out=PT3[:, j, :], in_=xn[:, (4 * half + j) * 128:(4 * half + j + 1) * 128], identity=identb[:]) for j in range(4)],
                    reads=["xn", "identb"], writes=[bank], banks=[bank])
                for j in range(4):
                    jj = 4 * half + j
                    for (lo, n, b) in mods:
                        if eng == "scalar":
                            yield O("scalar", lambda e, j=j, jj=jj, lo=lo, n=n, b=b, PT3=PT3: e.activation(
                                out=hT[:, jj, lo:lo + n], in_=PT3[:, j, lo:lo + n], func=AF.Identity,
                                scale=aT[:, jj, b:b + 1], bias=sT[:, jj, b:b + 1]),
                                reads=[bank, "aT", "sT"], writes=[f"hT{half}_{par}"], banks=[bank])
                        else:
                            yield O("vector", lambda e, j=j, jj=jj, lo=lo, n=n, b=b, PT3=PT3: e.tensor_scalar(
                                out=hT[:, jj, lo:lo + n], in0=PT3[:, j, lo:lo + n], scalar1=aT[:, jj, b:b + 1],
                                scalar2=sT[:, jj, b:b + 1], op0=ALU.mult, op1=ALU.add),
                                reads=[bank, "aT", "sT"], writes=[f"hT{half}_{par}"], banks=[bank])

        def stage2(ctx):
            i, segs, par = ctx["i"], ctx["segs"], ctx["i"] % 2
            hT = hT_[par]
            QT, QTa2, KT, V, ez, sm = QT_[par], QTa2_[par], KT_[par], V_[par], ez_[par], sm_[par]
            nQTh, nQTa, nKTa, nKTh, nVa, nVh, ngza, ngzh = (f"{n}{par}" for n in ("QTh", "QTa", "KTa", "KTh", "Va", "Vh", "gza", "gzh"))
            smE = f"smE{par}"
            HT = [f"hT0_{par}", f"hT1_{par}"]

            def fm(out_ap, col0, m):
                return [lambda e, j=j: e.matmul(out_ap, lhsT=w_in_bf[:, j, col0:col0 + m], rhs=hT[:, j, :],
                                                start=(j == 0), stop=(j == 7)) for j in range(8)]

            def tm(out_ap, col0):
                return [lambda e, j=j: e.matmul(out_ap, lhsT=hT[:, j, :], rhs=w_in_bf[:, j, col0:col0 + 512],
                                                start=(j == 0), stop=(j == 7)) for j in range(8)]

            yield O("tensor", fm(PA[:, 0:128], C_AL, 128), reads=HT + WIN, writes=["a0"], banks=["a0"])
            yield O("vector", lambda e: e.tensor_copy(out=alr[:], in_=PA[:, 0:128]), reads=["a0"], writes=["alr"], banks=["a0"])
            for c in range(4):
                yield O("tensor", fm(PA[:, 512 + c * 128:512 + (c + 1) * 128], C_FH + c * 128, 128), reads=HT + WIN, writes=["a1"], banks=["a1"])
            yield O("tensor", [lambda e, c=c: e.matmul(PA[:, 128 + c * 128:256 + c * 128], lhsT=walpha_bf[:, c * 128:(c + 1) * 128],
                                                    rhs=alr[:, :], start=True, stop=True) for c in range(2)],
                 reads=["alr", "walpha_bf"], writes=["a0"], banks=["a0"])
            yield O("scalar", lambda e: e.activation(out=eh[:], in_=PA[:, 512:1024], func=AF.Exp, scale=-1.0),
                 reads=["a1"], writes=["eh"], banks=["a1"])
            for c in range(4):
                yield O("tensor", fm(PC[:, c * 128:(c + 1) * 128], C_QH + c * 128, 128), reads=HT + WIN, writes=["c"], banks=["c"])
            for c in range(2):
                yield O("scalar", lambda e, c=c: e.activation(out=e1[:, c * 128:(c + 1) * 128], in_=PA[:, 128 + c * 128:256 + c * 128],
                                                           func=AF.Exp, scale=-1.0, bias=nbalpha[:, c:c + 1]),
                     reads=["a0", "nbalpha"], writes=["e1"], banks=["a0"])
            yield O("scalar", lambda e: e.activation(out=L1[:], in_=eh[:], func=AF.Ln, bias=1.0), reads=["eh"], writes=["L1"])
            for c in range(2):
                yield O("tensor", fm(PD[:, c * 128:(c + 1) * 128], C_QA + c * 128, 128), reads=HT + WIN, writes=["d"], banks=["d"])
            for c in range(2):
                yield O("tensor", fm(PD[:, (2 + c) * 128:(3 + c) * 128], C_KA + c * 128, 128), reads=HT + WIN, writes=["d"], banks=["d"])
            yield O("scalar", lambda e: e.activation(out=e1[:], in_=e1[:], func=AF.Ln, bias=1.0), reads=["e1"], writes=["e1"])
            yield O("gpsimd", lambda e: e.tensor_scalar(out=gT[:, 0:256], in0=e1[:], scalar1=-1.0 / 16.0, scalar2=1.0,
                                                       op0=ALU.mult, op1=ALU.mult), reads=["e1"], writes=["gTa"])
            for c in range(4):
                yield O("scalar", lambda e, c=c: e.activation(out=L2[:, c * 128:(c + 1) * 128], in_=eh[:, c * 128:(c + 1) * 128],
                                                           func=AF.Ln, bias=1.0, scale=lb[:, c:c + 1]),
                     reads=["eh", "lb"], writes=["L2"])
            yield O("scalar", lambda e: e.activation(out=qh_sb[:], in_=PC[:, :], func=AF.Copy), reads=["c"], writes=["qh_sb"], banks=["c"])
            yield O("scalar", lambda e: e.activation(out=qk_sb[:], in_=PD[:, :], func=AF.Copy), reads=["d"], writes=["qk_sb"], banks=["d"])
            yield ("MARK",)
            yield O("vector", lambda e: e.tensor_tensor(out=gT[:, 256:768], in0=L2[:], in1=L1[:], op=ALU.subtract),
                 reads=["L1", "L2"], writes=["gTh"])
            yield O("vector", lambda e: e.tensor_tensor(out=L1[:], in0=PA[:, 512:1024], in1=L1[:], op=ALU.add),
                 reads=["a1", "L1"], writes=["L1"], banks=["a1"])
            yield O("vector", lambda e: e.tensor_tensor_scan(out=bT[:], data0=smasks[:], data1=gT[:], initial=0.0,
                                                          op0=ALU.mult, op1=ALU.add),
                 reads=["gTa", "gTh", "smasks"], writes=["bT"])
            yield O("tensor", tm(PA[:, 0:512], C_ZA), reads=HT + WIN, writes=["a0"], banks=["a0"])
            yield O("tensor", tm(PA[:, 512:1024], C_VA), reads=HT + WIN, writes=["a1"], banks=["a1"])
            yield O("scalar", lambda e: e.activation(out=ez[:, 0:512], in_=PA[:, 0:512], func=AF.Copy), reads=["a0"], writes=[ngza], banks=["a0"])
            yield O("vector", lambda e: e.tensor_copy(out=V[:, 0:512], in_=PA[:, 512:1024]), reads=["a1"], writes=[nVa], banks=["a1"])
            yield O("tensor", tm(PA[:, 0:512], C_ZH), reads=HT + WIN, writes=["a0"], banks=["a0"])
            yield O("tensor", tm(PA[:, 512:1024], C_IH), reads=HT + WIN, writes=["a1"], banks=["a1"])
            bT4 = bT[:].rearrange("p (c s t) -> p c s t", s=2, t=64)
            bTc4 = bTc[:].rearrange("p (c s t) -> p c s t", s=2, t=64)
            yield O("vector", lambda e: e.tensor_tensor(out=bTc4, in0=bT4, in1=bT4[:, :, :, 31:32].broadcast_to([128, 6, 2, 64]),
                                                        op=ALU.subtract), reads=["bT"], writes=["bTc"])
            yield O("scalar", lambda e: e.activation(out=sm[:, 0, :, :], in_=bT4[:, :, :, 31], func=AF.Exp), reads=["bT"], writes=[smE])
            yield O("scalar", lambda e: e.activation(out=sm[:, 1, :, :], in_=bT4[:, :, :, 63], func=AF.Exp), reads=["bT"], writes=[smE])
            yield O("scalar", lambda e: e.activation(out=sm[:, 2, :, :], in_=bTc4[:, :, :, 63], func=AF.Exp), reads=["bTc"], writes=[smE])
            yield O("scalar", lambda e: e.activation(out=eqb[:], in_=qh_sb[:], func=AF.Exp, scale=-1.0),
                 reads=["qh_sb"], writes=["eqb"])
            yield O("scalar", lambda e: e.activation(out=eqb[:], in_=eqb[:], func=AF.Ln, bias=1.0), reads=["eqb"], writes=["eqb"])
            yield O("scalar", lambda e: e.activation(out=EQa[:], in_=bTc[:, 0:256], func=AF.Exp), reads=["bTc"], writes=["EQa"])
            yield O("scalar", lambda e: e.activation(out=EKa[:], in_=bTc[:, 0:256], func=AF.Exp, scale=-1.0), reads=["bTc"], writes=["EKa"])
            yield O("scalar", lambda e: e.activation(out=ez[:, 512:1024], in_=PA[:, 0:512], func=AF.Copy), reads=["a0"], writes=[ngzh], banks=["a0"])
            yield O("vector", lambda e: e.tensor_copy(out=V[:, 512:1024], in_=PA[:, 512:1024]), reads=["a1"], writes=[nVh], banks=["a1"])
            for hh in range(2):
                yield O("vector", lambda e, hh=hh: e.scalar_tensor_tensor(
                    out=QTa2[hh * 64:(hh + 1) * 64, hh, :, :], in0=v3(qk_sb[hh * 64:(hh + 1) * 64, 0:256]), scalar=0.125,
                    in1=v3(EQa[hh * 64:(hh + 1) * 64, :]), op0=ALU.mult, op1=ALU.mult),
                    reads=["qk_sb", "EQa"], writes=[nQTa])
            yield O("vector", lambda e: e.tensor_tensor(out=KT[:, 0:2, :], in0=v3(qk_sb[:, 256:512]), in1=v3(EKa[:]), op=ALU.mult),
                 reads=["qk_sb", "EKa"], writes=[nKTa])
            yield O("vector", lambda e: e.tensor_tensor(out=L1[:], in0=L1[:], in1=bTc[:, 256:768], op=ALU.add),
                 reads=["L1", "bTc"], writes=["L1"])
            for c in range(4):
                yield O("scalar", lambda e, c=c: e.activation(
                    out=KT[:, 2 + c, :], in_=L1[:, c * 128:(c + 1) * 128], func=AF.Exp, scale=-1.0, bias=ln1mlb[:, c:c + 1]),
                    reads=["L1", "ln1mlb"], writes=[nKTh])
            yield O("vector", lambda e: e.tensor_tensor(out=eqb[:], in0=bTc[:, 256:768], in1=eqb[:], op=ALU.subtract),
                 reads=["eqb", "bTc"], writes=["eqb"])
            yield O("scalar", lambda e: e.activation(out=eqb[:], in_=eqb[:], func=AF.Exp), reads=["eqb"], writes=["eqb"])
            yield O("vector", lambda e: e.tensor_tensor(out=QT[:, :, :], in0=v3(qh_sb[:]), in1=v3(eqb[:]), op=ALU.mult),
                 reads=["qh_sb", "eqb"], writes=[nQTh])


        def stage34(ctx):
            i, segs, par, gg = ctx["i"], ctx["segs"], ctx["i"] % 2, ctx["gg"]
            slot = i % NXS
            xs_, xb = x_sb[slot], f"x{slot}"
            QT, QTa2, KT, V, ez, sm = QT_[par], QTa2_[par], KT_[par], V_[par], ez_[par], sm_[par]
            nQTh, nQTa, nKTa, nKTh, nVa, nVh, ngza, ngzh = (f"{n}{par}" for n in ("QTh", "QTa", "KTa", "KTh", "Va", "Vh", "gza", "gzh"))
            smE = f"smE{par}"
            PT03, PT13 = v3(PT0b), v3(PT1b)
            yield O("tensor", [lambda e, c=c: e.transpose(out=PT13[:, c, :], in_=KT[:, c, :], identity=identb[:]) for c in range(6)],
                 reads=[nKTa, nKTh, "identb"], writes=["t1"], banks=["t1"])
            for si_, sg in enumerate(segs):
                lo, n = sg["lo"], sg["n"]
                yield O("vector", lambda e, si_=si_, lo=lo, n=n: e.tensor_copy(out=Ktm2[lo:lo + n, si_, 0:6, :], in_=PT13[lo:lo + n, 0:6, :]),
                     reads=["t1"], writes=["Ktm"], banks=["t1"])
            yield O("tensor", [lambda e, h=h: e.matmul(PT0[:, h * 128:(h + 1) * 128], lhsT=KT[:, h // 2, :], rhs=QTa2[:, h % 2, h // 2, :],
                                                    start=True, stop=True) for h in range(4)],
                 reads=[nKTa, nQTa], writes=["t0"], banks=["t0"])
            yield O("vector", lambda e: e.tensor_tensor(out=AT[:, 0:4, :], in0=v3(PT0[:, 0:512]),
                                                     in1=masks[:].unsqueeze(1).broadcast_to([128, 4, 128]), op=ALU.mult),
                 reads=["t0", "masks"], writes=["ATa"], banks=["t0"])
            yield O("tensor", [lambda e, h=h: e.matmul(PT1[:, h * 128:(h + 1) * 128], lhsT=KT[:, 2 + h, :], rhs=QT[:, h, :],
                                                    start=True, stop=True) for h in range(4)],
                 reads=[nKTh, nQTh], writes=["t1"], banks=["t1"])
            yield O("vector", lambda e: e.tensor_tensor(out=AT[:, 4:8, :], in0=v3(PT1[:, 0:512]),
                                                     in1=masks[:].unsqueeze(1).broadcast_to([128, 4, 128]), op=ALU.mult),
                 reads=["t1", "masks"], writes=["ATh"], banks=["t1"])
            for si_, sg in enumerate(segs):
                lo, n, st = sg["lo"], sg["n"], sg["st"]
                yield O("vector", lambda e, si_=si_, st=st: e.tensor_tensor(
                    out=Sp[:, si_], in0=S_all[:, st], in1=sm[:, 0, :, si_].unsqueeze(2).broadcast_to([128, 6, 128]), op=ALU.mult),
                    reads=[f"S{st}", f"S{st}h", smE], writes=[f"Sp{si_}"])
                yield O("gpsimd", lambda e, si_=si_, st=st: e.tensor_tensor(
                    out=S_all[:, st], in0=S_all[:, st], in1=sm[:, 1, :, si_].unsqueeze(2).broadcast_to([128, 6, 128]), op=ALU.mult),
                    reads=[smE], writes=[f"S{st}", f"S{st}h"])
                fl = []
                for h in range(4):
                    c = h // 2
                    fl.append(lambda e, h=h, lo=lo, n=n: e.matmul(
                        PB[lo:lo + n, h * 128:(h + 1) * 128], lhsT=AT[:, h, lo:lo + n], rhs=V[:, h * 128:(h + 1) * 128],
                        start=True, stop=False))
                    fl.append(lambda e, h=h, c=c, lo=lo, n=n, si_=si_: e.matmul(
                        PB[lo:lo + n, h * 128:(h + 1) * 128], lhsT=QTa2[:, h % 2, c, lo:lo + n], rhs=Sp[:, si_, c, :],
                        start=False, stop=True))
                yield O("tensor", fl, reads=["ATa", nVa, nQTa, f"Sp{si_}"], writes=["b0"], banks=["b0"])
                fl = []
                for h in range(4):
                    fl.append(lambda e, h=h, lo=lo, n=n: e.matmul(
                        PB[lo:lo + n, (4 + h) * 128:(5 + h) * 128], lhsT=AT[:, 4 + h, lo:lo + n],
                        rhs=V[:, (4 + h) * 128:(5 + h) * 128], start=True, stop=False))
                    fl.append(lambda e, h=h, lo=lo, n=n, si_=si_: e.matmul(
                        PB[lo:lo + n, (4 + h) * 128:(5 + h) * 128], lhsT=QT[:, h, lo:lo + n], rhs=Sp[:, si_, 2 + h, :],
                        start=False, stop=True))
                yield O("tensor", fl, reads=["ATh", nVh, nQTh, f"Sp{si_}"], writes=["b1"], banks=["b1"])
                fl = []
                for h in range(4):
                    c, r0 = h // 2, (h % 2) * 64
                    fl.append(lambda e, h=h, c=c, r0=r0, si_=si_: e.matmul(
                        PT0[r0:r0 + 64, c * 128:(c + 1) * 128], lhsT=Ktm2[:, si_, c, r0:r0 + 64], rhs=V[:, h * 128:(h + 1) * 128],
                        start=True, stop=True))
                yield O("tensor", fl, reads=["Ktm", nVa], writes=["t0"], banks=["t0"])
                fl = []
                for h in range(4):
                    fl.append(lambda e, h=h, si_=si_: e.matmul(
                        PT1[:, h * 128:(h + 1) * 128], lhsT=Ktm2[:, si_, 2 + h, :], rhs=V[:, (4 + h) * 128:(5 + h) * 128],
                        start=True, stop=True))
                yield O("tensor", fl, reads=["Ktm", nVh], writes=["t1"], banks=["t1"])
                for c in range(2):
                    yield O("vector", lambda e, c=c, si_=si_, st=st: e.scalar_tensor_tensor(
                        out=S_all[:, st, c, :], in0=PT0[:, c * 128:(c + 1) * 128], scalar=sm[:, 2, c, si_:si_ + 1], in1=S_all[:, st, c, :],
                        op0=ALU.mult, op1=ALU.add),
                        reads=["t0", smE], writes=[f"S{st}"], banks=["t0"])
                for h in range(4):
                    yield O("vector", lambda e, h=h, si_=si_, st=st: e.scalar_tensor_tensor(
                        out=S_all[:, st, 2 + h, :], in0=PT1[:, h * 128:(h + 1) * 128], scalar=sm[:, 2, 2 + h, si_:si_ + 1],
                        in1=S_all[:, st, 2 + h, :], op0=ALU.mult, op1=ALU.add),
                        reads=["t1", smE], writes=[f"S{st}h"], banks=["t1"])
            for hf, gname in ((0, ngza), (1, ngzh)):
                zs = slice(hf * 512, (hf + 1) * 512)
                yield O("scalar", lambda e, zs=zs: e.activation(out=ztmp[:, zs], in_=ez[:, zs], func=AF.Exp, scale=-1.0),
                        reads=[gname], writes=[f"ztmp{hf}"])
                yield O("scalar", lambda e, zs=zs: e.activation(out=ztmp[:, zs], in_=ztmp[:, zs], func=AF.Ln, bias=1.0),
                        reads=[f"ztmp{hf}"], writes=[f"ztmp{hf}"])
                yield O("scalar", lambda e, zs=zs: e.activation(out=ztmp[:, zs], in_=ztmp[:, zs], func=AF.Exp, scale=-1.0),
                        reads=[f"ztmp{hf}"], writes=[f"ztmp{hf}"])
                yield O("vector", lambda e, zs=zs: e.tensor_tensor(out=ez[:, zs], in0=ez[:, zs], in1=ztmp[:, zs], op=ALU.mult),
                        reads=[f"ztmp{hf}"], writes=[gname])
            yield O("scalar", lambda e: e.activation(out=sq[:, 0:512], in_=PB[:, 0:512], func=AF.Square),
                 reads=["b0"], writes=["sqa"], banks=["b0"])
            yield O("scalar", lambda e: e.activation(out=sq[:, 512:1024], in_=PB[:, 512:1024], func=AF.Square),
                 reads=["b1"], writes=["sqh"], banks=["b1"])
            yield O("vector", lambda e: e.reduce_sum(out=so[:, 0:8], in_=v3(sq[:]), axis=AX.X), reads=["sqa", "sqh"], writes=["so0"])
            yield O("scalar", lambda e: e.activation(out=so[:, 8:16], in_=so[:, 0:8], func=AF.Ln, scale=1.0 / 128, bias=EPS),
                 reads=["so0"], writes=["so1"])
            yield O("scalar", lambda e: e.activation(out=so[:, 16:24], in_=so[:, 8:16], func=AF.Exp, scale=-0.5),
                 reads=["so1"], writes=["so2"])
            for h in range(8):
                yield O("vector", lambda e, h=h: e.scalar_tensor_tensor(
                    out=ohat[:, h * 128:(h + 1) * 128], in0=PB[:, h * 128:(h + 1) * 128], scalar=so[:, 16 + h:17 + h],
                    in1=ez[:, h * 128:(h + 1) * 128], op0=ALU.mult, op1=ALU.mult),
                    reads=["b0" if h < 4 else "b1", "so2", ngza if h < 4 else ngzh], writes=["ohata" if h < 4 else "ohath"],
                    banks=["b0" if h < 4 else "b1"])
            yield O("tensor", [lambda e, j=j: e.transpose(out=PT03[:, j, :], in_=ohat[:, j * 128:(j + 1) * 128], identity=identb[:])
                            for j in range(4)], reads=["ohata", "identb"], writes=["t0"], banks=["t0"])
            yield O("scalar", lambda e: e.activation(out=ohT[:, 0:4, :], in_=PT03[:, 0:4, :], func=AF.Copy),
                 reads=["t0"], writes=["ohTa"], banks=["t0"])
            yield O("tensor", [lambda e, j=j: e.transpose(out=PT13[:, j, :], in_=ohat[:, (4 + j) * 128:(5 + j) * 128], identity=identb[:])
                            for j in range(4)], reads=["ohath", "identb"], writes=["t1"], banks=["t1"])
            yield O("vector", lambda e: e.tensor_copy(out=ohT[:, 4:8, :], in_=PT13[:, 0:4, :]), reads=["t1"], writes=["ohTh"], banks=["t1"])
            for n_ in range(2):
                yield O("tensor", [lambda e, j=j, n_=n_: e.matmul(PB[:, n_ * 512:(n_ + 1) * 512], lhsT=ohT[:, j, :],
                                                                 rhs=w_out_bf[:, j, n_ * 512:(n_ + 1) * 512], start=(j == 0), stop=(j == 7))
                                for j in range(8)], reads=["ohTa", "ohTh"] + WOUT, writes=[f"b{n_}"], banks=[f"b{n_}"])
            yield O("scalar", lambda e: e.activation(out=ohat[:], in_=PB[:, :], func=AF.Square, accum_out=statB[:, 0:1]),
                 reads=["b0", "b1"], writes=["ohata", "ohath", "sb0"], banks=["b0", "b1"])
            yield O("scalar", lambda e: e.activation(out=statB[:, 1:2], in_=statB[:, 0:1], func=AF.Ln, scale=1.0 / D, bias=EPS),
                 reads=["sb0"], writes=["sb1"])
            yield O("scalar", lambda e: e.activation(out=statB[:, 2:3], in_=statB[:, 1:2], func=AF.Exp, scale=-0.5),
                 reads=["sb1"], writes=["sb2"])
            yield O("vector", lambda e: e.scalar_tensor_tensor(out=sq[:], in0=PB[:, :], scalar=statB[:, 2:3], in1=GG[gg][:],
                                                            op0=ALU.mult, op1=ALU.mult),
                 reads=["b0", "b1", "sb2", f"GG{gg}"], writes=["sqa", "sqh"], banks=["b0", "b1"])
            yield O("gpsimd", lambda e: e.tensor_tensor(out=xs_[:], in0=xs_[:], in1=sq[:], op=ALU.add), reads=[xb, "sqa", "sqh"], writes=[xb])
            yield O("sync", lambda e: e.dma_start(out=ctx["ydst"], in_=xs_[:]), reads=[xb], dma_sem=f"yst{slot}")
            k = ctx["k"]
            if k is not None:
                for st, gdst, hdst in ((2 + 2 * k, sgs[2 * k], shs[2 * k]), (3 + 2 * k, sgs[2 * k + 1], shs[2 * k + 1])):
                    yield O("sync", lambda e, st=st, gdst=gdst: e.dma_start(out=gdst.rearrange("c p v -> p c v"), in_=S_all[:, st, 0:2, :]),
                            reads=[f"S{st}"], dma_sem="sout")
                    yield O("sync", lambda e, st=st, hdst=hdst: e.dma_start(out=hdst.rearrange("c p v -> p c v"), in_=S_all[:, st, 2:6, :]),
                            reads=[f"S{st}h"], dma_sem="sout")

        def store_state(st, gdst, hdst):
            P.op("sync", lambda e: e.dma_start(out=gdst.rearrange("c p v -> p c v"), in_=S_all[:, st, 0:2, :]),
                 reads=[f"S{st}"], dma_sem="sout")
            P.op("sync", lambda e: e.dma_start(out=hdst.rearrange("c p v -> p c v"), in_=S_all[:, st, 2:6, :]),
                 reads=[f"S{st}h"], dma_sem="sout")

        tiles = []
        for k in range(2):
            segs = [dict(lo=0, n=64, b=2 + 2 * k, st=2 + 2 * k), dict(lo=64, n=64, b=3 + 2 * k, st=3 + 2 * k)]
            tiles.append(dict(xsrc=xs[2 * k:2 * k + 2].rearrange("b t d -> (b t) d"), ydst=ys[2 * k:2 * k + 2].rearrange("b t d -> (b t) d"),
                              segs=segs, gg=2 + k, k=k))
        for t in range(tp_tiles):
            for s in range(2):
                tiles.append(dict(xsrc=xp[s, t * 128:(t + 1) * 128, :], ydst=yp[s, t * 128:(t + 1) * 128, :],
                                  segs=[dict(lo=0, n=64, b=s, st=s), dict(lo=64, n=64, b=s, st=s)], gg=s, k=None))
        for i, t in enumerate(tiles):
            t["i"] = i
        NT = len(tiles)
        PRE = 2
        import os as _os2
        _os_kverb = bool(_os2.environ.get("KVERB2"))
        ALPHA = float(_os2.environ.get("KALPHA", "0.0"))
        for i in range(min(PRE, NT)):
            load_x(i, tiles[i]["xsrc"])
        for r in range(NT + 1):
            if r + PRE < NT:
                load_x(r + PRE, tiles[r + PRE]["xsrc"])
            if _os_kverb:
                print("round", r, "model t_us", {k: round(v / 1e3, 1) for k, v in P.eng_free.items()})
            if r == 0:
                for d_ in stage1(tiles[0]):
                    P.op(d_[0], d_[1], reads=d_[2], writes=d_[3], banks=d_[4], dma_sem=d_[5])
            streams = []
            if r >= 1:
                streams.append([list(stage34(tiles[r - 1])), 0])
            if r < NT:
                streams.append([list(stage2(tiles[r])), 0])
            pending = [list(stage1(tiles[r + 1])), 0] if r + 1 < NT else None
            if pending is not None and r >= NT:
                streams.append(pending)
                pending = None
            while streams:
                best, bk = None, None
                for st_ in list(streams):
                    ops, k = st_
                    while k < len(ops) and ops[k][0] == "MARK":
                        k += 1
                        st_[1] = k
                        if pending is not None:
                            streams.append(pending)
                            pending = None
                    if k >= len(ops):
                        streams.remove(st_)
                        continue
                for st_ in streams:
                    ops, k = st_
                    key = P.est_start(ops[k])
                    if bk is None or key < bk:
                        best, bk = st_, key
                if best is None:
                    break
                ops, k = best
                eng, fns, reads, writes, banks, dma_sem = ops[k]
                P.op(eng, fns, reads=reads, writes=writes, banks=banks, dma_sem=dma_sem)
                best[1] = k + 1
                if best[1] >= len(ops):
                    streams.remove(best)
            if pending is not None:
                for d_ in pending[0]:
                    P.op(d_[0], d_[1], reads=d_[2], writes=d_[3], banks=d_[4], dma_sem=d_[5])
        for s in range(2):
            store_state(s, sgp[s], shp[s])
        for nm, s in list(P.sems.items()):
            if nm.startswith("yst") or nm == "sout":
                P.wait_token("sync", (nm, s[1]))
        with nc.Block() as block:
            P.replay(block)
        P.close()
        import os as _os
        if _os.environ.get("KVERB"):
            print("total ops recorded", P.count, "model makespan us", max(P.eng_free.values()) / 1e3)
    return nc


def host_inputs(core, x_prompt, x_sample, c_prompt, c_sample, state_gla, state_hgrn, w_ada, b_ada, g_pre,
                w_in, w_alpha, b_alpha, g_onorm_gla, hgrn_lb_logits, g_onorm_hgrn, w_out, g_post, consts):
    f = np.float32
    c6 = np.concatenate([c_prompt[2 * core:2 * core + 2], c_sample[4 * core:4 * core + 4]], 0)

    def pj(a):
        return np.ascontiguousarray(a.reshape(8, 128, a.shape[1]).transpose(1, 0, 2))

    m = {
        "xp": np.ascontiguousarray(x_prompt[2 * core:2 * core + 2]),
        "xs": np.ascontiguousarray(x_sample[4 * core:4 * core + 4]),
        "cT": np.ascontiguousarray(c6.T.reshape(8, 128, 6).transpose(1, 0, 2)),
        "stg": np.ascontiguousarray(state_gla[0, 4 * core:4 * core + 4].reshape(4, 2, 128, 128)),
        "sth": np.ascontiguousarray(state_hgrn[0, 4 * core:4 * core + 4]),
        "wada": pj(w_ada[0]),
        "bada": np.ascontiguousarray(np.broadcast_to(b_ada[0][None, :], (6, 3072))),
        "gpre": np.ascontiguousarray(g_pre[0].reshape(8, 128).T),
        "win": pj(w_in[0]),
        "walpha": np.ascontiguousarray(w_alpha[0]),
        "balpha": np.ascontiguousarray(b_alpha[0].reshape(2, 128).T),
        "gon": np.ascontiguousarray(np.stack([g_onorm_gla[0], g_onorm_hgrn[0]], 1)),
        "lbl": np.ascontiguousarray(hgrn_lb_logits.reshape(2, 4, 128).transpose(2, 0, 1)),
        "wout": pj(w_out[0]),
        "gpost": np.ascontiguousarray(np.broadcast_to(g_post[0][None, :], (128, 1024))),
    }
    m.update(consts)
    return {k: np.ascontiguousarray(v, dtype=f) for k, v in m.items()}


def make_consts():
    f = np.float32
    maskp = np.triu(np.ones((128, 128), f))
    masks = maskp.copy()
    masks[0:64, 64:128] = 0.0
    smaskp = np.ones((128, 768), f)
    smaskp[:, 0::128] = 0.0
    smasks = smaskp.copy()
    smasks[:, 64::128] = 0.0
    sel = np.zeros((6, 4, 128), f)
    sel[0, 0, :] = 1.0
    sel[1, 1, :] = 1.0
    sel[2, 2, 0:64] = 1.0
    sel[3, 2, 64:128] = 1.0
    sel[4, 3, 0:64] = 1.0
    sel[5, 3, 64:128] = 1.0
    return {"identf": np.eye(128, dtype=f), "maskp": maskp, "masks": masks, "smaskp": smaskp, "smasks": smasks, "sel": sel}


def assemble(results, TP):
    f = np.float32
    yp = np.concatenate([r["yp"] for r in results], 0).astype(f)
    ys = np.concatenate([r["ys"] for r in results], 0).astype(f)
    sgp = np.concatenate([r["sgp"].reshape(2, 4, 64, 128) for r in results], 0)[None].astype(f)
    shp = np.concatenate([r["shp"] for r in results], 0)[None].astype(f)
    sgs = np.concatenate([r["sgs"].reshape(4, 4, 64, 128) for r in results], 0)[None].astype(f)
    shs = np.concatenate([r["shs"] for r in results], 0)[None].astype(f)
    return (yp, ys, sgp, shp, sgs, shs)


def kernel(**inputs):
    inputs = {k: np.asarray(v) for k, v in inputs.items()}
    TP = inputs["x_prompt"].shape[1]
    nc = build(TP // 128)
    consts = make_consts()
    in_maps = [host_inputs(i, consts=consts, **inputs) for i in range(N_CORES)]
    res = run_bass_kernel_spmd(nc, in_maps, core_ids=list(range(N_CORES)))
    return assemble(res.results, TP)
```

```python
from contextlib import ExitStack

import numpy as np
import concourse.bass as bass
import concourse.mybir as mybir
from concourse.bass_utils import run_bass_kernel_spmd

F32 = mybir.dt.float32
BF16 = mybir.dt.bfloat16
AF = mybir.ActivationFunctionType
ALU = mybir.AluOpType
AX = mybir.AxisListType

D = 1024
NCOL = 3600
EPS = 1e-6
N_CORES = 8
SEQ = 4096
ENGS = ("sync", "scalar", "vector", "gpsimd", "tensor")

C_QA, C_KA, C_VA, C_ZA, C_AL, C_QH, C_FH, C_IH, C_ZH = 0, 256, 512, 1024, 1536, 1552, 2064, 2576, 3088


class _Probe:
    def __init__(self):
        self.calls = []

    def __getattr__(self, name):
        def f(*a, **k):
            out = k.get("out", a[0] if a else None)
            self.calls.append((name, out, k))
            return None
        return f


def _ap_n(ap):
    try:
        n = 1
        for d in ap.shape[1:]:
            n *= int(d)
        return n
    except Exception:
        return 0


def _est_ns(eng, fns):
    pr = _Probe()
    for f in fns:
        try:
            f(pr)
        except Exception:
            pass
    tot = 0.0
    for name, out, k in pr.calls:
        n = max([_ap_n(out)] + [_ap_n(k.get(kk)) for kk in ("in_", "in0", "data0", "rhs")] + [1])
        if eng == "tensor":
            tot += 95.0 if name == "transpose" else 15.0 + 0.43 * n
        elif eng == "scalar":
            tot += (480.0 if n <= 16 else 200.0 + 0.75 * n) + (100.0 if k.get("accum_out") is not None else 0.0)
        elif eng == "vector":
            tot += (60.0 + 2.1 * n) if name == "tensor_tensor_scan" else 150.0 + 1.04 * n
        elif eng == "gpsimd":
            tot += (100.0 + 1.0 * n) if name == "tensor_scalar" else 100.0 + 2.0 * n
        else:
            tot += 2000.0
    return max(tot, 50.0)


class Prog:
    def __init__(self, nc):
        self.nc = nc
        self.q = {e: [] for e in ENGS}
        self.sems = {}
        self.waited = {e: {} for e in ENGS}
        self.bufs = {}
        self.bank_last = {}
        self._cms = []
        self.eng_free = {e: 0.0 for e in ENGS}
        self.tok_time = {}
        import os as _os
        self.limit = int(_os.environ.get("KLIMIT", "0")) or None
        self.count = 0

    def sem(self, name):
        if name not in self.sems:
            cm = self.nc.semaphore(name)
            h = cm.__enter__()
            self._cms.append(cm)
            self.sems[name] = [h, 0]
        return self.sems[name]

    def close(self):
        for cm in reversed(self._cms):
            cm.__exit__(None, None, None)

    def _deps(self, eng, reads, writes, is_dma, banks):
        deps = []
        for b in banks:
            t = self.bank_last.get(b)
            if t is not None and t[2] != eng:
                deps.append((t, "bank"))
        for b in reads:
            st = self.bufs.get(b)
            if st and st[0] is not None:
                deps.append((st[0], "raw"))
        for b in writes:
            st = self.bufs.get(b)
            if st:
                if st[0] is not None:
                    deps.append((st[0], "waw"))
                for t in st[1]:
                    deps.append((t, "war"))
        waits = []
        for tok, kind in deps:
            sname, val, teng, tdma = tok
            if not tdma and teng == eng and not is_dma:
                if eng == "tensor" or kind in ("war", "waw"):
                    continue
            if self.waited[eng].get(sname, 0) >= val:
                continue
            self.waited[eng][sname] = val
            waits.append((sname, val))
        return waits

    def _dep_tokens(self, eng, reads, writes, banks):
        toks = []
        for b in banks:
            t = self.bank_last.get(b)
            if t is not None:
                toks.append(t)
        for b in reads:
            st = self.bufs.get(b)
            if st and st[0] is not None:
                toks.append(st[0])
        for b in writes:
            st = self.bufs.get(b)
            if st:
                if st[0] is not None:
                    toks.append(st[0])
                toks.extend(st[1])
        return toks

    def _ready(self, eng, toks):
        t = self.eng_free[eng]
        for tok in toks:
            tt = self.tok_time.get((tok[0], tok[1]), 0.0) + (0.0 if tok[2] == eng else 150.0)
            if tt > t:
                t = tt
        return t

    def est_start(self, desc):
        eng, fns, reads, writes, banks, dma_sem = desc
        return self._ready(eng, self._dep_tokens(eng, reads, writes, banks))

    def op(self, eng, fns, reads=(), writes=(), dma_sem=None, banks=()):
        if callable(fns):
            fns = [fns]
        _t0 = self._ready(eng, self._dep_tokens(eng, reads, writes, banks))
        _dur = _est_ns(eng, fns)
        self.count += 1
        if self.limit is not None and self.count > self.limit:
            return None
        is_dma = dma_sem is not None
        waits = self._deps(eng, reads, writes, is_dma, banks)
        if is_dma:
            s = self.sem(dma_sem)
            s[1] += 16
            tok = (dma_sem, s[1], eng, True)
            inc = (dma_sem, 16)
        else:
            sname = "p_" + eng
            s = self.sem(sname)
            s[1] += 1
            tok = (sname, s[1], eng, False)
            inc = (sname, 1)
        self.q[eng].append((waits, fns, inc))
        if is_dma:
            self.eng_free[eng] = _t0 + 60.0
        else:
            self.eng_free[eng] = _t0 + _dur
        self.tok_time[(tok[0], tok[1])] = _t0 + _dur
        for b in banks:
            self.bank_last[b] = tok
        for b in writes:
            self.bufs[b] = [tok, []]
        for b in reads:
            if b in writes:
                continue
            self.bufs.setdefault(b, [None, []])[1].append(tok)
        return tok

    def retoken(self, names, tok):
        for b in names:
            self.bufs[b] = [tok, []]

    def wait_token(self, eng, tok):
        sname, val = tok[0], tok[1]
        if self.waited[eng].get(sname, 0) >= val:
            return
        self.waited[eng][sname] = val
        self.q[eng].append(([(sname, val)], [], None))

    def barrier(self):
        snap = [(n, s[1]) for n, s in self.sems.items() if s[1] > 0]
        for e in ENGS:
            w = []
            for n, v in snap:
                if self.waited[e].get(n, 0) < v:
                    self.waited[e][n] = v
                    w.append((n, v))
            if w:
                self.q[e].append((w, [], None))

    def replay(self, block):
        P = self

        def run(engobj, name):
            for waits, fns, inc in P.q[name]:
                for sname, val in waits:
                    engobj.wait_ge(P.sems[sname][0], val)
                ins = None
                for f in fns:
                    ins = f(engobj)
                if inc is not None and ins is not None:
                    ins.then_inc(P.sems[inc[0]][0], inc[1])

        @block.sync
        def _(e):
            run(e, "sync")

        @block.scalar
        def _(e):
            run(e, "scalar")

        @block.vector
        def _(e):
            run(e, "vector")

        @block.gpsimd
        def _(e):
            run(e, "gpsimd")

        @block.tensor
        def _(e):
            run(e, "tensor")


def v3(ap, t=128):
    return ap.rearrange("p (c t) -> p c t", t=t)


def build(tp_tiles):
    TP = tp_tiles * 128
    nc = bass.Bass("TRN2", target_bir_lowering=False)

    def din(name, shape):
        return nc.dram_tensor(name, shape, F32, kind="ExternalInput").ap()

    def dout(name, shape):
        return nc.dram_tensor(name, shape, F32, kind="ExternalOutput").ap()

    xp = din("xp", [2, TP, D])
    xs = din("xs", [4, 64, D])
    cT_d = din("cT", [128, 8, 6])
    stg = din("stg", [4, 2, 128, 128])
    sth = din("sth", [4, 4, 128, 128])
    wada = din("wada", [128, 8, 3072])
    bada = din("bada", [6, 3072])
    gpre_d = din("gpre", [128, 8])
    win = din("win", [128, 8, NCOL])
    walpha_d = din("walpha", [16, 256])
    balpha_d = din("balpha", [128, 2])
    gon_d = din("gon", [128, 2])
    lbl_d = din("lbl", [128, 2, 4])
    wout = din("wout", [128, 8, D])
    gpost_d = din("gpost", [128, D])
    identf_d = din("identf", [128, 128])
    maskp_d = din("maskp", [128, 128])
    masks_d = din("masks", [128, 128])
    smaskp_d = din("smaskp", [128, 768])
    smasks_d = din("smasks", [128, 768])
    sel_d = din("sel", [6, 4, 128])

    yp = dout("yp", [2, TP, D])
    ys = dout("ys", [4, 64, D])
    sgp = dout("sgp", [2, 2, 128, 128])
    shp = dout("shp", [2, 4, 128, 128])
    sgs = dout("sgs", [4, 2, 128, 128])
    shs = dout("shs", [4, 4, 128, 128])

    es = ExitStack()
    with es:
        def sb(name, shape, dt=F32):
            return es.enter_context(nc.sbuf_tensor(name, shape, dt))

        def ps(name, shape, dt=F32):
            return es.enter_context(nc.psum_tensor(name, shape, dt))

        P = Prog(nc)

        w_in_bf = sb("w_in_bf", [128, 8, NCOL], BF16)
        w_out_bf = sb("w_out_bf", [128, 8, D], BF16)
        walpha_bf = sb("walpha_bf", [128, 256], BF16)
        identf = sb("identf_sb", [128, 128])
        identb = sb("identb", [128, 128], BF16)
        maskp = sb("maskp_sb", [128, 128])
        masks = sb("masks_sb", [128, 128])
        smaskp = sb("smaskp_sb", [128, 768])
        smasks = sb("smasks_sb", [128, 768])
        cst = sb("cst", [128, 32])
        nbalpha = cst[:, 0:2]
        lb = cst[:, 2:6]
        ln1mlb = cst[:, 6:10]
        balpha = cst[:, 10:12]
        gon = cst[:, 12:14]
        tmp4 = cst[:, 14:18]
        tmp4b = cst[:, 18:22]
        lbl = sb("lbl_sb", [128, 2, 4])
        gpre = sb("gpre_sb", [128, 8])
        aT = sb("aT", [128, 8, 6])
        sT = sb("sT", [128, 8, 6])
        GG = [sb(f"GG{g}", [128, D]) for g in range(4)]
        S_all = sb("S_all", [128, 6, 6, 128])

        PA = ps("PA", [128, 1024])
        PB = ps("PB", [128, 1024])
        PC = ps("PC", [128, 512])
        PD = ps("PD", [128, 512])
        PT0 = ps("PT0", [128, 512])
        PT1 = ps("PT1", [128, 512])

        ses = ExitStack()
        with ses:
            def ssb(name, shape, dt=F32):
                return ses.enter_context(nc.sbuf_tensor(name, shape, dt))
            NSTG = 6
            stage = [ssb(f"stage{i}", [128, 2048]) for i in range(NSTG)]
            mod_sb = ssb("mod_sb", [6, 3072])
            bada_sb = ssb("bada_sb", [6, 3072])
            gpost = ssb("gpost_sb", [128, D])
            cT = ssb("cT_sb", [128, 8, 6])
            sel = ssb("sel_sb", [6, 4, 128])
            tmp48 = ssb("tmp48", [128, 8, 6])

            small = [
                (cT[:], cT_d, "cT"), (bada_sb[:], bada, "bada"), (gpre[:], gpre_d, "gpre"),
                (balpha, balpha_d, "balpha"), (gon, gon_d, "gon"), (lbl[:], lbl_d, "lbl"),
                (identf[:], identf_d, "identf"), (maskp[:], maskp_d, "maskp"), (masks[:], masks_d, "masks"),
                (smaskp[:], smaskp_d, "smaskp"), (smasks[:], smasks_d, "smasks"), (sel[:], sel_d, "sel"),
                (gpost[:], gpost_d, "gpost"),
            ]
            names = []
            tok = None
            for o_, i_, nm in small:
                tok = P.op("sync", lambda e, o_=o_, i_=i_: e.dma_start(out=o_, in_=i_), writes=[nm], dma_sem="ld_s")
                names.append(nm)
            walpha_f = ssb("walpha_f", [16, 256])
            stage_wa = walpha_f[:]
            tok = P.op("sync", lambda e: e.dma_start(out=stage_wa, in_=walpha_d), writes=["walpha_f"], dma_sem="ld_s")
            names.append("walpha_f")
            for b in range(4):
                tok = P.op("sync", lambda e, b=b: e.dma_start(out=S_all[:, 2 + b, 0:2, :], in_=stg[b].rearrange("c p v -> p c v")),
                           writes=[f"S{2 + b}"], dma_sem="ld_s")
                tok = P.op("sync", lambda e, b=b: e.dma_start(out=S_all[:, 2 + b, 2:6, :], in_=sth[b].rearrange("c p v -> p c v")),
                           writes=[f"S{2 + b}h"], dma_sem="ld_s")
            P.retoken(names + [f"S{2 + b}" for b in range(4)] + [f"S{2 + b}h" for b in range(4)], tok)

            P.op("vector", lambda e: e.tensor_copy(out=identb[:], in_=identf[:]), reads=["identf"], writes=["identb"])
            P.op("gpsimd", lambda e: e.memset(walpha_bf[:], 0.0), writes=["walpha_bf"])
            P.op("vector", lambda e: e.tensor_copy(out=walpha_bf[0:16, :], in_=stage_wa), reads=["walpha_f", "walpha_bf"], writes=["walpha_bf"])
            P.op("vector", lambda e: e.tensor_scalar(out=nbalpha, in0=balpha, scalar1=-1.0, scalar2=None, op0=ALU.mult),
                 reads=["balpha"], writes=["nbalpha"])
            P.op("vector", lambda e: e.tensor_tensor(out=tmp4, in0=lbl[:, 1, :], in1=lbl[:, 0, :], op=ALU.subtract),
                 reads=["lbl"], writes=["tmp4"])
            P.op("scalar", lambda e: e.activation(out=tmp4, in_=tmp4, func=AF.Exp), reads=["tmp4"], writes=["tmp4"])
            P.op("vector", lambda e: e.tensor_scalar(out=tmp4b, in0=tmp4, scalar1=1.0, scalar2=None, op0=ALU.add),
                 reads=["tmp4"], writes=["tmp4b"])
            P.op("vector", lambda e: e.reciprocal(out=lb, in_=tmp4b), reads=["tmp4b"], writes=["lb"])
            P.op("vector", lambda e: e.tensor_tensor(out=tmp4b, in0=tmp4, in1=lb, op=ALU.mult), reads=["tmp4", "lb"], writes=["tmp4b"])
            P.op("scalar", lambda e: e.activation(out=ln1mlb, in_=tmp4b, func=AF.Ln), reads=["tmp4b"], writes=["ln1mlb"])
            P.op("gpsimd", lambda e: e.memset(S_all[:, 0:2, :, :], 0.0), writes=["S0", "S0h", "S1", "S1h"])

            si = 0
            for n in range(12):
                stg_ = stage[si % NSTG]
                sname = f"stage{si % NSTG}"
                st3 = stg_[:, 0:2048].rearrange("p (j n) -> p j n", n=256)
                P.op("sync" if si % 2 == 0 else "scalar", lambda e, st3=st3, n=n: e.dma_start(out=st3, in_=wada[:, :, n * 256:(n + 1) * 256]),
                     writes=[sname], dma_sem="ld_" + sname)
                P.op("tensor", [lambda e, j=j, st3=st3: e.matmul(PC[0:6, 0:256], lhsT=cT[:, j, :], rhs=st3[:, j, :],
                                                                  start=(j == 0), stop=(j == 7)) for j in range(8)],
                     reads=[sname, "cT"], writes=["pc"], banks=["c"])
                P.op("vector", lambda e, n=n: e.tensor_tensor(out=mod_sb[0:6, n * 256:(n + 1) * 256], in0=PC[0:6, 0:256],
                                                             in1=bada_sb[0:6, n * 256:(n + 1) * 256], op=ALU.add),
                     reads=["pc", "bada"], writes=["mod"], banks=["c"])
                si += 1
            P.op("tensor", [lambda e, k=k: e.transpose(out=PD[:, k * 6:(k + 1) * 6], in_=mod_sb[0:6, k * 128:(k + 1) * 128],
                                                       identity=identf[0:6, 0:6]) for k in range(16)],
                 reads=["mod", "identf"], writes=["pd"], banks=["d"])
            P.op("vector", lambda e: e.tensor_copy(out=sT[:], in_=PD[:, 0:48].rearrange("p (j b) -> p j b", b=6)),
                 reads=["pd"], writes=["sT"], banks=["d"])
            P.op("vector", lambda e: e.tensor_scalar(out=tmp48[:], in0=PD[:, 48:96].rearrange("p (j b) -> p j b", b=6),
                                                     scalar1=1.0, scalar2=None, op0=ALU.add),
                 reads=["pd"], writes=["tmp48"], banks=["d"])
            P.op("vector", lambda e: e.tensor_tensor(out=aT[:], in0=tmp48[:], in1=gpre[:].unsqueeze(2).broadcast_to([128, 8, 6]),
                                                     op=ALU.mult), reads=["tmp48", "gpre"], writes=["aT"])
            for g in range(4):
                for n in range(2):
                    P.op("tensor", lambda e, g=g, n=n: e.matmul(PC[:, 0:512], lhsT=sel[0:6, g, :],
                                                                  rhs=mod_sb[0:6, 2048 + n * 512:2048 + (n + 1) * 512],
                                                                  start=True, stop=True),
                         reads=["mod", "sel"], writes=["pc"], banks=["c"])
                    P.op("vector", lambda e, g=g, n=n: e.tensor_tensor(out=GG[g][:, n * 512:(n + 1) * 512], in0=PC[:, 0:512],
                                                                      in1=gpost[:, n * 512:(n + 1) * 512], op=ALU.mult),
                         reads=["pc", "gpost"], writes=[f"GG{g}"], banks=["c"])
            for j in range(8):
                for hlf in range(2):
                    stg_ = stage[si % NSTG]
                    sname = f"stage{si % NSTG}"
                    c0 = hlf * 1800
                    P.op("sync" if si % 2 == 0 else "scalar", lambda e, stg_=stg_, j=j, c0=c0: e.dma_start(out=stg_[:, 0:1800], in_=win[:, j, c0:c0 + 1800]),
                         writes=[sname], dma_sem="ld_" + sname)
                    eng = "vector" if hlf == 0 else "scalar"
                    if eng == "vector":
                        P.op("vector", lambda e, stg_=stg_, j=j, c0=c0: e.tensor_copy(out=w_in_bf[:, j, c0:c0 + 1800], in_=stg_[:, 0:1800]),
                             reads=[sname], writes=[f"win{j}_{hlf}"])
                    else:
                        P.op("scalar", lambda e, stg_=stg_, j=j, c0=c0: e.activation(out=w_in_bf[:, j, c0:c0 + 1800], in_=stg_[:, 0:1800],
                                                                                      func=AF.Copy),
                             reads=[sname], writes=[f"win{j}_{hlf}"])
                    si += 1
            for jj in range(4):
                stg_ = stage[si % NSTG]
                sname = f"stage{si % NSTG}"
                st3 = stg_[:, 0:2048].rearrange("p (j n) -> p j n", n=1024)
                P.op("sync", lambda e, st3=st3, jj=jj: e.dma_start(out=st3, in_=wout[:, 2 * jj:2 * jj + 2, :]),
                     writes=[sname], dma_sem="ld_" + sname)
                for jl in range(2):
                    j = 2 * jj + jl
                    gcol = gon[:, 0:1] if j < 4 else gon[:, 1:2]
                    P.op("gpsimd", lambda e, st3=st3, jl=jl, j=j, gcol=gcol: e.tensor_scalar(
                        out=w_out_bf[:, j, :], in0=st3[:, jl, :], scalar1=gcol, scalar2=1.0, op0=ALU.mult, op1=ALU.mult),
                        reads=[sname, "gon"], writes=[f"wout{j}"])
                si += 1
            P.barrier()
        WIN = [f"win{j}_{h}" for j in range(8) for h in range(2)]
        WOUT = [f"wout{j}" for j in range(8)]

        NXS = 4
        x_sb = [sb(f"x_sb{i}", [128, D]) for i in range(NXS)]
        statA = sb("statA", [128, 8])
        xn = sb("xn", [128, D], BF16)
        hT_ = [sb(f"hT{p}", [128, 8, 128], BF16) for p in range(2)]
        alr = sb("alr", [128, 128], BF16)
        e1 = sb("e1", [128, 256])
        eh = sb("eh", [128, 512])
        L1 = sb("L1", [128, 512])
        L2 = sb("L2", [128, 512])
        gT = sb("gT", [128, 768])
        bT = sb("bT", [128, 768])
        bTc = sb("bTc", [128, 768])
        eqb = sb("eqb", [128, 512])
        qh_sb = sb("qh_sb", [128, 512])
        qk_sb = sb("qk_sb", [128, 512])
        EQa = sb("EQa", [128, 256])
        EKa = sb("EKa", [128, 256])
        QT_ = [sb(f"QT{p}", [128, 4, 128], BF16) for p in range(2)]
        QTa2_ = [sb(f"QTa2{p}", [128, 2, 2, 128], BF16) for p in range(2)]
        KT_ = [sb(f"KT{p}", [128, 6, 128], BF16) for p in range(2)]
        V_ = [sb(f"V{p}", [128, D], BF16) for p in range(2)]
        ez_ = [sb(f"ez{p}", [128, D]) for p in range(2)]
        sm_ = [sb(f"sm{p}", [128, 3, 6, 2]) for p in range(2)]
        Ktm2 = sb("Ktm2", [128, 2, 6, 128], BF16)
        Sp = sb("Sp", [128, 2, 6, 128], BF16)
        AT = sb("AT", [128, 8, 128], BF16)
        sq = sb("sq", [128, D])
        so = sb("so", [128, 24])
        statB = sb("statB", [128, 8])
        ohat = sb("ohat", [128, D], BF16)
        ohT = sb("ohT", [128, 8, 128], BF16)
        ztmp = sb("ztmp", [128, D])

        bT3 = v3(bT[:])
        PT0b = PT0[:].bitcast(BF16)
        PT1b = PT1[:].bitcast(BF16)
        PCb = PC[:].bitcast(BF16)
        PDb = PD[:].bitcast(BF16)
        P.op("gpsimd", lambda e: e.memset(Ktm2[:], 0.0), writes=["Ktma", "Ktmh"])
        for p in range(2):
            P.op("gpsimd", lambda e, p=p: e.memset(QTa2_[p][:], 0.0), writes=[f"QTa{p}"])

        def O(eng, fns, reads=(), writes=(), banks=(), dma_sem=None):
            return (eng, fns, reads, writes, banks, dma_sem)

        def load_x(i, xsrc):
            slot = i % NXS
            P.op("sync", lambda e: e.dma_start(out=x_sb[slot][:], in_=xsrc), writes=[f"x{slot}"], dma_sem=f"xld{slot}")

        def stage1(ctx):
            i, segs, par = ctx["i"], ctx["segs"], ctx["i"] % 2
            slot = i % NXS
            xs_, xb = x_sb[slot], f"x{slot}"
            hT = hT_[par]
            if all(sg["b"] == segs[0]["b"] for sg in segs):
                mods = [(0, 128, segs[0]["b"])]
            else:
                mods = [(sg["lo"], sg["n"], sg["b"]) for sg in segs]
            yield O("scalar", lambda e: e.activation(out=xn[:], in_=xs_[:], func=AF.Square, accum_out=statA[:, 0:1]),
                 reads=[xb], writes=["xn", "sa0"])
            yield O("scalar", lambda e: e.activation(out=statA[:, 1:2], in_=statA[:, 0:1], func=AF.Ln, scale=1.0 / D, bias=EPS),
                 reads=["sa0"], writes=["sa1"])
            yield O("scalar", lambda e: e.activation(out=statA[:, 2:3], in_=statA[:, 1:2], func=AF.Exp, scale=-0.5),
                 reads=["sa1"], writes=["sa2"])
            yield O("scalar", lambda e: e.activation(out=xn[:], in_=xs_[:], func=AF.Copy, scale=statA[:, 2:3]),
                 reads=[xb, "sa2"], writes=["xn"])
            for half, (PTb, bank, eng) in enumerate(((PCb, "c", "scalar"), (PDb, "d", "vector"))):
                PT3 = v3(PTb)
                yield O("tensor", [lambda e, j=j, PT3=PT3, half=half: e.transpose(
                    out=PT3[:, j, :], in_=xn[:, (4 * half + j) * 128:(4 * half + j + 1) * 128], identity=identb[:]) for j in range(4)],
                    reads=["xn", "identb"], writes=[bank], banks=[bank])
                for j in range(4):
                    jj = 4 * half + j
                    for (lo, n, b) in mods:
                        if eng == "scalar":
                            yield O("scalar", lambda e, j=j, jj=jj, lo=lo, n=n, b=b, PT3=PT3: e.activation(
                                out=hT[:, jj, lo:lo + n], in_=PT3[:, j, lo:lo + n], func=AF.Identity,
                                scale=aT[:, jj, b:b + 1], bias=sT[:, jj, b:b + 1]),
                                reads=[bank, "aT", "sT"], writes=[f"hT{half}_{par}"], banks=[bank])
                        else:
                            yield O("vector", lambda e, j=j, jj=jj, lo=lo, n=n, b=b, PT3=PT3: e.tensor_scalar(
                                out=hT[:, jj, lo:lo + n], in0=PT3[:, j, lo:lo + n], scalar1=aT[:, jj, b:b + 1],
                                scalar2=sT[:, jj, b:b + 1], op0=ALU.mult, op1=ALU.add),
                                reads=[bank, "aT", "sT"], writes=[f"hT{half}_{par}"], banks=[bank])

        def stage2(ctx):
            i, segs, par = ctx["i"], ctx["segs"], ctx["i"] % 2
            hT = hT_[par]
            QT, QTa2, KT, V, ez, sm = QT_[par], QTa2_[par], KT_[par], V_[par], ez_[par], sm_[par]
            nQTh, nQTa, nKTa, nKTh, nVa, nVh, ngza, ngzh = (f"{n}{par}" for n in ("QTh", "QTa", "KTa", "KTh", "Va", "Vh", "gza", "gzh"))
            smE = f"smE{par}"
            HT = [f"hT0_{par}", f"hT1_{par}"]

            def fm(out_ap, col0, m):
                return [lambda e, j=j: e.matmul(out_ap, lhsT=w_in_bf[:, j, col0:col0 + m], rhs=hT[:, j, :],
                                                start=(j == 0), stop=(j == 7)) for j in range(8)]

            def tm(out_ap, col0):
                return [lambda e, j=j: e.matmul(out_ap, lhsT=hT[:, j, :], rhs=w_in_bf[:, j, col0:col0 + 512],
                                                start=(j == 0), stop=(j == 7)) for j in range(8)]

            yield O("tensor", fm(PA[:, 0:128], C_AL, 128), reads=HT + WIN, writes=["a0"], banks=["a0"])
            yield O("vector", lambda e: e.tensor_copy(out=alr[:], in_=PA[:, 0:128]), reads=["a0"], writes=["alr"], banks=["a0"])
            for c in range(4):
                yield O("tensor", fm(PA[:, 512 + c * 128:512 + (c + 1) * 128], C_FH + c * 128, 128), reads=HT + WIN, writes=["a1"], banks=["a1"])
            yield O("tensor", [lambda e, c=c: e.matmul(PA[:, 128 + c * 128:256 + c * 128], lhsT=walpha_bf[:, c * 128:(c + 1) * 128],
                                                    rhs=alr[:, :], start=True, stop=True) for c in range(2)],
                 reads=["alr", "walpha_bf"], writes=["a0"], banks=["a0"])
            yield O("scalar", lambda e: e.activation(out=eh[:], in_=PA[:, 512:1024], func=AF.Exp, scale=-1.0),
                 reads=["a1"], writes=["eh"], banks=["a1"])
            for c in range(4):
                yield O("tensor", fm(PC[:, c * 128:(c + 1) * 128], C_QH + c * 128, 128), reads=HT + WIN, writes=["c"], banks=["c"])
            for c in range(2):
                yield O("scalar", lambda e, c=c: e.activation(out=e1[:, c * 128:(c + 1) * 128], in_=PA[:, 128 + c * 128:256 + c * 128],
                                                           func=AF.Exp, scale=-1.0, bias=nbalpha[:, c:c + 1]),
                     reads=["a0", "nbalpha"], writes=["e1"], banks=["a0"])
            yield O("scalar", lambda e: e.activation(out=L1[:], in_=eh[:], func=AF.Ln, bias=1.0), reads=["eh"], writes=["L1"])
            for c in range(2):
                yield O("tensor", fm(PD[:, c * 128:(c + 1) * 128], C_QA + c * 128, 128), reads=HT + WIN, writes=["d"], banks=["d"])
            for c in range(2):
                yield O("tensor", fm(PD[:, (2 + c) * 128:(3 + c) * 128], C_KA + c * 128, 128), reads=HT + WIN, writes=["d"], banks=["d"])
            yield O("scalar", lambda e: e.activation(out=e1[:], in_=e1[:], func=AF.Ln, bias=1.0), reads=["e1"], writes=["e1"])
            yield O("gpsimd", lambda e: e.tensor_scalar(out=gT[:, 0:256], in0=e1[:], scalar1=-1.0 / 16.0, scalar2=1.0,
                                                       op0=ALU.mult, op1=ALU.mult), reads=["e1"], writes=["gTa"])
            for c in range(4):
                yield O("scalar", lambda e, c=c: e.activation(out=L2[:, c * 128:(c + 1) * 128], in_=eh[:, c * 128:(c + 1) * 128],
                                                           func=AF.Ln, bias=1.0, scale=lb[:, c:c + 1]),
                     reads=["eh", "lb"], writes=["L2"])
            yield O("scalar", lambda e: e.activation(out=qh_sb[:], in_=PC[:, :], func=AF.Copy), reads=["c"], writes=["qh_sb"], banks=["c"])
            yield O("scalar", lambda e: e.activation(out=qk_sb[:], in_=PD[:, :], func=AF.Copy), reads=["d"], writes=["qk_sb"], banks=["d"])
            yield ("MARK",)
            yield O("vector", lambda e: e.tensor_tensor(out=gT[:, 256:768], in0=L2[:], in1=L1[:], op=ALU.subtract),
                 reads=["L1", "L2"], writes=["gTh"])
            yield O("vector", lambda e: e.tensor_tensor(out=L1[:], in0=PA[:, 512:1024], in1=L1[:], op=ALU.add),
                 reads=["a1", "L1"], writes=["L1"], banks=["a1"])
            yield O("vector", lambda e: e.tensor_tensor_scan(out=bT[:], data0=smasks[:], data1=gT[:], initial=0.0,
                                                          op0=ALU.mult, op1=ALU.add),
                 reads=["gTa", "gTh", "smasks"], writes=["bT"])
            yield O("tensor", tm(PA[:, 0:512], C_ZA), reads=HT + WIN, writes=["a0"], banks=["a0"])
            yield O("tensor", tm(PA[:, 512:1024], C_VA), reads=HT + WIN, writes=["a1"], banks=["a1"])
            yield O("scalar", lambda e: e.activation(out=ez[:, 0:512], in_=PA[:, 0:512], func=AF.Copy), reads=["a0"], writes=[ngza], banks=["a0"])
            yield O("vector", lambda e: e.tensor_copy(out=V[:, 0:512], in_=PA[:, 512:1024]), reads=["a1"], writes=[nVa], banks=["a1"])
            yield O("tensor", tm(PA[:, 0:512], C_ZH), reads=HT + WIN, writes=["a0"], banks=["a0"])
            yield O("tensor", tm(PA[:, 512:1024], C_IH), reads=HT + WIN, writes=["a1"], banks=["a1"])
            bT4 = bT[:].rearrange("p (c s t) -> p c s t", s=2, t=64)
            bTc4 = bTc[:].rearrange("p (c s t) -> p c s t", s=2, t=64)
            yield O("vector", lambda e: e.tensor_tensor(out=bTc4, in0=bT4, in1=bT4[:, :, :, 31:32].broadcast_to([128, 6, 2, 64]),
                                                        op=ALU.subtract), reads=["bT"], writes=["bTc"])
            yield O("scalar", lambda e: e.activation(out=sm[:, 0, :, :], in_=bT4[:, :, :, 31], func=AF.Exp), reads=["bT"], writes=[smE])
            yield O("scalar", lambda e: e.activation(out=sm[:, 1, :, :], in_=bT4[:, :, :, 63], func=AF.Exp), reads=["bT"], writes=[smE])
            yield O("scalar", lambda e: e.activation(out=sm[:, 2, :, :], in_=bTc4[:, :, :, 63], func=AF.Exp), reads=["bTc"], writes=[smE])
            yield O("scalar", lambda e: e.activation(out=eqb[:], in_=qh_sb[:], func=AF.Exp, scale=-1.0),
                 reads=["qh_sb"], writes=["eqb"])
            yield O("scalar", lambda e: e.activation(out=eqb[:], in_=eqb[:], func=AF.Ln, bias=1.0), reads=["eqb"], writes=["eqb"])
            yield O("scalar", lambda e: e.activation(out=EQa[:], in_=bTc[:, 0:256], func=AF.Exp), reads=["bTc"], writes=["EQa"])
            yield O("scalar", lambda e: e.activation(out=EKa[:], in_=bTc[:, 0:256], func=AF.Exp, scale=-1.0), reads=["bTc"], writes=["EKa"])
            yield O("scalar", lambda e: e.activation(out=ez[:, 512:1024], in_=PA[:, 0:512], func=AF.Copy), reads=["a0"], writes=[ngzh], banks=["a0"])
            yield O("vector", lambda e: e.tensor_copy(out=V[:, 512:1024], in_=PA[:, 512:1024]), reads=["a1"], writes=[nVh], banks=["a1"])
            for hh in range(2):
                yield O("vector", lambda e, hh=hh: e.scalar_tensor_tensor(
                    out=QTa2[hh * 64:(hh + 1) * 64, hh, :, :], in0=v3(qk_sb[hh * 64:(hh + 1) * 64, 0:256]), scalar=0.125,
                    in1=v3(EQa[hh * 64:(hh + 1) * 64, :]), op0=ALU.mult, op1=ALU.mult),
                    reads=["qk_sb", "EQa"], writes=[nQTa])
            yield O("vector", lambda e: e.tensor_tensor(out=KT[:, 0:2, :], in0=v3(qk_sb[:, 256:512]), in1=v3(EKa[:]), op=ALU.mult),
                 reads=["qk_sb", "EKa"], writes=[nKTa])
            yield O("vector", lambda e: e.tensor_tensor(out=L1[:], in0=L1[:], in1=bTc[:, 256:768], op=ALU.add),
                 reads=["L1", "bTc"], writes=["L1"])
            for c in range(4):
                yield O("scalar", lambda e, c=c: e.activation(
                    out=KT[:, 2 + c, :], in_=L1[:, c * 128:(c + 1) * 128], func=AF.Exp, scale=-1.0, bias=ln1mlb[:, c:c + 1]),
                    reads=["L1", "ln1mlb"], writes=[nKTh])
            yield O("vector", lambda e: e.tensor_tensor(out=eqb[:], in0=bTc[:, 256:768], in1=eqb[:], op=ALU.subtract),
                 reads=["eqb", "bTc"], writes=["eqb"])
            yield O("scalar", lambda e: e.activation(out=eqb[:], in_=eqb[:], func=AF.Exp), reads=["eqb"], writes=["eqb"])
            yield O("vector", lambda e: e.tensor_tensor(out=QT[:, :, :], in0=v3(qh_sb[:]), in1=v3(eqb[:]), op=ALU.mult),
                 reads=["qh_sb", "eqb"], writes=[nQTh])


        def stage34(ctx):
            i, segs, par, gg = ctx["i"], ctx["segs"], ctx["i"] % 2, ctx["gg"]
            slot = i % NXS
            xs_, xb = x_sb[slot], f"x{slot}"
            QT, QTa2, KT, V, ez, sm = QT_[par], QTa2_[par], KT_[par], V_[par], ez_[par], sm_[par]
            nQTh, nQTa, nKTa, nKTh, nVa, nVh, ngza, ngzh = (f"{n}{par}" for n in ("QTh", "QTa", "KTa", "KTh", "Va", "Vh", "gza", "gzh"))
            smE = f"smE{par}"
            PT03, PT13 = v3(PT0b), v3(PT1b)
            yield O("tensor", [lambda e, c=c: e.transpose(out=PT13[:, c, :], in_=KT[:, c, :], identity=identb[:]) for c in range(6)],
                 reads=[nKTa, nKTh, "identb"], writes=["t1"], banks=["t1"])
            for si_, sg in enumerate(segs):
                lo, n = sg["lo"], sg["n"]
                yield O("vector", lambda e, si_=si_, lo=lo, n=n: e.tensor_copy(out=Ktm2[lo:lo + n, si_, 0:6, :], in_=PT13[lo:lo + n, 0:6, :]),
                     reads=["t1"], writes=["Ktm"], banks=["t1"])
            yield O("tensor", [lambda e, h=h: e.matmul(PT0[:, h * 128:(h + 1) * 128], lhsT=KT[:, h // 2, :], rhs=QTa2[:, h % 2, h // 2, :],
                                                    start=True, stop=True) for h in range(4)],
                 reads=[nKTa, nQTa], writes=["t0"], banks=["t0"])
            yield O("vector", lambda e: e.tensor_tensor(out=AT[:, 0:4, :], in0=v3(PT0[:, 0:512]),
                                                     in1=masks[:].unsqueeze(1).broadcast_to([128, 4, 128]), op=ALU.mult),
                 reads=["t0", "masks"], writes=["ATa"], banks=["t0"])
            yield O("tensor", [lambda e, h=h: e.matmul(PT1[:, h * 128:(h + 1) * 128], lhsT=KT[:, 2 + h, :], rhs=QT[:, h, :],
                                                    start=True, stop=True) for h in range(4)],
                 reads=[nKTh, nQTh], writes=["t1"], banks=["t1"])
            yield O("vector", lambda e: e.tensor_tensor(out=AT[:, 4:8, :], in0=v3(PT1[:, 0:512]),
                                                     in1=masks[:].unsqueeze(1).broadcast_to([128, 4, 128]), op=ALU.mult),
                 reads=["t1", "masks"], writes=["ATh"], banks=["t1"])
            for si_, sg in enumerate(segs):
                lo, n, st = sg["lo"], sg["n"], sg["st"]
                yield O("vector", lambda e, si_=si_, st=st: e.tensor_tensor(
                    out=Sp[:, si_], in0=S_all[:, st], in1=sm[:, 0, :, si_].unsqueeze(2).broadcast_to([128, 6, 128]), op=ALU.mult),
                    reads=[f"S{st}", f"S{st}h", smE], writes=[f"Sp{si_}"])
                yield O("gpsimd", lambda e, si_=si_, st=st: e.tensor_tensor(
                    out=S_all[:, st], in0=S_all[:, st], in1=sm[:, 1, :, si_].unsqueeze(2).broadcast_to([128, 6, 128]), op=ALU.mult),
                    reads=[smE], writes=[f"S{st}", f"S{st}h"])
                fl = []
                for h in range(4):
                    c = h // 2
                    fl.append(lambda e, h=h, lo=lo, n=n: e.matmul(
                        PB[lo:lo + n, h * 128:(h + 1) * 128], lhsT=AT[:, h, lo:lo + n], rhs=V[:, h * 128:(h + 1) * 128],
                        start=True, stop=False))
                    fl.append(lambda e, h=h, c=c, lo=lo, n=n, si_=si_: e.matmul(
                        PB[lo:lo + n, h * 128:(h + 1) * 128], lhsT=QTa2[:, h % 2, c, lo:lo + n], rhs=Sp[:, si_, c, :],
                        start=False, stop=True))
                yield O("tensor", fl, reads=["ATa", nVa, nQTa, f"Sp{si_}"], writes=["b0"], banks=["b0"])
                fl = []
                for h in range(4):
                    fl.append(lambda e, h=h, lo=lo, n=n: e.matmul(
                        PB[lo:lo + n, (4 + h) * 128:(5 + h) * 128], lhsT=AT[:, 4 + h, lo:lo + n],
                        rhs=V[:, (4 + h) * 128:(5 + h) * 128], start=True, stop=False))
                    fl.append(lambda e, h=h, lo=lo, n=n, si_=si_: e.matmul(
                        PB[lo:lo + n, (4 + h) * 128:(5 + h) * 128], lhsT=QT[:, h, lo:lo + n], rhs=Sp[:, si_, 2 + h, :],
                        start=False, stop=True))
                yield O("tensor", fl, reads=["ATh", nVh, nQTh, f"Sp{si_}"], writes=["b1"], banks=["b1"])
                fl = []
                for h in range(4):
                    c, r0 = h // 2, (h % 2) * 64
                    fl.append(lambda e, h=h, c=c, r0=r0, si_=si_: e.matmul(
                        PT0[r0:r0 + 64, c * 128:(c + 1) * 128], lhsT=Ktm2[:, si_, c, r0:r0 + 64], rhs=V[:, h * 128:(h + 1) * 128],
                        start=True, stop=True))
                yield O("tensor", fl, reads=["Ktm", nVa], writes=["t0"], banks=["t0"])
                fl = []
                for h in range(4):
                    fl.append(lambda e, h=h, si_=si_: e.matmul(
                        PT1[:, h * 128:(h + 1) * 128], lhsT=Ktm2[:, si_, 2 + h, :], rhs=V[:, (4 + h) * 128:(5 + h) * 128],
                        start=True, stop=True))
                yield O("tensor", fl, reads=["Ktm", nVh], writes=["t1"], banks=["t1"])
                for c in range(2):
                    yield O("vector", lambda e, c=c, si_=si_, st=st: e.scalar_tensor_tensor(
                        out=S_all[:, st, c, :], in0=PT0[:, c * 128:(c + 1) * 128], scalar=sm[:, 2, c, si_:si_ + 1], in1=S_all[:, st, c, :],
                        op0=ALU.mult, op1=ALU.add),
                        reads=["t0", smE], writes=[f"S{st}"], banks=["t0"])
                for h in range(4):
                    yield O("vector", lambda e, h=h, si_=si_, st=st: e.scalar_tensor_tensor(
                        out=S_all[:, st, 2 + h, :], in0=PT1[:, h * 128:(h + 1) * 128], scalar=sm[:, 2, 2 + h, si_:si_ + 1],
                        in1=S_all[:, st, 2 + h, :], op0=ALU.mult, op1=ALU.add),
                        reads=["t1", smE], writes=[f"S{st}h"], banks=["t1"])
            for hf, gname in ((0, ngza), (1, ngzh)):
                zs = slice(hf * 512, (hf + 1) * 512)
                yield O("scalar", lambda e, zs=zs: e.activation(out=ztmp[:, zs], in_=ez[:, zs], func=AF.Exp, scale=-1.0),
                        reads=[gname], writes=[f"ztmp{hf}"])
                yield O("scalar", lambda e, zs=zs: e.activation(out=ztmp[:, zs], in_=ztmp[:, zs], func=AF.Ln, bias=1.0),
                        reads=[f"ztmp{hf}"], writes=[f"ztmp{hf}"])
                yield O("scalar", lambda e, zs=zs: e.activation(out=ztmp[:, zs], in_=ztmp[:, zs], func=AF.Exp, scale=-1.0),
                        reads=[f"ztmp{hf}"], writes=[f"ztmp{hf}"])
                yield O("vector", lambda e, zs=zs: e.tensor_tensor(out=ez[:, zs], in0=ez[:, zs], in1=ztmp[:, zs], op=ALU.mult),
                        reads=[f"ztmp{hf}"], writes=[gname])
            yield O("scalar", lambda e: e.activation(out=sq[:, 0:512], in_=PB[:, 0:512], func=AF.Square),
                 reads=["b0"], writes=["sqa"], banks=["b0"])
            yield O("scalar", lambda e: e.activation(out=sq[:, 512:1024], in_=PB[:, 512:1024], func=AF.Square),
                 reads=["b1"], writes=["sqh"], banks=["b1"])
            yield O("vector", lambda e: e.reduce_sum(out=so[:, 0:8], in_=v3(sq[:]), axis=AX.X), reads=["sqa", "sqh"], writes=["so0"])
            yield O("scalar", lambda e: e.activation(out=so[:, 8:16], in_=so[:, 0:8], func=AF.Ln, scale=1.0 / 128, bias=EPS),
                 reads=["so0"], writes=["so1"])
            yield O("scalar", lambda e: e.activation(out=so[:, 16:24], in_=so[:, 8:16], func=AF.Exp, scale=-0.5),
                 reads=["so1"], writes=["so2"])
            for h in range(8):
                yield O("vector", lambda e, h=h: e.scalar_tensor_tensor(
                    out=ohat[:, h * 128:(h + 1) * 128], in0=PB[:, h * 128:(h + 1) * 128], scalar=so[:, 16 + h:17 + h],
                    in1=ez[:, h * 128:(h + 1) * 128], op0=ALU.mult, op1=ALU.mult),
                    reads=["b0" if h < 4 else "b1", "so2", ngza if h < 4 else ngzh], writes=["ohata" if h < 4 else "ohath"],
                    banks=["b0" if h < 4 else "b1"])
            yield O("tensor", [lambda e, j=j: e.transpose(out=PT03[:, j, :], in_=ohat[:, j * 128:(j + 1) * 128], identity=identb[:])
                            for j in range(4)], reads=["ohata", "identb"], writes=["t0"], banks=["t0"])
            yield O("scalar", lambda e: e.activation(out=ohT[:, 0:4, :], in_=PT03[:, 0:4, :], func=AF.Copy),
                 reads=["t0"], writes=["ohTa"], banks=["t0"])
            yield O("tensor", [lambda e, j=j: e.transpose(out=PT13[:, j, :], in_=ohat[:, (4 + j) * 128:(5 + j) * 128], identity=identb[:])
                            for j in range(4)], reads=["ohath", "identb"], writes=["t1"], banks=["t1"])
            yield O("vector", lambda e: e.tensor_copy(out=ohT[:, 4:8, :], in_=PT13[:, 0:4, :]), reads=["t1"], writes=["ohTh"], banks=["t1"])
            for n_ in range(2):
                yield O("tensor", [lambda e, j=j, n_=n_: e.matmul(PB[:, n_ * 512:(n_ + 1) * 512], lhsT=ohT[:, j, :],
                                                                 rhs=w_out_bf[:, j, n_ * 512:(n_ + 1) * 512], start=(j == 0), stop=(j == 7))
                                for j in range(8)], reads=["ohTa", "ohTh"] + WOUT, writes=[f"b{n_}"], banks=[f"b{n_}"])
            yield O("scalar", lambda e: e.activation(out=ohat[:], in_=PB[:, :], func=AF.Square, accum_out=statB[:, 0:1]),
                 reads=["b0", "b1"], writes=["ohata", "ohath", "sb0"], banks=["b0", "b1"])
            yield O("scalar", lambda e: e.activation(out=statB[:, 1:2], in_=statB[:, 0:1], func=AF.Ln, scale=1.0 / D, bias=EPS),
                 reads=["sb0"], writes=["sb1"])
            yield O("scalar", lambda e: e.activation(out=statB[:, 2:3], in_=statB[:, 1:2], func=AF.Exp, scale=-0.5),
                 reads=["sb1"], writes=["sb2"])
            yield O("vector", lambda e: e.scalar_tensor_tensor(out=sq[:], in0=PB[:, :], scalar=statB[:, 2:3], in1=GG[gg][:],
                                                            op0=ALU.mult, op1=ALU.mult),
                 reads=["b0", "b1", "sb2", f"GG{gg}"], writes=["sqa", "sqh"], banks=["b0", "b1"])
            yield O("gpsimd", lambda e: e.tensor_tensor(out=xs_[:], in0=xs_[:], in1=sq[:], op=ALU.add), reads=[xb, "sqa", "sqh"], writes=[xb])
            yield O("sync", lambda e: e.dma_start(out=ctx["ydst"], in_=xs_[:]), reads=[xb], dma_sem=f"yst{slot}")
            k = ctx["k"]
            if k is not None:
                for st, gdst, hdst in ((2 + 2 * k, sgs[2 * k], shs[2 * k]), (3 + 2 * k, sgs[2 * k + 1], shs[2 * k + 1])):
                    yield O("sync", lambda e, st=st, gdst=gdst: e.dma_start(out=gdst.rearrange("c p v -> p c v"), in_=S_all[:, st, 0:2, :]),
                            reads=[f"S{st}"], dma_sem="sout")
                    yield O("sync", lambda e, st=st, hdst=hdst: e.dma_start(out=hdst.rearrange("c p v -> p c v"), in_=S_all[:, st, 2:6, :]),
                            reads=[f"S{st}h"], dma_sem="sout")

        def store_state(st, gdst, hdst):
            P.op("sync", lambda e: e.dma_start(out=gdst.rearrange("c p v -> p c v"), in_=S_all[:, st, 0:2, :]),
                 reads=[f"S{st}"], dma_sem="sout")
            P.op("sync", lambda e: e.dma_start(out=hdst.rearrange("c p v -> p c v"), in_=S_all[:, st, 2:6, :]),
                 reads=[f"S{st}h"], dma_sem="sout")

        tiles = []
        for k in range(2):
            segs = [dict(lo=0, n=64, b=2 + 2 * k, st=2 + 2 * k), dict(lo=64, n=64, b=3 + 2 * k, st=3 + 2 * k)]
            tiles.append(dict(xsrc=xs[2 * k:2 * k + 2].rearrange("b t d -> (b t) d"), ydst=ys[2 * k:2 * k + 2].rearrange("b t d -> (b t) d"),
                              segs=segs, gg=2 + k, k=k))
        for t in range(tp_tiles):
            for s in range(2):
                tiles.append(dict(xsrc=xp[s, t * 128:(t + 1) * 128, :], ydst=yp[s, t * 128:(t + 1) * 128, :],
                                  segs=[dict(lo=0, n=64, b=s, st=s), dict(lo=64, n=64, b=s, st=s)], gg=s, k=None))
        for i, t in enumerate(tiles):
            t["i"] = i
        NT = len(tiles)
        PRE = 2
        import os as _os2
        _os_kverb = bool(_os2.environ.get("KVERB2"))
        ALPHA = float(_os2.environ.get("KALPHA", "0.0"))
        for i in range(min(PRE, NT)):
            load_x(i, tiles[i]["xsrc"])
        for r in range(NT + 1):
            if r + PRE < NT:
                load_x(r + PRE, tiles[r + PRE]["xsrc"])
            if _os_kverb:
                print("round", r, "model t_us", {k: round(v / 1e3, 1) for k, v in P.eng_free.items()})
            if r == 0:
                for d_ in stage1(tiles[0]):
                    P.op(d_[0], d_[1], reads=d_[2], writes=d_[3], banks=d_[4], dma_sem=d_[5])
            streams = []
            if r >= 1:
                streams.append([list(stage34(tiles[r - 1])), 0])
            if r < NT:
                streams.append([list(stage2(tiles[r])), 0])
            pending = [list(stage1(tiles[r + 1])), 0] if r + 1 < NT else None
            if pending is not None and r >= NT:
                streams.append(pending)
                pending = None
            while streams:
                best, bk = None, None
                for st_ in list(streams):
                    ops, k = st_
                    while k < len(ops) and ops[k][0] == "MARK":
                        k += 1
                        st_[1] = k
                        if pending is not None:
                            streams.append(pending)
                            pending = None
                    if k >= len(ops):
                        streams.remove(st_)
                        continue
                for st_ in streams:
                    ops, k = st_
                    key = P.est_start(ops[k])
                    if bk is None or key < bk:
                        best, bk = st_, key
                if best is None:
                    break
                ops, k = best
                eng, fns, reads, writes, banks, dma_sem = ops[k]
                P.op(eng, fns, reads=reads, writes=writes, banks=banks, dma_sem=dma_sem)
                best[1] = k + 1
                if best[1] >= len(ops):
                    streams.remove(best)
            if pending is not None:
                for d_ in pending[0]:
                    P.op(d_[0], d_[1], reads=d_[2], writes=d_[3], banks=d_[4], dma_sem=d_[5])
        for s in range(2):
            store_state(s, sgp[s], shp[s])
        for nm, s in list(P.sems.items()):
            if nm.startswith("yst") or nm == "sout":
                P.wait_token("sync", (nm, s[1]))
        with nc.Block() as block:
            P.replay(block)
        P.close()
        import os as _os
        if _os.environ.get("KVERB"):
            print("total ops recorded", P.count, "model makespan us", max(P.eng_free.values()) / 1e3)
    return nc


def host_inputs(core, x_prompt, x_sample, c_prompt, c_sample, state_gla, state_hgrn, w_ada, b_ada, g_pre,
                w_in, w_alpha, b_alpha, g_onorm_gla, hgrn_lb_logits, g_onorm_hgrn, w_out, g_post, consts):
    f = np.float32
    c6 = np.concatenate([c_prompt[2 * core:2 * core + 2], c_sample[4 * core:4 * core + 4]], 0)

    def pj(a):
        return np.ascontiguousarray(a.reshape(8, 128, a.shape[1]).transpose(1, 0, 2))

    m = {
        "xp": np.ascontiguousarray(x_prompt[2 * core:2 * core + 2]),
        "xs": np.ascontiguousarray(x_sample[4 * core:4 * core + 4]),
        "cT": np.ascontiguousarray(c6.T.reshape(8, 128, 6).transpose(1, 0, 2)),
        "stg": np.ascontiguousarray(state_gla[0, 4 * core:4 * core + 4].reshape(4, 2, 128, 128)),
        "sth": np.ascontiguousarray(state_hgrn[0, 4 * core:4 * core + 4]),
        "wada": pj(w_ada[0]),
        "bada": np.ascontiguousarray(np.broadcast_to(b_ada[0][None, :], (6, 3072))),
        "gpre": np.ascontiguousarray(g_pre[0].reshape(8, 128).T),
        "win": pj(w_in[0]),
        "walpha": np.ascontiguousarray(w_alpha[0]),
        "balpha": np.ascontiguousarray(b_alpha[0].reshape(2, 128).T),
        "gon": np.ascontiguousarray(np.stack([g_onorm_gla[0], g_onorm_hgrn[0]], 1)),
        "lbl": np.ascontiguousarray(hgrn_lb_logits.reshape(2, 4, 128).transpose(2, 0, 1)),
        "wout": pj(w_out[0]),
        "gpost": np.ascontiguousarray(np.broadcast_to(g_post[0][None, :], (128, 1024))),
    }
    m.update(consts)
    return {k: np.ascontiguousarray(v, dtype=f) for k, v in m.items()}


def make_consts():
    f = np.float32
    maskp = np.triu(np.ones((128, 128), f))
    masks = maskp.copy()
    masks[0:64, 64:128] = 0.0
    smaskp = np.ones((128, 768), f)
    smaskp[:, 0::128] = 0.0
    smasks = smaskp.copy()
    smasks[:, 64::128] = 0.0
    sel = np.zeros((6, 4, 128), f)
    sel[0, 0, :] = 1.0
    sel[1, 1, :] = 1.0
    sel[2, 2, 0:64] = 1.0
    sel[3, 2, 64:128] = 1.0
    sel[4, 3, 0:64] = 1.0
    sel[5, 3, 64:128] = 1.0
    return {"identf": np.eye(128, dtype=f), "maskp": maskp, "masks": masks, "smaskp": smaskp, "smasks": smasks, "sel": sel}


def assemble(results, TP):
    f = np.float32
    yp = np.concatenate([r["yp"] for r in results], 0).astype(f)
    ys = np.concatenate([r["ys"] for r in results], 0).astype(f)
    sgp = np.concatenate([r["sgp"].reshape(2, 4, 64, 128) for r in results], 0)[None].astype(f)
    shp = np.concatenate([r["shp"] for r in results], 0)[None].astype(f)
    sgs = np.concatenate([r["sgs"].reshape(4, 4, 64, 128) for r in results], 0)[None].astype(f)
    shs = np.concatenate([r["shs"] for r in results], 0)[None].astype(f)
    return (yp, ys, sgp, shp, sgs, shs)


def kernel(**inputs):
    inputs = {k: np.asarray(v) for k, v in inputs.items()}
    TP = inputs["x_prompt"].shape[1]
    nc = build(TP // 128)
    consts = make_consts()
    in_maps = [host_inputs(i, consts=consts, **inputs) for i in range(N_CORES)]
    res = run_bass_kernel_spmd(nc, in_maps, core_ids=list(range(N_CORES)))
    return assemble(res.results, TP)
```
